# Optimizing a Trainium2 kernel written in Bass

```python
import math
import jax, jax.numpy as jnp
from jax import lax
import numpy as np

D_MODEL = 1024
BATCH = 8
SEQ = 4096
DEPTH = 2

N_HEADS = 16
HEAD_DIM = D_MODEL // N_HEADS
ROPE_DIM = HEAD_DIM // 4
ROPE_THETA = 500000.0
ATTN_SCALE = HEAD_DIM ** -0.5
DILATED_PAIRS = ((128, 1), (512, 4), (2048, 16))
N_DIL = len(DILATED_PAIRS)
BAND_BLOCK = 128
MOBA_BLOCK = 256
MOBA_TOPK = 3
MOBA_QCHUNK = 128
D_FF = 2816
CONV_WIDTH = 3
RMS_EPS = 1e-6
N_LAYERS_A = (DEPTH + 1) // 2
N_LAYERS_B = DEPTH // 2

kernel_name = "hybrid_dilated_moba_convffn"


def rmsnorm(x, gain):
    xf = x.astype(jnp.float32)
    y = xf * lax.rsqrt(jnp.mean(xf * xf, axis=-1, keepdims=True) + RMS_EPS)
    return (y * gain.astype(jnp.float32)).astype(x.dtype)


def rope_tables(seq_len):
    pos = jnp.arange(seq_len, dtype=jnp.float32)
    inv_freq = ROPE_THETA ** (-jnp.arange(0, ROPE_DIM, 2, dtype=jnp.float32) / ROPE_DIM)
    ang = pos[:, None] * inv_freq[None, :]
    return jnp.cos(ang), jnp.sin(ang)


def apply_partial_rope(x, cos, sin):
    extra = x.ndim - 3
    c = cos.reshape((cos.shape[0],) + (1,) * extra + (cos.shape[1],)).astype(x.dtype)
    s = sin.reshape((sin.shape[0],) + (1,) * extra + (sin.shape[1],)).astype(x.dtype)
    half = ROPE_DIM // 2
    x1, x2, xp = x[..., :half], x[..., half:ROPE_DIM], x[..., ROPE_DIM:]
    return jnp.concatenate([x1 * c - x2 * s, x2 * c + x1 * s, xp], axis=-1)


def band_attention(q, k, v, dil, window):
    S, H, hd = q.shape
    reach = window // dil
    assert reach <= BAND_BLOCK
    span = dil * BAND_BLOCK
    s_pad = -(-S // span) * span
    L = s_pad // dil
    nb = L // BAND_BLOCK

    def to_res(t):
        t = jnp.pad(t, ((0, s_pad - S), (0, 0), (0, 0)))
        return t.reshape(L, dil, H, hd).transpose(1, 0, 2, 3).reshape(dil, nb, BAND_BLOCK, H, hd)

    qr, kr, vr = to_res(q), to_res(k), to_res(v)

    def with_prev(t):
        prev = jnp.concatenate([jnp.zeros_like(t[:, :1]), t[:, :-1]], axis=1)
        return jnp.concatenate([prev, t], axis=2)

    kc, vc = with_prev(kr), with_prev(vr)
    scores = jnp.einsum('rnqhd,rnkhd->rnhqk', qr, kc).astype(jnp.float32) * ATTN_SCALE
    qi = jnp.arange(BAND_BLOCK)
    kj = jnp.arange(2 * BAND_BLOCK)
    dist = qi[:, None] + BAND_BLOCK - kj[None, :]
    kpos = jnp.arange(nb)[:, None] * BAND_BLOCK - BAND_BLOCK + kj[None, :]
    mask = ((dist >= 0) & (dist <= reach))[None] & (kpos >= 0)[:, None, :]
    scores = jnp.where(mask[None, :, None], scores, -jnp.inf)
    m = scores.max(axis=-1)
    p = jnp.exp(scores - m[..., None])
    den = p.sum(axis=-1)
    num = jnp.einsum('rnhqk,rnkhd->rnqhd', p, vc.astype(jnp.float32))
    m_t = m.transpose(0, 1, 3, 2)
    den_t = den.transpose(0, 1, 3, 2)
    o = num / den_t[..., None]

    def from_res(t):
        t = t.reshape((dil, L) + t.shape[3:])
        t = jnp.swapaxes(t, 0, 1)
        return t.reshape((s_pad,) + t.shape[2:])[:S]

    return from_res(o), from_res(m_t), from_res(den_t)


def mixer_dilated(h, w_qkv, q_gain, k_gain, w_o, cos, sin):
    B, S, _ = h.shape
    qkv = (h @ w_qkv).reshape(B, S, 3, N_DIL, N_HEADS, HEAD_DIM)
    q = apply_partial_rope(rmsnorm(qkv[:, :, 0], q_gain[:, None, :]), cos, sin)
    k = apply_partial_rope(rmsnorm(qkv[:, :, 1], k_gain[:, None, :]), cos, sin)
    v = qkv[:, :, 2]

    def one_sequence(args):
        qb, kb, vb = args
        outs, maxes, dens = [], [], []
        for g, (window, dil) in enumerate(DILATED_PAIRS):
            o, m, d = band_attention(qb[:, g], kb[:, g], vb[:, g], dil, window)
            outs.append(o)
            maxes.append(m)
            dens.append(d)
        o = jnp.stack(outs)
        m = jnp.stack(maxes)
        d = jnp.stack(dens)
        wgt = d * jnp.exp(m - m.max(axis=0, keepdims=True))
        wgt = wgt / wgt.sum(axis=0, keepdims=True)
        return jnp.einsum('gsh,gshd->shd', wgt, o)

    attn = lax.map(one_sequence, (q, k, v))
    return attn.reshape(B, S, N_HEADS * HEAD_DIM).astype(h.dtype) @ w_o


def mixer_moba(h, w_qkv, q_gain, k_gain, w_o, cos, sin):
    B, S, _ = h.shape
    qkv = (h @ w_qkv).reshape(B, S, 3, N_HEADS, HEAD_DIM)
    q = apply_partial_rope(rmsnorm(qkv[:, :, 0], q_gain), cos, sin)
    k = apply_partial_rope(rmsnorm(qkv[:, :, 1], k_gain), cos, sin)
    v = qkv[:, :, 2]
    s_pad = -(-S // MOBA_BLOCK) * MOBA_BLOCK
    nblk = s_pad // MOBA_BLOCK
    n_chunk = s_pad // MOBA_QCHUNK
    topk = min(MOBA_TOPK, nblk)

    def to_bhsd(t):
        return jnp.pad(t, ((0, 0), (0, s_pad - S), (0, 0), (0, 0))).transpose(0, 2, 1, 3)

    q, k, v = to_bhsd(q), to_bhsd(k), to_bhsd(v)
    k_blocks = k.reshape(B, N_HEADS, nblk, MOBA_BLOCK, HEAD_DIM)
    v_blocks = v.reshape(B, N_HEADS, nblk, MOBA_BLOCK, HEAD_DIM)
    k_mean = k_blocks.astype(jnp.float32).mean(axis=3)
    gate = jnp.einsum('bhsd,bhnd->bhsn', q.astype(jnp.float32), k_mean)
    own_blk = jnp.arange(s_pad) // MOBA_BLOCK
    fully_past = jnp.arange(nblk)[None, :] < own_blk[:, None]
    gate = jnp.where(fully_past, gate, -jnp.inf)
    _, sel = lax.top_k(gate, topk)

    def to_chunks(t):
        t = t.reshape(B, N_HEADS, n_chunk, MOBA_QCHUNK, t.shape[-1])
        return t.transpose(0, 2, 1, 3, 4).reshape(B * n_chunk, N_HEADS, MOBA_QCHUNK, t.shape[-1])

    q_chunks = to_chunks(q)
    sel_chunks = to_chunks(sel)
    chunk_ids = jnp.arange(B * n_chunk, dtype=jnp.int32)
    n_sel = topk * MOBA_BLOCK

    def one_chunk(args):
        qc, selc, cid = args
        b = cid // n_chunk
        c = cid % n_chunk
        kb = k_blocks[b]
        vb = v_blocks[b]
        qpos = c * MOBA_QCHUNK + jnp.arange(MOBA_QCHUNK)
        ob = (c * MOBA_QCHUNK) // MOBA_BLOCK
        h_idx = jnp.arange(N_HEADS)[:, None, None]
        k_sel = kb[h_idx, selc]
        v_sel = vb[h_idx, selc]
        s_sel = jnp.einsum('hqd,hqnkd->hqnk', qc, k_sel).astype(jnp.float32) * ATTN_SCALE
        valid = jnp.arange(topk) < ob
        s_sel = jnp.where(valid[None, None, :, None], s_sel, -jnp.inf)
        s_sel = s_sel.reshape(N_HEADS, MOBA_QCHUNK, n_sel)
        k_own = lax.dynamic_index_in_dim(kb, ob, axis=1, keepdims=False)
        v_own = lax.dynamic_index_in_dim(vb, ob, axis=1, keepdims=False)
        s_own = jnp.einsum('hqd,hkd->hqk', qc, k_own).astype(jnp.float32) * ATTN_SCALE
        kpos = ob * MOBA_BLOCK + jnp.arange(MOBA_BLOCK)
        s_own = jnp.where((kpos[None, :] <= qpos[:, None])[None], s_own, -jnp.inf)
        p = jax.nn.softmax(jnp.concatenate([s_sel, s_own], axis=-1), axis=-1).astype(v_sel.dtype)
        o = jnp.einsum('hqn,hqnd->hqd', p[..., :n_sel],
                       v_sel.reshape(N_HEADS, MOBA_QCHUNK, n_sel, HEAD_DIM))
        return o + jnp.einsum('hqk,hkd->hqd', p[..., n_sel:], v_own)

    out = lax.map(one_chunk, (q_chunks, sel_chunks, chunk_ids))
    out = out.reshape(B, n_chunk, N_HEADS, MOBA_QCHUNK, HEAD_DIM).transpose(0, 1, 3, 2, 4)
    out = out.reshape(B, s_pad, N_HEADS * HEAD_DIM)[:, :S]
    return out.astype(h.dtype) @ w_o


def conv_ffn(h, w_up, conv_w, conv_b, w_down):
    S = h.shape[1]
    u = h @ w_up
    up = jnp.pad(u, ((0, 0), (CONV_WIDTH - 1, 0), (0, 0)))
    uc = conv_b + sum(conv_w[j] * up[:, j:j + S] for j in range(CONV_WIDTH))
    gate, val = uc[..., :D_FF], uc[..., D_FF:]
    return (jax.nn.silu(gate) * val) @ w_down


def setup_inputs(seed: int = 0) -> dict:
    key = jax.random.key(seed)
    ks = jax.random.split(key, 16)
    hd_all = N_HEADS * HEAD_DIM

    def w(k, shape, fan_in):
        return jax.random.normal(k, shape, jnp.float32) * fan_in ** -0.5

    def gain(k, shape):
        return 1.0 + 0.1 * jax.random.normal(k, shape, jnp.float32)

    return {
        'x': jax.random.normal(ks[0], (BATCH, SEQ, D_MODEL), jnp.float32),
        'attn_norm': gain(ks[1], (DEPTH, D_MODEL)),
        'a_w_qkv': w(ks[2], (N_LAYERS_A, D_MODEL, 3 * N_DIL * hd_all), D_MODEL),
        'a_q_norm': gain(ks[3], (N_LAYERS_A, N_DIL, HEAD_DIM)),
        'a_k_norm': gain(ks[4], (N_LAYERS_A, N_DIL, HEAD_DIM)),
        'a_w_o': w(ks[5], (N_LAYERS_A, hd_all, D_MODEL), hd_all),
        'b_w_qkv': w(ks[6], (N_LAYERS_B, D_MODEL, 3 * hd_all), D_MODEL),
        'b_q_norm': gain(ks[7], (N_LAYERS_B, HEAD_DIM)),
        'b_k_norm': gain(ks[8], (N_LAYERS_B, HEAD_DIM)),
        'b_w_o': w(ks[9], (N_LAYERS_B, hd_all, D_MODEL), hd_all),
        'ffn_norm': gain(ks[10], (DEPTH, D_MODEL)),
        'ffn_w_up': w(ks[11], (DEPTH, D_MODEL, 2 * D_FF), D_MODEL),
        'ffn_conv_w': w(ks[12], (DEPTH, CONV_WIDTH, 2 * D_FF), CONV_WIDTH),
        'ffn_conv_b': 0.01 * jax.random.normal(ks[13], (DEPTH, 2 * D_FF), jnp.float32),
        'ffn_w_down': w(ks[14], (DEPTH, D_FF, D_MODEL), D_FF),
    }


def reference(x, attn_norm, a_w_qkv, a_q_norm, a_k_norm, a_w_o,
              b_w_qkv, b_q_norm, b_k_norm, b_w_o,
              ffn_norm, ffn_w_up, ffn_conv_w, ffn_conv_b, ffn_w_down):
    cos, sin = rope_tables(x.shape[1])
    for i in range(DEPTH):
        hn = rmsnorm(x, attn_norm[i])
        if i % 2 == 0:
            j = i // 2
            x = x + mixer_dilated(hn, a_w_qkv[j], a_q_norm[j], a_k_norm[j], a_w_o[j], cos, sin)
        else:
            j = i // 2
            x = x + mixer_moba(hn, b_w_qkv[j], b_q_norm[j], b_k_norm[j], b_w_o[j], cos, sin)
        hn = rmsnorm(x, ffn_norm[i])
        x = x + conv_ffn(hn, ffn_w_up[i], ffn_conv_w[i], ffn_conv_b[i], ffn_w_down[i])
    return x
```

```python
import contextlib
import os
import numpy as np
import ml_dtypes
import concourse.bass as bass
import concourse.mybir as mybir
from concourse.bass_utils import run_bass_kernel_spmd

F32 = mybir.dt.float32
BF16 = mybir.dt.bfloat16
AF = mybir.ActivationFunctionType
ALU = mybir.AluOpType
AX = mybir.AxisListType

S = 4096
D = 1024
NT = S // 128
H = 16
HD = 64
DFF = 2816
NFF = DFF // 128
EPS = 1e-6
BIG = 30000.0
DILS = (1, 4, 16)
ENGS = ("tensor", "vector", "scalar", "gpsimd", "sync")


class Tok:
    __slots__ = ("sem", "val")

    def __init__(self, sem, val):
        self.sem = sem
        self.val = val


class Sems:
    def __init__(self, nc):
        self.nc = nc
        self.stack = contextlib.ExitStack()
        self.h = {}
        self.cnt = {}

    def get(self, name):
        if name not in self.h:
            self.h[name] = self.stack.enter_context(self.nc.semaphore(name))
            self.cnt[name] = 0
        return name


class Prog:
    def __init__(self, nc, sems, tag):
        self.nc = nc
        self.S = sems
        self.tag = tag
        self.q = {e: [] for e in ENGS}
        self.stack = contextlib.ExitStack()
        self.seen = {e: {} for e in ENGS}

    def sb(self, name, shape, dt):
        return self.stack.enter_context(self.nc.sbuf_tensor(self.tag + name, shape, dt))

    def ps(self, name, shape, dt):
        return self.stack.enter_context(self.nc.psum_tensor(self.tag + name, shape, dt))

    def op(self, eng, fn, waits=(), sem=None, dma=False):
        if sem is None:
            sem = "s_" + eng
        sem = self.S.get(self.tag + sem)
        need = {}

        def add(t):
            if t is None:
                return
            if isinstance(t, (list, tuple)):
                for u in t:
                    add(u)
                return
            if need.get(t.sem, -1) < t.val:
                need[t.sem] = t.val
        add(list(waits))
        seen = self.seen[eng]
        wl = []
        for s, v in need.items():
            if seen.get(s, -1) >= v:
                continue
            seen[s] = v
            wl.append((s, v))
        inc = 16 if dma else 1
        self.S.cnt[sem] += inc
        self.q[eng].append((wl, fn, sem, inc))
        return Tok(sem, self.S.cnt[sem])

    def flush(self):
        nc = self.nc
        sems = self.S.h
        qs = self.q
        with nc.Block() as block:
            def mk(name):
                def body(e):
                    for wl, fn, sem, inc in qs[name]:
                        for s, v in wl:
                            e.wait_ge(sems[s], v)
                        fn(e).then_inc(sems[sem], inc)
                return body
            for name in ENGS:
                if qs[name]:
                    getattr(block, name)(mk(name))
        self.stack.close()


def mk_identity(P, ident, dt_one=1.0):
    t0 = P.op("gpsimd", lambda e: e.memset(ident[:], 0.0))
    return P.op("gpsimd", lambda e: e.affine_select(out=ident[:], in_=ident[:], pattern=[[-1, 128]],
                                                    compare_op=ALU.not_equal, fill=1.0, base=0,
                                                    channel_multiplier=1), waits=[t0])


def phase_qkv(nc, sems, tag, T, layer):
    P = Prog(nc, sems, tag)
    moba = layer == 1
    ngrp = 1 if moba else 3
    xsrc = T["xin"][layer]
    wq = T["b_w_qkv"] if moba else T["a_w_qkv"]

    ident_b = P.sb("identb", [128, 128], BF16)
    ident_f = P.sb("identf", [128, 128], F32)
    t_idb = mk_identity(P, ident_b)
    t_idf = mk_identity(P, ident_f)
    gnorm = P.sb("gnorm", [128, D], F32)
    t_gn = P.op("sync", lambda e: e.dma_start(out=gnorm[:], in_=T["attn_norm"][layer:layer + 1, :].partition_broadcast(128)),
                sem="d_c", dma=True)
    gqk = P.sb("gqk", [128, ngrp, 2, HD], F32)
    t_gq = []
    for g in range(ngrp):
        for s_, nm in ((0, "q"), (1, "k")):
            src = (T["b_%s_norm" % nm][0:1, :] if moba else T["a_%s_norm" % nm][0, g:g + 1, :])
            t_gq.append(P.op("sync", lambda e, g=g, s_=s_, src=src: e.dma_start(
                out=gqk[:, g, s_, :], in_=src.partition_broadcast(128)), sem="d_c", dma=True))
    ropec = P.sb("ropec", [128, NT, 8], F32)
    ropes = P.sb("ropes", [128, NT, 8], F32)
    wsb = [P.sb("w%d" % i, [128, 8, 3, 1024], BF16) for i in range(2 if not moba else 1)]
    xt = [P.sb("xt%d" % i, [128, D], F32) for i in range(2)]
    junk = P.sb("junk", [128, 2048], F32)
    ssq = P.sb("ssq", [128, 1], F32)
    sd = P.sb("sd", [128, 1], F32)
    rstd = P.sb("rstd", [128, 1], F32)
    hn = P.sb("hn", [128, D], BF16)
    hnT = [P.sb("hnT%d" % i, [128, 8, 128], BF16) for i in range(2)]
    ss2 = P.sb("ss2", [128, 32], F32)
    sd2 = P.sb("sd2", [128, 32], F32)
    rs2 = P.sb("rs2", [128, 32], F32)
    qk = [P.sb("qk%d" % i, [128, 2, H, HD], F32) for i in range(2)]
    rt = [P.sb("rt%d" % i, [128, 2 * H, 8], F32) for i in range(4)]
    vaug = [P.sb("vaug%d" % i, [128, H, 128], BF16) for i in range(2)]
    qT4 = [P.sb("qT4_%d" % i, [128, 8, 512], BF16) for i in range(2)]
    kT4 = [P.sb("kT4_%d" % i, [128, 8, 512], BF16) for i in range(2)]
    pT = P.ps("pT", [128, 512], F32)
    pTb = pT[:].bitcast(BF16)
    pqk = P.ps("pqk", [128, 2048], F32)
    pv = P.ps("pv", [128, 512], F32)
    ptr = P.ps("ptr", [128, 8, 128], F32)
    if moba:
        qT32 = P.sb("qT32", [128, 8, 128], F32)
        ksum = P.sb("ksum", [128, 8, 16], F32)
        kpart = P.sb("kpart", [128, 8], F32)
        gs = P.sb("gs", [128, H, 16], F32)
        t8 = P.sb("t8", [128, H, 8], F32)
        mv = P.sb("mv", [128, H, 16], F32)
        mT4 = [P.sb("mT4_%d" % i, [128, 2, 512], BF16) for i in range(2)]

    t_ones = []
    for i in range(2):
        t_ones.append(P.op("gpsimd", lambda e, i=i: e.memset(vaug[i][:], 1.0)))
    t_ks0 = P.op("gpsimd", lambda e: e.memset(ksum[:], 0.0)) if moba else None

    last = {}

    def L(k):
        return last.get(k)

    tw_use = [None, None]
    t_outdma = []
    tiles = [(g, n) for g in range(ngrp) for n in range(NT)]
    state = {}

    def load_w(g):
        wb = wsb[g % len(wsb)]
        toks = []
        for kc in range(8):
            for s_ in range(3):
                if moba:
                    c0 = s_ * 1024
                else:
                    c0 = (s_ * 3 + g) * 1024
                toks.append(P.op("gpsimd", lambda e, wb=wb, kc=kc, s_=s_, c0=c0: e.dma_start(
                    out=wb[:, kc, s_, :], in_=wq[0, kc * 128:(kc + 1) * 128, c0:c0 + 1024]),
                    waits=[tw_use[g % len(wsb)]], sem="d_w%d" % (g % len(wsb)), dma=True))
        return toks

    def load_rope(g):
        v = 0 if moba else g
        a = P.op("sync", lambda e: e.dma_start(out=ropec[:], in_=T["ropec"][v]), waits=[L("rope_use")], sem="d_r", dma=True)
        b = P.op("sync", lambda e: e.dma_start(out=ropes[:], in_=T["ropes"][v]), waits=[L("rope_use")], sem="d_r", dma=True)
        return [a, b]

    def x_rows(g, n):
        dil = 1 if moba else DILS[g]
        Lg = S // dil
        p0 = n * 128
        r, a0 = divmod(p0, Lg)
        v = xsrc.rearrange("(a r) d -> r a d", r=dil)
        return v[r, a0:a0 + 128, :]

    def stage1(idx):
        g, n = tiles[idx]
        st = state.setdefault(idx, {})
        b = idx % 2
        if n == 0:
            if g == 0:
                state["w0"] = load_w(0)
            state["rope%d" % g] = load_rope(g)
        if n == 1 and g + 1 < ngrp:
            state["w%d" % (g + 1)] = load_w(g + 1)
        tw = state["w%d" % g]
        wb = wsb[g % len(wsb)]
        t_x = P.op("sync", lambda e: e.dma_start(out=xt[b][:], in_=x_rows(g, n)),
                   waits=[L("xt_use%d" % b)], sem="d_x%d" % b, dma=True)
        t_sq = P.op("scalar", lambda e: e.activation(out=junk[:, 0:D], in_=xt[b][:], func=AF.Square, accum_out=ssq[:]),
                    waits=[t_x, L("junk"), L("ssq_use")])
        t_sd = P.op("scalar", lambda e: e.activation(out=sd[:], in_=ssq[:], func=AF.Sqrt, scale=1.0 / D, bias=EPS),
                    waits=[t_sq, L("sd_use")])
        last["ssq_use"] = t_sd
        t_rs = P.op("vector", lambda e: e.reciprocal(out=rstd[:], in_=sd[:]), waits=[t_sd, L("rstd_use")])
        last["sd_use"] = t_rs
        t_hn = P.op("vector", lambda e: e.scalar_tensor_tensor(out=hn[:], in0=xt[b][:], scalar=rstd[:, 0:1], in1=gnorm[:],
                                                               op0=ALU.mult, op1=ALU.mult),
                    waits=[t_rs, t_gn, L("hn_use")])
        last["rstd_use"] = t_hn
        last["xt_use%d" % b] = t_hn
        last["junk"] = t_sq
        tt = None
        for kc in range(8):
            tt = P.op("tensor", lambda e, kc=kc: e.transpose(out=pTb[:, kc * 128:(kc + 1) * 128], in_=hn[:, kc * 128:(kc + 1) * 128],
                                                             identity=ident_b[:]),
                      waits=[t_hn, t_idb, L("pT_use")])
        last["hn_use"] = tt
        t_hT = P.op("scalar", lambda e: e.activation(out=hnT[b][:].rearrange("p a b -> p (a b)"), in_=pTb, func=AF.Copy),
                    waits=[tt, L("hnT_use%d" % b)])
        last["pT_use"] = t_hT
        st["hT"] = t_hT

        def mm(dst, s_, c0, extra):
            t = None
            for kc in range(8):
                t = P.op("tensor", lambda e, kc=kc: e.matmul(dst, lhsT=hnT[b][:, kc, :], rhs=wb[:, kc, s_, c0:c0 + 512],
                                                            start=(kc == 0), stop=(kc == 7)),
                         waits=[t_hT, tw, extra])
            return t
        tq = None
        for s_ in range(2):
            for hf in range(2):
                tq = mm(pqk[:, s_ * 1024 + hf * 512: s_ * 1024 + hf * 512 + 512], s_, hf * 512, L("pqk_use"))
        st["tqk"] = tq
        qkb = qk[b]
        t_sq2 = P.op("scalar", lambda e: e.activation(out=junk[:], in_=pqk[:], func=AF.Square), waits=[tq, L("junk")])
        t_ss2 = P.op("vector", lambda e: e.tensor_reduce(out=ss2[:], in_=junk[:].rearrange("p (h d) -> p h d", d=HD),
                                                         op=ALU.add, axis=AX.X), waits=[t_sq2, L("ss2_use")])
        last["junk"] = t_ss2
        t_sd2 = P.op("scalar", lambda e: e.activation(out=sd2[:], in_=ss2[:], func=AF.Sqrt, scale=1.0 / HD, bias=EPS),
                     waits=[t_ss2, L("sd2_use")])
        last["ss2_use"] = t_sd2
        t_rs2 = P.op("vector", lambda e: e.reciprocal(out=rs2[:], in_=sd2[:]), waits=[t_sd2, L("rs2_use")])
        last["sd2_use"] = t_rs2
        t_nm = P.op("vector", lambda e: e.tensor_tensor(out=qkb[:].rearrange("p s h d -> p (s h) d"),
                                                        in0=pqk[:].rearrange("p (h d) -> p h d", d=HD),
                                                        in1=rs2[:].unsqueeze(2).broadcast_to([128, 32, HD]), op=ALU.mult),
                    waits=[t_rs2, L("qk_use%d" % b)])
        last["rs2_use"] = t_nm
        last["pqk_use"] = t_nm
        gsel = 0 if moba else g
        t_g = P.op("gpsimd", lambda e: e.tensor_tensor(out=qkb[:], in0=qkb[:],
                                                       in1=gqk[:, gsel, :, :].unsqueeze(2).broadcast_to([128, 2, H, HD]),
                                                       op=ALU.mult), waits=[t_nm] + t_gq)
        qv = qkb[:].rearrange("p s h d -> p (s h) d")
        x1 = qv[:, :, 0:8]
        x2 = qv[:, :, 8:16]
        cb = ropec[:, n, :].unsqueeze(1).broadcast_to([128, 32, 8])
        sbb = ropes[:, n, :].unsqueeze(1).broadcast_to([128, 32, 8])
        tr = state["rope%d" % g]
        w0 = [t_g, tr, L("rt_use")]
        a1 = P.op("vector", lambda e: e.tensor_tensor(out=rt[0][:], in0=x1, in1=cb, op=ALU.mult), waits=w0)
        a2 = P.op("vector", lambda e: e.tensor_tensor(out=rt[1][:], in0=x2, in1=sbb, op=ALU.mult), waits=w0)
        a3 = P.op("vector", lambda e: e.tensor_tensor(out=rt[2][:], in0=x2, in1=cb, op=ALU.mult), waits=w0)
        a4 = P.op("vector", lambda e: e.tensor_tensor(out=rt[3][:], in0=x1, in1=sbb, op=ALU.mult), waits=w0)
        a5 = P.op("vector", lambda e: e.tensor_tensor(out=x1, in0=rt[0][:], in1=rt[1][:], op=ALU.subtract), waits=[a1, a2, a3, a4])
        a6 = P.op("vector", lambda e: e.tensor_tensor(out=x2, in0=rt[2][:], in1=rt[3][:], op=ALU.add), waits=[a1, a2, a3, a4, a5])
        last["rt_use"] = a6
        last["rope_use"] = a6
        st["qkg"] = [a5, a6]
        va = vaug[b]
        tv = None
        for hf in range(2):
            tvm = mm(pv[:], 2, hf * 512, L("pv_use"))
            pvv = pv[:].rearrange("p (i e d) -> p i e d", e=2, d=HD)
            vv = va[:, hf * 8:(hf + 1) * 8, :].rearrange("p (i e) c -> p i e c", e=2)
            tv0 = P.op("scalar", lambda e, pvv=pvv, vv=vv: e.activation(out=vv[:, :, 0, 0:HD], in_=pvv[:, :, 0, :], func=AF.Copy),
                       waits=[tvm, L("vaug_use%d" % b)] + t_ones)
            tv = P.op("scalar", lambda e, pvv=pvv, vv=vv: e.activation(out=vv[:, :, 1, HD:128], in_=pvv[:, :, 1, :], func=AF.Copy),
                      waits=[tvm, L("vaug_use%d" % b)] + t_ones)
            last["pv_use"] = tv
        tw_use[g % len(wsb)] = tv
        t_vo = P.op("sync", lambda e: e.dma_start(out=T["Vs"][layer][g, n * 128:(n + 1) * 128, :, :], in_=va[:]),
                    waits=[tv], sem="d_vo%d" % b, dma=True)
        last["vaug_use%d" % b] = t_vo
        t_outdma.append(t_vo)
        last["hnT_use%d" % b] = tvm

    def stage2(idx):
        g, n = tiles[idx]
        st = state[idx]
        b = idx % 2
        qkb = qk[b]
        n4, j4 = divmod(n, 4)
        b4 = (idx // 4) % 2
        for s_, dst4, nm in ((0, qT4[b4], "Q"), (1, kT4[b4], "K")):
            tt = None
            for c in range(8):
                tt = P.op("tensor", lambda e, c=c, s_=s_: e.transpose(out=ptr[:, c, :], in_=qkb[:, s_, 2 * c:2 * c + 2, :].rearrange("p h d -> p (h d)"),
                                                                      identity=ident_f[:]),
                          waits=[st["qkg"], t_idf, L("ptr_use")])
            wv = [tt, L("%s4_use%d" % (nm, b4))]
            te = P.op("scalar", lambda e, dst4=dst4: e.activation(out=dst4[:, :, j4 * 128:(j4 + 1) * 128], in_=ptr[:], func=AF.Copy),
                      waits=wv)
            tl = [te]
            if moba and s_ == 0:
                tl.append(P.op("vector", lambda e: e.tensor_copy(out=qT32[:], in_=ptr[:]), waits=[tt, te, L("qT32_use")]))
            if moba and s_ == 1:
                tkp = P.op("vector", lambda e: e.tensor_reduce(out=kpart[:], in_=ptr[:], op=ALU.add, axis=AX.X),
                           waits=[tt, te, L("kpart_use")])
                j = n // 2
                tks = P.op("vector", lambda e, j=j: e.tensor_tensor(out=ksum[:, :, j], in0=ksum[:, :, j], in1=kpart[:], op=ALU.add),
                           waits=[tkp, t_ks0, L("ksum_w"), L("qT32_use")])
                last["kpart_use"] = tks
                last["ksum_w"] = tks
                tl.append(tks)
            last["ptr_use"] = tl
            st["te%d" % s_] = te
        last["qk_use%d" % b] = last["ptr_use"]
        if moba:
            stage_gate(idx)
        if j4 == 3:
            for s_, src4, nm, dstT in ((0, qT4[b4], "Q", T["QTs"][layer]), (1, kT4[b4], "K", T["KTs"][layer])):
                tds = []
                for e2 in range(2):
                    dv = dstT[g].rearrange("(c e) r t -> e r c t", e=2)[e2, 0:HD, :, n4 * 512:(n4 + 1) * 512]
                    tds.append(P.op("sync", lambda e, dv=dv, src4=src4, e2=e2: e.dma_start(out=dv, in_=src4[e2 * HD:(e2 + 1) * HD, :, :]),
                                    waits=[state[idx - 3]["te%d" % s_], state[idx - 2]["te%d" % s_], state[idx - 1]["te%d" % s_], st["te%d" % s_]],
                                    sem="d_%so%d" % (nm, b4), dma=True))
                last["%s4_use%d" % (nm, b4)] = tds
                t_outdma.extend(tds)
            if moba:
                tds = []
                m4 = mT4[b4]
                for h in range(H):
                    dv = T["QTs"][layer][0, h, HD:HD + 16, n4 * 512:(n4 + 1) * 512]
                    tds.append(P.op("gpsimd", lambda e, dv=dv, h=h, m4=m4: e.dma_start(out=dv, in_=m4[(h % 8) * 16:(h % 8) * 16 + 16, h // 8, :]),
                                    waits=[state[idx - 3]["tm"], state[idx - 2]["tm"], state[idx - 1]["tm"], st["tm"]],
                                    sem="d_mo%d" % b4, dma=True))
                last["m4_use%d" % b4] = tds
                t_outdma.extend(tds)

    def stage_gate(idx):
        g, n = tiles[idx]
        st = state[idx]
        b4 = (idx // 4) % 2
        j4 = n % 4
        ob = n // 2
        pg = pT[:, 0:256]
        tg = None
        tq32 = last["ptr_use"]
        for h in range(H):
            r0 = (h % 2) * HD
            tg = P.op("tensor", lambda e, h=h, r0=r0: e.matmul(pg[:, h * 16:(h + 1) * 16], lhsT=qT32[r0:r0 + HD, h // 2, :],
                                                                rhs=ksum[r0:r0 + HD, h // 2, :], start=True, stop=True),
                      waits=[tq32, L("ksum_w"), L("pT_use")])
        last["qT32_use"] = tg
        t_gs = P.op("vector", lambda e: e.tensor_copy(out=gs[:].rearrange("p h j -> p (h j)"), in_=pg), waits=[tg, L("gs_use")])
        last["pT_use"] = t_gs
        t_ms = t_gs
        if ob < 16:
            t_ms = P.op("vector", lambda e: e.memset(gs[:, :, ob:16], -1e30), waits=[t_gs])
        tm8 = []
        for h in range(H):
            tm8.append(P.op("vector", lambda e, h=h: e.max(out=t8[:, h, :], in_=gs[:, h, :]), waits=[t_ms, L("t8_use")]))
        t_sel = P.op("vector", lambda e: e.tensor_tensor(out=mv[:], in0=gs[:], in1=t8[:, :, 2:3].broadcast_to([128, H, 16]), op=ALU.is_ge),
                     waits=tm8 + [L("mv_use")])
        last["t8_use"] = t_sel
        last["gs_use"] = t_sel
        t_mv = P.op("vector", lambda e: e.tensor_scalar(out=mv[:], in0=mv[:], scalar1=-1.0, scalar2=BIG, op0=ALU.add, op1=ALU.mult),
                    waits=[t_sel])
        t_own = P.op("vector", lambda e: e.memset(mv[:, :, ob:ob + 1], 0.0), waits=[t_mv])
        pm = pT[:, 256:512].rearrange("p (a q) -> p a q", a=2)
        tt = None
        for a in range(2):
            tt = P.op("tensor", lambda e, a=a: e.transpose(out=pm[:, a, :], in_=mv[:, a * 8:(a + 1) * 8, :].rearrange("p h j -> p (h j)"),
                                                           identity=ident_f[:]), waits=[t_own, t_idf, L("pm_use")])
        last["mv_use"] = tt
        tm = P.op("vector", lambda e: e.tensor_copy(out=mT4[b4][:, :, j4 * 128:(j4 + 1) * 128], in_=pm), waits=[tt, L("m4_use%d" % b4)])
        last["pm_use"] = tm
        last["pT_use"] = [last["pT_use"], tm]
        st["tm"] = tm

    for idx in range(len(tiles)):
        stage1(idx)
        if idx >= 1:
            stage2(idx - 1)
    stage2(len(tiles) - 1)
    P.op("gpsimd", lambda e: e.memset(ssq[:], 0.0), waits=t_outdma + [L("ssq_use")])
    P.flush()


def phase_attn(nc, sems, tag, T, layer):
    P = Prog(nc, sems, tag)
    moba = layer == 1
    ngrp = 1 if moba else 3
    xsrc = T["xin"][layer]
    xdst = T["xmid"][layer]
    wo = T["b_w_o"] if moba else T["a_w_o"]
    QTs, KTs, Vs = T["QTs"][layer], T["KTs"][layer], T["Vs"][layer]
    KR = HD + 16 if moba else HD

    attnT = P.sb("attnT", [128, 8, S], BF16)
    acc = [P.sb("acc%d" % i, [128, S], F32) for i in range(2)]
    den = P.sb("den", [128, S], F32)
    qt = [P.sb("qt%d" % i, [KR, S], BF16) for i in range(2)]
    kt = [P.sb("kt%d" % i, [KR, S], BF16) for i in range(2)]
    vt = [P.sb("vt%d" % i, [128, NT, 128], BF16) for i in range(2)]
    pbuf = [P.sb("pb%d" % i, [128, 1024], BF16) for i in range(3)]
    masks = P.sb("masks", [128, 3, 1024], BF16)
    wosb = P.sb("wosb", [128, 8, D], BF16)
    fin = P.sb("fin", [128, 1], F32)
    psS = [P.ps("psS%d" % i, [128, 1024], F32) for i in range(3)]
    psO = [P.ps("psO%d" % i, [128, 512], F32) for i in range(2)]

    t_mk = P.op("sync", lambda e: e.dma_start(out=masks[:], in_=T["bmask"]), sem="d_c", dma=True)
    t_oh = []
    if moba:
        for i in range(2):
            t_oh.append(P.op("sync", lambda e, i=i: e.dma_start(out=kt[i][HD:HD + 16, :], in_=T["onehot"]), sem="d_c", dma=True))
    t_wo = []
    for kc in range(8):
        t_wo.append(P.op("gpsimd", lambda e, kc=kc: e.dma_start(out=wosb[:, kc, :], in_=wo[0, kc * 128:(kc + 1) * 128, :]),
                         sem="d_wo", dma=True))
    last = {}

    def L(k):
        return last.get(k)

    cnt = {"s": 0, "o": 0, "p": 0, "ld": 0}
    t_attn = []

    def load_head(g, h):
        i = cnt["ld"] % 2
        cnt["ld"] += 1
        c, e2 = divmod(h, 2)
        toks = []
        if moba:
            qsrc = QTs[0, h, :, :]
            toks.append(P.op("sync", lambda e: e.dma_start(out=qt[i][:], in_=qsrc), waits=[L("ld_use%d" % i)], sem="d_ld%d" % i, dma=True))
        else:
            qsrc = QTs[g, h, 0:HD, :]
            toks.append(P.op("sync", lambda e: e.dma_start(out=qt[i][0:HD, :], in_=qsrc), waits=[L("ld_use%d" % i)], sem="d_ld%d" % i, dma=True))
        ksrc = KTs[g, h, 0:HD, :]
        toks.append(P.op("sync", lambda e: e.dma_start(out=kt[i][0:HD, :], in_=ksrc), waits=[L("ld_use%d" % i)], sem="d_ld%d" % i, dma=True))
        vsrc = Vs[g].rearrange("(n p) h c -> p n h c", p=128)[:, :, h, :]
        toks.append(P.op("gpsimd", lambda e: e.dma_start(out=vt[i][:], in_=vsrc), waits=[L("ld_use%d" % i)], sem="d_ld%d" % i, dma=True))
        return i, toks

    def acc_view(a, g, b):
        if moba or g == 0:
            return a[:, b * 512:(b + 1) * 512]
        dil = DILS[g]
        nbseg = 32 // dil
        if g == 1:
            n0 = 4 * b
            r, a0 = divmod(n0, nbseg)
            a0 *= 128
            return a[:].rearrange("p (a r) -> p r a", r=dil)[:, r, a0:a0 + 512]
        return a[:].rearrange("p (a r) -> p r a", r=dil)[:, 2 * b:2 * b + 2, :]

    def band_head(g, h, i, tl, first):
        e2 = h % 2
        a = acc[e2]
        dil = DILS[g]
        nbseg = 32 // dil
        tos = []

        def do_batch(b):
            si = cnt["s"] % 3
            cnt["s"] += 1
            oi = cnt["o"] % 2
            cnt["o"] += 1
            ps = psS[si]
            po = psO[oi]
            pb = pbuf[si]
            ts = None
            for j in range(4):
                n = 4 * b + j
                npv = max(n - 1, 0)
                ts = P.op("tensor", lambda e, j=j, npv=npv: e.matmul(ps[:, (2 * j) * 128:(2 * j + 1) * 128], lhsT=kt[i][0:HD, npv * 128:(npv + 1) * 128],
                                                                      rhs=qt[i][0:HD, (4 * b + j) * 128:(4 * b + j + 1) * 128], start=True, stop=True),
                          waits=[tl, L("psS_use%d" % si)])
                ts = P.op("tensor", lambda e, j=j, n=n: e.matmul(ps[:, (2 * j + 1) * 128:(2 * j + 2) * 128], lhsT=kt[i][0:HD, n * 128:(n + 1) * 128],
                                                                  rhs=qt[i][0:HD, n * 128:(n + 1) * 128], start=True, stop=True),
                          waits=[tl, L("psS_use%d" % si)])
            te = P.op("scalar", lambda e: e.activation(out=pb[:], in_=ps[:], func=AF.Exp, scale=0.125),
                      waits=[ts, L("pb_use%d" % si)])
            last["psS_use%d" % si] = te
            if g == 2:
                mvv = 2
            elif (4 * b) % nbseg == 0:
                mvv = 1
            else:
                mvv = 0
            tm = P.op("vector", lambda e, mvv=mvv: e.tensor_tensor(out=pb[:], in0=pb[:], in1=masks[:, mvv, :], op=ALU.mult),
                      waits=[te, t_mk])
            to = None
            for j in range(4):
                n = 4 * b + j
                npv = max(n - 1, 0)
                to = P.op("tensor", lambda e, j=j, npv=npv: e.matmul(po[:, j * 128:(j + 1) * 128], lhsT=vt[i][:, npv, :], rhs=pb[:, (2 * j) * 128:(2 * j + 1) * 128],
                                                                      start=True, stop=False), waits=[tm, tl, L("psO_use%d" % oi)])
                to = P.op("tensor", lambda e, j=j, n=n: e.matmul(po[:, j * 128:(j + 1) * 128], lhsT=vt[i][:, n, :], rhs=pb[:, (2 * j + 1) * 128:(2 * j + 2) * 128],
                                                                  start=False, stop=True), waits=[tm, tl, L("psO_use%d" % oi)])
            last["pb_use%d" % si] = to
            av = acc_view(a, g, b)
            if g == 2:
                pov = po[:].rearrange("p (r a) -> p r a", r=2)
            else:
                pov = po[:]
            if first:
                ta = P.op("vector", lambda e: e.tensor_copy(out=av, in_=pov), waits=[to, L("acc_use%d" % e2)])
            else:
                ta = P.op("vector", lambda e: e.tensor_tensor(out=av, in0=av, in1=pov, op=ALU.add), waits=[to, L("acc_w%d" % e2)])
            last["psO_use%d" % oi] = ta
            last["acc_w%d" % e2] = ta
            tos.append(to)
        for b in range(8):
            do_batch(b)
        last["ld_use%d" % i] = tos[-1]

    def moba_head(h, i, tl):
        e2 = h % 2
        a = acc[e2]
        tos = []

        def do_tile(sc, ktile, po, oi, nk):
            if True:
                si = cnt["s"] % 3
                cnt["s"] += 1
                ps = psS[si]
                pb = pbuf[si]
                c0 = max(0, ktile - 4 * sc) * 128
                q0 = sc * 512 + c0
                wq_ = 512 - c0
                ts = P.op("tensor", lambda e, ktile=ktile, c0=c0, q0=q0, wq_=wq_: e.matmul(
                    ps[:, c0:512], lhsT=kt[i][:, ktile * 128:(ktile + 1) * 128], rhs=qt[i][:, q0:q0 + wq_], start=True, stop=True),
                    waits=[tl, L("psS_use%d" % si)] + t_oh)
                te = P.op("scalar", lambda e, c0=c0: e.activation(out=pb[:, c0:512], in_=ps[:, c0:512], func=AF.Exp, scale=0.125),
                          waits=[ts, L("pb_use%d" % si)])
                last["psS_use%d" % si] = te
                tm = te
                if ktile >= 4 * sc:
                    tm = P.op("vector", lambda e, c0=c0: e.tensor_tensor(out=pb[:, c0:c0 + 128], in0=pb[:, c0:c0 + 128],
                                                                           in1=masks[:, 0, 128:256], op=ALU.mult), waits=[te, t_mk])
                to = P.op("tensor", lambda e, ktile=ktile, c0=c0: e.matmul(po[:, c0:512], lhsT=vt[i][:, ktile, :], rhs=pb[:, c0:512],
                                                                           start=(ktile == 0), stop=(ktile == nk - 1)),
                          waits=[tm, tl, L("psO_use%d" % oi)])
                last["pb_use%d" % si] = to
                return to

        def do_sc(sc):
            oi = cnt["o"] % 2
            cnt["o"] += 1
            po = psO[oi]
            nk = 4 * sc + 4
            to = None
            for ktile in range(nk):
                to = do_tile(sc, ktile, po, oi, nk)
            ta = P.op("vector", lambda e: e.tensor_copy(out=a[:, sc * 512:(sc + 1) * 512], in_=po[:]), waits=[to, L("acc_use%d" % e2)])
            last["psO_use%d" % oi] = ta
            last["acc_w%d" % e2] = ta
            tos.append(to)
        for sc in range(8):
            do_sc(sc)
        last["ld_use%d" % i] = tos[-1]

    for pr in range(int(os.environ.get('KB_PAIRS', 8))):
        for g in range(int(os.environ.get('KB_GRP', ngrp))):
            for e2 in range(2):
                h = 2 * pr + e2
                i, tl = load_head(g, h)
                if moba:
                    moba_head(h, i, tl)
                else:
                    band_head(g, h, i, tl, first=(g == 0))
        t0 = P.op("sync", lambda e: e.dma_start(out=den[0:HD, :], in_=acc[0][HD:128, :]), waits=[L("acc_w0"), L("den_use")], sem="d_den", dma=True)
        t1 = P.op("sync", lambda e: e.dma_start(out=den[HD:128, :], in_=acc[1][0:HD, :]), waits=[L("acc_w1"), L("den_use")], sem="d_den", dma=True)
        tln = P.op("scalar", lambda e: e.activation(out=den[:], in_=den[:], func=AF.Ln), waits=[t0, t1])
        tex = P.op("scalar", lambda e: e.activation(out=den[:], in_=den[:], func=AF.Exp, scale=-1.0), waits=[tln])
        ta0 = P.op("vector", lambda e, pr=pr: e.tensor_tensor(out=attnT[0:HD, pr, :], in0=acc[0][0:HD, :], in1=den[0:HD, :], op=ALU.mult),
                   waits=[tex, L("acc_w0")])
        ta1 = P.op("vector", lambda e, pr=pr: e.tensor_tensor(out=attnT[HD:128, pr, :], in0=acc[1][HD:128, :], in1=den[HD:128, :], op=ALU.mult),
                   waits=[tex, L("acc_w1")])
        last["den_use"] = [ta0, ta1]
        last["acc_use0"] = [t0, ta0]
        last["acc_use1"] = [t1, ta1]
        t_attn += [ta0, ta1]

    xt = [P.sb("xo%d" % i, [128, D], F32) for i in range(2)]
    t_fin = []
    for n in range(int(os.environ.get('KB_WO', NT))):
        b = n % 2
        t_x = P.op("sync", lambda e, n=n, b=b: e.dma_start(out=xt[b][:], in_=xsrc[n * 128:(n + 1) * 128, :]),
                   waits=[L("xo_use%d" % b)], sem="d_xo%d" % b, dma=True)
        tadds = []
        for hf in range(2):
            si = cnt["s"] % 3
            cnt["s"] += 1
            ps = psS[si]
            tm = None
            for kc in range(8):
                tm = P.op("tensor", lambda e, kc=kc, n=n, hf=hf, ps=ps: e.matmul(ps[:, 0:512], lhsT=attnT[:, kc, n * 128:(n + 1) * 128],
                                                                           rhs=wosb[:, kc, hf * 512:(hf + 1) * 512], start=(kc == 0), stop=(kc == 7)),
                          waits=t_attn + t_wo + [L("psS_use%d" % si)])
            tadd = P.op("vector", lambda e, hf=hf, b=b, ps=ps: e.tensor_tensor(out=xt[b][:, hf * 512:(hf + 1) * 512], in0=xt[b][:, hf * 512:(hf + 1) * 512],
                                                                                  in1=ps[:, 0:512], op=ALU.add), waits=[tm, t_x])
            last["psS_use%d" % si] = tadd
            tadds.append(tadd)
        t_o = P.op("sync", lambda e, n=n, b=b: e.dma_start(out=xdst[n * 128:(n + 1) * 128, :], in_=xt[b][:]), waits=tadds, sem="d_xw%d" % b, dma=True)
        last["xo_use%d" % b] = t_o
        t_fin.append(t_o)
    P.op("gpsimd", lambda e: e.memset(fin[:], 0.0), waits=t_fin)
    P.flush()


def phase_ffn(nc, sems, tag, T, layer):
    P = Prog(nc, sems, tag)
    xsrc = T["xmid"][layer]
    xdst = T["xout"][layer]
    NB = 512
    nbat = S // NB
    NU = 3

    wup = P.sb("wup", [128, 8, 2 * DFF], BF16)
    wdn = P.sb("wdn", [128, NFF, D], BF16)
    gnorm = P.sb("gnorm", [128, D], F32)
    cw = P.sb("cw", [128, 3, 2 * NFF], F32)
    cb = P.sb("cb", [128, 2 * NFF], F32)
    ident_b = P.sb("identb", [128, 128], BF16)
    t_idb = mk_identity(P, ident_b)
    xt = [P.sb("xt%d" % i, [128, D], F32) for i in range(2)]
    xr = [P.sb("xr%d" % i, [128, D], F32) for i in range(2)]
    hn = P.sb("hn", [128, D], BF16)
    hnT = P.sb("hnT", [128, 8, NB], BF16)
    hT = P.sb("hT", [128, NFF, NB], BF16)
    U = [P.sb("U%d" % i, [128, NB + 2], F32) for i in range(NU)]
    halo = P.sb("halo", [128, 2 * NFF, 2], F32)
    t3 = [P.sb("t3_%d" % i, [128, NB], F32) for i in range(2)]
    junk = P.sb("junk", [128, D], BF16)
    ssq = P.sb("ssq", [128, 1], F32)
    sd = P.sb("sd", [128, 1], F32)
    rstd = P.sb("rstd", [128, 1], F32)
    fin = P.sb("fin", [128, 1], F32)
    pT = P.ps("pT", [128, 512], F32)
    pTb = pT[:].bitcast(BF16)
    pu = [P.ps("pu%d" % i, [128, NB], F32) for i in range(NU)]
    pg = P.ps("pg", [128, NB], F32)
    pd = [P.ps("pd%d" % i, [128, 512], F32) for i in range(2)]

    t_c = []
    t_c.append(P.op("sync", lambda e: e.dma_start(out=gnorm[:], in_=T["ffn_norm"][layer:layer + 1, :].partition_broadcast(128)), sem="d_c", dma=True))
    for j in range(3):
        t_c.append(P.op("sync", lambda e, j=j: e.dma_start(out=cw[:, j, :], in_=T["ffn_conv_w"][layer, j, :].rearrange("(c p) -> p c", p=128),
                                                             allow_slow_non_contiguous=True),
                        sem="d_c", dma=True))
    t_c.append(P.op("sync", lambda e: e.dma_start(out=cb[:], in_=T["ffn_conv_b"][layer, :].rearrange("(c p) -> p c", p=128),
                                                   allow_slow_non_contiguous=True), sem="d_c", dma=True))
    t_h0 = P.op("gpsimd", lambda e: e.memset(halo[:], 0.0))
    t_wu = []
    for c in range(2 * NFF // 4):
        for kc in range(8):
            t_wu.append(P.op("gpsimd", lambda e, c=c, kc=kc: e.dma_start(out=wup[:, kc, c * 512:(c + 1) * 512],
                                                                          in_=T["ffn_w_up"][layer, kc * 128:(kc + 1) * 128, c * 512:(c + 1) * 512]),
                             sem="d_wu", dma=True))
    t_wd = []
    for j in range(NFF):
        t_wd.append(P.op("gpsimd", lambda e, j=j: e.dma_start(out=wdn[:, j, :], in_=T["ffn_w_down"][layer, j * 128:(j + 1) * 128, :]),
                         sem="d_wd", dma=True))
    last = {}

    def L(k):
        return last.get(k)

    cnt = {"u": 0, "t": 0, "d": 0, "x": 0, "r": 0}
    t_fin = []
    for bt in range(nbat):
        thT = []
        for j in range(4):
            b = cnt["x"] % 2
            cnt["x"] += 1
            r0 = bt * NB + j * 128
            t_x = P.op("sync", lambda e, r0=r0, b=b: e.dma_start(out=xt[b][:], in_=xsrc[r0:r0 + 128, :]),
                       waits=[L("xt_use%d" % b)], sem="d_x%d" % b, dma=True)
            t_sq = P.op("scalar", lambda e, b=b: e.activation(out=junk[:], in_=xt[b][:], func=AF.Square, accum_out=ssq[:]),
                        waits=[t_x, L("ssq_use"), L("junk")])
            last["junk"] = t_sq
            t_sd = P.op("scalar", lambda e: e.activation(out=sd[:], in_=ssq[:], func=AF.Sqrt, scale=1.0 / D, bias=EPS), waits=[t_sq, L("sd_use")])
            last["ssq_use"] = t_sd
            t_rs = P.op("vector", lambda e: e.reciprocal(out=rstd[:], in_=sd[:]), waits=[t_sd, L("rstd_use")])
            last["sd_use"] = t_rs
            t_hn = P.op("vector", lambda e, b=b: e.scalar_tensor_tensor(out=hn[:], in0=xt[b][:], scalar=rstd[:, 0:1], in1=gnorm[:],
                                                                         op0=ALU.mult, op1=ALU.mult), waits=[t_rs, L("hn_use")] + t_c)
            last["rstd_use"] = t_hn
            last["xt_use%d" % b] = t_hn
            tt = None
            for kc in range(8):
                tt = P.op("tensor", lambda e, kc=kc: e.transpose(out=pTb[:, kc * 128:(kc + 1) * 128], in_=hn[:, kc * 128:(kc + 1) * 128],
                                                                 identity=ident_b[:]), waits=[t_hn, t_idb, L("pT_use")])
            last["hn_use"] = tt
            te = P.op("vector", lambda e, j=j: e.tensor_copy(out=hnT[:, :, j * 128:(j + 1) * 128], in_=pTb.rearrange("p (a q) -> p a q", a=8)),
                      waits=[tt, L("hnT_use")])
            last["pT_use"] = te
            thT.append(te)
        t_hT = []
        tm = None
        for j in range(NFF):
            res = {}
            for kind, ch in (("g", j), ("v", NFF + j)):
                ui = cnt["u"] % NU
                cnt["u"] += 1
                p_ = pu[ui]
                u_ = U[ui]
                wtok = t_wu[(ch // 4) * 8:(ch // 4) * 8 + 8]
                for kc in range(8):
                    tm = P.op("tensor", lambda e, kc=kc, ch=ch, p_=p_: e.matmul(p_[:], lhsT=wup[:, kc, ch * 128:(ch + 1) * 128], rhs=hnT[:, kc, :],
                                                                                start=(kc == 0), stop=(kc == 7)),
                              waits=thT + wtok + [L("pu_use%d" % ui)])
                tA = P.op("scalar", lambda e, u_=u_, p_=p_: e.activation(out=u_[:, 2:NB + 2], in_=p_[:], func=AF.Copy), waits=[tm, L("U_use%d" % ui)])
                tH = P.op("gpsimd", lambda e, u_=u_, ch=ch: e.tensor_copy(out=u_[:, 0:2], in_=halo[:, ch, :]),
                          waits=[t_h0, L("halo_w%d" % ch), L("U_use%d" % ui)])
                tB = P.op("scalar", lambda e, p_=p_, ch=ch: e.activation(out=p_[:], in_=p_[:], func=AF.Identity, scale=cw[:, 2, ch:ch + 1], bias=cb[:, ch:ch + 1]),
                          waits=[tA] + t_c)
                tC = P.op("vector", lambda e, p_=p_, u_=u_, ch=ch: e.scalar_tensor_tensor(out=p_[:], in0=u_[:, 1:NB + 1], scalar=cw[:, 1, ch:ch + 1], in1=p_[:],
                                                                                          op0=ALU.mult, op1=ALU.add), waits=[tB, tH])
                tN = P.op("gpsimd", lambda e, u_=u_, ch=ch: e.tensor_copy(out=halo[:, ch, :], in_=u_[:, NB:NB + 2]), waits=[tA, tH])
                last["halo_w%d" % ch] = tN
                res[kind] = (p_, u_, tC, ui, tN, ch)
            p_, u_, tC, ui, tN, ch = res["g"]
            ti = cnt["t"] % 2
            cnt["t"] += 1
            tD = P.op("vector", lambda e, p_=p_, u_=u_, ch=ch, ti=ti: e.scalar_tensor_tensor(out=t3[ti][:], in0=u_[:, 0:NB], scalar=cw[:, 0, ch:ch + 1], in1=p_[:],
                                                                                               op0=ALU.mult, op1=ALU.add), waits=[tC, L("t3_use%d" % ti)])
            last["pu_use%d" % ui] = tD
            last["U_use%d" % ui] = [tD, tN]
            tS = P.op("scalar", lambda e, ti=ti: e.activation(out=pg[:], in_=t3[ti][:], func=AF.Silu), waits=[tD, L("pg_use")])
            last["t3_use%d" % ti] = tS
            p_, u_, tC, ui, tN, ch = res["v"]
            ti2 = cnt["t"] % 2
            cnt["t"] += 1
            tD2 = P.op("vector", lambda e, p_=p_, u_=u_, ch=ch, ti2=ti2: e.scalar_tensor_tensor(out=t3[ti2][:], in0=u_[:, 0:NB], scalar=cw[:, 0, ch:ch + 1], in1=p_[:],
                                                                                                  op0=ALU.mult, op1=ALU.add), waits=[tC, L("t3_use%d" % ti2)])
            last["pu_use%d" % ui] = tD2
            last["U_use%d" % ui] = [tD2, tN]
            tF = P.op("vector", lambda e, j=j, ti2=ti2: e.tensor_tensor(out=hT[:, j, :], in0=t3[ti2][:], in1=pg[:], op=ALU.mult),
                      waits=[tD2, tS, L("hT_use")])
            last["pg_use"] = tF
            last["t3_use%d" % ti2] = tF
            t_hT.append(tF)
        last["hnT_use"] = tm
        tdn = None
        for j4 in range(4):
            rb = cnt["r"] % 2
            cnt["r"] += 1
            r0 = bt * NB + j4 * 128
            t_xr = P.op("sync", lambda e, r0=r0, rb=rb: e.dma_start(out=xr[rb][:], in_=xsrc[r0:r0 + 128, :]),
                        waits=[L("xr_use%d" % rb)], sem="d_r%d" % rb, dma=True)
            tadds = []
            for hf in range(2):
                di = cnt["d"] % 2
                cnt["d"] += 1
                for j in range(NFF):
                    tdn = P.op("tensor", lambda e, j=j, j4=j4, hf=hf, di=di: e.matmul(pd[di][:], lhsT=hT[:, j, j4 * 128:(j4 + 1) * 128],
                                                                                      rhs=wdn[:, j, hf * 512:(hf + 1) * 512], start=(j == 0), stop=(j == NFF - 1)),
                               waits=t_hT + t_wd + [L("pd_use%d" % di)])
                tadd = P.op("vector", lambda e, hf=hf, di=di, rb=rb: e.tensor_tensor(out=xr[rb][:, hf * 512:(hf + 1) * 512],
                                                                                      in0=xr[rb][:, hf * 512:(hf + 1) * 512], in1=pd[di][:], op=ALU.add),
                            waits=[tdn, t_xr])
                last["pd_use%d" % di] = tadd
                tadds.append(tadd)
            t_o = P.op("sync", lambda e, r0=r0, rb=rb: e.dma_start(out=xdst[r0:r0 + 128, :], in_=xr[rb][:]),
                       waits=tadds, sem="d_xw%d" % rb, dma=True)
            last["xr_use%d" % rb] = t_o
            t_fin.append(t_o)
        last["hT_use"] = tdn
    P.op("gpsimd", lambda e: e.memset(fin[:], 0.0), waits=t_fin)
    P.flush()


def host_consts():
    pos = np.arange(S, dtype=np.float32)
    inv = (np.float32(500000.0) ** (-np.arange(0, 16, 2, dtype=np.float32) / np.float32(16))).astype(np.float32)
    ang = (pos[:, None] * inv[None, :]).astype(np.float32)
    cos = np.cos(ang).astype(np.float32)
    sin = np.sin(ang).astype(np.float32)
    ropec = np.zeros((3, 128, NT, 8), np.float32)
    ropes = np.zeros((3, 128, NT, 8), np.float32)
    for v, dil in enumerate(DILS):
        Lg = S // dil
        pp = np.arange(S)
        r, a = np.divmod(pp, Lg)
        t = a * dil + r
        ropec[v] = cos[t].reshape(NT, 128, 8).transpose(1, 0, 2)
        ropes[v] = sin[t].reshape(NT, 128, 8).transpose(1, 0, 2)
    k = np.arange(128)[:, None]
    q = np.arange(128)[None, :]
    prev = (k >= q).astype(np.float32)
    cur = (k <= q).astype(np.float32)
    zero = np.zeros_like(prev)
    bm = np.zeros((128, 3, 8, 128), np.float32)
    for j in range(4):
        bm[:, 0, 2 * j] = prev
        bm[:, 0, 2 * j + 1] = cur
        bm[:, 1, 2 * j] = zero if j == 0 else prev
        bm[:, 1, 2 * j + 1] = cur
        bm[:, 2, 2 * j] = zero if j % 2 == 0 else prev
        bm[:, 2, 2 * j + 1] = cur
    bm = bm.reshape(128, 3, 1024).astype(ml_dtypes.bfloat16)
    oh = (np.arange(S)[None, :] // 256 == np.arange(16)[:, None]).astype(np.float32).astype(ml_dtypes.bfloat16)
    return {"ropec": ropec, "ropes": ropes, "bmask": bm, "onehot": oh}


WEIGHT_SHAPES = {
    "attn_norm": [2, D], "a_w_qkv": [1, D, 9216], "a_q_norm": [1, 3, HD], "a_k_norm": [1, 3, HD], "a_w_o": [1, D, D],
    "b_w_qkv": [1, D, 3072], "b_q_norm": [1, HD], "b_k_norm": [1, HD], "b_w_o": [1, D, D],
    "ffn_norm": [2, D], "ffn_w_up": [2, D, 2 * DFF], "ffn_conv_w": [2, 3, 2 * DFF], "ffn_conv_b": [2, 2 * DFF],
    "ffn_w_down": [2, DFF, D],
}


def build_nc(phases=("A0", "B0", "C0", "A1", "B1", "C1"), debug_out=None):
    nc = bass.Bass("TRN2", target_bir_lowering=False)
    T = {}
    x = nc.dram_tensor("x", [S, D], F32, kind="ExternalInput").ap()
    for k, shp in WEIGHT_SHAPES.items():
        T[k] = nc.dram_tensor(k, shp, F32, kind="ExternalInput").ap()
    T["ropec"] = nc.dram_tensor("ropec", [3, 128, NT, 8], F32, kind="ExternalInput").ap()
    T["ropes"] = nc.dram_tensor("ropes", [3, 128, NT, 8], F32, kind="ExternalInput").ap()
    T["bmask"] = nc.dram_tensor("bmask", [128, 3, 1024], BF16, kind="ExternalInput").ap()
    T["onehot"] = nc.dram_tensor("onehot", [16, S], BF16, kind="ExternalInput").ap()
    y = nc.dram_tensor("y", [S, D], F32, kind="ExternalOutput").ap()

    def scratch(name, shape, dt):
        kind = "ExternalOutput" if (debug_out and name in debug_out) else "Internal"
        return nc.dram_tensor(name, shape, dt, kind=kind).ap()
    R1 = scratch("R1", [S, D], F32)
    R2 = scratch("R2", [S, D], F32)
    R3 = scratch("R3", [S, D], F32)
    T["xin"] = [x, R2]
    T["xmid"] = [R1, R3]
    T["xout"] = [R2, y]
    T["QTs"] = [scratch("QT0", [3, H, HD, S], BF16), scratch("QT1", [1, H, HD + 16, S], BF16)]
    T["KTs"] = [scratch("KT0", [3, H, HD, S], BF16), scratch("KT1", [1, H, HD, S], BF16)]
    T["Vs"] = [scratch("V0", [3, S, H, 128], BF16), scratch("V1", [1, S, H, 128], BF16)]
    sems = Sems(nc)
    fns = {"A": phase_qkv, "B": phase_attn, "C": phase_ffn}
    for ph in phases:
        fns[ph[0]](nc, sems, ph + "_", T, int(ph[1]))
    sems.stack.close()
    return nc


_CACHE = {}


def kernel(**inputs):
    if "nc" not in _CACHE:
        _CACHE["nc"] = build_nc()
        _CACHE["consts"] = host_consts()
    nc = _CACHE["nc"]
    consts = _CACHE["consts"]
    x = np.ascontiguousarray(np.asarray(inputs["x"], dtype=np.float32))
    shared = {k: np.ascontiguousarray(np.asarray(inputs[k], dtype=np.float32)) for k in WEIGHT_SHAPES}
    shared.update(consts)
    in_maps = []
    for b in range(8):
        m = dict(shared)
        m["x"] = x[b]
        in_maps.append(m)
    res = run_bass_kernel_spmd(nc, in_maps, core_ids=list(range(8)))
    return np.stack([np.asarray(r["y"], dtype=np.float32) for r in res.results], axis=0)
```

```python
import contextlib
import os
import numpy as np
import ml_dtypes
import concourse.bass as bass
import concourse.mybir as mybir
from concourse.bass_utils import run_bass_kernel_spmd

F32 = mybir.dt.float32
BF16 = mybir.dt.bfloat16
AF = mybir.ActivationFunctionType
ALU = mybir.AluOpType
AX = mybir.AxisListType

S = 4096
D = 1024
NT = S // 128
H = 16
HD = 64
DFF = 2816
NFF = DFF // 128
EPS = 1e-6
BIG = 30000.0
DILS = (1, 4, 16)
ENGS = ("tensor", "vector", "scalar", "gpsimd", "sync")


class Tok:
    __slots__ = ("sem", "val")

    def __init__(self, sem, val):
        self.sem = sem
        self.val = val


class Sems:
    def __init__(self, nc):
        self.nc = nc
        self.stack = contextlib.ExitStack()
        self.h = {}
        self.cnt = {}

    def get(self, name):
        if name not in self.h:
            self.h[name] = self.stack.enter_context(self.nc.semaphore(name))
            self.cnt[name] = 0
        return name


class Prog:
    def __init__(self, nc, sems, tag):
        self.nc = nc
        self.S = sems
        self.tag = tag
        self.q = {e: [] for e in ENGS}
        self.stack = contextlib.ExitStack()
        self.seen = {e: {} for e in ENGS}

    def sb(self, name, shape, dt):
        return self.stack.enter_context(self.nc.sbuf_tensor(self.tag + name, shape, dt))

    def ps(self, name, shape, dt):
        return self.stack.enter_context(self.nc.psum_tensor(self.tag + name, shape, dt))

    def op(self, eng, fn, waits=(), sem=None, dma=False):
        if sem is None:
            sem = "s_" + eng
        sem = self.S.get(self.tag + sem)
        need = {}

        def add(t):
            if t is None:
                return
            if isinstance(t, (list, tuple)):
                for u in t:
                    add(u)
                return
            if need.get(t.sem, -1) < t.val:
                need[t.sem] = t.val
        add(list(waits))
        seen = self.seen[eng]
        wl = []
        for s, v in need.items():
            if seen.get(s, -1) >= v:
                continue
            seen[s] = v
            wl.append((s, v))
        inc = 16 if dma else 1
        self.S.cnt[sem] += inc
        self.q[eng].append((wl, fn, sem, inc))
        return Tok(sem, self.S.cnt[sem])

    def flush(self):
        nc = self.nc
        sems = self.S.h
        qs = self.q
        with nc.Block() as block:
            def mk(name):
                def body(e):
                    for wl, fn, sem, inc in qs[name]:
                        for s, v in wl:
                            e.wait_ge(sems[s], v)
                        fn(e).then_inc(sems[sem], inc)
                return body
            for name in ENGS:
                if qs[name]:
                    getattr(block, name)(mk(name))
        self.stack.close()


def mk_identity(P, ident, dt_one=1.0):
    t0 = P.op("gpsimd", lambda e: e.memset(ident[:], 0.0))
    return P.op("gpsimd", lambda e: e.affine_select(out=ident[:], in_=ident[:], pattern=[[-1, 128]],
                                                    compare_op=ALU.not_equal, fill=1.0, base=0,
                                                    channel_multiplier=1), waits=[t0])


def phase_qkv(nc, sems, tag, T, layer):
    P = Prog(nc, sems, tag)
    moba = layer == 1
    ngrp = 1 if moba else 3
    xsrc = T["xin"][layer]
    wq = T["b_w_qkv"] if moba else T["a_w_qkv"]

    ident_b = P.sb("identb", [128, 128], BF16)
    ident_f = P.sb("identf", [128, 128], F32)
    t_idb = mk_identity(P, ident_b)
    t_idf = mk_identity(P, ident_f)
    gnorm = P.sb("gnorm", [128, D], F32)
    t_gn = P.op("sync", lambda e: e.dma_start(out=gnorm[:], in_=T["attn_norm"][layer:layer + 1, :].partition_broadcast(128)),
                sem="d_c", dma=True)
    gqk = P.sb("gqk", [128, ngrp, 2, HD], F32)
    t_gq = []
    for g in range(ngrp):
        for s_, nm in ((0, "q"), (1, "k")):
            src = (T["b_%s_norm" % nm][0:1, :] if moba else T["a_%s_norm" % nm][0, g:g + 1, :])
            t_gq.append(P.op("sync", lambda e, g=g, s_=s_, src=src: e.dma_start(
                out=gqk[:, g, s_, :], in_=src.partition_broadcast(128)), sem="d_c", dma=True))
    ropec = P.sb("ropec", [128, NT, 8], F32)
    ropes = P.sb("ropes", [128, NT, 8], F32)
    wsb = [P.sb("w%d" % i, [128, 8, 3, 1024], BF16) for i in range(2 if not moba else 1)]
    xt = [P.sb("xt%d" % i, [128, D], F32) for i in range(2)]
    junk = P.sb("junk", [128, 2048], F32)
    ssq = P.sb("ssq", [128, 1], F32)
    sd = P.sb("sd", [128, 1], F32)
    rstd = P.sb("rstd", [128, 1], F32)
    hn = P.sb("hn", [128, D], BF16)
    hnT = [P.sb("hnT%d" % i, [128, 8, 128], BF16) for i in range(2)]
    ss2 = P.sb("ss2", [128, 32], F32)
    sd2 = P.sb("sd2", [128, 32], F32)
    rs2 = P.sb("rs2", [128, 32], F32)
    qk = [P.sb("qk%d" % i, [128, 2, H, HD], F32) for i in range(2)]
    rt = [P.sb("rt%d" % i, [128, 2 * H, 8], F32) for i in range(4)]
    vaug = [P.sb("vaug%d" % i, [128, H, 128], BF16) for i in range(2)]
    qT4 = [P.sb("qT4_%d" % i, [128, 8, 512], BF16) for i in range(2)]
    kT4 = [P.sb("kT4_%d" % i, [128, 8, 512], BF16) for i in range(2)]
    pT = P.ps("pT", [128, 512], F32)
    pTb = pT[:].bitcast(BF16)
    pqk = P.ps("pqk", [128, 2048], F32)
    pv = P.ps("pv", [128, 512], F32)
    ptr = P.ps("ptr", [128, 8, 128], F32)
    if moba:
        qT32 = P.sb("qT32", [128, 8, 128], F32)
        ksum = P.sb("ksum", [128, 8, 16], F32)
        kpart = P.sb("kpart", [128, 8], F32)
        gs = P.sb("gs", [128, H, 16], F32)
        t8 = P.sb("t8", [128, H, 8], F32)
        mv = P.sb("mv", [128, H, 16], F32)
        mT4 = [P.sb("mT4_%d" % i, [128, 2, 512], BF16) for i in range(2)]

    t_ones = []
    for i in range(2):
        t_ones.append(P.op("gpsimd", lambda e, i=i: e.memset(vaug[i][:], 1.0)))
    t_ks0 = P.op("gpsimd", lambda e: e.memset(ksum[:], 0.0)) if moba else None

    last = {}

    def L(k):
        return last.get(k)

    tw_use = [None, None]
    t_outdma = []
    tiles = [(g, n) for g in range(ngrp) for n in range(NT)]
    state = {}

    def load_w(g):
        wb = wsb[g % len(wsb)]
        toks = []
        for kc in range(8):
            for s_ in range(3):
                if moba:
                    c0 = s_ * 1024
                else:
                    c0 = (s_ * 3 + g) * 1024
                toks.append(P.op("gpsimd", lambda e, wb=wb, kc=kc, s_=s_, c0=c0: e.dma_start(
                    out=wb[:, kc, s_, :], in_=wq[0, kc * 128:(kc + 1) * 128, c0:c0 + 1024]),
                    waits=[tw_use[g % len(wsb)]], sem="d_w%d" % (g % len(wsb)), dma=True))
        return toks

    def load_rope(g):
        v = 0 if moba else g
        a = P.op("sync", lambda e: e.dma_start(out=ropec[:], in_=T["ropec"][v]), waits=[L("rope_use")], sem="d_r", dma=True)
        b = P.op("sync", lambda e: e.dma_start(out=ropes[:], in_=T["ropes"][v]), waits=[L("rope_use")], sem="d_r", dma=True)
        return [a, b]

    def x_rows(g, n):
        dil = 1 if moba else DILS[g]
        Lg = S // dil
        p0 = n * 128
        r, a0 = divmod(p0, Lg)
        v = xsrc.rearrange("(a r) d -> r a d", r=dil)
        return v[r, a0:a0 + 128, :]

    def stage1(idx):
        g, n = tiles[idx]
        st = state.setdefault(idx, {})
        b = idx % 2
        if n == 0:
            if g == 0:
                state["w0"] = load_w(0)
            state["rope%d" % g] = load_rope(g)
        if n == 1 and g + 1 < ngrp:
            state["w%d" % (g + 1)] = load_w(g + 1)
        tw = state["w%d" % g]
        wb = wsb[g % len(wsb)]
        t_x = P.op("sync", lambda e: e.dma_start(out=xt[b][:], in_=x_rows(g, n)),
                   waits=[L("xt_use%d" % b)], sem="d_x%d" % b, dma=True)
        t_sq = P.op("scalar", lambda e: e.activation(out=junk[:, 0:D], in_=xt[b][:], func=AF.Square, accum_out=ssq[:]),
                    waits=[t_x, L("junk"), L("ssq_use")])
        t_sd = P.op("scalar", lambda e: e.activation(out=sd[:], in_=ssq[:], func=AF.Sqrt, scale=1.0 / D, bias=EPS),
                    waits=[t_sq, L("sd_use")])
        last["ssq_use"] = t_sd
        t_rs = P.op("vector", lambda e: e.reciprocal(out=rstd[:], in_=sd[:]), waits=[t_sd, L("rstd_use")])
        last["sd_use"] = t_rs
        t_hn = P.op("vector", lambda e: e.scalar_tensor_tensor(out=hn[:], in0=xt[b][:], scalar=rstd[:, 0:1], in1=gnorm[:],
                                                               op0=ALU.mult, op1=ALU.mult),
                    waits=[t_rs, t_gn, L("hn_use")])
        last["rstd_use"] = t_hn
        last["xt_use%d" % b] = t_hn
        last["junk"] = t_sq
        tt = None
        for kc in range(8):
            tt = P.op("tensor", lambda e, kc=kc: e.transpose(out=pTb[:, kc * 128:(kc + 1) * 128], in_=hn[:, kc * 128:(kc + 1) * 128],
                                                             identity=ident_b[:]),
                      waits=[t_hn, t_idb, L("pT_use")])
        last["hn_use"] = tt
        t_hT = P.op("scalar", lambda e: e.activation(out=hnT[b][:].rearrange("p a b -> p (a b)"), in_=pTb, func=AF.Copy),
                    waits=[tt, L("hnT_use%d" % b)])
        last["pT_use"] = t_hT
        st["hT"] = t_hT

        def mm(dst, s_, c0, extra):
            t = None
            for kc in range(8):
                t = P.op("tensor", lambda e, kc=kc: e.matmul(dst, lhsT=hnT[b][:, kc, :], rhs=wb[:, kc, s_, c0:c0 + 512],
                                                            start=(kc == 0), stop=(kc == 7)),
                         waits=[t_hT, tw, extra])
            return t
        tq = None
        for s_ in range(2):
            for hf in range(2):
                tq = mm(pqk[:, s_ * 1024 + hf * 512: s_ * 1024 + hf * 512 + 512], s_, hf * 512, L("pqk_use"))
        st["tqk"] = tq
        qkb = qk[b]
        t_sq2 = P.op("scalar", lambda e: e.activation(out=junk[:], in_=pqk[:], func=AF.Square), waits=[tq, L("junk")])
        t_ss2 = P.op("vector", lambda e: e.tensor_reduce(out=ss2[:], in_=junk[:].rearrange("p (h d) -> p h d", d=HD),
                                                         op=ALU.add, axis=AX.X), waits=[t_sq2, L("ss2_use")])
        last["junk"] = t_ss2
        t_sd2 = P.op("scalar", lambda e: e.activation(out=sd2[:], in_=ss2[:], func=AF.Sqrt, scale=1.0 / HD, bias=EPS),
                     waits=[t_ss2, L("sd2_use")])
        last["ss2_use"] = t_sd2
        t_rs2 = P.op("vector", lambda e: e.reciprocal(out=rs2[:], in_=sd2[:]), waits=[t_sd2, L("rs2_use")])
        last["sd2_use"] = t_rs2
        t_nm = P.op("vector", lambda e: e.tensor_tensor(out=qkb[:].rearrange("p s h d -> p (s h) d"),
                                                        in0=pqk[:].rearrange("p (h d) -> p h d", d=HD),
                                                        in1=rs2[:].unsqueeze(2).broadcast_to([128, 32, HD]), op=ALU.mult),
                    waits=[t_rs2, L("qk_use%d" % b)])
        last["rs2_use"] = t_nm
        last["pqk_use"] = t_nm
        gsel = 0 if moba else g
        t_g = P.op("gpsimd", lambda e: e.tensor_tensor(out=qkb[:], in0=qkb[:],
                                                       in1=gqk[:, gsel, :, :].unsqueeze(2).broadcast_to([128, 2, H, HD]),
                                                       op=ALU.mult), waits=[t_nm] + t_gq)
        qv = qkb[:].rearrange("p s h d -> p (s h) d")
        x1 = qv[:, :, 0:8]
        x2 = qv[:, :, 8:16]
        cb = ropec[:, n, :].unsqueeze(1).broadcast_to([128, 32, 8])
        sbb = ropes[:, n, :].unsqueeze(1).broadcast_to([128, 32, 8])
        tr = state["rope%d" % g]
        w0 = [t_g, tr, L("rt_use")]
        a1 = P.op("vector", lambda e: e.tensor_tensor(out=rt[0][:], in0=x1, in1=cb, op=ALU.mult), waits=w0)
        a2 = P.op("vector", lambda e: e.tensor_tensor(out=rt[1][:], in0=x2, in1=sbb, op=ALU.mult), waits=w0)
        a3 = P.op("vector", lambda e: e.tensor_tensor(out=rt[2][:], in0=x2, in1=cb, op=ALU.mult), waits=w0)
        a4 = P.op("vector", lambda e: e.tensor_tensor(out=rt[3][:], in0=x1, in1=sbb, op=ALU.mult), waits=w0)
        a5 = P.op("vector", lambda e: e.tensor_tensor(out=x1, in0=rt[0][:], in1=rt[1][:], op=ALU.subtract), waits=[a1, a2, a3, a4])
        a6 = P.op("vector", lambda e: e.tensor_tensor(out=x2, in0=rt[2][:], in1=rt[3][:], op=ALU.add), waits=[a1, a2, a3, a4, a5])
        last["rt_use"] = a6
        last["rope_use"] = a6
        st["qkg"] = [a5, a6]
        va = vaug[b]
        tv = None
        for hf in range(2):
            tvm = mm(pv[:], 2, hf * 512, L("pv_use"))
            pvv = pv[:].rearrange("p (i e d) -> p i e d", e=2, d=HD)
            vv = va[:, hf * 8:(hf + 1) * 8, :].rearrange("p (i e) c -> p i e c", e=2)
            tv0 = P.op("scalar", lambda e, pvv=pvv, vv=vv: e.activation(out=vv[:, :, 0, 0:HD], in_=pvv[:, :, 0, :], func=AF.Copy),
                       waits=[tvm, L("vaug_use%d" % b)] + t_ones)
            tv = P.op("scalar", lambda e, pvv=pvv, vv=vv: e.activation(out=vv[:, :, 1, HD:128], in_=pvv[:, :, 1, :], func=AF.Copy),
                      waits=[tvm, L("vaug_use%d" % b)] + t_ones)
            last["pv_use"] = tv
        tw_use[g % len(wsb)] = tv
        t_vo = P.op("sync", lambda e: e.dma_start(out=T["Vs"][layer][g, n * 128:(n + 1) * 128, :, :], in_=va[:]),
                    waits=[tv], sem="d_vo%d" % b, dma=True)
        last["vaug_use%d" % b] = t_vo
        t_outdma.append(t_vo)
        last["hnT_use%d" % b] = tvm

    def stage2(idx):
        g, n = tiles[idx]
        st = state[idx]
        b = idx % 2
        qkb = qk[b]
        n4, j4 = divmod(n, 4)
        b4 = (idx // 4) % 2
        for s_, dst4, nm in ((0, qT4[b4], "Q"), (1, kT4[b4], "K")):
            tt = None
            for c in range(8):
                tt = P.op("tensor", lambda e, c=c, s_=s_: e.transpose(out=ptr[:, c, :], in_=qkb[:, s_, 2 * c:2 * c + 2, :].rearrange("p h d -> p (h d)"),
                                                                      identity=ident_f[:]),
                          waits=[st["qkg"], t_idf, L("ptr_use")])
            wv = [tt, L("%s4_use%d" % (nm, b4))]
            te = P.op("scalar", lambda e, dst4=dst4: e.activation(out=dst4[:, :, j4 * 128:(j4 + 1) * 128], in_=ptr[:], func=AF.Copy),
                      waits=wv)
            tl = [te]
            if moba and s_ == 0:
                tl.append(P.op("vector", lambda e: e.tensor_copy(out=qT32[:], in_=ptr[:]), waits=[tt, te, L("qT32_use")]))
            if moba and s_ == 1:
                tkp = P.op("vector", lambda e: e.tensor_reduce(out=kpart[:], in_=ptr[:], op=ALU.add, axis=AX.X),
                           waits=[tt, te, L("kpart_use")])
                j = n // 2
                tks = P.op("vector", lambda e, j=j: e.tensor_tensor(out=ksum[:, :, j], in0=ksum[:, :, j], in1=kpart[:], op=ALU.add),
                           waits=[tkp, t_ks0, L("ksum_w"), L("qT32_use")])
                last["kpart_use"] = tks
                last["ksum_w"] = tks
                tl.append(tks)
            last["ptr_use"] = tl
            st["te%d" % s_] = te
        last["qk_use%d" % b] = last["ptr_use"]
        if moba:
            stage_gate(idx)
        if j4 == 3:
            for s_, src4, nm, dstT in ((0, qT4[b4], "Q", T["QTs"][layer]), (1, kT4[b4], "K", T["KTs"][layer])):
                tds = []
                for e2 in range(2):
                    dv = dstT[g].rearrange("(c e) r t -> e r c t", e=2)[e2, 0:HD, :, n4 * 512:(n4 + 1) * 512]
                    tds.append(P.op("sync", lambda e, dv=dv, src4=src4, e2=e2: e.dma_start(out=dv, in_=src4[e2 * HD:(e2 + 1) * HD, :, :]),
                                    waits=[state[idx - 3]["te%d" % s_], state[idx - 2]["te%d" % s_], state[idx - 1]["te%d" % s_], st["te%d" % s_]],
                                    sem="d_%so%d" % (nm, b4), dma=True))
                last["%s4_use%d" % (nm, b4)] = tds
                t_outdma.extend(tds)
            if moba:
                tds = []
                m4 = mT4[b4]
                for h in range(H):
                    dv = T["QTs"][layer][0, h, HD:HD + 16, n4 * 512:(n4 + 1) * 512]
                    tds.append(P.op("gpsimd", lambda e, dv=dv, h=h, m4=m4: e.dma_start(out=dv, in_=m4[(h % 8) * 16:(h % 8) * 16 + 16, h // 8, :]),
                                    waits=[state[idx - 3]["tm"], state[idx - 2]["tm"], state[idx - 1]["tm"], st["tm"]],
                                    sem="d_mo%d" % b4, dma=True))
                last["m4_use%d" % b4] = tds
                t_outdma.extend(tds)

    def stage_gate(idx):
        g, n = tiles[idx]
        st = state[idx]
        b4 = (idx // 4) % 2
        j4 = n % 4
        ob = n // 2
        pg = pT[:, 0:256]
        tg = None
        tq32 = last["ptr_use"]
        for h in range(H):
            r0 = (h % 2) * HD
            tg = P.op("tensor", lambda e, h=h, r0=r0: e.matmul(pg[:, h * 16:(h + 1) * 16], lhsT=qT32[r0:r0 + HD, h // 2, :],
                                                                rhs=ksum[r0:r0 + HD, h // 2, :], start=True, stop=True),
                      waits=[tq32, L("ksum_w"), L("pT_use")])
        last["qT32_use"] = tg
        t_gs = P.op("vector", lambda e: e.tensor_copy(out=gs[:].rearrange("p h j -> p (h j)"), in_=pg), waits=[tg, L("gs_use")])
        last["pT_use"] = t_gs
        t_ms = t_gs
        if ob < 16:
            t_ms = P.op("vector", lambda e: e.memset(gs[:, :, ob:16], -1e30), waits=[t_gs])
        tm8 = []
        for h in range(H):
            tm8.append(P.op("vector", lambda e, h=h: e.max(out=t8[:, h, :], in_=gs[:, h, :]), waits=[t_ms, L("t8_use")]))
        t_sel = P.op("vector", lambda e: e.tensor_tensor(out=mv[:], in0=gs[:], in1=t8[:, :, 2:3].broadcast_to([128, H, 16]), op=ALU.is_ge),
                     waits=tm8 + [L("mv_use")])
        last["t8_use"] = t_sel
        last["gs_use"] = t_sel
        t_mv = P.op("vector", lambda e: e.tensor_scalar(out=mv[:], in0=mv[:], scalar1=-1.0, scalar2=BIG, op0=ALU.add, op1=ALU.mult),
                    waits=[t_sel])
        t_own = P.op("vector", lambda e: e.memset(mv[:, :, ob:ob + 1], 0.0), waits=[t_mv])
        pm = pT[:, 256:512].rearrange("p (a q) -> p a q", a=2)
        tt = None
        for a in range(2):
            tt = P.op("tensor", lambda e, a=a: e.transpose(out=pm[:, a, :], in_=mv[:, a * 8:(a + 1) * 8, :].rearrange("p h j -> p (h j)"),
                                                           identity=ident_f[:]), waits=[t_own, t_idf, L("pm_use")])
        last["mv_use"] = tt
        tm = P.op("vector", lambda e: e.tensor_copy(out=mT4[b4][:, :, j4 * 128:(j4 + 1) * 128], in_=pm), waits=[tt, L("m4_use%d" % b4)])
        last["pm_use"] = tm
        last["pT_use"] = [last["pT_use"], tm]
        st["tm"] = tm

    for idx in range(len(tiles)):
        stage1(idx)
        if idx >= 1:
            stage2(idx - 1)
    stage2(len(tiles) - 1)
    P.op("gpsimd", lambda e: e.memset(ssq[:], 0.0), waits=t_outdma + [L("ssq_use")])
    P.flush()


def phase_attn(nc, sems, tag, T, layer):
    P = Prog(nc, sems, tag)
    moba = layer == 1
    ngrp = 1 if moba else 3
    xsrc = T["xin"][layer]
    xdst = T["xmid"][layer]
    wo = T["b_w_o"] if moba else T["a_w_o"]
    QTs, KTs, Vs = T["QTs"][layer], T["KTs"][layer], T["Vs"][layer]
    KR = HD + 16 if moba else HD

    attnT = P.sb("attnT", [128, 8, S], BF16)
    acc = [P.sb("acc%d" % i, [128, S], F32) for i in range(2)]
    den = P.sb("den", [128, S], F32)
    qt = [P.sb("qt%d" % i, [KR, S], BF16) for i in range(2)]
    kt = [P.sb("kt%d" % i, [KR, S], BF16) for i in range(2)]
    vt = [P.sb("vt%d" % i, [128, NT, 128], BF16) for i in range(2)]
    pbuf = [P.sb("pb%d" % i, [128, 1024], BF16) for i in range(3)]
    masks = P.sb("masks", [128, 3, 1024], BF16)
    wosb = P.sb("wosb", [128, 8, D], BF16)
    fin = P.sb("fin", [128, 1], F32)
    psS = [P.ps("psS%d" % i, [128, 1024], F32) for i in range(3)]
    psO = [P.ps("psO%d" % i, [128, 512], F32) for i in range(2)]

    t_mk = P.op("sync", lambda e: e.dma_start(out=masks[:], in_=T["bmask"]), sem="d_c", dma=True)
    t_oh = []
    if moba:
        for i in range(2):
            t_oh.append(P.op("sync", lambda e, i=i: e.dma_start(out=kt[i][HD:HD + 16, :], in_=T["onehot"]), sem="d_c", dma=True))
    t_wo = []
    for kc in range(8):
        t_wo.append(P.op("gpsimd", lambda e, kc=kc: e.dma_start(out=wosb[:, kc, :], in_=wo[0, kc * 128:(kc + 1) * 128, :]),
                         sem="d_wo", dma=True))
    last = {}

    def L(k):
        return last.get(k)

    cnt = {"s": 0, "o": 0, "p": 0, "ld": 0}
    t_attn = []

    def load_head(g, h):
        i = cnt["ld"] % 2
        cnt["ld"] += 1
        c, e2 = divmod(h, 2)
        toks = []
        if moba:
            qsrc = QTs[0, h, :, :]
            toks.append(P.op("sync", lambda e: e.dma_start(out=qt[i][:], in_=qsrc), waits=[L("ld_use%d" % i)], sem="d_ld%d" % i, dma=True))
        else:
            qsrc = QTs[g, h, 0:HD, :]
            toks.append(P.op("sync", lambda e: e.dma_start(out=qt[i][0:HD, :], in_=qsrc), waits=[L("ld_use%d" % i)], sem="d_ld%d" % i, dma=True))
        ksrc = KTs[g, h, 0:HD, :]
        toks.append(P.op("sync", lambda e: e.dma_start(out=kt[i][0:HD, :], in_=ksrc), waits=[L("ld_use%d" % i)], sem="d_ld%d" % i, dma=True))
        vsrc = Vs[g].rearrange("(n p) h c -> p n h c", p=128)[:, :, h, :]
        toks.append(P.op("gpsimd", lambda e: e.dma_start(out=vt[i][:], in_=vsrc), waits=[L("ld_use%d" % i)], sem="d_ld%d" % i, dma=True))
        return i, toks

    def acc_view(a, g, b):
        if moba or g == 0:
            return a[:, b * 512:(b + 1) * 512]
        dil = DILS[g]
        nbseg = 32 // dil
        if g == 1:
            n0 = 4 * b
            r, a0 = divmod(n0, nbseg)
            a0 *= 128
            return a[:].rearrange("p (a r) -> p r a", r=dil)[:, r, a0:a0 + 512]
        return a[:].rearrange("p (a r) -> p r a", r=dil)[:, 2 * b:2 * b + 2, :]

    items = []

    def band_item(g, h, b, ld, first, lastb):
        it = {}
        e2 = h % 2
        a = acc[e2]
        nbseg = 32 // DILS[g]

        def p1():
            i, tl = ld["get"]()
            si = cnt["s"] % 3
            cnt["s"] += 1
            ps = psS[si]
            pb = pbuf[si]
            ts = None
            for j in range(4):
                n = 4 * b + j
                npv = max(n - 1, 0)
                ts = P.op("tensor", lambda e, j=j, npv=npv, n=n: e.matmul(ps[:, (2 * j) * 128:(2 * j + 1) * 128], lhsT=kt[i][0:HD, npv * 128:(npv + 1) * 128],
                                                                           rhs=qt[i][0:HD, n * 128:(n + 1) * 128], start=True, stop=True),
                          waits=[tl, L("psS_use%d" % si)])
                ts = P.op("tensor", lambda e, j=j, n=n: e.matmul(ps[:, (2 * j + 1) * 128:(2 * j + 2) * 128], lhsT=kt[i][0:HD, n * 128:(n + 1) * 128],
                                                                  rhs=qt[i][0:HD, n * 128:(n + 1) * 128], start=True, stop=True),
                          waits=[tl, L("psS_use%d" % si)])
            te = P.op("scalar", lambda e: e.activation(out=pb[:], in_=ps[:], func=AF.Exp, scale=0.125),
                      waits=[ts, L("pb_use%d" % si)])
            last["psS_use%d" % si] = te
            if g == 2:
                mvv = 2
            elif (4 * b) % nbseg == 0:
                mvv = 1
            else:
                mvv = 0
            tm = P.op("vector", lambda e: e.tensor_tensor(out=pb[:], in0=pb[:], in1=masks[:, mvv, :], op=ALU.mult),
                      waits=[te, t_mk])
            it.update(i=i, tl=tl, si=si, pb=pb, tm=tm)

        def p2():
            i, tl, si, pb, tm = it["i"], it["tl"], it["si"], it["pb"], it["tm"]
            oi = cnt["o"] % 2
            cnt["o"] += 1
            po = psO[oi]
            to = None
            for j in range(4):
                n = 4 * b + j
                npv = max(n - 1, 0)
                to = P.op("tensor", lambda e, j=j, npv=npv: e.matmul(po[:, j * 128:(j + 1) * 128], lhsT=vt[i][:, npv, :], rhs=pb[:, (2 * j) * 128:(2 * j + 1) * 128],
                                                                      start=True, stop=False), waits=[tm, tl, L("psO_use%d" % oi)])
                to = P.op("tensor", lambda e, j=j, n=n: e.matmul(po[:, j * 128:(j + 1) * 128], lhsT=vt[i][:, n, :], rhs=pb[:, (2 * j + 1) * 128:(2 * j + 2) * 128],
                                                                  start=False, stop=True), waits=[tm, tl, L("psO_use%d" % oi)])
            last["pb_use%d" % si] = to
            av = acc_view(a, g, b)
            pov = po[:].rearrange("p (r a) -> p r a", r=2) if g == 2 else po[:]
            if first:
                ta = P.op("vector", lambda e: e.tensor_copy(out=av, in_=pov), waits=[to, L("acc_use%d" % e2)])
            else:
                ta = P.op("vector", lambda e: e.tensor_tensor(out=av, in0=av, in1=pov, op=ALU.add), waits=[to, L("acc_w%d" % e2)])
            last["psO_use%d" % oi] = ta
            last["acc_w%d" % e2] = ta
            if lastb:
                last["ld_use%d" % i] = to
        it["p1"] = p1
        it["p2"] = p2
        return it

    def moba_item(h, sc, kp, ld, po_box):
        it = {}
        e2 = h % 2
        a = acc[e2]
        nk = 4 * sc + 4
        kts = (2 * kp, 2 * kp + 1)

        def p1():
            i, tl = ld["get"]()
            si = cnt["s"] % 3
            cnt["s"] += 1
            ps = psS[si]
            pb = pbuf[si]
            ts = None
            for u, ktile in enumerate(kts):
                c0 = max(0, ktile - 4 * sc) * 128
                ts = P.op("tensor", lambda e, u=u, ktile=ktile, c0=c0: e.matmul(
                    ps[:, u * 512 + c0:(u + 1) * 512], lhsT=kt[i][:, ktile * 128:(ktile + 1) * 128],
                    rhs=qt[i][:, sc * 512 + c0:(sc + 1) * 512], start=True, stop=True),
                    waits=[tl, L("psS_use%d" % si)] + t_oh)
            c00 = max(0, kts[0] - 4 * sc) * 128
            te = P.op("scalar", lambda e: e.activation(out=pb[:, c00:1024], in_=ps[:, c00:1024], func=AF.Exp, scale=0.125),
                      waits=[ts, L("pb_use%d" % si)])
            last["psS_use%d" % si] = te
            tms = [te]
            for u, ktile in enumerate(kts):
                if ktile >= 4 * sc:
                    c0 = (ktile - 4 * sc) * 128
                    tms.append(P.op("vector", lambda e, u=u, c0=c0: e.tensor_tensor(out=pb[:, u * 512 + c0:u * 512 + c0 + 128],
                                                                                      in0=pb[:, u * 512 + c0:u * 512 + c0 + 128],
                                                                                      in1=masks[:, 0, 128:256], op=ALU.mult), waits=[te, t_mk]))
            it.update(i=i, tl=tl, si=si, pb=pb, tm=tms)

        def p2():
            i, tl, si, pb, tm = it["i"], it["tl"], it["si"], it["pb"], it["tm"]
            if kp == 0:
                po_box["oi"] = cnt["o"] % 2
                cnt["o"] += 1
            oi = po_box["oi"]
            po = psO[oi]
            to = None
            for u, ktile in enumerate(kts):
                c0 = max(0, ktile - 4 * sc) * 128
                to = P.op("tensor", lambda e, u=u, ktile=ktile, c0=c0: e.matmul(po[:, c0:512], lhsT=vt[i][:, ktile, :], rhs=pb[:, u * 512 + c0:(u + 1) * 512],
                                                                                start=(ktile == 0), stop=(ktile == nk - 1)),
                          waits=[tm, tl, L("psO_use%d" % oi)])
            last["pb_use%d" % si] = to
            if kts[1] == nk - 1:
                ta = P.op("vector", lambda e: e.tensor_copy(out=a[:, sc * 512:(sc + 1) * 512], in_=po[:]), waits=[to, L("acc_use%d" % e2)])
                last["psO_use%d" % oi] = ta
                last["acc_w%d" % e2] = ta
                if sc == 7:
                    last["ld_use%d" % i] = to
        it["p1"] = p1
        it["p2"] = p2
        return it

    def finalize_item(pr):
        def fin_():
            t0 = P.op("sync", lambda e: e.dma_start(out=den[0:HD, :], in_=acc[0][HD:128, :]), waits=[L("acc_w0"), L("den_use")], sem="d_den", dma=True)
            t1 = P.op("sync", lambda e: e.dma_start(out=den[HD:128, :], in_=acc[1][0:HD, :]), waits=[L("acc_w1"), L("den_use")], sem="d_den", dma=True)
            tln = P.op("scalar", lambda e: e.activation(out=den[:], in_=den[:], func=AF.Ln), waits=[t0, t1])
            tex = P.op("scalar", lambda e: e.activation(out=den[:], in_=den[:], func=AF.Exp, scale=-1.0), waits=[tln])
            ta0 = P.op("vector", lambda e: e.tensor_tensor(out=attnT[0:HD, pr, :], in0=acc[0][0:HD, :], in1=den[0:HD, :], op=ALU.mult),
                       waits=[tex, L("acc_w0")])
            ta1 = P.op("vector", lambda e: e.tensor_tensor(out=attnT[HD:128, pr, :], in0=acc[1][HD:128, :], in1=den[HD:128, :], op=ALU.mult),
                       waits=[tex, L("acc_w1")])
            last["den_use"] = [ta0, ta1]
            last["acc_use0"] = [t0, ta0]
            last["acc_use1"] = [t1, ta1]
            t_attn.extend([ta0, ta1])
        return fin_

    npairs = int(os.environ.get('KB_PAIRS', 8))
    for pr in range(npairs):
        for g in range(ngrp):
            for e2 in range(2):
                h = 2 * pr + e2
                ld = {}

                def get(ld=ld, g=g, h=h):
                    if "v" not in ld:
                        ld["v"] = load_head(g, h)
                    return ld["v"]
                ld["get"] = get
                if moba:
                    for sc in range(8):
                        box = {}
                        for kp in range(2 * sc + 2):
                            items.append(moba_item(h, sc, kp, ld, box))
                else:
                    for b in range(8):
                        items.append(band_item(g, h, b, ld, g == 0, b == 7))
        items[-1]["after"] = finalize_item(pr)
    for k in range(len(items)):
        if k == 0:
            items[0]["p1"]()
        if k + 1 < len(items):
            items[k + 1]["p1"]()
        items[k]["p2"]()
        if "after" in items[k]:
            items[k]["after"]()

    xt = [P.sb("xo%d" % i, [128, D], F32) for i in range(2)]
    t_fin = []
    for n in range(int(os.environ.get('KB_WO', NT))):
        b = n % 2
        t_x = P.op("sync", lambda e, n=n, b=b: e.dma_start(out=xt[b][:], in_=xsrc[n * 128:(n + 1) * 128, :]),
                   waits=[L("xo_use%d" % b)], sem="d_xo%d" % b, dma=True)
        tadds = []
        for hf in range(2):
            si = cnt["s"] % 3
            cnt["s"] += 1
            ps = psS[si]
            tm = None
            for kc in range(8):
                tm = P.op("tensor", lambda e, kc=kc, n=n, hf=hf, ps=ps: e.matmul(ps[:, 0:512], lhsT=attnT[:, kc, n * 128:(n + 1) * 128],
                                                                           rhs=wosb[:, kc, hf * 512:(hf + 1) * 512], start=(kc == 0), stop=(kc == 7)),
                          waits=t_attn + t_wo + [L("psS_use%d" % si)])
            tadd = P.op("vector", lambda e, hf=hf, b=b, ps=ps: e.tensor_tensor(out=xt[b][:, hf * 512:(hf + 1) * 512], in0=xt[b][:, hf * 512:(hf + 1) * 512],
                                                                                  in1=ps[:, 0:512], op=ALU.add), waits=[tm, t_x])
            last["psS_use%d" % si] = tadd
            tadds.append(tadd)
        t_o = P.op("sync", lambda e, n=n, b=b: e.dma_start(out=xdst[n * 128:(n + 1) * 128, :], in_=xt[b][:]), waits=tadds, sem="d_xw%d" % b, dma=True)
        last["xo_use%d" % b] = t_o
        t_fin.append(t_o)
    P.op("gpsimd", lambda e: e.memset(fin[:], 0.0), waits=t_fin)
    P.flush()


def phase_ffn(nc, sems, tag, T, layer):
    P = Prog(nc, sems, tag)
    xsrc = T["xmid"][layer]
    xdst = T["xout"][layer]
    NB = 512
    nbat = S // NB
    NU = 3

    wup = P.sb("wup", [128, 8, 2 * DFF], BF16)
    wdn = P.sb("wdn", [128, NFF, D], BF16)
    gnorm = P.sb("gnorm", [128, D], F32)
    cw = P.sb("cw", [128, 3, 2 * NFF], F32)
    cb = P.sb("cb", [128, 2 * NFF], F32)
    ident_b = P.sb("identb", [128, 128], BF16)
    t_idb = mk_identity(P, ident_b)
    xt = [P.sb("xt%d" % i, [128, D], F32) for i in range(2)]
    xr = [P.sb("xr%d" % i, [128, D], F32) for i in range(2)]
    hn = P.sb("hn", [128, D], BF16)
    hnT = P.sb("hnT", [128, 8, NB], BF16)
    hT = P.sb("hT", [128, NFF, NB], BF16)
    U = [P.sb("U%d" % i, [128, NB + 2], F32) for i in range(NU)]
    halo = P.sb("halo", [128, 2 * NFF, 2], F32)
    t3 = [P.sb("t3_%d" % i, [128, NB], F32) for i in range(2)]
    junk = P.sb("junk", [128, D], BF16)
    ssq = P.sb("ssq", [128, 1], F32)
    sd = P.sb("sd", [128, 1], F32)
    rstd = P.sb("rstd", [128, 1], F32)
    fin = P.sb("fin", [128, 1], F32)
    pT = P.ps("pT", [128, 512], F32)
    pTb = pT[:].bitcast(BF16)
    pu = [P.ps("pu%d" % i, [128, NB], F32) for i in range(NU)]
    pg = P.ps("pg", [128, NB], F32)
    pd = [P.ps("pd%d" % i, [128, 512], F32) for i in range(2)]

    t_c = []
    t_c.append(P.op("sync", lambda e: e.dma_start(out=gnorm[:], in_=T["ffn_norm"][layer:layer + 1, :].partition_broadcast(128)), sem="d_c", dma=True))
    for j in range(3):
        t_c.append(P.op("sync", lambda e, j=j: e.dma_start(out=cw[:, j, :], in_=T["ffn_conv_w"][layer, j, :].rearrange("(c p) -> p c", p=128),
                                                             allow_slow_non_contiguous=True),
                        sem="d_c", dma=True))
    t_c.append(P.op("sync", lambda e: e.dma_start(out=cb[:], in_=T["ffn_conv_b"][layer, :].rearrange("(c p) -> p c", p=128),
                                                   allow_slow_non_contiguous=True), sem="d_c", dma=True))
    t_h0 = P.op("gpsimd", lambda e: e.memset(halo[:], 0.0))
    t_wu = []
    for c in range(2 * NFF // 4):
        for kc in range(8):
            t_wu.append(P.op("gpsimd", lambda e, c=c, kc=kc: e.dma_start(out=wup[:, kc, c * 512:(c + 1) * 512],
                                                                          in_=T["ffn_w_up"][layer, kc * 128:(kc + 1) * 128, c * 512:(c + 1) * 512]),
                             sem="d_wu", dma=True))
    t_wd = []
    for j in range(NFF):
        t_wd.append(P.op("gpsimd", lambda e, j=j: e.dma_start(out=wdn[:, j, :], in_=T["ffn_w_down"][layer, j * 128:(j + 1) * 128, :]),
                         sem="d_wd", dma=True))
    last = {}

    def L(k):
        return last.get(k)

    cnt = {"u": 0, "t": 0, "d": 0, "x": 0, "r": 0}
    t_fin = []
    for bt in range(nbat):
        thT = []
        for j in range(4):
            b = cnt["x"] % 2
            cnt["x"] += 1
            r0 = bt * NB + j * 128
            t_x = P.op("sync", lambda e, r0=r0, b=b: e.dma_start(out=xt[b][:], in_=xsrc[r0:r0 + 128, :]),
                       waits=[L("xt_use%d" % b)], sem="d_x%d" % b, dma=True)
            t_sq = P.op("scalar", lambda e, b=b: e.activation(out=junk[:], in_=xt[b][:], func=AF.Square, accum_out=ssq[:]),
                        waits=[t_x, L("ssq_use"), L("junk")])
            last["junk"] = t_sq
            t_sd = P.op("scalar", lambda e: e.activation(out=sd[:], in_=ssq[:], func=AF.Sqrt, scale=1.0 / D, bias=EPS), waits=[t_sq, L("sd_use")])
            last["ssq_use"] = t_sd
            t_rs = P.op("vector", lambda e: e.reciprocal(out=rstd[:], in_=sd[:]), waits=[t_sd, L("rstd_use")])
            last["sd_use"] = t_rs
            t_hn = P.op("vector", lambda e, b=b: e.scalar_tensor_tensor(out=hn[:], in0=xt[b][:], scalar=rstd[:, 0:1], in1=gnorm[:],
                                                                         op0=ALU.mult, op1=ALU.mult), waits=[t_rs, L("hn_use")] + t_c)
            last["rstd_use"] = t_hn
            last["xt_use%d" % b] = t_hn
            tt = None
            for kc in range(8):
                tt = P.op("tensor", lambda e, kc=kc: e.transpose(out=pTb[:, kc * 128:(kc + 1) * 128], in_=hn[:, kc * 128:(kc + 1) * 128],
                                                                 identity=ident_b[:]), waits=[t_hn, t_idb, L("pT_use")])
            last["hn_use"] = tt
            te = P.op("vector", lambda e, j=j: e.tensor_copy(out=hnT[:, :, j * 128:(j + 1) * 128], in_=pTb.rearrange("p (a q) -> p a q", a=8)),
                      waits=[tt, L("hnT_use")])
            last["pT_use"] = te
            thT.append(te)
        t_hT = []
        tm = None
        for j in range(NFF):
            res = {}
            for kind, ch in (("g", j), ("v", NFF + j)):
                ui = cnt["u"] % NU
                cnt["u"] += 1
                p_ = pu[ui]
                u_ = U[ui]
                wtok = t_wu[(ch // 4) * 8:(ch // 4) * 8 + 8]
                for kc in range(8):
                    tm = P.op("tensor", lambda e, kc=kc, ch=ch, p_=p_: e.matmul(p_[:], lhsT=wup[:, kc, ch * 128:(ch + 1) * 128], rhs=hnT[:, kc, :],
                                                                                start=(kc == 0), stop=(kc == 7)),
                              waits=thT + wtok + [L("pu_use%d" % ui)])
                tA = P.op("scalar", lambda e, u_=u_, p_=p_: e.activation(out=u_[:, 2:NB + 2], in_=p_[:], func=AF.Copy), waits=[tm, L("U_use%d" % ui)])
                tH = P.op("gpsimd", lambda e, u_=u_, ch=ch: e.tensor_copy(out=u_[:, 0:2], in_=halo[:, ch, :]),
                          waits=[t_h0, L("halo_w%d" % ch), L("U_use%d" % ui)])
                tB = P.op("scalar", lambda e, p_=p_, ch=ch: e.activation(out=p_[:], in_=p_[:], func=AF.Identity, scale=cw[:, 2, ch:ch + 1], bias=cb[:, ch:ch + 1]),
                          waits=[tA] + t_c)
                tC = P.op("vector", lambda e, p_=p_, u_=u_, ch=ch: e.scalar_tensor_tensor(out=p_[:], in0=u_[:, 1:NB + 1], scalar=cw[:, 1, ch:ch + 1], in1=p_[:],
                                                                                          op0=ALU.mult, op1=ALU.add), waits=[tB, tH])
                tN = P.op("gpsimd", lambda e, u_=u_, ch=ch: e.tensor_copy(out=halo[:, ch, :], in_=u_[:, NB:NB + 2]), waits=[tA, tH])
                last["halo_w%d" % ch] = tN
                res[kind] = (p_, u_, tC, ui, tN, ch)
            p_, u_, tC, ui, tN, ch = res["g"]
            ti = cnt["t"] % 2
            cnt["t"] += 1
            tD = P.op("vector", lambda e, p_=p_, u_=u_, ch=ch, ti=ti: e.scalar_tensor_tensor(out=t3[ti][:], in0=u_[:, 0:NB], scalar=cw[:, 0, ch:ch + 1], in1=p_[:],
                                                                                               op0=ALU.mult, op1=ALU.add), waits=[tC, L("t3_use%d" % ti)])
            last["pu_use%d" % ui] = tD
            last["U_use%d" % ui] = [tD, tN]
            tS = P.op("scalar", lambda e, ti=ti: e.activation(out=pg[:], in_=t3[ti][:], func=AF.Silu), waits=[tD, L("pg_use")])
            last["t3_use%d" % ti] = tS
            p_, u_, tC, ui, tN, ch = res["v"]
            ti2 = cnt["t"] % 2
            cnt["t"] += 1
            tD2 = P.op("vector", lambda e, p_=p_, u_=u_, ch=ch, ti2=ti2: e.scalar_tensor_tensor(out=t3[ti2][:], in0=u_[:, 0:NB], scalar=cw[:, 0, ch:ch + 1], in1=p_[:],
                                                                                                  op0=ALU.mult, op1=ALU.add), waits=[tC, L("t3_use%d" % ti2)])
            last["pu_use%d" % ui] = tD2
            last["U_use%d" % ui] = [tD2, tN]
            tF = P.op("vector", lambda e, j=j, ti2=ti2: e.tensor_tensor(out=hT[:, j, :], in0=t3[ti2][:], in1=pg[:], op=ALU.mult),
                      waits=[tD2, tS, L("hT_use")])
            last["pg_use"] = tF
            last["t3_use%d" % ti2] = tF
            t_hT.append(tF)
        last["hnT_use"] = tm
        tdn = None
        for j4 in range(4):
            rb = cnt["r"] % 2
            cnt["r"] += 1
            r0 = bt * NB + j4 * 128
            t_xr = P.op("sync", lambda e, r0=r0, rb=rb: e.dma_start(out=xr[rb][:], in_=xsrc[r0:r0 + 128, :]),
                        waits=[L("xr_use%d" % rb)], sem="d_r%d" % rb, dma=True)
            tadds = []
            for hf in range(2):
                di = cnt["d"] % 2
                cnt["d"] += 1
                for j in range(NFF):
                    tdn = P.op("tensor", lambda e, j=j, j4=j4, hf=hf, di=di: e.matmul(pd[di][:], lhsT=hT[:, j, j4 * 128:(j4 + 1) * 128],
                                                                                      rhs=wdn[:, j, hf * 512:(hf + 1) * 512], start=(j == 0), stop=(j == NFF - 1)),
                               waits=t_hT + t_wd + [L("pd_use%d" % di)])
                tadd = P.op("vector", lambda e, hf=hf, di=di, rb=rb: e.tensor_tensor(out=xr[rb][:, hf * 512:(hf + 1) * 512],
                                                                                      in0=xr[rb][:, hf * 512:(hf + 1) * 512], in1=pd[di][:], op=ALU.add),
                            waits=[tdn, t_xr])
                last["pd_use%d" % di] = tadd
                tadds.append(tadd)
            t_o = P.op("sync", lambda e, r0=r0, rb=rb: e.dma_start(out=xdst[r0:r0 + 128, :], in_=xr[rb][:]),
                       waits=tadds, sem="d_xw%d" % rb, dma=True)
            last["xr_use%d" % rb] = t_o
            t_fin.append(t_o)
        last["hT_use"] = tdn
    P.op("gpsimd", lambda e: e.memset(fin[:], 0.0), waits=t_fin)
    P.flush()


def host_consts():
    pos = np.arange(S, dtype=np.float32)
    inv = (np.float32(500000.0) ** (-np.arange(0, 16, 2, dtype=np.float32) / np.float32(16))).astype(np.float32)
    ang = (pos[:, None] * inv[None, :]).astype(np.float32)
    cos = np.cos(ang).astype(np.float32)
    sin = np.sin(ang).astype(np.float32)
    ropec = np.zeros((3, 128, NT, 8), np.float32)
    ropes = np.zeros((3, 128, NT, 8), np.float32)
    for v, dil in enumerate(DILS):
        Lg = S // dil
        pp = np.arange(S)
        r, a = np.divmod(pp, Lg)
        t = a * dil + r
        ropec[v] = cos[t].reshape(NT, 128, 8).transpose(1, 0, 2)
        ropes[v] = sin[t].reshape(NT, 128, 8).transpose(1, 0, 2)
    k = np.arange(128)[:, None]
    q = np.arange(128)[None, :]
    prev = (k >= q).astype(np.float32)
    cur = (k <= q).astype(np.float32)
    zero = np.zeros_like(prev)
    bm = np.zeros((128, 3, 8, 128), np.float32)
    for j in range(4):
        bm[:, 0, 2 * j] = prev
        bm[:, 0, 2 * j + 1] = cur
        bm[:, 1, 2 * j] = zero if j == 0 else prev
        bm[:, 1, 2 * j + 1] = cur
        bm[:, 2, 2 * j] = zero if j % 2 == 0 else prev
        bm[:, 2, 2 * j + 1] = cur
    bm = bm.reshape(128, 3, 1024).astype(ml_dtypes.bfloat16)
    oh = (np.arange(S)[None, :] // 256 == np.arange(16)[:, None]).astype(np.float32).astype(ml_dtypes.bfloat16)
    return {"ropec": ropec, "ropes": ropes, "bmask": bm, "onehot": oh}


WEIGHT_SHAPES = {
    "attn_norm": [2, D], "a_w_qkv": [1, D, 9216], "a_q_norm": [1, 3, HD], "a_k_norm": [1, 3, HD], "a_w_o": [1, D, D],
    "b_w_qkv": [1, D, 3072], "b_q_norm": [1, HD], "b_k_norm": [1, HD], "b_w_o": [1, D, D],
    "ffn_norm": [2, D], "ffn_w_up": [2, D, 2 * DFF], "ffn_conv_w": [2, 3, 2 * DFF], "ffn_conv_b": [2, 2 * DFF],
    "ffn_w_down": [2, DFF, D],
}


def build_nc(phases=("A0", "B0", "C0", "A1", "B1", "C1"), debug_out=None):
    nc = bass.Bass("TRN2", target_bir_lowering=False)
    T = {}
    x = nc.dram_tensor("x", [S, D], F32, kind="ExternalInput").ap()
    for k, shp in WEIGHT_SHAPES.items():
        T[k] = nc.dram_tensor(k, shp, F32, kind="ExternalInput").ap()
    T["ropec"] = nc.dram_tensor("ropec", [3, 128, NT, 8], F32, kind="ExternalInput").ap()
    T["ropes"] = nc.dram_tensor("ropes", [3, 128, NT, 8], F32, kind="ExternalInput").ap()
    T["bmask"] = nc.dram_tensor("bmask", [128, 3, 1024], BF16, kind="ExternalInput").ap()
    T["onehot"] = nc.dram_tensor("onehot", [16, S], BF16, kind="ExternalInput").ap()
    y = nc.dram_tensor("y", [S, D], F32, kind="ExternalOutput").ap()

    def scratch(name, shape, dt):
        kind = "ExternalOutput" if (debug_out and name in debug_out) else "Internal"
        return nc.dram_tensor(name, shape, dt, kind=kind).ap()
    R1 = scratch("R1", [S, D], F32)
    R2 = scratch("R2", [S, D], F32)
    R3 = scratch("R3", [S, D], F32)
    T["xin"] = [x, R2]
    T["xmid"] = [R1, R3]
    T["xout"] = [R2, y]
    T["QTs"] = [scratch("QT0", [3, H, HD, S], BF16), scratch("QT1", [1, H, HD + 16, S], BF16)]
    T["KTs"] = [scratch("KT0", [3, H, HD, S], BF16), scratch("KT1", [1, H, HD, S], BF16)]
    T["Vs"] = [scratch("V0", [3, S, H, 128], BF16), scratch("V1", [1, S, H, 128], BF16)]
    sems = Sems(nc)
    fns = {"A": phase_qkv, "B": phase_attn, "C": phase_ffn}
    for ph in phases:
        fns[ph[0]](nc, sems, ph + "_", T, int(ph[1]))
    sems.stack.close()
    return nc


_CACHE = {}


def kernel(**inputs):
    if "nc" not in _CACHE:
        _CACHE["nc"] = build_nc()
        _CACHE["consts"] = host_consts()
    nc = _CACHE["nc"]
    consts = _CACHE["consts"]
    x = np.ascontiguousarray(np.asarray(inputs["x"], dtype=np.float32))
    shared = {k: np.ascontiguousarray(np.asarray(inputs[k], dtype=np.float32)) for k in WEIGHT_SHAPES}
    shared.update(consts)
    in_maps = []
    for b in range(8):
        m = dict(shared)
        m["x"] = x[b]
        in_maps.append(m)
    res = run_bass_kernel_spmd(nc, in_maps, core_ids=list(range(8)))
    return np.stack([np.asarray(r["y"], dtype=np.float32) for r in res.results], axis=0)
```

```python
import contextlib
import os
import numpy as np
import ml_dtypes
import concourse.bass as bass
import concourse.mybir as mybir
from concourse.bass_utils import run_bass_kernel_spmd

F32 = mybir.dt.float32
BF16 = mybir.dt.bfloat16
AF = mybir.ActivationFunctionType
ALU = mybir.AluOpType
AX = mybir.AxisListType

S = 4096
D = 1024
NT = S // 128
H = 16
HD = 64
DFF = 2816
NFF = DFF // 128
EPS = 1e-6
BIG = 30000.0
DILS = (1, 4, 16)
ENGS = ("tensor", "vector", "scalar", "gpsimd", "sync")


class Tok:
    __slots__ = ("sem", "val")

    def __init__(self, sem, val):
        self.sem = sem
        self.val = val


class Sems:
    def __init__(self, nc):
        self.nc = nc
        self.stack = contextlib.ExitStack()
        self.h = {}
        self.cnt = {}

    def get(self, name):
        if name not in self.h:
            self.h[name] = self.stack.enter_context(self.nc.semaphore(name))
            self.cnt[name] = 0
        return name


class Prog:
    def __init__(self, nc, sems, tag):
        self.nc = nc
        self.S = sems
        self.tag = tag
        self.q = {e: [] for e in ENGS}
        self.stack = contextlib.ExitStack()
        self.seen = {e: {} for e in ENGS}

    def sb(self, name, shape, dt):
        return self.stack.enter_context(self.nc.sbuf_tensor(self.tag + name, shape, dt))

    def ps(self, name, shape, dt):
        return self.stack.enter_context(self.nc.psum_tensor(self.tag + name, shape, dt))

    def op(self, eng, fn, waits=(), sem=None, dma=False):
        if sem is None:
            sem = "s_" + eng
        sem = self.S.get(self.tag + sem)
        need = {}

        def add(t):
            if t is None:
                return
            if isinstance(t, (list, tuple)):
                for u in t:
                    add(u)
                return
            if need.get(t.sem, -1) < t.val:
                need[t.sem] = t.val
        add(list(waits))
        seen = self.seen[eng]
        wl = []
        for s, v in need.items():
            if seen.get(s, -1) >= v:
                continue
            seen[s] = v
            wl.append((s, v))
        inc = 16 if dma else 1
        self.S.cnt[sem] += inc
        self.q[eng].append((wl, fn, sem, inc))
        return Tok(sem, self.S.cnt[sem])

    def flush(self):
        nc = self.nc
        sems = self.S.h
        qs = self.q
        with nc.Block() as block:
            def mk(name):
                def body(e):
                    for wl, fn, sem, inc in qs[name]:
                        for s, v in wl:
                            e.wait_ge(sems[s], v)
                        fn(e).then_inc(sems[sem], inc)
                return body
            for name in ENGS:
                if qs[name]:
                    getattr(block, name)(mk(name))
        self.stack.close()


def mk_identity(P, ident, dt_one=1.0):
    t0 = P.op("gpsimd", lambda e: e.memset(ident[:], 0.0))
    return P.op("gpsimd", lambda e: e.affine_select(out=ident[:], in_=ident[:], pattern=[[-1, 128]],
                                                    compare_op=ALU.not_equal, fill=1.0, base=0,
                                                    channel_multiplier=1), waits=[t0])


def phase_qkv(nc, sems, tag, T, layer):
    P = Prog(nc, sems, tag)
    moba = layer == 1
    ngrp = 1 if moba else 3
    xsrc = T["xin"][layer]
    wq = T["b_w_qkv"] if moba else T["a_w_qkv"]

    ident_b = P.sb("identb", [128, 128], BF16)
    ident_f = P.sb("identf", [128, 128], F32)
    t_idb = mk_identity(P, ident_b)
    t_idf = mk_identity(P, ident_f)
    gnorm = P.sb("gnorm", [128, D], F32)
    t_gn = P.op("sync", lambda e: e.dma_start(out=gnorm[:], in_=T["attn_norm"][layer:layer + 1, :].partition_broadcast(128)),
                sem="d_c", dma=True)
    gqk = P.sb("gqk", [128, ngrp, 2, HD], F32)
    t_gq = []
    for g in range(ngrp):
        for s_, nm in ((0, "q"), (1, "k")):
            src = (T["b_%s_norm" % nm][0:1, :] if moba else T["a_%s_norm" % nm][0, g:g + 1, :])
            t_gq.append(P.op("sync", lambda e, g=g, s_=s_, src=src: e.dma_start(
                out=gqk[:, g, s_, :], in_=src.partition_broadcast(128)), sem="d_c", dma=True))
    ropecs = [P.sb("ropec%d" % i, [128, NT, 8], F32) for i in range(2)]
    ropess = [P.sb("ropes%d" % i, [128, NT, 8], F32) for i in range(2)]
    junkx = P.sb("junkx", [128, D], BF16)
    wsb = [P.sb("w%d" % i, [128, 8, 3, 1024], BF16) for i in range(2 if not moba else 1)]
    xt = [P.sb("xt%d" % i, [128, D], F32) for i in range(3)]
    junks = [P.sb("junk%d" % i, [128, 1024], F32) for i in range(2)]
    ssq = P.sb("ssq", [128, 1], F32)
    sd = P.sb("sd", [128, 1], F32)
    rstd = P.sb("rstd", [128, 1], F32)
    hn = P.sb("hn", [128, D], BF16)
    hnT = [P.sb("hnT%d" % i, [128, 8, 128], BF16) for i in range(2)]
    ss2 = P.sb("ss2", [128, 32], F32)
    sd2 = P.sb("sd2", [128, 32], F32)
    rs2 = P.sb("rs2", [128, 32], F32)
    qk = [P.sb("qk%d" % i, [128, 2, H, HD], F32) for i in range(2)]
    rt = [P.sb("rt%d" % i, [128, 2 * H, 8], F32) for i in range(4)]
    vaug = [P.sb("vaug%d" % i, [128, H, 128], BF16) for i in range(2)]
    qT4 = [P.sb("qT4_%d" % i, [128, 8, 512], BF16) for i in range(2)]
    kT4 = [P.sb("kT4_%d" % i, [128, 8, 512], BF16) for i in range(2)]
    pT = P.ps("pT", [128, 512], F32)
    pTb = pT[:].bitcast(BF16)
    pqs = [P.ps("pq%d" % i, [128, 1024], F32) for i in range(2)]
    pv = P.ps("pv", [128, 512], F32)
    ptr = P.ps("ptr", [128, 8, 128], F32)
    if moba:
        qT32 = P.sb("qT32", [128, 8, 128], F32)
        ksum = P.sb("ksum", [128, 8, 16], F32)
        kpart = P.sb("kpart", [128, 8], F32)
        gs = P.sb("gs", [128, H, 16], F32)
        t8 = P.sb("t8", [128, H, 8], F32)
        mv = P.sb("mv", [128, H, 16], F32)
        mT4 = [P.sb("mT4_%d" % i, [128, 2, 512], BF16) for i in range(2)]

    t_ones = []
    for i in range(2):
        t_ones.append(P.op("gpsimd", lambda e, i=i: e.memset(vaug[i][:], 1.0)))
    t_ks0 = P.op("gpsimd", lambda e: e.memset(ksum[:], 0.0)) if moba else None

    last = {}

    def L(k):
        return last.get(k)

    tw_use = [None, None]
    t_outdma = []
    tiles = [(g, n) for g in range(ngrp) for n in range(NT)]
    state = {}

    def load_w(g):
        wb = wsb[g % len(wsb)]
        toks = []
        for kc in range(8):
            for s_ in range(3):
                if moba:
                    c0 = s_ * 1024
                else:
                    c0 = (s_ * 3 + g) * 1024
                toks.append(P.op("gpsimd", lambda e, wb=wb, kc=kc, s_=s_, c0=c0: e.dma_start(
                    out=wb[:, kc, s_, :], in_=wq[0, kc * 128:(kc + 1) * 128, c0:c0 + 1024]),
                    waits=[tw_use[g % len(wsb)]], sem="d_w%d" % (g % len(wsb)), dma=True))
        return toks

    def load_rope(g):
        v = 0 if moba else g
        rb = g % 2
        a = P.op("sync", lambda e: e.dma_start(out=ropecs[rb][:], in_=T["ropec"][v]), waits=[L("rope_use%d" % rb)], sem="d_r%d" % rb, dma=True)
        b = P.op("sync", lambda e: e.dma_start(out=ropess[rb][:], in_=T["ropes"][v]), waits=[L("rope_use%d" % rb)], sem="d_r%d" % rb, dma=True)
        return [a, b]

    def x_rows(g, n):
        dil = 1 if moba else DILS[g]
        Lg = S // dil
        p0 = n * 128
        r, a0 = divmod(p0, Lg)
        v = xsrc.rearrange("(a r) d -> r a d", r=dil)
        return v[r, a0:a0 + 128, :]

    def xload(idx):
        g, n = tiles[idx]
        b3 = idx % 3
        state["xl%d" % idx] = P.op("sync", lambda e: e.dma_start(out=xt[b3][:], in_=x_rows(g, n)),
                                   waits=[L("xt_use%d" % b3)], sem="d_x%d" % b3, dma=True)

    def prepA(idx):
        g, n = tiles[idx]
        st = state.setdefault(idx, {})
        if n == 0:
            if g == 0:
                state["w0"] = load_w(0)
            state["rope%d" % g] = load_rope(g)
        if n == 1 and g + 1 < ngrp:
            state["w%d" % (g + 1)] = load_w(g + 1)
        b3 = idx % 3
        t_x = state["xl%d" % idx]
        t_sq = P.op("scalar", lambda e: e.activation(out=junkx[:], in_=xt[b3][:], func=AF.Square, accum_out=ssq[:]),
                    waits=[t_x, L("ssq_use")])
        t_sd = P.op("scalar", lambda e: e.activation(out=sd[:], in_=ssq[:], func=AF.Sqrt, scale=1.0 / D, bias=EPS),
                    waits=[t_sq, L("sd_use")])
        last["ssq_use"] = t_sd
        t_rs = P.op("vector", lambda e: e.reciprocal(out=rstd[:], in_=sd[:]), waits=[t_sd, L("rstd_use")])
        last["sd_use"] = t_rs
        t_hn = P.op("vector", lambda e: e.scalar_tensor_tensor(out=hn[:], in0=xt[b3][:], scalar=rstd[:, 0:1], in1=gnorm[:],
                                                               op0=ALU.mult, op1=ALU.mult),
                    waits=[t_rs, t_gn, L("hn_use")])
        last["rstd_use"] = t_hn
        last["xt_use%d" % b3] = t_hn
        st["t_hn"] = t_hn

    def prepB(idx):
        st = state[idx]
        b = idx % 2
        t_hn = st["t_hn"]
        tt = None
        for kc in range(8):
            tt = P.op("tensor", lambda e, kc=kc: e.transpose(out=pTb[:, kc * 128:(kc + 1) * 128], in_=hn[:, kc * 128:(kc + 1) * 128],
                                                             identity=ident_b[:]),
                      waits=[t_hn, t_idb, L("pT_use")])
        last["hn_use"] = tt
        t_hT = P.op("scalar", lambda e: e.activation(out=hnT[b][:].rearrange("p a b -> p (a b)"), in_=pTb, func=AF.Copy),
                    waits=[tt, L("hnT_use%d" % b)])
        last["pT_use"] = t_hT
        st["hT"] = t_hT

    def main_tile(idx):
        g, n = tiles[idx]
        st = state[idx]
        b = idx % 2
        t_hT = st["hT"]
        tw = state["w%d" % g]
        wb = wsb[g % len(wsb)]
        qkb = qk[b]
        va = vaug[b]
        gsel = 0 if moba else g

        def mm(dst, s_, c0, extra):
            t = None
            for kc in range(8):
                t = P.op("tensor", lambda e, kc=kc: e.matmul(dst, lhsT=hnT[b][:, kc, :], rhs=wb[:, kc, s_, c0:c0 + 512],
                                                            start=(kc == 0), stop=(kc == 7)),
                         waits=[t_hT, tw, extra])
            return t
        t_gs_ = []
        tv = None
        tvm = None
        for s_ in range(2):
            pp = pqs[s_]
            jk = junks[s_]
            tq = None
            for hf in range(2):
                tq = mm(pp[:, hf * 512:(hf + 1) * 512], s_, hf * 512, L("pq_use%d" % s_))
            hf = s_
            tvm = mm(pv[:], 2, hf * 512, L("pv_use"))
            t_sq2 = P.op("scalar", lambda e, pp=pp, jk=jk: e.activation(out=jk[:], in_=pp[:], func=AF.Square), waits=[tq, L("junk%d" % s_)])
            pvv = pv[:].rearrange("p (i e d) -> p i e d", e=2, d=HD)
            vv = va[:, hf * 8:(hf + 1) * 8, :].rearrange("p (i e) c -> p i e c", e=2)
            tv0 = P.op("scalar", lambda e, pvv=pvv, vv=vv: e.activation(out=vv[:, :, 0, 0:HD], in_=pvv[:, :, 0, :], func=AF.Copy),
                       waits=[tvm, L("vaug_use%d" % b)] + t_ones)
            tv = P.op("scalar", lambda e, pvv=pvv, vv=vv: e.activation(out=vv[:, :, 1, HD:128], in_=pvv[:, :, 1, :], func=AF.Copy),
                      waits=[tvm, L("vaug_use%d" % b)] + t_ones)
            last["pv_use"] = tv
            t_ss2 = P.op("vector", lambda e, jk=jk, s_=s_: e.tensor_reduce(out=ss2[:, s_ * 16:(s_ + 1) * 16], in_=jk[:].rearrange("p (h d) -> p h d", d=HD),
                                                                        op=ALU.add, axis=AX.X), waits=[t_sq2, L("ss2_use%d" % s_)])
            last["junk%d" % s_] = t_ss2
            t_sd2 = P.op("scalar", lambda e, s_=s_: e.activation(out=sd2[:, s_ * 16:(s_ + 1) * 16], in_=ss2[:, s_ * 16:(s_ + 1) * 16], func=AF.Sqrt,
                                                                  scale=1.0 / HD, bias=EPS), waits=[t_ss2, L("sd2_use%d" % s_)])
            last["ss2_use%d" % s_] = t_sd2
            t_rs2 = P.op("vector", lambda e, s_=s_: e.reciprocal(out=rs2[:, s_ * 16:(s_ + 1) * 16], in_=sd2[:, s_ * 16:(s_ + 1) * 16]),
                         waits=[t_sd2, L("rs2_use%d" % s_)])
            last["sd2_use%d" % s_] = t_rs2
            t_nm = P.op("vector", lambda e, pp=pp, s_=s_: e.tensor_tensor(out=qkb[:, s_, :, :],
                                                                           in0=pp[:].rearrange("p (h d) -> p h d", d=HD),
                                                                           in1=rs2[:, s_ * 16:(s_ + 1) * 16].unsqueeze(2).broadcast_to([128, H, HD]), op=ALU.mult),
                        waits=[t_rs2, L("qk_use%d" % b)])
            last["rs2_use%d" % s_] = t_nm
            last["pq_use%d" % s_] = t_nm
            t_gs_.append(P.op("gpsimd", lambda e, s_=s_: e.tensor_tensor(out=qkb[:, s_, :, :], in0=qkb[:, s_, :, :],
                                                                          in1=gqk[:, gsel, s_, :].unsqueeze(1).broadcast_to([128, H, HD]),
                                                                          op=ALU.mult), waits=[t_nm] + t_gq))
        st["t_g"] = t_gs_
        tw_use[g % len(wsb)] = tv
        t_vo = P.op("sync", lambda e: e.dma_start(out=T["Vs"][layer][g, n * 128:(n + 1) * 128, :, :], in_=va[:]),
                    waits=[tv], sem="d_vo%d" % b, dma=True)
        last["vaug_use%d" % b] = t_vo
        t_outdma.append(t_vo)
        last["hnT_use%d" % b] = tvm

    def stage1r(idx):
        g, n = tiles[idx]
        st = state[idx]
        b = idx % 2
        qkb = qk[b]
        t_g = st["t_g"]
        qv = qkb[:].rearrange("p s h d -> p (s h) d")
        x1 = qv[:, :, 0:8]
        x2 = qv[:, :, 8:16]
        cb = ropecs[g % 2][:, n, :].unsqueeze(1).broadcast_to([128, 32, 8])
        sbb = ropess[g % 2][:, n, :].unsqueeze(1).broadcast_to([128, 32, 8])
        tr = state["rope%d" % g]
        w0 = [t_g, tr, L("rt_use")]
        a1 = P.op("vector", lambda e: e.tensor_tensor(out=rt[0][:], in0=x1, in1=cb, op=ALU.mult), waits=w0)
        a2 = P.op("vector", lambda e: e.tensor_tensor(out=rt[1][:], in0=x2, in1=sbb, op=ALU.mult), waits=w0)
        a3 = P.op("vector", lambda e: e.tensor_tensor(out=rt[2][:], in0=x2, in1=cb, op=ALU.mult), waits=w0)
        a4 = P.op("vector", lambda e: e.tensor_tensor(out=rt[3][:], in0=x1, in1=sbb, op=ALU.mult), waits=w0)
        a5 = P.op("vector", lambda e: e.tensor_tensor(out=x1, in0=rt[0][:], in1=rt[1][:], op=ALU.subtract), waits=[a1, a2, a3, a4])
        a6 = P.op("vector", lambda e: e.tensor_tensor(out=x2, in0=rt[2][:], in1=rt[3][:], op=ALU.add), waits=[a1, a2, a3, a4, a5])
        last["rt_use"] = a6
        last["rope_use%d" % (g % 2)] = a6
        st["qkg"] = [a5, a6]

    def stage2(idx):
        g, n = tiles[idx]
        st = state[idx]
        b = idx % 2
        qkb = qk[b]
        n4, j4 = divmod(n, 4)
        b4 = (idx // 4) % 2
        for s_, dst4, nm in ((0, qT4[b4], "Q"), (1, kT4[b4], "K")):
            tt = None
            for c in range(8):
                tt = P.op("tensor", lambda e, c=c, s_=s_: e.transpose(out=ptr[:, c, :], in_=qkb[:, s_, 2 * c:2 * c + 2, :].rearrange("p h d -> p (h d)"),
                                                                      identity=ident_f[:]),
                          waits=[st["qkg"], t_idf, L("ptr_use")])
            wv = [tt, L("%s4_use%d" % (nm, b4))]
            te = P.op("scalar", lambda e, dst4=dst4: e.activation(out=dst4[:, :, j4 * 128:(j4 + 1) * 128], in_=ptr[:], func=AF.Copy),
                      waits=wv)
            tl = [te]
            if moba and s_ == 0:
                tl.append(P.op("vector", lambda e: e.tensor_copy(out=qT32[:], in_=ptr[:]), waits=[tt, te, L("qT32_use")]))
            if moba and s_ == 1:
                tkp = P.op("vector", lambda e: e.tensor_reduce(out=kpart[:], in_=ptr[:], op=ALU.add, axis=AX.X),
                           waits=[tt, te, L("kpart_use")])
                j = n // 2
                tks = P.op("vector", lambda e, j=j: e.tensor_tensor(out=ksum[:, :, j], in0=ksum[:, :, j], in1=kpart[:], op=ALU.add),
                           waits=[tkp, t_ks0, L("ksum_w"), L("qT32_use")])
                last["kpart_use"] = tks
                last["ksum_w"] = tks
                tl.append(tks)
            last["ptr_use"] = tl
            st["te%d" % s_] = te
        last["qk_use%d" % b] = last["ptr_use"]
        if moba:
            stage_gate(idx)
        if j4 == 3:
            for s_, src4, nm, dstT in ((0, qT4[b4], "Q", T["QTs"][layer]), (1, kT4[b4], "K", T["KTs"][layer])):
                tds = []
                for e2 in range(2):
                    dv = dstT[g].rearrange("(c e) r t -> e r c t", e=2)[e2, 0:HD, :, n4 * 512:(n4 + 1) * 512]
                    tds.append(P.op("sync", lambda e, dv=dv, src4=src4, e2=e2: e.dma_start(out=dv, in_=src4[e2 * HD:(e2 + 1) * HD, :, :]),
                                    waits=[state[idx - 3]["te%d" % s_], state[idx - 2]["te%d" % s_], state[idx - 1]["te%d" % s_], st["te%d" % s_]],
                                    sem="d_%so%d" % (nm, b4), dma=True))
                last["%s4_use%d" % (nm, b4)] = tds
                t_outdma.extend(tds)
            if moba:
                tds = []
                m4 = mT4[b4]
                for h in range(H):
                    dv = T["QTs"][layer][0, h, HD:HD + 16, n4 * 512:(n4 + 1) * 512]
                    tds.append(P.op("gpsimd", lambda e, dv=dv, h=h, m4=m4: e.dma_start(out=dv, in_=m4[(h % 8) * 16:(h % 8) * 16 + 16, h // 8, :]),
                                    waits=[state[idx - 3]["tm"], state[idx - 2]["tm"], state[idx - 1]["tm"], st["tm"]],
                                    sem="d_mo%d" % b4, dma=True))
                last["m4_use%d" % b4] = tds
                t_outdma.extend(tds)

    def stage_gate(idx):
        g, n = tiles[idx]
        st = state[idx]
        b4 = (idx // 4) % 2
        j4 = n % 4
        ob = n // 2
        pg = pT[:, 0:256]
        tg = None
        tq32 = last["ptr_use"]
        for h in range(H):
            r0 = (h % 2) * HD
            tg = P.op("tensor", lambda e, h=h, r0=r0: e.matmul(pg[:, h * 16:(h + 1) * 16], lhsT=qT32[r0:r0 + HD, h // 2, :],
                                                                rhs=ksum[r0:r0 + HD, h // 2, :], start=True, stop=True),
                      waits=[tq32, L("ksum_w"), L("pT_use")])
        last["qT32_use"] = tg
        t_gs = P.op("vector", lambda e: e.tensor_copy(out=gs[:].rearrange("p h j -> p (h j)"), in_=pg), waits=[tg, L("gs_use")])
        last["pT_use"] = t_gs
        t_ms = t_gs
        if ob < 16:
            t_ms = P.op("vector", lambda e: e.memset(gs[:, :, ob:16], -1e30), waits=[t_gs])
        tm8 = []
        for h in range(H):
            tm8.append(P.op("vector", lambda e, h=h: e.max(out=t8[:, h, :], in_=gs[:, h, :]), waits=[t_ms, L("t8_use")]))
        t_sel = P.op("vector", lambda e: e.tensor_tensor(out=mv[:], in0=gs[:], in1=t8[:, :, 2:3].broadcast_to([128, H, 16]), op=ALU.is_ge),
                     waits=tm8 + [L("mv_use")])
        last["t8_use"] = t_sel
        last["gs_use"] = t_sel
        t_mv = P.op("vector", lambda e: e.tensor_scalar(out=mv[:], in0=mv[:], scalar1=-1.0, scalar2=BIG, op0=ALU.add, op1=ALU.mult),
                    waits=[t_sel])
        t_own = P.op("vector", lambda e: e.memset(mv[:, :, ob:ob + 1], 0.0), waits=[t_mv])
        pm = pT[:, 256:512].rearrange("p (a q) -> p a q", a=2)
        tt = None
        for a in range(2):
            tt = P.op("tensor", lambda e, a=a: e.transpose(out=pm[:, a, :], in_=mv[:, a * 8:(a + 1) * 8, :].rearrange("p h j -> p (h j)"),
                                                           identity=ident_f[:]), waits=[t_own, t_idf, L("pm_use")])
        last["mv_use"] = tt
        tm = P.op("vector", lambda e: e.tensor_copy(out=mT4[b4][:, :, j4 * 128:(j4 + 1) * 128], in_=pm), waits=[tt, L("m4_use%d" % b4)])
        last["pm_use"] = tm
        last["pT_use"] = [last["pT_use"], tm]
        st["tm"] = tm

    xload(0)
    xload(1)
    prepA(0)
    prepB(0)
    for idx in range(len(tiles)):
        if idx + 2 < len(tiles):
            xload(idx + 2)
        if idx + 1 < len(tiles):
            prepA(idx + 1)
        main_tile(idx)
        if idx + 1 < len(tiles):
            prepB(idx + 1)
        stage1r(idx)
        if idx >= 1:
            stage2(idx - 1)
    stage2(len(tiles) - 1)
    P.op("gpsimd", lambda e: e.memset(ssq[:], 0.0), waits=t_outdma + [L("ssq_use")])
    P.flush()


def phase_attn(nc, sems, tag, T, layer):
    P = Prog(nc, sems, tag)
    moba = layer == 1
    ngrp = 1 if moba else 3
    xsrc = T["xin"][layer]
    xdst = T["xmid"][layer]
    wo = T["b_w_o"] if moba else T["a_w_o"]
    QTs, KTs, Vs = T["QTs"][layer], T["KTs"][layer], T["Vs"][layer]
    KR = HD + 16 if moba else HD

    attnT = P.sb("attnT", [128, 8, S], BF16)
    acc = [P.sb("acc%d" % i, [128, S], F32) for i in range(2)]
    den = P.sb("den", [128, S], F32)
    qt = [P.sb("qt%d" % i, [KR, S], BF16) for i in range(2)]
    kt = [P.sb("kt%d" % i, [KR, S], BF16) for i in range(2)]
    vt = [P.sb("vt%d" % i, [128, NT, 128], BF16) for i in range(2)]
    pbuf = [P.sb("pb%d" % i, [128, 1024], BF16) for i in range(3)]
    masks = P.sb("masks", [128, 3, 1024], BF16)
    wosb = P.sb("wosb", [128, 8, D], BF16)
    fin = P.sb("fin", [128, 1], F32)
    psS = [P.ps("psS%d" % i, [128, 1024], F32) for i in range(3)]
    psO = [P.ps("psO%d" % i, [128, 512], F32) for i in range(2)]

    t_mk = P.op("sync", lambda e: e.dma_start(out=masks[:], in_=T["bmask"]), sem="d_c", dma=True)
    t_oh = []
    if moba:
        for i in range(2):
            t_oh.append(P.op("sync", lambda e, i=i: e.dma_start(out=kt[i][HD:HD + 16, :], in_=T["onehot"]), sem="d_c", dma=True))
    t_wo = []
    for kc in range(8):
        t_wo.append(P.op("gpsimd", lambda e, kc=kc: e.dma_start(out=wosb[:, kc, :], in_=wo[0, kc * 128:(kc + 1) * 128, :]),
                         sem="d_wo", dma=True))
    last = {}

    def L(k):
        return last.get(k)

    cnt = {"s": 0, "o": 0, "p": 0, "ld": 0}
    t_attn = []

    def load_head(g, h):
        i = cnt["ld"] % 2
        cnt["ld"] += 1
        c, e2 = divmod(h, 2)
        toks = []
        if moba:
            qsrc = QTs[0, h, :, :]
            toks.append(P.op("sync", lambda e: e.dma_start(out=qt[i][:], in_=qsrc), waits=[L("ld_use%d" % i)], sem="d_ld%d" % i, dma=True))
        else:
            qsrc = QTs[g, h, 0:HD, :]
            toks.append(P.op("sync", lambda e: e.dma_start(out=qt[i][0:HD, :], in_=qsrc), waits=[L("ld_use%d" % i)], sem="d_ld%d" % i, dma=True))
        ksrc = KTs[g, h, 0:HD, :]
        toks.append(P.op("sync", lambda e: e.dma_start(out=kt[i][0:HD, :], in_=ksrc), waits=[L("ld_use%d" % i)], sem="d_ld%d" % i, dma=True))
        vsrc = Vs[g].rearrange("(n p) h c -> p n h c", p=128)[:, :, h, :]
        toks.append(P.op("gpsimd", lambda e: e.dma_start(out=vt[i][:], in_=vsrc), waits=[L("ld_use%d" % i)], sem="d_ld%d" % i, dma=True))
        return i, toks

    def acc_view(a, g, b):
        if moba or g == 0:
            return a[:, b * 512:(b + 1) * 512]
        dil = DILS[g]
        nbseg = 32 // dil
        if g == 1:
            n0 = 4 * b
            r, a0 = divmod(n0, nbseg)
            a0 *= 128
            return a[:].rearrange("p (a r) -> p r a", r=dil)[:, r, a0:a0 + 512]
        return a[:].rearrange("p (a r) -> p r a", r=dil)[:, 2 * b:2 * b + 2, :]

    items = []

    def band_item(g, h, b, ld, first, lastb):
        it = {}
        e2 = h % 2
        a = acc[e2]
        nbseg = 32 // DILS[g]

        def p1():
            i, tl = ld["get"]()
            si = cnt["s"] % 3
            cnt["s"] += 1
            ps = psS[si]
            pb = pbuf[si]
            ts = None
            for j in range(4):
                n = 4 * b + j
                npv = max(n - 1, 0)
                ts = P.op("tensor", lambda e, j=j, npv=npv, n=n: e.matmul(ps[:, (2 * j) * 128:(2 * j + 1) * 128], lhsT=kt[i][0:HD, npv * 128:(npv + 1) * 128],
                                                                           rhs=qt[i][0:HD, n * 128:(n + 1) * 128], start=True, stop=True),
                          waits=[tl, L("psS_use%d" % si)])
                ts = P.op("tensor", lambda e, j=j, n=n: e.matmul(ps[:, (2 * j + 1) * 128:(2 * j + 2) * 128], lhsT=kt[i][0:HD, n * 128:(n + 1) * 128],
                                                                  rhs=qt[i][0:HD, n * 128:(n + 1) * 128], start=True, stop=True),
                          waits=[tl, L("psS_use%d" % si)])
            te = P.op("scalar", lambda e: e.activation(out=pb[:], in_=ps[:], func=AF.Exp, scale=0.125),
                      waits=[ts, L("pb_use%d" % si)])
            last["psS_use%d" % si] = te
            if g == 2:
                mvv = 2
            elif (4 * b) % nbseg == 0:
                mvv = 1
            else:
                mvv = 0
            tm = P.op("vector", lambda e: e.tensor_tensor(out=pb[:], in0=pb[:], in1=masks[:, mvv, :], op=ALU.mult),
                      waits=[te, t_mk])
            it.update(i=i, tl=tl, si=si, pb=pb, tm=tm)

        def p2():
            i, tl, si, pb, tm = it["i"], it["tl"], it["si"], it["pb"], it["tm"]
            oi = cnt["o"] % 2
            cnt["o"] += 1
            po = psO[oi]
            to = None
            for j in range(4):
                n = 4 * b + j
                npv = max(n - 1, 0)
                to = P.op("tensor", lambda e, j=j, npv=npv: e.matmul(po[:, j * 128:(j + 1) * 128], lhsT=vt[i][:, npv, :], rhs=pb[:, (2 * j) * 128:(2 * j + 1) * 128],
                                                                      start=True, stop=False), waits=[tm, tl, L("psO_use%d" % oi)])
                to = P.op("tensor", lambda e, j=j, n=n: e.matmul(po[:, j * 128:(j + 1) * 128], lhsT=vt[i][:, n, :], rhs=pb[:, (2 * j + 1) * 128:(2 * j + 2) * 128],
                                                                  start=False, stop=True), waits=[tm, tl, L("psO_use%d" % oi)])
            last["pb_use%d" % si] = to
            av = acc_view(a, g, b)
            pov = po[:].rearrange("p (r a) -> p r a", r=2) if g == 2 else po[:]
            if first:
                ta = P.op("vector", lambda e: e.tensor_copy(out=av, in_=pov), waits=[to, L("acc_use%d" % e2)])
            else:
                ta = P.op("vector", lambda e: e.tensor_tensor(out=av, in0=av, in1=pov, op=ALU.add), waits=[to, L("acc_w%d" % e2)])
            last["psO_use%d" % oi] = ta
            last["acc_w%d" % e2] = ta
            if lastb:
                last["ld_use%d" % i] = to
        it["p1"] = p1
        it["p2"] = p2
        return it

    def moba_item(h, sc, kp, ld, po_box):
        it = {}
        e2 = h % 2
        a = acc[e2]
        nk = 4 * sc + 4
        kts = (2 * kp, 2 * kp + 1)

        def p1():
            i, tl = ld["get"]()
            si = cnt["s"] % 3
            cnt["s"] += 1
            ps = psS[si]
            pb = pbuf[si]
            ts = None
            for u, ktile in enumerate(kts):
                c0 = max(0, ktile - 4 * sc) * 128
                ts = P.op("tensor", lambda e, u=u, ktile=ktile, c0=c0: e.matmul(
                    ps[:, u * 512 + c0:(u + 1) * 512], lhsT=kt[i][:, ktile * 128:(ktile + 1) * 128],
                    rhs=qt[i][:, sc * 512 + c0:(sc + 1) * 512], start=True, stop=True),
                    waits=[tl, L("psS_use%d" % si)] + t_oh)
            c00 = max(0, kts[0] - 4 * sc) * 128
            te = P.op("scalar", lambda e: e.activation(out=pb[:, c00:1024], in_=ps[:, c00:1024], func=AF.Exp, scale=0.125),
                      waits=[ts, L("pb_use%d" % si)])
            last["psS_use%d" % si] = te
            tms = [te]
            for u, ktile in enumerate(kts):
                if ktile >= 4 * sc:
                    c0 = (ktile - 4 * sc) * 128
                    tms.append(P.op("vector", lambda e, u=u, c0=c0: e.tensor_tensor(out=pb[:, u * 512 + c0:u * 512 + c0 + 128],
                                                                                      in0=pb[:, u * 512 + c0:u * 512 + c0 + 128],
                                                                                      in1=masks[:, 0, 128:256], op=ALU.mult), waits=[te, t_mk]))
            it.update(i=i, tl=tl, si=si, pb=pb, tm=tms)

        def p2():
            i, tl, si, pb, tm = it["i"], it["tl"], it["si"], it["pb"], it["tm"]
            if kp == 0:
                po_box["oi"] = cnt["o"] % 2
                cnt["o"] += 1
            oi = po_box["oi"]
            po = psO[oi]
            to = None
            for u, ktile in enumerate(kts):
                c0 = max(0, ktile - 4 * sc) * 128
                to = P.op("tensor", lambda e, u=u, ktile=ktile, c0=c0: e.matmul(po[:, c0:512], lhsT=vt[i][:, ktile, :], rhs=pb[:, u * 512 + c0:(u + 1) * 512],
                                                                                start=(ktile == 0), stop=(ktile == nk - 1)),
                          waits=[tm, tl, L("psO_use%d" % oi)])
            last["pb_use%d" % si] = to
            if kts[1] == nk - 1:
                ta = P.op("vector", lambda e: e.tensor_copy(out=a[:, sc * 512:(sc + 1) * 512], in_=po[:]), waits=[to, L("acc_use%d" % e2)])
                last["psO_use%d" % oi] = ta
                last["acc_w%d" % e2] = ta
                if sc == 7:
                    last["ld_use%d" % i] = to
        it["p1"] = p1
        it["p2"] = p2
        return it

    def finalize_item(pr):
        def fin_():
            t0 = P.op("sync", lambda e: e.dma_start(out=den[0:HD, :], in_=acc[0][HD:128, :]), waits=[L("acc_w0"), L("den_use")], sem="d_den", dma=True)
            t1 = P.op("sync", lambda e: e.dma_start(out=den[HD:128, :], in_=acc[1][0:HD, :]), waits=[L("acc_w1"), L("den_use")], sem="d_den", dma=True)
            tln = P.op("scalar", lambda e: e.activation(out=den[:], in_=den[:], func=AF.Ln), waits=[t0, t1])
            tex = P.op("scalar", lambda e: e.activation(out=den[:], in_=den[:], func=AF.Exp, scale=-1.0), waits=[tln])
            ta0 = P.op("vector", lambda e: e.tensor_tensor(out=attnT[0:HD, pr, :], in0=acc[0][0:HD, :], in1=den[0:HD, :], op=ALU.mult),
                       waits=[tex, L("acc_w0")])
            ta1 = P.op("vector", lambda e: e.tensor_tensor(out=attnT[HD:128, pr, :], in0=acc[1][HD:128, :], in1=den[HD:128, :], op=ALU.mult),
                       waits=[tex, L("acc_w1")])
            last["den_use"] = [ta0, ta1]
            last["acc_use0"] = [t0, ta0]
            last["acc_use1"] = [t1, ta1]
            t_attn.extend([ta0, ta1])
        return fin_

    npairs = int(os.environ.get('KB_PAIRS', 8))
    for pr in range(npairs):
        for g in range(ngrp):
            for e2 in range(2):
                h = 2 * pr + e2
                ld = {}

                def get(ld=ld, g=g, h=h):
                    if "v" not in ld:
                        ld["v"] = load_head(g, h)
                    return ld["v"]
                ld["get"] = get
                if moba:
                    for sc in range(8):
                        box = {}
                        for kp in range(2 * sc + 2):
                            items.append(moba_item(h, sc, kp, ld, box))
                else:
                    for b in range(8):
                        items.append(band_item(g, h, b, ld, g == 0, b == 7))
        items[-1]["after"] = finalize_item(pr)
    for k in range(len(items)):
        if k == 0:
            items[0]["p1"]()
        if k + 1 < len(items):
            items[k + 1]["p1"]()
        items[k]["p2"]()
        if "after" in items[k]:
            items[k]["after"]()

    xt = [P.sb("xo%d" % i, [128, D], F32) for i in range(2)]
    t_fin = []
    for n in range(int(os.environ.get('KB_WO', NT))):
        b = n % 2
        t_x = P.op("sync", lambda e, n=n, b=b: e.dma_start(out=xt[b][:], in_=xsrc[n * 128:(n + 1) * 128, :]),
                   waits=[L("xo_use%d" % b)], sem="d_xo%d" % b, dma=True)
        tadds = []
        for hf in range(2):
            si = cnt["s"] % 3
            cnt["s"] += 1
            ps = psS[si]
            tm = None
            for kc in range(8):
                tm = P.op("tensor", lambda e, kc=kc, n=n, hf=hf, ps=ps: e.matmul(ps[:, 0:512], lhsT=attnT[:, kc, n * 128:(n + 1) * 128],
                                                                           rhs=wosb[:, kc, hf * 512:(hf + 1) * 512], start=(kc == 0), stop=(kc == 7)),
                          waits=t_attn + t_wo + [L("psS_use%d" % si)])
            tadd = P.op("vector", lambda e, hf=hf, b=b, ps=ps: e.tensor_tensor(out=xt[b][:, hf * 512:(hf + 1) * 512], in0=xt[b][:, hf * 512:(hf + 1) * 512],
                                                                                  in1=ps[:, 0:512], op=ALU.add), waits=[tm, t_x])
            last["psS_use%d" % si] = tadd
            tadds.append(tadd)
        t_o = P.op("sync", lambda e, n=n, b=b: e.dma_start(out=xdst[n * 128:(n + 1) * 128, :], in_=xt[b][:]), waits=tadds, sem="d_xw%d" % b, dma=True)
        last["xo_use%d" % b] = t_o
        t_fin.append(t_o)
    P.op("gpsimd", lambda e: e.memset(fin[:], 0.0), waits=t_fin)
    P.flush()


def phase_ffn(nc, sems, tag, T, layer):
    P = Prog(nc, sems, tag)
    xsrc = T["xmid"][layer]
    xdst = T["xout"][layer]
    NB = 512
    nbat = S // NB
    NU = 3

    wup = P.sb("wup", [128, 8, 2 * DFF], BF16)
    wdn = P.sb("wdn", [128, NFF, D], BF16)
    gnorm = P.sb("gnorm", [128, D], F32)
    cw = P.sb("cw", [128, 3, 2 * NFF], F32)
    cb = P.sb("cb", [128, 2 * NFF], F32)
    ident_b = P.sb("identb", [128, 128], BF16)
    t_idb = mk_identity(P, ident_b)
    xt = [P.sb("xt%d" % i, [128, D], F32) for i in range(2)]
    xr = [P.sb("xr%d" % i, [128, D], F32) for i in range(2)]
    hn = P.sb("hn", [128, D], BF16)
    hnT = P.sb("hnT", [128, 8, NB], BF16)
    hT = P.sb("hT", [128, NFF, NB], BF16)
    U = [P.sb("U%d" % i, [128, NB + 2], F32) for i in range(NU)]
    halo = P.sb("halo", [128, 2 * NFF, 2], F32)
    t3 = [P.sb("t3_%d" % i, [128, NB], F32) for i in range(2)]
    junk = P.sb("junk", [128, D], BF16)
    ssq = P.sb("ssq", [128, 1], F32)
    sd = P.sb("sd", [128, 1], F32)
    rstd = P.sb("rstd", [128, 1], F32)
    fin = P.sb("fin", [128, 1], F32)
    pT = P.ps("pT", [128, 512], F32)
    pTb = pT[:].bitcast(BF16)
    pu = [P.ps("pu%d" % i, [128, NB], F32) for i in range(NU)]
    pg = P.ps("pg", [128, NB], F32)
    pd = [P.ps("pd%d" % i, [128, 512], F32) for i in range(2)]

    t_c = []
    t_c.append(P.op("sync", lambda e: e.dma_start(out=gnorm[:], in_=T["ffn_norm"][layer:layer + 1, :].partition_broadcast(128)), sem="d_c", dma=True))
    for j in range(3):
        t_c.append(P.op("sync", lambda e, j=j: e.dma_start(out=cw[:, j, :], in_=T["ffn_conv_w"][layer, j, :].rearrange("(c p) -> p c", p=128),
                                                             allow_slow_non_contiguous=True),
                        sem="d_c", dma=True))
    t_c.append(P.op("sync", lambda e: e.dma_start(out=cb[:], in_=T["ffn_conv_b"][layer, :].rearrange("(c p) -> p c", p=128),
                                                   allow_slow_non_contiguous=True), sem="d_c", dma=True))
    t_h0 = P.op("gpsimd", lambda e: e.memset(halo[:], 0.0))
    t_wu = []
    for c in range(2 * NFF // 4):
        for kc in range(8):
            t_wu.append(P.op("gpsimd", lambda e, c=c, kc=kc: e.dma_start(out=wup[:, kc, c * 512:(c + 1) * 512],
                                                                          in_=T["ffn_w_up"][layer, kc * 128:(kc + 1) * 128, c * 512:(c + 1) * 512]),
                             sem="d_wu", dma=True))
    t_wd = []
    for j in range(NFF):
        t_wd.append(P.op("gpsimd", lambda e, j=j: e.dma_start(out=wdn[:, j, :], in_=T["ffn_w_down"][layer, j * 128:(j + 1) * 128, :]),
                         sem="d_wd", dma=True))
    last = {}

    def L(k):
        return last.get(k)

    cnt = {"u": 0, "t": 0, "d": 0, "x": 0, "r": 0}
    t_fin = []
    for bt in range(nbat):
        thT = []
        for j in range(4):
            b = cnt["x"] % 2
            cnt["x"] += 1
            r0 = bt * NB + j * 128
            t_x = P.op("sync", lambda e, r0=r0, b=b: e.dma_start(out=xt[b][:], in_=xsrc[r0:r0 + 128, :]),
                       waits=[L("xt_use%d" % b)], sem="d_x%d" % b, dma=True)
            t_sq = P.op("scalar", lambda e, b=b: e.activation(out=junk[:], in_=xt[b][:], func=AF.Square, accum_out=ssq[:]),
                        waits=[t_x, L("ssq_use"), L("junk")])
            last["junk"] = t_sq
            t_sd = P.op("scalar", lambda e: e.activation(out=sd[:], in_=ssq[:], func=AF.Sqrt, scale=1.0 / D, bias=EPS), waits=[t_sq, L("sd_use")])
            last["ssq_use"] = t_sd
            t_rs = P.op("vector", lambda e: e.reciprocal(out=rstd[:], in_=sd[:]), waits=[t_sd, L("rstd_use")])
            last["sd_use"] = t_rs
            t_hn = P.op("vector", lambda e, b=b: e.scalar_tensor_tensor(out=hn[:], in0=xt[b][:], scalar=rstd[:, 0:1], in1=gnorm[:],
                                                                         op0=ALU.mult, op1=ALU.mult), waits=[t_rs, L("hn_use")] + t_c)
            last["rstd_use"] = t_hn
            last["xt_use%d" % b] = t_hn
            tt = None
            for kc in range(8):
                tt = P.op("tensor", lambda e, kc=kc: e.transpose(out=pTb[:, kc * 128:(kc + 1) * 128], in_=hn[:, kc * 128:(kc + 1) * 128],
                                                                 identity=ident_b[:]), waits=[t_hn, t_idb, L("pT_use")])
            last["hn_use"] = tt
            te = P.op("vector", lambda e, j=j: e.tensor_copy(out=hnT[:, :, j * 128:(j + 1) * 128], in_=pTb.rearrange("p (a q) -> p a q", a=8)),
                      waits=[tt, L("hnT_use")])
            last["pT_use"] = te
            thT.append(te)
        t_hT = []
        tm = None
        for j in range(NFF):
            res = {}
            for kind, ch in (("g", j), ("v", NFF + j)):
                ui = cnt["u"] % NU
                cnt["u"] += 1
                p_ = pu[ui]
                u_ = U[ui]
                wtok = t_wu[(ch // 4) * 8:(ch // 4) * 8 + 8]
                for kc in range(8):
                    tm = P.op("tensor", lambda e, kc=kc, ch=ch, p_=p_: e.matmul(p_[:], lhsT=wup[:, kc, ch * 128:(ch + 1) * 128], rhs=hnT[:, kc, :],
                                                                                start=(kc == 0), stop=(kc == 7)),
                              waits=thT + wtok + [L("pu_use%d" % ui)])
                tA = P.op("scalar", lambda e, u_=u_, p_=p_: e.activation(out=u_[:, 2:NB + 2], in_=p_[:], func=AF.Copy), waits=[tm, L("U_use%d" % ui)])
                tH = P.op("gpsimd", lambda e, u_=u_, ch=ch: e.tensor_copy(out=u_[:, 0:2], in_=halo[:, ch, :]),
                          waits=[t_h0, L("halo_w%d" % ch), L("U_use%d" % ui)])
                tB = P.op("scalar", lambda e, p_=p_, ch=ch: e.activation(out=p_[:], in_=p_[:], func=AF.Identity, scale=cw[:, 2, ch:ch + 1], bias=cb[:, ch:ch + 1]),
                          waits=[tA] + t_c)
                tC = P.op("vector", lambda e, p_=p_, u_=u_, ch=ch: e.scalar_tensor_tensor(out=p_[:], in0=u_[:, 1:NB + 1], scalar=cw[:, 1, ch:ch + 1], in1=p_[:],
                                                                                          op0=ALU.mult, op1=ALU.add), waits=[tB, tH])
                tN = P.op("gpsimd", lambda e, u_=u_, ch=ch: e.tensor_copy(out=halo[:, ch, :], in_=u_[:, NB:NB + 2]), waits=[tA, tH])
                last["halo_w%d" % ch] = tN
                res[kind] = (p_, u_, tC, ui, tN, ch)
            p_, u_, tC, ui, tN, ch = res["g"]
            ti = cnt["t"] % 2
            cnt["t"] += 1
            tD = P.op("vector", lambda e, p_=p_, u_=u_, ch=ch, ti=ti: e.scalar_tensor_tensor(out=t3[ti][:], in0=u_[:, 0:NB], scalar=cw[:, 0, ch:ch + 1], in1=p_[:],
                                                                                               op0=ALU.mult, op1=ALU.add), waits=[tC, L("t3_use%d" % ti)])
            last["pu_use%d" % ui] = tD
            last["U_use%d" % ui] = [tD, tN]
            tS = P.op("scalar", lambda e, ti=ti: e.activation(out=pg[:], in_=t3[ti][:], func=AF.Silu), waits=[tD, L("pg_use")])
            last["t3_use%d" % ti] = tS
            p_, u_, tC, ui, tN, ch = res["v"]
            ti2 = cnt["t"] % 2
            cnt["t"] += 1
            tD2 = P.op("vector", lambda e, p_=p_, u_=u_, ch=ch, ti2=ti2: e.scalar_tensor_tensor(out=t3[ti2][:], in0=u_[:, 0:NB], scalar=cw[:, 0, ch:ch + 1], in1=p_[:],
                                                                                                  op0=ALU.mult, op1=ALU.add), waits=[tC, L("t3_use%d" % ti2)])
            last["pu_use%d" % ui] = tD2
            last["U_use%d" % ui] = [tD2, tN]
            tF = P.op("vector", lambda e, j=j, ti2=ti2: e.tensor_tensor(out=hT[:, j, :], in0=t3[ti2][:], in1=pg[:], op=ALU.mult),
                      waits=[tD2, tS, L("hT_use")])
            last["pg_use"] = tF
            last["t3_use%d" % ti2] = tF
            t_hT.append(tF)
        last["hnT_use"] = tm
        tdn = None
        for j4 in range(4):
            rb = cnt["r"] % 2
            cnt["r"] += 1
            r0 = bt * NB + j4 * 128
            t_xr = P.op("sync", lambda e, r0=r0, rb=rb: e.dma_start(out=xr[rb][:], in_=xsrc[r0:r0 + 128, :]),
                        waits=[L("xr_use%d" % rb)], sem="d_r%d" % rb, dma=True)
            tadds = []
            for hf in range(2):
                di = cnt["d"] % 2
                cnt["d"] += 1
                for j in range(NFF):
                    tdn = P.op("tensor", lambda e, j=j, j4=j4, hf=hf, di=di: e.matmul(pd[di][:], lhsT=hT[:, j, j4 * 128:(j4 + 1) * 128],
                                                                                      rhs=wdn[:, j, hf * 512:(hf + 1) * 512], start=(j == 0), stop=(j == NFF - 1)),
                               waits=t_hT + t_wd + [L("pd_use%d" % di)])
                tadd = P.op("vector", lambda e, hf=hf, di=di, rb=rb: e.tensor_tensor(out=xr[rb][:, hf * 512:(hf + 1) * 512],
                                                                                      in0=xr[rb][:, hf * 512:(hf + 1) * 512], in1=pd[di][:], op=ALU.add),
                            waits=[tdn, t_xr])
                last["pd_use%d" % di] = tadd
                tadds.append(tadd)
            t_o = P.op("sync", lambda e, r0=r0, rb=rb: e.dma_start(out=xdst[r0:r0 + 128, :], in_=xr[rb][:]),
                       waits=tadds, sem="d_xw%d" % rb, dma=True)
            last["xr_use%d" % rb] = t_o
            t_fin.append(t_o)
        last["hT_use"] = tdn
    P.op("gpsimd", lambda e: e.memset(fin[:], 0.0), waits=t_fin)
    P.flush()


def host_consts():
    pos = np.arange(S, dtype=np.float32)
    inv = (np.float32(500000.0) ** (-np.arange(0, 16, 2, dtype=np.float32) / np.float32(16))).astype(np.float32)
    ang = (pos[:, None] * inv[None, :]).astype(np.float32)
    cos = np.cos(ang).astype(np.float32)
    sin = np.sin(ang).astype(np.float32)
    ropec = np.zeros((3, 128, NT, 8), np.float32)
    ropes = np.zeros((3, 128, NT, 8), np.float32)
    for v, dil in enumerate(DILS):
        Lg = S // dil
        pp = np.arange(S)
        r, a = np.divmod(pp, Lg)
        t = a * dil + r
        ropec[v] = cos[t].reshape(NT, 128, 8).transpose(1, 0, 2)
        ropes[v] = sin[t].reshape(NT, 128, 8).transpose(1, 0, 2)
    k = np.arange(128)[:, None]
    q = np.arange(128)[None, :]
    prev = (k >= q).astype(np.float32)
    cur = (k <= q).astype(np.float32)
    zero = np.zeros_like(prev)
    bm = np.zeros((128, 3, 8, 128), np.float32)
    for j in range(4):
        bm[:, 0, 2 * j] = prev
        bm[:, 0, 2 * j + 1] = cur
        bm[:, 1, 2 * j] = zero if j == 0 else prev
        bm[:, 1, 2 * j + 1] = cur
        bm[:, 2, 2 * j] = zero if j % 2 == 0 else prev
        bm[:, 2, 2 * j + 1] = cur
    bm = bm.reshape(128, 3, 1024).astype(ml_dtypes.bfloat16)
    oh = (np.arange(S)[None, :] // 256 == np.arange(16)[:, None]).astype(np.float32).astype(ml_dtypes.bfloat16)
    return {"ropec": ropec, "ropes": ropes, "bmask": bm, "onehot": oh}


WEIGHT_SHAPES = {
    "attn_norm": [2, D], "a_w_qkv": [1, D, 9216], "a_q_norm": [1, 3, HD], "a_k_norm": [1, 3, HD], "a_w_o": [1, D, D],
    "b_w_qkv": [1, D, 3072], "b_q_norm": [1, HD], "b_k_norm": [1, HD], "b_w_o": [1, D, D],
    "ffn_norm": [2, D], "ffn_w_up": [2, D, 2 * DFF], "ffn_conv_w": [2, 3, 2 * DFF], "ffn_conv_b": [2, 2 * DFF],
    "ffn_w_down": [2, DFF, D],
}


def build_nc(phases=("A0", "B0", "C0", "A1", "B1", "C1"), debug_out=None):
    nc = bass.Bass("TRN2", target_bir_lowering=False)
    T = {}
    x = nc.dram_tensor("x", [S, D], F32, kind="ExternalInput").ap()
    for k, shp in WEIGHT_SHAPES.items():
        T[k] = nc.dram_tensor(k, shp, F32, kind="ExternalInput").ap()
    T["ropec"] = nc.dram_tensor("ropec", [3, 128, NT, 8], F32, kind="ExternalInput").ap()
    T["ropes"] = nc.dram_tensor("ropes", [3, 128, NT, 8], F32, kind="ExternalInput").ap()
    T["bmask"] = nc.dram_tensor("bmask", [128, 3, 1024], BF16, kind="ExternalInput").ap()
    T["onehot"] = nc.dram_tensor("onehot", [16, S], BF16, kind="ExternalInput").ap()
    y = nc.dram_tensor("y", [S, D], F32, kind="ExternalOutput").ap()

    def scratch(name, shape, dt):
        kind = "ExternalOutput" if (debug_out and name in debug_out) else "Internal"
        return nc.dram_tensor(name, shape, dt, kind=kind).ap()
    R1 = scratch("R1", [S, D], F32)
    R2 = scratch("R2", [S, D], F32)
    R3 = scratch("R3", [S, D], F32)
    T["xin"] = [x, R2]
    T["xmid"] = [R1, R3]
    T["xout"] = [R2, y]
    T["QTs"] = [scratch("QT0", [3, H, HD, S], BF16), scratch("QT1", [1, H, HD + 16, S], BF16)]
    T["KTs"] = [scratch("KT0", [3, H, HD, S], BF16), scratch("KT1", [1, H, HD, S], BF16)]
    T["Vs"] = [scratch("V0", [3, S, H, 128], BF16), scratch("V1", [1, S, H, 128], BF16)]
    sems = Sems(nc)
    fns = {"A": phase_qkv, "B": phase_attn, "C": phase_ffn}
    for ph in phases:
        fns[ph[0]](nc, sems, ph + "_", T, int(ph[1]))
    sems.stack.close()
    return nc


_CACHE = {}


def kernel(**inputs):
    if "nc" not in _CACHE:
        _CACHE["nc"] = build_nc()
        _CACHE["consts"] = host_consts()
    nc = _CACHE["nc"]
    consts = _CACHE["consts"]
    x = np.ascontiguousarray(np.asarray(inputs["x"], dtype=np.float32))
    shared = {k: np.ascontiguousarray(np.asarray(inputs[k], dtype=np.float32)) for k in WEIGHT_SHAPES}
    shared.update(consts)
    in_maps = []
    for b in range(8):
        m = dict(shared)
        m["x"] = x[b]
        in_maps.append(m)
    res = run_bass_kernel_spmd(nc, in_maps, core_ids=list(range(8)))
    return np.stack([np.asarray(r["y"], dtype=np.float32) for r in res.results], axis=0)
```

```python
import contextlib
import os
import numpy as np
import ml_dtypes
import concourse.bass as bass
import concourse.mybir as mybir
from concourse.bass_utils import run_bass_kernel_spmd

F32 = mybir.dt.float32
BF16 = mybir.dt.bfloat16
AF = mybir.ActivationFunctionType
ALU = mybir.AluOpType
AX = mybir.AxisListType

S = 4096
D = 1024
NT = S // 128
H = 16
HD = 64
DFF = 2816
NFF = DFF // 128
EPS = 1e-6
BIG = 30000.0
DILS = (1, 4, 16)
ENGS = ("tensor", "vector", "scalar", "gpsimd", "sync")


class Tok:
    __slots__ = ("sem", "val")

    def __init__(self, sem, val):
        self.sem = sem
        self.val = val


class Sems:
    def __init__(self, nc):
        self.nc = nc
        self.stack = contextlib.ExitStack()
        self.h = {}
        self.cnt = {}

    def get(self, name):
        if name not in self.h:
            self.h[name] = self.stack.enter_context(self.nc.semaphore(name))
            self.cnt[name] = 0
        return name


class Prog:
    def __init__(self, nc, sems, tag):
        self.nc = nc
        self.S = sems
        self.tag = tag
        self.q = {e: [] for e in ENGS}
        self.stack = contextlib.ExitStack()
        self.seen = {e: {} for e in ENGS}

    def sb(self, name, shape, dt):
        return self.stack.enter_context(self.nc.sbuf_tensor(self.tag + name, shape, dt))

    def ps(self, name, shape, dt):
        return self.stack.enter_context(self.nc.psum_tensor(self.tag + name, shape, dt))

    def op(self, eng, fn, waits=(), sem=None, dma=False):
        if sem is None:
            sem = "s_" + eng
        sem = self.S.get(self.tag + sem)
        need = {}

        def add(t):
            if t is None:
                return
            if isinstance(t, (list, tuple)):
                for u in t:
                    add(u)
                return
            if need.get(t.sem, -1) < t.val:
                need[t.sem] = t.val
        add(list(waits))
        seen = self.seen[eng]
        wl = []
        for s, v in need.items():
            if seen.get(s, -1) >= v:
                continue
            seen[s] = v
            wl.append((s, v))
        inc = 16 if dma else 1
        self.S.cnt[sem] += inc
        self.q[eng].append((wl, fn, sem, inc))
        return Tok(sem, self.S.cnt[sem])

    def flush(self):
        nc = self.nc
        sems = self.S.h
        qs = self.q
        with nc.Block() as block:
            def mk(name):
                def body(e):
                    for wl, fn, sem, inc in qs[name]:
                        for s, v in wl:
                            e.wait_ge(sems[s], v)
                        fn(e).then_inc(sems[sem], inc)
                return body
            for name in ENGS:
                if qs[name]:
                    getattr(block, name)(mk(name))
        self.stack.close()


def mk_identity(P, ident, dt_one=1.0):
    t0 = P.op("gpsimd", lambda e: e.memset(ident[:], 0.0))
    return P.op("gpsimd", lambda e: e.affine_select(out=ident[:], in_=ident[:], pattern=[[-1, 128]],
                                                    compare_op=ALU.not_equal, fill=1.0, base=0,
                                                    channel_multiplier=1), waits=[t0])


def phase_qkv(nc, sems, tag, T, layer):
    P = Prog(nc, sems, tag)
    moba = layer == 1
    ngrp = 1 if moba else 3
    xsrc = T["xin"][layer]
    wq = T["b_w_qkv"] if moba else T["a_w_qkv"]

    ident_b = P.sb("identb", [128, 128], BF16)
    ident_f = P.sb("identf", [128, 128], F32)
    t_idb = mk_identity(P, ident_b)
    t_idf = mk_identity(P, ident_f)
    gnorm = P.sb("gnorm", [128, D], F32)
    t_gn = P.op("sync", lambda e: e.dma_start(out=gnorm[:], in_=T["attn_norm"][layer:layer + 1, :].partition_broadcast(128)),
                sem="d_c", dma=True)
    gqk = P.sb("gqk", [128, ngrp, 2, HD], F32)
    t_gq = []
    for g in range(ngrp):
        for s_, nm in ((0, "q"), (1, "k")):
            src = (T["b_%s_norm" % nm][0:1, :] if moba else T["a_%s_norm" % nm][0, g:g + 1, :])
            t_gq.append(P.op("sync", lambda e, g=g, s_=s_, src=src: e.dma_start(
                out=gqk[:, g, s_, :], in_=src.partition_broadcast(128)), sem="d_c", dma=True))
    ropecs = [P.sb("ropec%d" % i, [128, NT, 8], F32) for i in range(2)]
    ropess = [P.sb("ropes%d" % i, [128, NT, 8], F32) for i in range(2)]
    junkx = P.sb("junkx", [128, D], BF16)
    wsb = [P.sb("w%d" % i, [128, 8, 3, 1024], BF16) for i in range(2 if not moba else 1)]
    xt = [P.sb("xt%d" % i, [128, D], F32) for i in range(3)]
    junks = [P.sb("junk%d" % i, [128, 1024], F32) for i in range(2)]
    ssq = P.sb("ssq", [128, 1], F32)
    sd = P.sb("sd", [128, 1], F32)
    rstd = P.sb("rstd", [128, 1], F32)
    hn = P.sb("hn", [128, D], BF16)
    hnT = [P.sb("hnT%d" % i, [128, 8, 128], BF16) for i in range(2)]
    ss2 = P.sb("ss2", [128, 32], F32)
    sd2 = P.sb("sd2", [128, 32], F32)
    rs2 = P.sb("rs2", [128, 32], F32)
    qk = [P.sb("qk%d" % i, [128, 2, H, HD], F32) for i in range(2)]
    rt = [P.sb("rt%d" % i, [128, 2 * H, 8], F32) for i in range(4)]
    vaug = [P.sb("vaug%d" % i, [128, H, 128], BF16) for i in range(2)]
    qT4 = [P.sb("qT4_%d" % i, [128, 8, 512], BF16) for i in range(2)]
    kT4 = [P.sb("kT4_%d" % i, [128, 8, 512], BF16) for i in range(2)]
    pT = P.ps("pT", [128, 512], F32)
    pTb = pT[:].bitcast(BF16)
    pqs = [P.ps("pq%d" % i, [128, 1024], F32) for i in range(2)]
    pv = P.ps("pv", [128, 512], F32)
    ptr = P.ps("ptr", [128, 8, 128], F32)
    if moba:
        qT32 = P.sb("qT32", [128, 8, 128], F32)
        ksum = P.sb("ksum", [128, 8, 16], F32)
        kpart = P.sb("kpart", [128, 8], F32)
        gs = P.sb("gs", [128, H, 16], F32)
        t8 = P.sb("t8", [128, H, 8], F32)
        mv = P.sb("mv", [128, H, 16], F32)
        mT4 = [P.sb("mT4_%d" % i, [128, 2, 512], BF16) for i in range(2)]

    t_ones = []
    for i in range(2):
        t_ones.append(P.op("gpsimd", lambda e, i=i: e.memset(vaug[i][:], 1.0)))
    t_ks0 = P.op("gpsimd", lambda e: e.memset(ksum[:], 0.0)) if moba else None

    last = {}

    def L(k):
        return last.get(k)

    tw_use = [None, None]
    t_outdma = []
    tiles = [(g, n) for g in range(ngrp) for n in range(NT)]
    state = {}

    def load_w(g):
        wb = wsb[g % len(wsb)]
        toks = []
        for kc in range(8):
            for s_ in range(3):
                if moba:
                    c0 = s_ * 1024
                else:
                    c0 = (s_ * 3 + g) * 1024
                toks.append(P.op("gpsimd", lambda e, wb=wb, kc=kc, s_=s_, c0=c0: e.dma_start(
                    out=wb[:, kc, s_, :], in_=wq[0, kc * 128:(kc + 1) * 128, c0:c0 + 1024]),
                    waits=[tw_use[g % len(wsb)]], sem="d_w%d" % (g % len(wsb)), dma=True))
        return toks

    def load_rope(g):
        v = 0 if moba else g
        rb = g % 2
        a = P.op("sync", lambda e: e.dma_start(out=ropecs[rb][:], in_=T["ropec"][v]), waits=[L("rope_use%d" % rb)], sem="d_r%d" % rb, dma=True)
        b = P.op("sync", lambda e: e.dma_start(out=ropess[rb][:], in_=T["ropes"][v]), waits=[L("rope_use%d" % rb)], sem="d_r%d" % rb, dma=True)
        return [a, b]

    def x_rows(g, n):
        dil = 1 if moba else DILS[g]
        Lg = S // dil
        p0 = n * 128
        r, a0 = divmod(p0, Lg)
        v = xsrc.rearrange("(a r) d -> r a d", r=dil)
        return v[r, a0:a0 + 128, :]

    def xload(idx):
        g, n = tiles[idx]
        b3 = idx % 3
        state["xl%d" % idx] = P.op("sync", lambda e: e.dma_start(out=xt[b3][:], in_=x_rows(g, n)),
                                   waits=[L("xt_use%d" % b3)], sem="d_x%d" % b3, dma=True)

    def prepA(idx):
        g, n = tiles[idx]
        st = state.setdefault(idx, {})
        if n == 0:
            if g == 0:
                state["w0"] = load_w(0)
            state["rope%d" % g] = load_rope(g)
        if n == 1 and g + 1 < ngrp:
            state["w%d" % (g + 1)] = load_w(g + 1)
        b3 = idx % 3
        t_x = state["xl%d" % idx]
        t_sq = P.op("scalar", lambda e: e.activation(out=junkx[:], in_=xt[b3][:], func=AF.Square, accum_out=ssq[:]),
                    waits=[t_x, L("ssq_use")])
        t_sd = P.op("scalar", lambda e: e.activation(out=sd[:], in_=ssq[:], func=AF.Sqrt, scale=1.0 / D, bias=EPS),
                    waits=[t_sq, L("sd_use")])
        last["ssq_use"] = t_sd
        t_rs = P.op("vector", lambda e: e.reciprocal(out=rstd[:], in_=sd[:]), waits=[t_sd, L("rstd_use")])
        last["sd_use"] = t_rs
        t_hn = P.op("vector", lambda e: e.scalar_tensor_tensor(out=hn[:], in0=xt[b3][:], scalar=rstd[:, 0:1], in1=gnorm[:],
                                                               op0=ALU.mult, op1=ALU.mult),
                    waits=[t_rs, t_gn, L("hn_use")])
        last["rstd_use"] = t_hn
        last["xt_use%d" % b3] = t_hn
        st["t_hn"] = t_hn

    def prepB(idx):
        st = state[idx]
        b = idx % 2
        t_hn = st["t_hn"]
        tt = None
        for kc in range(8):
            tt = P.op("tensor", lambda e, kc=kc: e.transpose(out=pTb[:, kc * 128:(kc + 1) * 128], in_=hn[:, kc * 128:(kc + 1) * 128],
                                                             identity=ident_b[:]),
                      waits=[t_hn, t_idb, L("pT_use")])
        last["hn_use"] = tt
        t_hT = P.op("scalar", lambda e: e.activation(out=hnT[b][:].rearrange("p a b -> p (a b)"), in_=pTb, func=AF.Copy),
                    waits=[tt, L("hnT_use%d" % b)])
        last["pT_use"] = t_hT
        st["hT"] = t_hT

    def main_tile(idx):
        g, n = tiles[idx]
        st = state[idx]
        b = idx % 2
        t_hT = st["hT"]
        tw = state["w%d" % g]
        wb = wsb[g % len(wsb)]
        qkb = qk[b]
        va = vaug[b]
        gsel = 0 if moba else g

        def mm(dst, s_, c0, extra):
            t = None
            for kc in range(8):
                t = P.op("tensor", lambda e, kc=kc: e.matmul(dst, lhsT=hnT[b][:, kc, :], rhs=wb[:, kc, s_, c0:c0 + 512],
                                                            start=(kc == 0), stop=(kc == 7)),
                         waits=[t_hT, tw, extra])
            return t
        t_gs_ = []
        tv = None
        tvm = None
        for s_ in range(2):
            pp = pqs[s_]
            jk = junks[s_]
            tq = None
            for hf in range(2):
                tq = mm(pp[:, hf * 512:(hf + 1) * 512], s_, hf * 512, L("pq_use%d" % s_))
            hf = s_
            tvm = mm(pv[:], 2, hf * 512, L("pv_use"))
            t_sq2 = P.op("scalar", lambda e, pp=pp, jk=jk: e.activation(out=jk[:], in_=pp[:], func=AF.Square), waits=[tq, L("junk%d" % s_)])
            pvv = pv[:].rearrange("p (i e d) -> p i e d", e=2, d=HD)
            vv = va[:, hf * 8:(hf + 1) * 8, :].rearrange("p (i e) c -> p i e c", e=2)
            tv0 = P.op("scalar", lambda e, pvv=pvv, vv=vv: e.activation(out=vv[:, :, 0, 0:HD], in_=pvv[:, :, 0, :], func=AF.Copy),
                       waits=[tvm, L("vaug_use%d" % b)] + t_ones)
            tv = P.op("scalar", lambda e, pvv=pvv, vv=vv: e.activation(out=vv[:, :, 1, HD:128], in_=pvv[:, :, 1, :], func=AF.Copy),
                      waits=[tvm, L("vaug_use%d" % b)] + t_ones)
            last["pv_use"] = tv
            t_ss2 = P.op("vector", lambda e, jk=jk, s_=s_: e.tensor_reduce(out=ss2[:, s_ * 16:(s_ + 1) * 16], in_=jk[:].rearrange("p (h d) -> p h d", d=HD),
                                                                        op=ALU.add, axis=AX.X), waits=[t_sq2, L("ss2_use%d" % s_)])
            last["junk%d" % s_] = t_ss2
            t_sd2 = P.op("scalar", lambda e, s_=s_: e.activation(out=sd2[:, s_ * 16:(s_ + 1) * 16], in_=ss2[:, s_ * 16:(s_ + 1) * 16], func=AF.Sqrt,
                                                                  scale=1.0 / HD, bias=EPS), waits=[t_ss2, L("sd2_use%d" % s_)])
            last["ss2_use%d" % s_] = t_sd2
            t_rs2 = P.op("vector", lambda e, s_=s_: e.reciprocal(out=rs2[:, s_ * 16:(s_ + 1) * 16], in_=sd2[:, s_ * 16:(s_ + 1) * 16]),
                         waits=[t_sd2, L("rs2_use%d" % s_)])
            last["sd2_use%d" % s_] = t_rs2
            t_nm = P.op("vector", lambda e, pp=pp, s_=s_: e.tensor_tensor(out=qkb[:, s_, :, :],
                                                                           in0=pp[:].rearrange("p (h d) -> p h d", d=HD),
                                                                           in1=rs2[:, s_ * 16:(s_ + 1) * 16].unsqueeze(2).broadcast_to([128, H, HD]), op=ALU.mult),
                        waits=[t_rs2, L("qk_use%d" % b)])
            last["rs2_use%d" % s_] = t_nm
            last["pq_use%d" % s_] = t_nm
            t_gs_.append(P.op("gpsimd", lambda e, s_=s_: e.tensor_tensor(out=qkb[:, s_, :, :], in0=qkb[:, s_, :, :],
                                                                          in1=gqk[:, gsel, s_, :].unsqueeze(1).broadcast_to([128, H, HD]),
                                                                          op=ALU.mult), waits=[t_nm] + t_gq))
        st["t_g"] = t_gs_
        tw_use[g % len(wsb)] = tv
        t_vo = P.op("sync", lambda e: e.dma_start(out=T["Vs"][layer][g, n * 128:(n + 1) * 128, :, :], in_=va[:]),
                    waits=[tv], sem="d_vo%d" % b, dma=True)
        last["vaug_use%d" % b] = t_vo
        t_outdma.append(t_vo)
        last["hnT_use%d" % b] = tvm

    def stage1r(idx):
        g, n = tiles[idx]
        st = state[idx]
        b = idx % 2
        qkb = qk[b]
        t_g = st["t_g"]
        qv = qkb[:].rearrange("p s h d -> p (s h) d")
        x1 = qv[:, :, 0:8]
        x2 = qv[:, :, 8:16]
        cb = ropecs[g % 2][:, n, :].unsqueeze(1).broadcast_to([128, 32, 8])
        sbb = ropess[g % 2][:, n, :].unsqueeze(1).broadcast_to([128, 32, 8])
        tr = state["rope%d" % g]
        w0 = [t_g, tr, L("rt_use")]
        a1 = P.op("vector", lambda e: e.tensor_tensor(out=rt[0][:], in0=x1, in1=cb, op=ALU.mult), waits=w0)
        a2 = P.op("vector", lambda e: e.tensor_tensor(out=rt[1][:], in0=x2, in1=sbb, op=ALU.mult), waits=w0)
        a3 = P.op("vector", lambda e: e.tensor_tensor(out=rt[2][:], in0=x2, in1=cb, op=ALU.mult), waits=w0)
        a4 = P.op("vector", lambda e: e.tensor_tensor(out=rt[3][:], in0=x1, in1=sbb, op=ALU.mult), waits=w0)
        a5 = P.op("vector", lambda e: e.tensor_tensor(out=x1, in0=rt[0][:], in1=rt[1][:], op=ALU.subtract), waits=[a1, a2, a3, a4])
        a6 = P.op("vector", lambda e: e.tensor_tensor(out=x2, in0=rt[2][:], in1=rt[3][:], op=ALU.add), waits=[a1, a2, a3, a4, a5])
        last["rt_use"] = a6
        last["rope_use%d" % (g % 2)] = a6
        st["qkg"] = [a5, a6]

    def stage2(idx):
        g, n = tiles[idx]
        st = state[idx]
        b = idx % 2
        qkb = qk[b]
        n4, j4 = divmod(n, 4)
        b4 = (idx // 4) % 2
        for s_, dst4, nm in ((0, qT4[b4], "Q"), (1, kT4[b4], "K")):
            tt = None
            for c in range(8):
                tt = P.op("tensor", lambda e, c=c, s_=s_: e.transpose(out=ptr[:, c, :], in_=qkb[:, s_, 2 * c:2 * c + 2, :].rearrange("p h d -> p (h d)"),
                                                                      identity=ident_f[:]),
                          waits=[st["qkg"], t_idf, L("ptr_use")])
            wv = [tt, L("%s4_use%d" % (nm, b4))]
            te = P.op("scalar", lambda e, dst4=dst4: e.activation(out=dst4[:, :, j4 * 128:(j4 + 1) * 128], in_=ptr[:], func=AF.Copy),
                      waits=wv)
            tl = [te]
            if moba and s_ == 0:
                tl.append(P.op("vector", lambda e: e.tensor_copy(out=qT32[:], in_=ptr[:]), waits=[tt, te, L("qT32_use")]))
            if moba and s_ == 1:
                tkp = P.op("vector", lambda e: e.tensor_reduce(out=kpart[:], in_=ptr[:], op=ALU.add, axis=AX.X),
                           waits=[tt, te, L("kpart_use")])
                j = n // 2
                tks = P.op("vector", lambda e, j=j: e.tensor_tensor(out=ksum[:, :, j], in0=ksum[:, :, j], in1=kpart[:], op=ALU.add),
                           waits=[tkp, t_ks0, L("ksum_w"), L("qT32_use")])
                last["kpart_use"] = tks
                last["ksum_w"] = tks
                tl.append(tks)
            last["ptr_use"] = tl
            st["te%d" % s_] = te
        last["qk_use%d" % b] = last["ptr_use"]
        if moba:
            stage_gate(idx)
        if j4 == 3:
            for s_, src4, nm, dstT in ((0, qT4[b4], "Q", T["QTs"][layer]), (1, kT4[b4], "K", T["KTs"][layer])):
                tds = []
                for e2 in range(2):
                    dv = dstT[g].rearrange("(c e) r t -> e r c t", e=2)[e2, 0:HD, :, n4 * 512:(n4 + 1) * 512]
                    tds.append(P.op("sync", lambda e, dv=dv, src4=src4, e2=e2: e.dma_start(out=dv, in_=src4[e2 * HD:(e2 + 1) * HD, :, :]),
                                    waits=[state[idx - 3]["te%d" % s_], state[idx - 2]["te%d" % s_], state[idx - 1]["te%d" % s_], st["te%d" % s_]],
                                    sem="d_%so%d" % (nm, b4), dma=True))
                last["%s4_use%d" % (nm, b4)] = tds
                t_outdma.extend(tds)
            if moba:
                tds = []
                m4 = mT4[b4]
                for h in range(H):
                    dv = T["QTs"][layer][0, h, HD:HD + 16, n4 * 512:(n4 + 1) * 512]
                    tds.append(P.op("gpsimd", lambda e, dv=dv, h=h, m4=m4: e.dma_start(out=dv, in_=m4[(h % 8) * 16:(h % 8) * 16 + 16, h // 8, :]),
                                    waits=[state[idx - 3]["tm"], state[idx - 2]["tm"], state[idx - 1]["tm"], st["tm"]],
                                    sem="d_mo%d" % b4, dma=True))
                last["m4_use%d" % b4] = tds
                t_outdma.extend(tds)

    def stage_gate(idx):
        g, n = tiles[idx]
        st = state[idx]
        b4 = (idx // 4) % 2
        j4 = n % 4
        ob = n // 2
        pg = pT[:, 0:256]
        tg = None
        tq32 = last["ptr_use"]
        for h in range(H):
            r0 = (h % 2) * HD
            tg = P.op("tensor", lambda e, h=h, r0=r0: e.matmul(pg[:, h * 16:(h + 1) * 16], lhsT=qT32[r0:r0 + HD, h // 2, :],
                                                                rhs=ksum[r0:r0 + HD, h // 2, :], start=True, stop=True),
                      waits=[tq32, L("ksum_w"), L("pT_use")])
        last["qT32_use"] = tg
        t_gs = P.op("vector", lambda e: e.tensor_copy(out=gs[:].rearrange("p h j -> p (h j)"), in_=pg), waits=[tg, L("gs_use")])
        last["pT_use"] = t_gs
        t_ms = t_gs
        if ob < 16:
            t_ms = P.op("vector", lambda e: e.memset(gs[:, :, ob:16], -1e30), waits=[t_gs])
        tm8 = []
        for h in range(H):
            tm8.append(P.op("vector", lambda e, h=h: e.max(out=t8[:, h, :], in_=gs[:, h, :]), waits=[t_ms, L("t8_use")]))
        t_sel = P.op("vector", lambda e: e.tensor_tensor(out=mv[:], in0=gs[:], in1=t8[:, :, 2:3].broadcast_to([128, H, 16]), op=ALU.is_ge),
                     waits=tm8 + [L("mv_use")])
        last["t8_use"] = t_sel
        last["gs_use"] = t_sel
        t_mv = P.op("vector", lambda e: e.tensor_scalar(out=mv[:], in0=mv[:], scalar1=-1.0, scalar2=BIG, op0=ALU.add, op1=ALU.mult),
                    waits=[t_sel])
        t_own = P.op("vector", lambda e: e.memset(mv[:, :, ob:ob + 1], 0.0), waits=[t_mv])
        pm = pT[:, 256:512].rearrange("p (a q) -> p a q", a=2)
        tt = None
        for a in range(2):
            tt = P.op("tensor", lambda e, a=a: e.transpose(out=pm[:, a, :], in_=mv[:, a * 8:(a + 1) * 8, :].rearrange("p h j -> p (h j)"),
                                                           identity=ident_f[:]), waits=[t_own, t_idf, L("pm_use")])
        last["mv_use"] = tt
        tm = P.op("vector", lambda e: e.tensor_copy(out=mT4[b4][:, :, j4 * 128:(j4 + 1) * 128], in_=pm), waits=[tt, L("m4_use%d" % b4)])
        last["pm_use"] = tm
        last["pT_use"] = [last["pT_use"], tm]
        st["tm"] = tm

    xload(0)
    xload(1)
    prepA(0)
    prepB(0)
    for idx in range(len(tiles)):
        if idx + 2 < len(tiles):
            xload(idx + 2)
        if idx + 1 < len(tiles):
            prepA(idx + 1)
        main_tile(idx)
        if idx + 1 < len(tiles):
            prepB(idx + 1)
        stage1r(idx)
        if idx >= 1:
            stage2(idx - 1)
    stage2(len(tiles) - 1)
    P.op("gpsimd", lambda e: e.memset(ssq[:], 0.0), waits=t_outdma + [L("ssq_use")])
    P.flush()


def phase_attn(nc, sems, tag, T, layer):
    P = Prog(nc, sems, tag)
    moba = layer == 1
    ngrp = 1 if moba else 3
    xsrc = T["xin"][layer]
    xdst = T["xmid"][layer]
    wo = T["b_w_o"] if moba else T["a_w_o"]
    QTs, KTs, Vs = T["QTs"][layer], T["KTs"][layer], T["Vs"][layer]
    KR = HD + 16 if moba else HD

    attnT = P.sb("attnT", [128, 8, S], BF16)
    acc = [P.sb("acc%d" % i, [128, S], F32) for i in range(2)]
    den = P.sb("den", [128, S], F32)
    qt = [P.sb("qt%d" % i, [KR, S], BF16) for i in range(2)]
    kt = [P.sb("kt%d" % i, [KR, S], BF16) for i in range(2)]
    vt = [P.sb("vt%d" % i, [128, NT, 128], BF16) for i in range(2)]
    pbuf = [P.sb("pb%d" % i, [128, 1024], BF16) for i in range(3)]
    masks = P.sb("masks", [128, 3, 1024], BF16)
    wosb = P.sb("wosb", [128, 8, D], BF16)
    fin = P.sb("fin", [128, 1], F32)
    psS = [P.ps("psS%d" % i, [128, 1024], F32) for i in range(3)]
    psO = [P.ps("psO%d" % i, [128, 512], F32) for i in range(2)]

    t_mk = P.op("sync", lambda e: e.dma_start(out=masks[:], in_=T["bmask"]), sem="d_c", dma=True)
    t_oh = []
    if moba:
        for i in range(2):
            t_oh.append(P.op("sync", lambda e, i=i: e.dma_start(out=kt[i][HD:HD + 16, :], in_=T["onehot"]), sem="d_c", dma=True))
    t_wo = []
    for kc in range(8):
        t_wo.append(P.op("gpsimd", lambda e, kc=kc: e.dma_start(out=wosb[:, kc, :], in_=wo[0, kc * 128:(kc + 1) * 128, :]),
                         sem="d_wo", dma=True))
    last = {}

    def L(k):
        return last.get(k)

    cnt = {"s": 0, "o": 0, "p": 0, "ld": 0}
    t_attn = []

    def load_head(g, h):
        i = cnt["ld"] % 2
        cnt["ld"] += 1
        c, e2 = divmod(h, 2)
        toks = []
        if moba:
            qsrc = QTs[0, h, :, :]
            toks.append(P.op("sync", lambda e: e.dma_start(out=qt[i][:], in_=qsrc), waits=[L("ld_use%d" % i)], sem="d_ld%d" % i, dma=True))
        else:
            qsrc = QTs[g, h, 0:HD, :]
            toks.append(P.op("sync", lambda e: e.dma_start(out=qt[i][0:HD, :], in_=qsrc), waits=[L("ld_use%d" % i)], sem="d_ld%d" % i, dma=True))
        ksrc = KTs[g, h, 0:HD, :]
        toks.append(P.op("sync", lambda e: e.dma_start(out=kt[i][0:HD, :], in_=ksrc), waits=[L("ld_use%d" % i)], sem="d_ld%d" % i, dma=True))
        vsrc = Vs[g].rearrange("(n p) h c -> p n h c", p=128)[:, :, h, :]
        toks.append(P.op("gpsimd", lambda e: e.dma_start(out=vt[i][:], in_=vsrc), waits=[L("ld_use%d" % i)], sem="d_ld%d" % i, dma=True))
        return i, toks

    def acc_view(a, g, b):
        if moba or g == 0:
            return a[:, b * 512:(b + 1) * 512]
        dil = DILS[g]
        nbseg = 32 // dil
        if g == 1:
            n0 = 4 * b
            r, a0 = divmod(n0, nbseg)
            a0 *= 128
            return a[:].rearrange("p (a r) -> p r a", r=dil)[:, r, a0:a0 + 512]
        return a[:].rearrange("p (a r) -> p r a", r=dil)[:, 2 * b:2 * b + 2, :]

    items = []

    def band_item(g, h, b, ld, first, lastb):
        it = {}
        e2 = h % 2
        a = acc[e2]
        nbseg = 32 // DILS[g]

        def p1():
            i, tl = ld["get"]()
            si = cnt["s"] % 3
            cnt["s"] += 1
            ps = psS[si]
            pb = pbuf[si]
            ts = None
            for j in range(4):
                n = 4 * b + j
                npv = max(n - 1, 0)
                ts = P.op("tensor", lambda e, j=j, npv=npv, n=n: e.matmul(ps[:, (2 * j) * 128:(2 * j + 1) * 128], lhsT=kt[i][0:HD, npv * 128:(npv + 1) * 128],
                                                                           rhs=qt[i][0:HD, n * 128:(n + 1) * 128], start=True, stop=True),
                          waits=[tl, L("psS_use%d" % si)])
                ts = P.op("tensor", lambda e, j=j, n=n: e.matmul(ps[:, (2 * j + 1) * 128:(2 * j + 2) * 128], lhsT=kt[i][0:HD, n * 128:(n + 1) * 128],
                                                                  rhs=qt[i][0:HD, n * 128:(n + 1) * 128], start=True, stop=True),
                          waits=[tl, L("psS_use%d" % si)])
            te = P.op("scalar", lambda e: e.activation(out=pb[:], in_=ps[:], func=AF.Exp, scale=0.125),
                      waits=[ts, L("pb_use%d" % si)])
            last["psS_use%d" % si] = te
            if g == 2:
                mvv = 2
            elif (4 * b) % nbseg == 0:
                mvv = 1
            else:
                mvv = 0
            tm = P.op("vector", lambda e: e.tensor_tensor(out=pb[:], in0=pb[:], in1=masks[:, mvv, :], op=ALU.mult),
                      waits=[te, t_mk])
            it.update(i=i, tl=tl, si=si, pb=pb, tm=tm)

        def p2():
            i, tl, si, pb, tm = it["i"], it["tl"], it["si"], it["pb"], it["tm"]
            oi = cnt["o"] % 2
            cnt["o"] += 1
            po = psO[oi]
            to = None
            for j in range(4):
                n = 4 * b + j
                npv = max(n - 1, 0)
                to = P.op("tensor", lambda e, j=j, npv=npv: e.matmul(po[:, j * 128:(j + 1) * 128], lhsT=vt[i][:, npv, :], rhs=pb[:, (2 * j) * 128:(2 * j + 1) * 128],
                                                                      start=True, stop=False), waits=[tm, tl, L("psO_use%d" % oi)])
                to = P.op("tensor", lambda e, j=j, n=n: e.matmul(po[:, j * 128:(j + 1) * 128], lhsT=vt[i][:, n, :], rhs=pb[:, (2 * j + 1) * 128:(2 * j + 2) * 128],
                                                                  start=False, stop=True), waits=[tm, tl, L("psO_use%d" % oi)])
            last["pb_use%d" % si] = to
            av = acc_view(a, g, b)
            pov = po[:].rearrange("p (r a) -> p r a", r=2) if g == 2 else po[:]
            if first:
                ta = P.op("vector", lambda e: e.tensor_copy(out=av, in_=pov), waits=[to, L("acc_use%d" % e2)])
            else:
                ta = P.op("vector", lambda e: e.tensor_tensor(out=av, in0=av, in1=pov, op=ALU.add), waits=[to, L("acc_w%d" % e2)])
            last["psO_use%d" % oi] = ta
            last["acc_w%d" % e2] = ta
            if lastb:
                last["ld_use%d" % i] = to
        it["p1"] = p1
        it["p2"] = p2
        return it

    def moba_item(h, sc, kp, ld, po_box):
        it = {}
        e2 = h % 2
        a = acc[e2]
        nk = 4 * sc + 4
        kts = (2 * kp, 2 * kp + 1)

        def p1():
            i, tl = ld["get"]()
            si = cnt["s"] % 3
            cnt["s"] += 1
            ps = psS[si]
            pb = pbuf[si]
            ts = None
            for u, ktile in enumerate(kts):
                c0 = max(0, ktile - 4 * sc) * 128
                ts = P.op("tensor", lambda e, u=u, ktile=ktile, c0=c0: e.matmul(
                    ps[:, u * 512 + c0:(u + 1) * 512], lhsT=kt[i][:, ktile * 128:(ktile + 1) * 128],
                    rhs=qt[i][:, sc * 512 + c0:(sc + 1) * 512], start=True, stop=True),
                    waits=[tl, L("psS_use%d" % si)] + t_oh)
            c00 = max(0, kts[0] - 4 * sc) * 128
            te = P.op("scalar", lambda e: e.activation(out=pb[:, c00:1024], in_=ps[:, c00:1024], func=AF.Exp, scale=0.125),
                      waits=[ts, L("pb_use%d" % si)])
            last["psS_use%d" % si] = te
            tms = [te]
            for u, ktile in enumerate(kts):
                if ktile >= 4 * sc:
                    c0 = (ktile - 4 * sc) * 128
                    tms.append(P.op("vector", lambda e, u=u, c0=c0: e.tensor_tensor(out=pb[:, u * 512 + c0:u * 512 + c0 + 128],
                                                                                      in0=pb[:, u * 512 + c0:u * 512 + c0 + 128],
                                                                                      in1=masks[:, 0, 128:256], op=ALU.mult), waits=[te, t_mk]))
            it.update(i=i, tl=tl, si=si, pb=pb, tm=tms)

        def p2():
            i, tl, si, pb, tm = it["i"], it["tl"], it["si"], it["pb"], it["tm"]
            if kp == 0:
                po_box["oi"] = cnt["o"] % 2
                cnt["o"] += 1
            oi = po_box["oi"]
            po = psO[oi]
            to = None
            for u, ktile in enumerate(kts):
                c0 = max(0, ktile - 4 * sc) * 128
                to = P.op("tensor", lambda e, u=u, ktile=ktile, c0=c0: e.matmul(po[:, c0:512], lhsT=vt[i][:, ktile, :], rhs=pb[:, u * 512 + c0:(u + 1) * 512],
                                                                                start=(ktile == 0), stop=(ktile == nk - 1)),
                          waits=[tm, tl, L("psO_use%d" % oi)])
            last["pb_use%d" % si] = to
            if kts[1] == nk - 1:
                ta = P.op("vector", lambda e: e.tensor_copy(out=a[:, sc * 512:(sc + 1) * 512], in_=po[:]), waits=[to, L("acc_use%d" % e2)])
                last["psO_use%d" % oi] = ta
                last["acc_w%d" % e2] = ta
                if sc == 7:
                    last["ld_use%d" % i] = to
        it["p1"] = p1
        it["p2"] = p2
        return it

    def finalize_item(pr):
        def fin_():
            t0 = P.op("sync", lambda e: e.dma_start(out=den[0:HD, :], in_=acc[0][HD:128, :]), waits=[L("acc_w0"), L("den_use")], sem="d_den", dma=True)
            t1 = P.op("sync", lambda e: e.dma_start(out=den[HD:128, :], in_=acc[1][0:HD, :]), waits=[L("acc_w1"), L("den_use")], sem="d_den", dma=True)
            tln = P.op("scalar", lambda e: e.activation(out=den[:], in_=den[:], func=AF.Ln), waits=[t0, t1])
            tex = P.op("scalar", lambda e: e.activation(out=den[:], in_=den[:], func=AF.Exp, scale=-1.0), waits=[tln])
            ta0 = P.op("vector", lambda e: e.tensor_tensor(out=attnT[0:HD, pr, :], in0=acc[0][0:HD, :], in1=den[0:HD, :], op=ALU.mult),
                       waits=[tex, L("acc_w0")])
            ta1 = P.op("vector", lambda e: e.tensor_tensor(out=attnT[HD:128, pr, :], in0=acc[1][HD:128, :], in1=den[HD:128, :], op=ALU.mult),
                       waits=[tex, L("acc_w1")])
            last["den_use"] = [ta0, ta1]
            last["acc_use0"] = [t0, ta0]
            last["acc_use1"] = [t1, ta1]
            t_attn.extend([ta0, ta1])
        return fin_

    npairs = int(os.environ.get('KB_PAIRS', 8))
    for pr in range(npairs):
        for g in range(ngrp):
            for e2 in range(2):
                h = 2 * pr + e2
                ld = {}

                def get(ld=ld, g=g, h=h):
                    if "v" not in ld:
                        ld["v"] = load_head(g, h)
                    return ld["v"]
                ld["get"] = get
                if moba:
                    for sc in range(8):
                        box = {}
                        for kp in range(2 * sc + 2):
                            items.append(moba_item(h, sc, kp, ld, box))
                else:
                    for b in range(8):
                        items.append(band_item(g, h, b, ld, g == 0, b == 7))
        items[-1]["after"] = finalize_item(pr)
    for k in range(len(items)):
        if k == 0:
            items[0]["p1"]()
        if k + 1 < len(items):
            items[k + 1]["p1"]()
        items[k]["p2"]()
        if "after" in items[k]:
            items[k]["after"]()

    xt = [P.sb("xo%d" % i, [128, D], F32) for i in range(2)]
    t_fin = []
    for n in range(int(os.environ.get('KB_WO', NT))):
        b = n % 2
        t_x = P.op("sync", lambda e, n=n, b=b: e.dma_start(out=xt[b][:], in_=xsrc[n * 128:(n + 1) * 128, :]),
                   waits=[L("xo_use%d" % b)], sem="d_xo%d" % b, dma=True)
        tadds = []
        for hf in range(2):
            si = cnt["s"] % 3
            cnt["s"] += 1
            ps = psS[si]
            tm = None
            for kc in range(8):
                tm = P.op("tensor", lambda e, kc=kc, n=n, hf=hf, ps=ps: e.matmul(ps[:, 0:512], lhsT=attnT[:, kc, n * 128:(n + 1) * 128],
                                                                           rhs=wosb[:, kc, hf * 512:(hf + 1) * 512], start=(kc == 0), stop=(kc == 7)),
                          waits=t_attn + t_wo + [L("psS_use%d" % si)])
            tadd = P.op("vector", lambda e, hf=hf, b=b, ps=ps: e.tensor_tensor(out=xt[b][:, hf * 512:(hf + 1) * 512], in0=xt[b][:, hf * 512:(hf + 1) * 512],
                                                                                  in1=ps[:, 0:512], op=ALU.add), waits=[tm, t_x])
            last["psS_use%d" % si] = tadd
            tadds.append(tadd)
        t_o = P.op("sync", lambda e, n=n, b=b: e.dma_start(out=xdst[n * 128:(n + 1) * 128, :], in_=xt[b][:]), waits=tadds, sem="d_xw%d" % b, dma=True)
        last["xo_use%d" % b] = t_o
        t_fin.append(t_o)
    P.op("gpsimd", lambda e: e.memset(fin[:], 0.0), waits=t_fin)
    P.flush()


def phase_ffn(nc, sems, tag, T, layer):
    P = Prog(nc, sems, tag)
    xsrc = T["xmid"][layer]
    xdst = T["xout"][layer]
    NB = 512
    nbat = S // NB
    NU = 5
    NT1 = 5
    NG = 3

    wup = P.sb("wup", [128, 8, 2 * DFF], BF16)
    wdn = P.sb("wdn", [128, NFF, D], BF16)
    gnorm = P.sb("gnorm", [128, D], F32)
    cw = P.sb("cw", [128, 3, 2 * NFF], F32)
    cb = P.sb("cb", [128, 2 * NFF], F32)
    ident_b = P.sb("identb", [128, 128], BF16)
    t_idb = mk_identity(P, ident_b)
    xt = [P.sb("xt%d" % i, [128, D], F32) for i in range(1)]
    xr = [P.sb("xr%d" % i, [128, D], F32) for i in range(1)]
    hn = P.sb("hn", [128, D], BF16)
    junk = hn
    hnT = [P.sb("hnT%d" % i, [128, 8, NB], BF16) for i in range(2)]
    hT = P.sb("hT", [128, NFF, NB], BF16)
    T1 = [P.sb("T1_%d" % i, [128, NB], F32) for i in range(NT1)]
    G = [P.sb("G%d" % i, [128, NB], F32) for i in range(NG)]
    hcur = P.sb("hcur", [128, 2 * NFF, 2], F32)
    hw = P.sb("hw", [128, 2 * NFF, 2], F32)
    htmp = [P.sb("htmp%d" % i, [128, 2 * NFF], F32) for i in range(2)]
    ssq = P.sb("ssq", [128, 1], F32)
    sd = P.sb("sd", [128, 1], F32)
    rstd = P.sb("rstd", [128, 1], F32)
    fin = P.sb("fin", [128, 1], F32)
    pT = P.ps("pT", [128, 512], F32)
    pTb = pT[:].bitcast(BF16)
    pu = [P.ps("pu%d" % i, [128, NB], F32) for i in range(NU)]
    pd = [P.ps("pd%d" % i, [128, 512], F32) for i in range(2)]

    t_c = []
    t_c.append(P.op("sync", lambda e: e.dma_start(out=gnorm[:], in_=T["ffn_norm"][layer:layer + 1, :].partition_broadcast(128)), sem="d_c", dma=True))
    for j in range(3):
        t_c.append(P.op("sync", lambda e, j=j: e.dma_start(out=cw[:, j, :], in_=T["ffn_conv_w"][layer, j, :].rearrange("(c p) -> p c", p=128),
                                                             allow_slow_non_contiguous=True),
                        sem="d_c", dma=True))
    t_c.append(P.op("sync", lambda e: e.dma_start(out=cb[:], in_=T["ffn_conv_b"][layer, :].rearrange("(c p) -> p c", p=128),
                                                   allow_slow_non_contiguous=True), sem="d_c", dma=True))
    t_hw0 = P.op("gpsimd", lambda e: e.memset(hw[:], 0.0))
    t_wu = []
    for c in range(2 * NFF // 4):
        for kc in range(8):
            t_wu.append(P.op("gpsimd", lambda e, c=c, kc=kc: e.dma_start(out=wup[:, kc, c * 512:(c + 1) * 512],
                                                                          in_=T["ffn_w_up"][layer, kc * 128:(kc + 1) * 128, c * 512:(c + 1) * 512]),
                             sem="d_wu", dma=True))
    t_wd = []
    for j in range(NFF):
        t_wd.append(P.op("gpsimd", lambda e, j=j: e.dma_start(out=wdn[:, j, :], in_=T["ffn_w_down"][layer, j * 128:(j + 1) * 128, :]),
                         sem="d_wd", dma=True))
    last = {}

    def L(k):
        return last.get(k)

    cnt = {"u": 0, "t": 0, "d": 0, "x": 0, "r": 0, "g": 0}
    t_fin = []
    bstate = {}

    def proA(bt, j):
        b = 0
        r0 = bt * NB + j * 128
        t_x = P.op("sync", lambda e: e.dma_start(out=xt[b][:], in_=xsrc[r0:r0 + 128, :]),
                   waits=[L("xt_use%d" % b)], sem="d_x%d" % b, dma=True)
        t_sq = P.op("scalar", lambda e: e.activation(out=junk[:], in_=xt[b][:], func=AF.Square, accum_out=ssq[:]),
                    waits=[t_x, L("ssq_use"), L("hn_use")])
        t_sd = P.op("scalar", lambda e: e.activation(out=sd[:], in_=ssq[:], func=AF.Sqrt, scale=1.0 / D, bias=EPS), waits=[t_sq, L("sd_use")])
        last["ssq_use"] = t_sd
        t_rs = P.op("vector", lambda e: e.reciprocal(out=rstd[:], in_=sd[:]), waits=[t_sd, L("rstd_use")])
        last["sd_use"] = t_rs
        t_hn = P.op("vector", lambda e: e.scalar_tensor_tensor(out=hn[:], in0=xt[b][:], scalar=rstd[:, 0:1], in1=gnorm[:],
                                                               op0=ALU.mult, op1=ALU.mult), waits=[t_rs, L("hn_use")] + t_c)
        last["rstd_use"] = t_hn
        last["xt_use%d" % b] = t_hn
        bstate[(bt, j)] = t_hn

    def proB(bt, j):
        hb = hnT[bt % 2]
        t_hn = bstate[(bt, j)]
        tt = None
        for kc in range(8):
            tt = P.op("tensor", lambda e, kc=kc: e.transpose(out=pTb[:, kc * 128:(kc + 1) * 128], in_=hn[:, kc * 128:(kc + 1) * 128],
                                                             identity=ident_b[:]), waits=[t_hn, t_idb, L("pT_use")])
        last["hn_use"] = tt
        te = P.op("vector", lambda e: e.tensor_copy(out=hb[:, :, j * 128:(j + 1) * 128], in_=pTb.rearrange("p (a q) -> p a q", a=8)),
                  waits=[tt, L("hnT_use%d" % (bt % 2))])
        last["pT_use"] = te
        bstate.setdefault(bt, []).append(te)

    def prologue(bt):
        for j in range(4):
            proA(bt, j)
            proB(bt, j)

    def chunk(bt, ch, hb, thT):
        ui = cnt["u"] % NU
        cnt["u"] += 1
        ti = cnt["t"] % NT1
        cnt["t"] += 1
        p_ = pu[ui]
        t1 = T1[ti]
        wtok = t_wu[(ch // 4) * 8:(ch // 4) * 8 + 8]
        tm = None
        for kc in range(8):
            tm = P.op("tensor", lambda e, kc=kc: e.matmul(p_[:], lhsT=wup[:, kc, ch * 128:(ch + 1) * 128], rhs=hb[:, kc, :],
                                                          start=(kc == 0), stop=(kc == 7)),
                      waits=thT + wtok + [L("pu_use%d" % ui)])
        tB = P.op("scalar", lambda e: e.activation(out=t1[:], in_=p_[:], func=AF.Identity, scale=cw[:, 2, ch:ch + 1], bias=cb[:, ch:ch + 1]),
                  waits=[tm, L("T1_use%d" % ti)] + t_c)
        tS = P.op("vector", lambda e: e.tensor_copy(out=hcur[:, ch, :], in_=p_[:, NB - 2:NB]), waits=[tB, L("hcur_use")])
        tC = P.op("vector", lambda e: e.scalar_tensor_tensor(out=t1[:, 1:NB], in0=p_[:, 0:NB - 1], scalar=cw[:, 1, ch:ch + 1], in1=t1[:, 1:NB],
                                                             op0=ALU.mult, op1=ALU.add), waits=[tB])
        tD = P.op("vector", lambda e: e.scalar_tensor_tensor(out=t1[:, 2:NB], in0=p_[:, 0:NB - 2], scalar=cw[:, 0, ch:ch + 1], in1=t1[:, 2:NB],
                                                             op0=ALU.mult, op1=ALU.add), waits=[tC])
        last["pu_use%d" % ui] = tD
        tH = P.op("gpsimd", lambda e: e.tensor_tensor(out=t1[:, 0:2], in0=t1[:, 0:2], in1=hw[:, ch, :], op=ALU.add), waits=[tC, t_hw0, L("hw_w")])
        return tm, tS, tD, tH, ti, t1

    def batch(bt):
        hb = hnT[bt % 2]
        thT = bstate[bt]
        t_hT = []
        tSs = []
        tHs = []
        tm = None
        pend = []

        def finish(j, tDg, tHg, tig, t1g, tDv, tHv, tiv, t1v):
            gi = cnt["g"] % NG
            cnt["g"] += 1
            gg = G[gi]
            tSil = P.op("scalar", lambda e: e.activation(out=gg[:], in_=t1g[:], func=AF.Silu), waits=[tDg, tHg, L("G_use%d" % gi)])
            last["T1_use%d" % tig] = tSil
            tF = P.op("gpsimd", lambda e: e.tensor_tensor(out=hT[:, j, :], in0=t1v[:], in1=gg[:], op=ALU.mult),
                      waits=[tDv, tHv, tSil, L("hT_use")])
            last["G_use%d" % gi] = tF
            last["T1_use%d" % tiv] = tF
            t_hT.append(tF)

        for j in range(NFF):
            tm, tS1, tDg, tHg, tig, t1g = chunk(bt, j, hb, thT)
            tm, tS2, tDv, tHv, tiv, t1v = chunk(bt, NFF + j, hb, thT)
            if pend:
                finish(*pend.pop())
            pend.append((j, tDg, tHg, tig, t1g, tDv, tHv, tiv, t1v))
            tSs += [tS1, tS2]
            tHs += [tHg, tHv]
            if bt + 1 < nbat:
                if j in (2, 7, 12, 17):
                    proA(bt + 1, (j - 2) // 5)
                if j in (4, 9, 14, 19):
                    proB(bt + 1, (j - 4) // 5)
        finish(*pend.pop())
        last["hnT_use%d" % (bt % 2)] = tm
        a1 = P.op("gpsimd", lambda e: e.tensor_tensor(out=htmp[0][:], in0=hcur[:, :, 1], in1=cw[:, 1, :], op=ALU.mult), waits=tSs + tHs + t_c)
        a2 = P.op("gpsimd", lambda e: e.tensor_tensor(out=htmp[1][:], in0=hcur[:, :, 0], in1=cw[:, 0, :], op=ALU.mult), waits=tSs + tHs)
        a3 = P.op("gpsimd", lambda e: e.tensor_tensor(out=hw[:, :, 0], in0=htmp[0][:], in1=htmp[1][:], op=ALU.add), waits=[a1, a2])
        a4 = P.op("gpsimd", lambda e: e.tensor_tensor(out=hw[:, :, 1], in0=hcur[:, :, 1], in1=cw[:, 0, :], op=ALU.mult), waits=[a1, a2, a3])
        last["hw_w"] = [a3, a4]
        last["hcur_use"] = [a1, a2, a4]
        tdn = None
        for j4 in range(4):
            rb = 0
            r0 = bt * NB + j4 * 128
            t_xr = P.op("sync", lambda e, r0=r0, rb=rb: e.dma_start(out=xr[rb][:], in_=xsrc[r0:r0 + 128, :]),
                        waits=[L("xr_use%d" % rb)], sem="d_r%d" % rb, dma=True)
            tadds = []
            for hf in range(2):
                di = cnt["d"] % 2
                cnt["d"] += 1
                for j in range(NFF):
                    tdn = P.op("tensor", lambda e, j=j, j4=j4, hf=hf, di=di: e.matmul(pd[di][:], lhsT=hT[:, j, j4 * 128:(j4 + 1) * 128],
                                                                                      rhs=wdn[:, j, hf * 512:(hf + 1) * 512], start=(j == 0), stop=(j == NFF - 1)),
                               waits=t_hT + t_wd + [L("pd_use%d" % di)])
                tadd = P.op("vector", lambda e, hf=hf, di=di, rb=rb: e.tensor_tensor(out=xr[rb][:, hf * 512:(hf + 1) * 512],
                                                                                      in0=xr[rb][:, hf * 512:(hf + 1) * 512], in1=pd[di][:], op=ALU.add),
                            waits=[tdn, t_xr])
                last["pd_use%d" % di] = tadd
                tadds.append(tadd)
            t_o = P.op("sync", lambda e, r0=r0, rb=rb: e.dma_start(out=xdst[r0:r0 + 128, :], in_=xr[rb][:]),
                       waits=tadds, sem="d_xw%d" % rb, dma=True)
            last["xr_use%d" % rb] = t_o
            t_fin.append(t_o)
        last["hT_use"] = tdn

    prologue(0)
    for bt in range(nbat):
        batch(bt)
    P.op("gpsimd", lambda e: e.memset(fin[:], 0.0), waits=t_fin)
    P.flush()


def host_consts():
    pos = np.arange(S, dtype=np.float32)
    inv = (np.float32(500000.0) ** (-np.arange(0, 16, 2, dtype=np.float32) / np.float32(16))).astype(np.float32)
    ang = (pos[:, None] * inv[None, :]).astype(np.float32)
    cos = np.cos(ang).astype(np.float32)
    sin = np.sin(ang).astype(np.float32)
    ropec = np.zeros((3, 128, NT, 8), np.float32)
    ropes = np.zeros((3, 128, NT, 8), np.float32)
    for v, dil in enumerate(DILS):
        Lg = S // dil
        pp = np.arange(S)
        r, a = np.divmod(pp, Lg)
        t = a * dil + r
        ropec[v] = cos[t].reshape(NT, 128, 8).transpose(1, 0, 2)
        ropes[v] = sin[t].reshape(NT, 128, 8).transpose(1, 0, 2)
    k = np.arange(128)[:, None]
    q = np.arange(128)[None, :]
    prev = (k >= q).astype(np.float32)
    cur = (k <= q).astype(np.float32)
    zero = np.zeros_like(prev)
    bm = np.zeros((128, 3, 8, 128), np.float32)
    for j in range(4):
        bm[:, 0, 2 * j] = prev
        bm[:, 0, 2 * j + 1] = cur
        bm[:, 1, 2 * j] = zero if j == 0 else prev
        bm[:, 1, 2 * j + 1] = cur
        bm[:, 2, 2 * j] = zero if j % 2 == 0 else prev
        bm[:, 2, 2 * j + 1] = cur
    bm = bm.reshape(128, 3, 1024).astype(ml_dtypes.bfloat16)
    oh = (np.arange(S)[None, :] // 256 == np.arange(16)[:, None]).astype(np.float32).astype(ml_dtypes.bfloat16)
    return {"ropec": ropec, "ropes": ropes, "bmask": bm, "onehot": oh}


WEIGHT_SHAPES = {
    "attn_norm": [2, D], "a_w_qkv": [1, D, 9216], "a_q_norm": [1, 3, HD], "a_k_norm": [1, 3, HD], "a_w_o": [1, D, D],
    "b_w_qkv": [1, D, 3072], "b_q_norm": [1, HD], "b_k_norm": [1, HD], "b_w_o": [1, D, D],
    "ffn_norm": [2, D], "ffn_w_up": [2, D, 2 * DFF], "ffn_conv_w": [2, 3, 2 * DFF], "ffn_conv_b": [2, 2 * DFF],
    "ffn_w_down": [2, DFF, D],
}


def build_nc(phases=("A0", "B0", "C0", "A1", "B1", "C1"), debug_out=None):
    nc = bass.Bass("TRN2", target_bir_lowering=False)
    T = {}
    x = nc.dram_tensor("x", [S, D], F32, kind="ExternalInput").ap()
    for k, shp in WEIGHT_SHAPES.items():
        T[k] = nc.dram_tensor(k, shp, F32, kind="ExternalInput").ap()
    T["ropec"] = nc.dram_tensor("ropec", [3, 128, NT, 8], F32, kind="ExternalInput").ap()
    T["ropes"] = nc.dram_tensor("ropes", [3, 128, NT, 8], F32, kind="ExternalInput").ap()
    T["bmask"] = nc.dram_tensor("bmask", [128, 3, 1024], BF16, kind="ExternalInput").ap()
    T["onehot"] = nc.dram_tensor("onehot", [16, S], BF16, kind="ExternalInput").ap()
    y = nc.dram_tensor("y", [S, D], F32, kind="ExternalOutput").ap()

    def scratch(name, shape, dt):
        kind = "ExternalOutput" if (debug_out and name in debug_out) else "Internal"
        return nc.dram_tensor(name, shape, dt, kind=kind).ap()
    R1 = scratch("R1", [S, D], F32)
    R2 = scratch("R2", [S, D], F32)
    R3 = scratch("R3", [S, D], F32)
    T["xin"] = [x, R2]
    T["xmid"] = [R1, R3]
    T["xout"] = [R2, y]
    T["QTs"] = [scratch("QT0", [3, H, HD, S], BF16), scratch("QT1", [1, H, HD + 16, S], BF16)]
    T["KTs"] = [scratch("KT0", [3, H, HD, S], BF16), scratch("KT1", [1, H, HD, S], BF16)]
    T["Vs"] = [scratch("V0", [3, S, H, 128], BF16), scratch("V1", [1, S, H, 128], BF16)]
    sems = Sems(nc)
    fns = {"A": phase_qkv, "B": phase_attn, "C": phase_ffn}
    for ph in phases:
        fns[ph[0]](nc, sems, ph + "_", T, int(ph[1]))
    sems.stack.close()
    return nc


_CACHE = {}


def kernel(**inputs):
    if "nc" not in _CACHE:
        _CACHE["nc"] = build_nc()
        _CACHE["consts"] = host_consts()
    nc = _CACHE["nc"]
    consts = _CACHE["consts"]
    x = np.ascontiguousarray(np.asarray(inputs["x"], dtype=np.float32))
    shared = {k: np.ascontiguousarray(np.asarray(inputs[k], dtype=np.float32)) for k in WEIGHT_SHAPES}
    shared.update(consts)
    in_maps = []
    for b in range(8):
        m = dict(shared)
        m["x"] = x[b]
        in_maps.append(m)
    res = run_bass_kernel_spmd(nc, in_maps, core_ids=list(range(8)))
    return np.stack([np.asarray(r["y"], dtype=np.float32) for r in res.results], axis=0)
```

```python
import contextlib
import os
import numpy as np
import ml_dtypes
import concourse.bass as bass
import concourse.mybir as mybir
from concourse.bass_utils import run_bass_kernel_spmd

F32 = mybir.dt.float32
BF16 = mybir.dt.bfloat16
AF = mybir.ActivationFunctionType
ALU = mybir.AluOpType
AX = mybir.AxisListType

S = 4096
D = 1024
NT = S // 128
H = 16
HD = 64
DFF = 2816
NFF = DFF // 128
EPS = 1e-6
BIG = 30000.0
DILS = (1, 4, 16)
ENGS = ("tensor", "vector", "scalar", "gpsimd", "sync")


class Tok:
    __slots__ = ("sem", "val")

    def __init__(self, sem, val):
        self.sem = sem
        self.val = val


class Sems:
    def __init__(self, nc):
        self.nc = nc
        self.h = {}
        self.cnt = {}

    def get(self, name):
        if name not in self.h:
            self.h[name] = self.nc.alloc_semaphore(name=name)
            self.cnt[name] = 0
        return name

    def release(self):
        if self.h:
            self.nc.clear_and_free_semaphores(list(self.h.values()))
            self.nc.all_engine_barrier()
        self.h = {}
        self.cnt = {}


class Prog:
    def __init__(self, nc, sems, tag):
        self.nc = nc
        self.S = Sems(nc)
        self.tag = tag
        self.q = {e: [] for e in ENGS}
        self.stack = contextlib.ExitStack()
        self.seen = {e: {} for e in ENGS}
        self.used = {}

    def sb(self, name, shape, dt):
        return self.stack.enter_context(self.nc.sbuf_tensor(self.tag + name, shape, dt))

    def ps(self, name, shape, dt):
        return self.stack.enter_context(self.nc.psum_tensor(self.tag + name, shape, dt))

    def op(self, eng, fn, waits=(), sem=None, dma=False):
        if sem is None:
            sem = "s_" + eng
        sem = self.S.get(self.tag + sem)
        need = {}

        def add(t):
            if t is None:
                return
            if isinstance(t, (list, tuple)):
                for u in t:
                    add(u)
                return
            if need.get(t.sem, -1) < t.val:
                need[t.sem] = t.val
        add(list(waits))
        seen = self.seen[eng]
        wl = []
        for s, v in need.items():
            if seen.get(s, -1) >= v:
                continue
            seen[s] = v
            wl.append((s, v))
            self.used.setdefault(s, set()).add(v)
        self.S.cnt[sem] += 1
        idx = self.S.cnt[sem]
        if dma:
            self.used.setdefault(sem, set()).add(idx)
        self.q[eng].append((wl, fn, sem, idx, dma))
        return Tok(sem, idx)

    def flush(self):
        nc = self.nc
        sems = self.S.h
        qs = self.q
        val = {}
        for s, n in self.S.cnt.items():
            used = self.used.get(s, set())
            run = 0
            m = {}
            for i in range(1, n + 1):
                if i in used:
                    run += 1
                m[i] = run
            val[s] = m
        used_all = self.used
        with nc.Block() as block:
            def mk(name):
                def body(e):
                    for wl, fn, sem, idx, dma in qs[name]:
                        for s, v in wl:
                            e.wait_ge(sems[s], val[s][v] * (16 if s in dma_sems else 1))
                        ins = fn(e)
                        if idx in used_all.get(sem, ()):
                            ins.then_inc(sems[sem], 16 if dma else 1)
                return body
            dma_sems = set()
            for name in ENGS:
                for wl, fn, sem, idx, dma in qs[name]:
                    if dma:
                        dma_sems.add(sem)
            for name in ENGS:
                if qs[name]:
                    getattr(block, name)(mk(name))
        self.stack.close()
        self.S.release()


def mk_identity(P, ident, dt_one=1.0):
    t0 = P.op("gpsimd", lambda e: e.memset(ident[:], 0.0))
    return P.op("gpsimd", lambda e: e.affine_select(out=ident[:], in_=ident[:], pattern=[[-1, 128]],
                                                    compare_op=ALU.not_equal, fill=1.0, base=0,
                                                    channel_multiplier=1), waits=[t0])


def phase_qkv(nc, sems, tag, T, layer):
    P = Prog(nc, sems, tag)
    moba = layer == 1
    ngrp = 1 if moba else 3
    xsrc = T["xin"][layer]
    wq = T["b_w_qkv"] if moba else T["a_w_qkv"]

    ident_b = P.sb("identb", [128, 128], BF16)
    ident_f = P.sb("identf", [128, 128], F32)
    t_idb = mk_identity(P, ident_b)
    t_idf = mk_identity(P, ident_f)
    gnorm = P.sb("gnorm", [128, D], F32)
    t_gn = P.op("sync", lambda e: e.dma_start(out=gnorm[:], in_=T["attn_norm"][layer:layer + 1, :].partition_broadcast(128)),
                sem="d_c", dma=True)
    gqk = P.sb("gqk", [128, ngrp, 2, HD], F32)
    t_gq = []
    for g in range(ngrp):
        for s_, nm in ((0, "q"), (1, "k")):
            src = (T["b_%s_norm" % nm][0:1, :] if moba else T["a_%s_norm" % nm][0, g:g + 1, :])
            t_gq.append(P.op("sync", lambda e, g=g, s_=s_, src=src: e.dma_start(
                out=gqk[:, g, s_, :], in_=src.partition_broadcast(128)), sem="d_c", dma=True))
    t_gn = t_gq[-1]
    t_gq = [t_gq[-1]]
    ropecs = [P.sb("ropec%d" % i, [128, NT, 8], F32) for i in range(2)]
    ropess = [P.sb("ropes%d" % i, [128, NT, 8], F32) for i in range(2)]
    junkx = P.sb("junkx", [128, D], BF16)
    wsb = [P.sb("w%d" % i, [128, 8, 3, 1024], BF16) for i in range(2 if not moba else 1)]
    xt = [P.sb("xt%d" % i, [128, D], F32) for i in range(3)]
    junks = [P.sb("junk%d" % i, [128, 1024], F32) for i in range(2)]
    ssq = P.sb("ssq", [128, 1], F32)
    sd = P.sb("sd", [128, 1], F32)
    rstd = P.sb("rstd", [128, 1], F32)
    hn = P.sb("hn", [128, D], BF16)
    hnT = [P.sb("hnT%d" % i, [128, 8, 128], BF16) for i in range(2)]
    ss2 = P.sb("ss2", [128, 32], F32)
    sd2 = P.sb("sd2", [128, 32], F32)
    rs2 = P.sb("rs2", [128, 32], F32)
    qk = [P.sb("qk%d" % i, [128, 2, H, HD], F32) for i in range(2)]
    rt = [P.sb("rt%d" % i, [128, 2 * H, 8], F32) for i in range(4)]
    vaug = [P.sb("vaug%d" % i, [128, H, 128], BF16) for i in range(2)]
    qT4 = [P.sb("qT4_%d" % i, [128, 8, 512], BF16) for i in range(2)]
    kT4 = [P.sb("kT4_%d" % i, [128, 8, 512], BF16) for i in range(2)]
    pT = P.ps("pT", [128, 512], F32)
    pTb = pT[:].bitcast(BF16)
    pqs = [P.ps("pq%d" % i, [128, 1024], F32) for i in range(2)]
    pv = P.ps("pv", [128, 512], F32)
    ptr = P.ps("ptr", [128, 8, 128], F32)
    if moba:
        qT32 = P.sb("qT32", [128, 8, 128], F32)
        ksum = P.sb("ksum", [128, 8, 16], F32)
        kpart = P.sb("kpart", [128, 8], F32)
        gs = P.sb("gs", [128, H, 16], F32)
        t8 = P.sb("t8", [128, H, 8], F32)
        mv = P.sb("mv", [128, H, 16], F32)
        mT4 = [P.sb("mT4_%d" % i, [128, 2, 512], BF16) for i in range(2)]

    t_ones = []
    for i in range(2):
        t_ones.append(P.op("gpsimd", lambda e, i=i: e.memset(vaug[i][:], 1.0)))
    t_ks0 = P.op("gpsimd", lambda e: e.memset(ksum[:], 0.0)) if moba else None

    last = {}

    def L(k):
        return last.get(k)

    tw_use = [None, None]
    t_outdma = []
    tiles = [(g, n) for g in range(ngrp) for n in range(NT)]
    state = {}

    def load_w(g):
        wb = wsb[g % len(wsb)]
        toks = []
        for kc in range(8):
            for s_ in range(3):
                if moba:
                    c0 = s_ * 1024
                else:
                    c0 = (s_ * 3 + g) * 1024
                toks.append(P.op("gpsimd", lambda e, wb=wb, kc=kc, s_=s_, c0=c0: e.dma_start(
                    out=wb[:, kc, s_, :], in_=wq[0, kc * 128:(kc + 1) * 128, c0:c0 + 1024]),
                    waits=[tw_use[g % len(wsb)]], sem="d_w%d" % (g % len(wsb)), dma=True))
        return toks

    def load_rope(g):
        v = 0 if moba else g
        rb = g % 2
        a = P.op("sync", lambda e: e.dma_start(out=ropecs[rb][:], in_=T["ropec"][v]), waits=[L("rope_use%d" % rb)], sem="d_r%d" % rb, dma=True)
        b = P.op("sync", lambda e: e.dma_start(out=ropess[rb][:], in_=T["ropes"][v]), waits=[L("rope_use%d" % rb)], sem="d_r%d" % rb, dma=True)
        return [a, b]

    def x_rows(g, n):
        dil = 1 if moba else DILS[g]
        Lg = S // dil
        p0 = n * 128
        r, a0 = divmod(p0, Lg)
        v = xsrc.rearrange("(a r) d -> r a d", r=dil)
        return v[r, a0:a0 + 128, :]

    def xload(idx):
        g, n = tiles[idx]
        b3 = idx % 3
        state["xl%d" % idx] = P.op("sync", lambda e: e.dma_start(out=xt[b3][:], in_=x_rows(g, n)),
                                   waits=[L("xt_use%d" % b3)], sem="d_x%d" % b3, dma=True)

    def prepA(idx):
        g, n = tiles[idx]
        st = state.setdefault(idx, {})
        if n == 0:
            if g == 0:
                state["w0"] = load_w(0)
            state["rope%d" % g] = load_rope(g)
        if n == 1 and g + 1 < ngrp:
            state["w%d" % (g + 1)] = load_w(g + 1)
        b3 = idx % 3
        t_x = state["xl%d" % idx]
        t_sq = P.op("scalar", lambda e: e.activation(out=junkx[:], in_=xt[b3][:], func=AF.Square, accum_out=ssq[:]),
                    waits=[t_x, L("ssq_use")])
        t_sd = P.op("scalar", lambda e: e.activation(out=sd[:], in_=ssq[:], func=AF.Sqrt, scale=1.0 / D, bias=EPS),
                    waits=[t_sq, L("sd_use")])
        last["ssq_use"] = t_sd
        t_rs = P.op("vector", lambda e: e.reciprocal(out=rstd[:], in_=sd[:]), waits=[t_sd, L("rstd_use")])
        last["sd_use"] = t_rs
        t_hn = P.op("vector", lambda e: e.scalar_tensor_tensor(out=hn[:], in0=xt[b3][:], scalar=rstd[:, 0:1], in1=gnorm[:],
                                                               op0=ALU.mult, op1=ALU.mult),
                    waits=[t_rs, t_gn, L("hn_use")])
        last["rstd_use"] = t_hn
        last["xt_use%d" % b3] = t_hn
        st["t_hn"] = t_hn

    def prepB(idx):
        st = state[idx]
        b = idx % 2
        t_hn = st["t_hn"]
        tt = None
        for kc in range(8):
            tt = P.op("tensor", lambda e, kc=kc: e.transpose(out=pTb[:, kc * 128:(kc + 1) * 128], in_=hn[:, kc * 128:(kc + 1) * 128],
                                                             identity=ident_b[:]),
                      waits=[t_hn, t_idb, L("pT_use")])
        last["hn_use"] = tt
        t_hT = P.op("scalar", lambda e: e.activation(out=hnT[b][:].rearrange("p a b -> p (a b)"), in_=pTb, func=AF.Copy),
                    waits=[tt, L("hnT_use%d" % b)])
        last["pT_use"] = t_hT
        st["hT"] = t_hT

    def main_tile(idx):
        g, n = tiles[idx]
        st = state[idx]
        b = idx % 2
        t_hT = st["hT"]
        tw = state["w%d" % g]
        wb = wsb[g % len(wsb)]
        qkb = qk[b]
        va = vaug[b]
        gsel = 0 if moba else g

        def mm(dst, s_, c0, extra):
            t = None
            for kc in range(8):
                t = P.op("tensor", lambda e, kc=kc: e.matmul(dst, lhsT=hnT[b][:, kc, :], rhs=wb[:, kc, s_, c0:c0 + 512],
                                                            start=(kc == 0), stop=(kc == 7)),
                         waits=[t_hT, tw, extra])
            return t
        t_gs_ = []
        tv = None
        tvm = None
        for s_ in range(2):
            pp = pqs[s_]
            jk = junks[s_]
            tq = None
            for hf in range(2):
                tq = mm(pp[:, hf * 512:(hf + 1) * 512], s_, hf * 512, L("pq_use%d" % s_))
            hf = s_
            tvm = mm(pv[:], 2, hf * 512, L("pv_use"))
            t_sq2 = P.op("scalar", lambda e, pp=pp, jk=jk: e.activation(out=jk[:], in_=pp[:], func=AF.Square), waits=[tq, L("junk%d" % s_)])
            pvv = pv[:].rearrange("p (i e d) -> p i e d", e=2, d=HD)
            vv = va[:, hf * 8:(hf + 1) * 8, :].rearrange("p (i e) c -> p i e c", e=2)
            tv0 = P.op("scalar", lambda e, pvv=pvv, vv=vv: e.activation(out=vv[:, :, 0, 0:HD], in_=pvv[:, :, 0, :], func=AF.Copy),
                       waits=[tvm, L("vaug_use%d" % b)] + t_ones)
            tv = P.op("scalar", lambda e, pvv=pvv, vv=vv: e.activation(out=vv[:, :, 1, HD:128], in_=pvv[:, :, 1, :], func=AF.Copy),
                      waits=[tvm, L("vaug_use%d" % b)] + t_ones)
            last["pv_use"] = tv
            t_ss2 = P.op("vector", lambda e, jk=jk, s_=s_: e.tensor_reduce(out=ss2[:, s_ * 16:(s_ + 1) * 16], in_=jk[:].rearrange("p (h d) -> p h d", d=HD),
                                                                        op=ALU.add, axis=AX.X), waits=[t_sq2, L("ss2_use%d" % s_)])
            last["junk%d" % s_] = t_ss2
            t_sd2 = P.op("scalar", lambda e, s_=s_: e.activation(out=sd2[:, s_ * 16:(s_ + 1) * 16], in_=ss2[:, s_ * 16:(s_ + 1) * 16], func=AF.Sqrt,
                                                                  scale=1.0 / HD, bias=EPS), waits=[t_ss2, L("sd2_use%d" % s_)])
            last["ss2_use%d" % s_] = t_sd2
            t_rs2 = P.op("vector", lambda e, s_=s_: e.reciprocal(out=rs2[:, s_ * 16:(s_ + 1) * 16], in_=sd2[:, s_ * 16:(s_ + 1) * 16]),
                         waits=[t_sd2, L("rs2_use%d" % s_)])
            last["sd2_use%d" % s_] = t_rs2
            t_nm = P.op("vector", lambda e, pp=pp, s_=s_: e.tensor_tensor(out=qkb[:, s_, :, :],
                                                                           in0=pp[:].rearrange("p (h d) -> p h d", d=HD),
                                                                           in1=rs2[:, s_ * 16:(s_ + 1) * 16].unsqueeze(2).broadcast_to([128, H, HD]), op=ALU.mult),
                        waits=[t_rs2, L("qk_use%d" % b)])
            last["rs2_use%d" % s_] = t_nm
            last["pq_use%d" % s_] = t_nm
            t_gs_.append(P.op("gpsimd", lambda e, s_=s_: e.tensor_tensor(out=qkb[:, s_, :, :], in0=qkb[:, s_, :, :],
                                                                          in1=gqk[:, gsel, s_, :].unsqueeze(1).broadcast_to([128, H, HD]),
                                                                          op=ALU.mult), waits=[t_nm] + t_gq))
        st["t_g"] = t_gs_
        tw_use[g % len(wsb)] = tv
        t_vo = P.op("sync", lambda e: e.dma_start(out=T["Vs"][layer][g, n * 128:(n + 1) * 128, :, :], in_=va[:]),
                    waits=[tv], sem="d_vo%d" % b, dma=True)
        last["vaug_use%d" % b] = t_vo
        t_outdma.append(t_vo)
        last["hnT_use%d" % b] = tvm

    def stage1r(idx):
        g, n = tiles[idx]
        st = state[idx]
        b = idx % 2
        qkb = qk[b]
        t_g = st["t_g"]
        qv = qkb[:].rearrange("p s h d -> p (s h) d")
        x1 = qv[:, :, 0:8]
        x2 = qv[:, :, 8:16]
        cb = ropecs[g % 2][:, n, :].unsqueeze(1).broadcast_to([128, 32, 8])
        sbb = ropess[g % 2][:, n, :].unsqueeze(1).broadcast_to([128, 32, 8])
        tr = state["rope%d" % g]
        w0 = [t_g, tr, L("rt_use")]
        a1 = P.op("vector", lambda e: e.tensor_tensor(out=rt[0][:], in0=x1, in1=cb, op=ALU.mult), waits=w0)
        a2 = P.op("vector", lambda e: e.tensor_tensor(out=rt[1][:], in0=x2, in1=sbb, op=ALU.mult), waits=w0)
        a3 = P.op("vector", lambda e: e.tensor_tensor(out=rt[2][:], in0=x2, in1=cb, op=ALU.mult), waits=w0)
        a4 = P.op("vector", lambda e: e.tensor_tensor(out=rt[3][:], in0=x1, in1=sbb, op=ALU.mult), waits=w0)
        a5 = P.op("vector", lambda e: e.tensor_tensor(out=x1, in0=rt[0][:], in1=rt[1][:], op=ALU.subtract), waits=[a1, a2, a3, a4])
        a6 = P.op("vector", lambda e: e.tensor_tensor(out=x2, in0=rt[2][:], in1=rt[3][:], op=ALU.add), waits=[a1, a2, a3, a4, a5])
        last["rt_use"] = a6
        last["rope_use%d" % (g % 2)] = a6
        st["qkg"] = [a5, a6]

    def stage2(idx):
        g, n = tiles[idx]
        st = state[idx]
        b = idx % 2
        qkb = qk[b]
        n4, j4 = divmod(n, 4)
        b4 = (idx // 4) % 2
        for s_, dst4, nm in ((0, qT4[b4], "Q"), (1, kT4[b4], "K")):
            tt = None
            for c in range(8):
                tt = P.op("tensor", lambda e, c=c, s_=s_: e.transpose(out=ptr[:, c, :], in_=qkb[:, s_, 2 * c:2 * c + 2, :].rearrange("p h d -> p (h d)"),
                                                                      identity=ident_f[:]),
                          waits=[st["qkg"], t_idf, L("ptr_use")])
            wv = [tt, L("%s4_use%d" % (nm, b4))]
            te = P.op("scalar", lambda e, dst4=dst4: e.activation(out=dst4[:, :, j4 * 128:(j4 + 1) * 128], in_=ptr[:], func=AF.Copy),
                      waits=wv)
            tl = [te]
            if moba and s_ == 0:
                tl.append(P.op("vector", lambda e: e.tensor_copy(out=qT32[:], in_=ptr[:]), waits=[tt, te, L("qT32_use")]))
            if moba and s_ == 1:
                tkp = P.op("vector", lambda e: e.tensor_reduce(out=kpart[:], in_=ptr[:], op=ALU.add, axis=AX.X),
                           waits=[tt, te, L("kpart_use")])
                j = n // 2
                tks = P.op("vector", lambda e, j=j: e.tensor_tensor(out=ksum[:, :, j], in0=ksum[:, :, j], in1=kpart[:], op=ALU.add),
                           waits=[tkp, t_ks0, L("ksum_w"), L("qT32_use")])
                last["kpart_use"] = tks
                last["ksum_w"] = tks
                tl.append(tks)
            last["ptr_use"] = tl
            st["te%d" % s_] = te
        last["qk_use%d" % b] = last["ptr_use"]
        if moba:
            stage_gate(idx)
        if j4 == 3:
            for s_, src4, nm, dstT in ((0, qT4[b4], "Q", T["QTs"][layer]), (1, kT4[b4], "K", T["KTs"][layer])):
                tds = []
                for e2 in range(2):
                    dv = dstT[g].rearrange("(c e) r t -> e r c t", e=2)[e2, 0:HD, :, n4 * 512:(n4 + 1) * 512]
                    tds.append(P.op("sync", lambda e, dv=dv, src4=src4, e2=e2: e.dma_start(out=dv, in_=src4[e2 * HD:(e2 + 1) * HD, :, :]),
                                    waits=[state[idx - 3]["te%d" % s_], state[idx - 2]["te%d" % s_], state[idx - 1]["te%d" % s_], st["te%d" % s_]],
                                    sem="d_%so%d" % (nm, b4), dma=True))
                last["%s4_use%d" % (nm, b4)] = tds
                t_outdma.extend(tds)
            if moba:
                tds = []
                m4 = mT4[b4]
                for h in range(H):
                    dv = T["QTs"][layer][0, h, HD:HD + 16, n4 * 512:(n4 + 1) * 512]
                    tds.append(P.op("gpsimd", lambda e, dv=dv, h=h, m4=m4: e.dma_start(out=dv, in_=m4[(h % 8) * 16:(h % 8) * 16 + 16, h // 8, :]),
                                    waits=[state[idx - 3]["tm"], state[idx - 2]["tm"], state[idx - 1]["tm"], st["tm"]],
                                    sem="d_mo%d" % b4, dma=True))
                last["m4_use%d" % b4] = tds
                t_outdma.extend(tds)

    def stage_gate(idx):
        g, n = tiles[idx]
        st = state[idx]
        b4 = (idx // 4) % 2
        j4 = n % 4
        ob = n // 2
        pg = pT[:, 0:256]
        tg = None
        tq32 = last["ptr_use"]
        for h in range(H):
            r0 = (h % 2) * HD
            tg = P.op("tensor", lambda e, h=h, r0=r0: e.matmul(pg[:, h * 16:(h + 1) * 16], lhsT=qT32[r0:r0 + HD, h // 2, :],
                                                                rhs=ksum[r0:r0 + HD, h // 2, :], start=True, stop=True),
                      waits=[tq32, L("ksum_w"), L("pT_use")])
        last["qT32_use"] = tg
        t_gs = P.op("vector", lambda e: e.tensor_copy(out=gs[:].rearrange("p h j -> p (h j)"), in_=pg), waits=[tg, L("gs_use")])
        last["pT_use"] = t_gs
        t_ms = t_gs
        if ob < 16:
            t_ms = P.op("vector", lambda e: e.memset(gs[:, :, ob:16], -1e30), waits=[t_gs])
        tm8 = []
        for h in range(H):
            tm8.append(P.op("vector", lambda e, h=h: e.max(out=t8[:, h, :], in_=gs[:, h, :]), waits=[t_ms, L("t8_use")]))
        t_sel = P.op("vector", lambda e: e.tensor_tensor(out=mv[:], in0=gs[:], in1=t8[:, :, 2:3].broadcast_to([128, H, 16]), op=ALU.is_ge),
                     waits=tm8 + [L("mv_use")])
        last["t8_use"] = t_sel
        last["gs_use"] = t_sel
        t_mv = P.op("vector", lambda e: e.tensor_scalar(out=mv[:], in0=mv[:], scalar1=-1.0, scalar2=BIG, op0=ALU.add, op1=ALU.mult),
                    waits=[t_sel])
        t_own = P.op("vector", lambda e: e.memset(mv[:, :, ob:ob + 1], 0.0), waits=[t_mv])
        pm = pT[:, 256:512].rearrange("p (a q) -> p a q", a=2)
        tt = None
        for a in range(2):
            tt = P.op("tensor", lambda e, a=a: e.transpose(out=pm[:, a, :], in_=mv[:, a * 8:(a + 1) * 8, :].rearrange("p h j -> p (h j)"),
                                                           identity=ident_f[:]), waits=[t_own, t_idf, L("pm_use")])
        last["mv_use"] = tt
        tm = P.op("vector", lambda e: e.tensor_copy(out=mT4[b4][:, :, j4 * 128:(j4 + 1) * 128], in_=pm), waits=[tt, L("m4_use%d" % b4)])
        last["pm_use"] = tm
        last["pT_use"] = [last["pT_use"], tm]
        st["tm"] = tm

    xload(0)
    xload(1)
    prepA(0)
    prepB(0)
    for idx in range(len(tiles)):
        if idx + 2 < len(tiles):
            xload(idx + 2)
        if idx + 1 < len(tiles):
            prepA(idx + 1)
        main_tile(idx)
        if idx + 1 < len(tiles):
            prepB(idx + 1)
        stage1r(idx)
        if idx >= 1:
            stage2(idx - 1)
    stage2(len(tiles) - 1)
    P.op("gpsimd", lambda e: e.memset(ssq[:], 0.0), waits=t_outdma + [L("ssq_use")])
    P.flush()


def phase_attn(nc, sems, tag, T, layer):
    P = Prog(nc, sems, tag)
    moba = layer == 1
    ngrp = 1 if moba else 3
    xsrc = T["xin"][layer]
    xdst = T["xmid"][layer]
    wo = T["b_w_o"] if moba else T["a_w_o"]
    QTs, KTs, Vs = T["QTs"][layer], T["KTs"][layer], T["Vs"][layer]
    KR = HD + 16 if moba else HD

    attnT = P.sb("attnT", [128, 8, S], BF16)
    acc = [P.sb("acc%d" % i, [128, S], F32) for i in range(2)]
    den = P.sb("den", [128, S], F32)
    qt = [P.sb("qt%d" % i, [KR, S], BF16) for i in range(2)]
    kt = [P.sb("kt%d" % i, [KR, S], BF16) for i in range(2)]
    vt = [P.sb("vt%d" % i, [128, NT, 128], BF16) for i in range(2)]
    pbuf = [P.sb("pb%d" % i, [128, 1024], BF16) for i in range(3)]
    masks = P.sb("masks", [128, 3, 1024], BF16)
    wosb = P.sb("wosb", [128, 8, D], BF16)
    fin = P.sb("fin", [128, 1], F32)
    psS = [P.ps("psS%d" % i, [128, 1024], F32) for i in range(3)]
    psO = [P.ps("psO%d" % i, [128, 512], F32) for i in range(2)]

    t_mk = P.op("sync", lambda e: e.dma_start(out=masks[:], in_=T["bmask"]), sem="d_c", dma=True)
    t_oh = []
    if moba:
        for i in range(2):
            t_oh.append(P.op("sync", lambda e, i=i: e.dma_start(out=kt[i][HD:HD + 16, :], in_=T["onehot"]), sem="d_c", dma=True))
    if t_oh:
        t_mk = t_oh[-1]
        t_oh = [t_oh[-1]]
    t_wo = []
    for kc in range(8):
        t_wo.append(P.op("gpsimd", lambda e, kc=kc: e.dma_start(out=wosb[:, kc, :], in_=wo[0, kc * 128:(kc + 1) * 128, :]),
                         sem="d_wo", dma=True))
    last = {}

    def L(k):
        return last.get(k)

    cnt = {"s": 0, "o": 0, "p": 0, "ld": 0}
    t_attn = []

    def load_head(g, h):
        i = cnt["ld"] % 2
        cnt["ld"] += 1
        c, e2 = divmod(h, 2)
        toks = []
        if moba:
            qsrc = QTs[0, h, :, :]
            toks.append(P.op("sync", lambda e: e.dma_start(out=qt[i][:], in_=qsrc), waits=[L("ld_use%d" % i)], sem="d_ld%d" % i, dma=True))
        else:
            qsrc = QTs[g, h, 0:HD, :]
            toks.append(P.op("sync", lambda e: e.dma_start(out=qt[i][0:HD, :], in_=qsrc), waits=[L("ld_use%d" % i)], sem="d_ld%d" % i, dma=True))
        ksrc = KTs[g, h, 0:HD, :]
        toks.append(P.op("sync", lambda e: e.dma_start(out=kt[i][0:HD, :], in_=ksrc), waits=[L("ld_use%d" % i)], sem="d_ld%d" % i, dma=True))
        vsrc = Vs[g].rearrange("(n p) h c -> p n h c", p=128)[:, :, h, :]
        toks.append(P.op("gpsimd", lambda e: e.dma_start(out=vt[i][:], in_=vsrc), waits=[L("ld_use%d" % i)], sem="d_lv%d" % i, dma=True))
        return i, toks

    def acc_view(a, g, b):
        if moba or g == 0:
            return a[:, b * 512:(b + 1) * 512]
        dil = DILS[g]
        nbseg = 32 // dil
        if g == 1:
            n0 = 4 * b
            r, a0 = divmod(n0, nbseg)
            a0 *= 128
            return a[:].rearrange("p (a r) -> p r a", r=dil)[:, r, a0:a0 + 512]
        return a[:].rearrange("p (a r) -> p r a", r=dil)[:, 2 * b:2 * b + 2, :]

    items = []

    def band_item(g, h, b, ld, first, lastb):
        it = {}
        e2 = h % 2
        a = acc[e2]
        nbseg = 32 // DILS[g]

        def p1():
            i, tl = ld["get"]()
            si = cnt["s"] % 3
            cnt["s"] += 1
            ps = psS[si]
            pb = pbuf[si]
            ts = None
            for j in range(4):
                n = 4 * b + j
                npv = max(n - 1, 0)
                ts = P.op("tensor", lambda e, j=j, npv=npv, n=n: e.matmul(ps[:, (2 * j) * 128:(2 * j + 1) * 128], lhsT=kt[i][0:HD, npv * 128:(npv + 1) * 128],
                                                                           rhs=qt[i][0:HD, n * 128:(n + 1) * 128], start=True, stop=True),
                          waits=[tl, L("psS_use%d" % si)])
                ts = P.op("tensor", lambda e, j=j, n=n: e.matmul(ps[:, (2 * j + 1) * 128:(2 * j + 2) * 128], lhsT=kt[i][0:HD, n * 128:(n + 1) * 128],
                                                                  rhs=qt[i][0:HD, n * 128:(n + 1) * 128], start=True, stop=True),
                          waits=[tl, L("psS_use%d" % si)])
            te = P.op("scalar", lambda e: e.activation(out=pb[:], in_=ps[:], func=AF.Exp, scale=0.125),
                      waits=[ts, L("pb_use%d" % si)])
            last["psS_use%d" % si] = te
            if g == 2:
                mvv = 2
            elif (4 * b) % nbseg == 0:
                mvv = 1
            else:
                mvv = 0
            tm = P.op("vector", lambda e: e.tensor_tensor(out=pb[:], in0=pb[:], in1=masks[:, mvv, :], op=ALU.mult),
                      waits=[te, t_mk])
            it.update(i=i, tl=tl, si=si, pb=pb, tm=tm)

        def p2():
            i, tl, si, pb, tm = it["i"], it["tl"], it["si"], it["pb"], it["tm"]
            oi = cnt["o"] % 2
            cnt["o"] += 1
            po = psO[oi]
            to = None
            for j in range(4):
                n = 4 * b + j
                npv = max(n - 1, 0)
                to = P.op("tensor", lambda e, j=j, npv=npv: e.matmul(po[:, j * 128:(j + 1) * 128], lhsT=vt[i][:, npv, :], rhs=pb[:, (2 * j) * 128:(2 * j + 1) * 128],
                                                                      start=True, stop=False), waits=[tm, tl, L("psO_use%d" % oi)])
                to = P.op("tensor", lambda e, j=j, n=n: e.matmul(po[:, j * 128:(j + 1) * 128], lhsT=vt[i][:, n, :], rhs=pb[:, (2 * j + 1) * 128:(2 * j + 2) * 128],
                                                                  start=False, stop=True), waits=[tm, tl, L("psO_use%d" % oi)])
            last["pb_use%d" % si] = to
            av = acc_view(a, g, b)
            pov = po[:].rearrange("p (r a) -> p r a", r=2) if g == 2 else po[:]
            if first:
                ta = P.op("vector", lambda e: e.tensor_copy(out=av, in_=pov), waits=[to, L("acc_use%d" % e2)])
            else:
                ta = P.op("vector", lambda e: e.tensor_tensor(out=av, in0=av, in1=pov, op=ALU.add), waits=[to, L("acc_w%d" % e2)])
            last["psO_use%d" % oi] = ta
            last["acc_w%d" % e2] = ta
            if lastb:
                last["ld_use%d" % i] = to
        it["p1"] = p1
        it["p2"] = p2
        return it

    def moba_item(h, sc, kp, ld, po_box):
        it = {}
        e2 = h % 2
        a = acc[e2]
        nk = 4 * sc + 4
        kts = (2 * kp, 2 * kp + 1)

        def p1():
            i, tl = ld["get"]()
            si = cnt["s"] % 3
            cnt["s"] += 1
            ps = psS[si]
            pb = pbuf[si]
            ts = None
            for u, ktile in enumerate(kts):
                c0 = max(0, ktile - 4 * sc) * 128
                ts = P.op("tensor", lambda e, u=u, ktile=ktile, c0=c0: e.matmul(
                    ps[:, u * 512 + c0:(u + 1) * 512], lhsT=kt[i][:, ktile * 128:(ktile + 1) * 128],
                    rhs=qt[i][:, sc * 512 + c0:(sc + 1) * 512], start=True, stop=True),
                    waits=[tl, L("psS_use%d" % si)] + t_oh)
            c00 = max(0, kts[0] - 4 * sc) * 128
            te = P.op("scalar", lambda e: e.activation(out=pb[:, c00:1024], in_=ps[:, c00:1024], func=AF.Exp, scale=0.125),
                      waits=[ts, L("pb_use%d" % si)])
            last["psS_use%d" % si] = te
            tms = [te]
            for u, ktile in enumerate(kts):
                if ktile >= 4 * sc:
                    c0 = (ktile - 4 * sc) * 128
                    tms.append(P.op("vector", lambda e, u=u, c0=c0: e.tensor_tensor(out=pb[:, u * 512 + c0:u * 512 + c0 + 128],
                                                                                      in0=pb[:, u * 512 + c0:u * 512 + c0 + 128],
                                                                                      in1=masks[:, 0, 128:256], op=ALU.mult), waits=[te, t_mk]))
            it.update(i=i, tl=tl, si=si, pb=pb, tm=tms)

        def p2():
            i, tl, si, pb, tm = it["i"], it["tl"], it["si"], it["pb"], it["tm"]
            if kp == 0:
                po_box["oi"] = cnt["o"] % 2
                cnt["o"] += 1
            oi = po_box["oi"]
            po = psO[oi]
            to = None
            for u, ktile in enumerate(kts):
                c0 = max(0, ktile - 4 * sc) * 128
                to = P.op("tensor", lambda e, u=u, ktile=ktile, c0=c0: e.matmul(po[:, c0:512], lhsT=vt[i][:, ktile, :], rhs=pb[:, u * 512 + c0:(u + 1) * 512],
                                                                                start=(ktile == 0), stop=(ktile == nk - 1)),
                          waits=[tm, tl, L("psO_use%d" % oi)])
            last["pb_use%d" % si] = to
            if kts[1] == nk - 1:
                ta = P.op("vector", lambda e: e.tensor_copy(out=a[:, sc * 512:(sc + 1) * 512], in_=po[:]), waits=[to, L("acc_use%d" % e2)])
                last["psO_use%d" % oi] = ta
                last["acc_w%d" % e2] = ta
                if sc == 7:
                    last["ld_use%d" % i] = to
        it["p1"] = p1
        it["p2"] = p2
        return it

    def finalize_item(pr):
        def fin_():
            t0 = P.op("sync", lambda e: e.dma_start(out=den[0:HD, :], in_=acc[0][HD:128, :]), waits=[L("acc_w0"), L("den_use")], sem="d_den", dma=True)
            t1 = P.op("sync", lambda e: e.dma_start(out=den[HD:128, :], in_=acc[1][0:HD, :]), waits=[L("acc_w1"), L("den_use")], sem="d_den", dma=True)
            tln = P.op("scalar", lambda e: e.activation(out=den[:], in_=den[:], func=AF.Ln), waits=[t0, t1])
            tex = P.op("scalar", lambda e: e.activation(out=den[:], in_=den[:], func=AF.Exp, scale=-1.0), waits=[tln])
            ta0 = P.op("vector", lambda e: e.tensor_tensor(out=attnT[0:HD, pr, :], in0=acc[0][0:HD, :], in1=den[0:HD, :], op=ALU.mult),
                       waits=[tex, L("acc_w0")])
            ta1 = P.op("vector", lambda e: e.tensor_tensor(out=attnT[HD:128, pr, :], in0=acc[1][HD:128, :], in1=den[HD:128, :], op=ALU.mult),
                       waits=[tex, L("acc_w1")])
            last["den_use"] = [ta0, ta1]
            last["acc_use0"] = [t0, ta0]
            last["acc_use1"] = [t1, ta1]
            t_attn.extend([ta0, ta1])
        return fin_

    npairs = int(os.environ.get('KB_PAIRS', 8))
    for pr in range(npairs):
        for g in range(ngrp):
            for e2 in range(2):
                h = 2 * pr + e2
                ld = {}

                def get(ld=ld, g=g, h=h):
                    if "v" not in ld:
                        ld["v"] = load_head(g, h)
                    return ld["v"]
                ld["get"] = get
                if moba:
                    for sc in range(8):
                        box = {}
                        for kp in range(2 * sc + 2):
                            items.append(moba_item(h, sc, kp, ld, box))
                else:
                    for b in range(8):
                        items.append(band_item(g, h, b, ld, g == 0, b == 7))
        items[-1]["after"] = finalize_item(pr)
    LOOK = 2
    for k in range(min(LOOK, len(items))):
        items[k]["p1"]()
    for k in range(len(items)):
        if k + LOOK < len(items):
            items[k + LOOK]["p1"]()
        items[k]["p2"]()
        if "after" in items[k]:
            items[k]["after"]()

    xt = [P.sb("xo%d" % i, [128, D], F32) for i in range(2)]
    t_fin = []
    for n in range(int(os.environ.get('KB_WO', NT))):
        b = n % 2
        t_x = P.op("sync", lambda e, n=n, b=b: e.dma_start(out=xt[b][:], in_=xsrc[n * 128:(n + 1) * 128, :]),
                   waits=[L("xo_use%d" % b)], sem="d_xo%d" % b, dma=True)
        tadds = []
        for hf in range(2):
            si = cnt["s"] % 3
            cnt["s"] += 1
            ps = psS[si]
            tm = None
            for kc in range(8):
                tm = P.op("tensor", lambda e, kc=kc, n=n, hf=hf, ps=ps: e.matmul(ps[:, 0:512], lhsT=attnT[:, kc, n * 128:(n + 1) * 128],
                                                                           rhs=wosb[:, kc, hf * 512:(hf + 1) * 512], start=(kc == 0), stop=(kc == 7)),
                          waits=t_attn + t_wo + [L("psS_use%d" % si)])
            tadd = P.op("vector", lambda e, hf=hf, b=b, ps=ps: e.tensor_tensor(out=xt[b][:, hf * 512:(hf + 1) * 512], in0=xt[b][:, hf * 512:(hf + 1) * 512],
                                                                                  in1=ps[:, 0:512], op=ALU.add), waits=[tm, t_x])
            last["psS_use%d" % si] = tadd
            tadds.append(tadd)
        t_o = P.op("sync", lambda e, n=n, b=b: e.dma_start(out=xdst[n * 128:(n + 1) * 128, :], in_=xt[b][:]), waits=tadds, sem="d_xw%d" % b, dma=True)
        last["xo_use%d" % b] = t_o
        t_fin.append(t_o)
    P.op("gpsimd", lambda e: e.memset(fin[:], 0.0), waits=t_fin)
    P.flush()


def phase_ffn(nc, sems, tag, T, layer):
    P = Prog(nc, sems, tag)
    xsrc = T["xmid"][layer]
    xdst = T["xout"][layer]
    NB = 512
    nbat = S // NB
    NU = 5
    NT1 = 5
    NG = 3

    wup = P.sb("wup", [128, 8, 2 * DFF], BF16)
    wdn = P.sb("wdn", [128, NFF, D], BF16)
    gnorm = P.sb("gnorm", [128, D], F32)
    cw = P.sb("cw", [128, 3, 2 * NFF], F32)
    cb = P.sb("cb", [128, 2 * NFF], F32)
    ident_b = P.sb("identb", [128, 128], BF16)
    t_idb = mk_identity(P, ident_b)
    xt = [P.sb("xt%d" % i, [128, D], F32) for i in range(1)]
    xr = [P.sb("xr%d" % i, [128, D], F32) for i in range(1)]
    hn = P.sb("hn", [128, D], BF16)
    junk = hn
    hnT = [P.sb("hnT%d" % i, [128, 8, NB], BF16) for i in range(2)]
    hT = P.sb("hT", [128, NFF, NB], BF16)
    T1 = [P.sb("T1_%d" % i, [128, NB], F32) for i in range(NT1)]
    G = [P.sb("G%d" % i, [128, NB], F32) for i in range(NG)]
    hcur = P.sb("hcur", [128, 2 * NFF, 2], F32)
    hw = P.sb("hw", [128, 2 * NFF, 2], F32)
    htmp = [P.sb("htmp%d" % i, [128, 2 * NFF], F32) for i in range(2)]
    ssq = P.sb("ssq", [128, 1], F32)
    sd = P.sb("sd", [128, 1], F32)
    rstd = P.sb("rstd", [128, 1], F32)
    fin = P.sb("fin", [128, 1], F32)
    pT = P.ps("pT", [128, 512], F32)
    pTb = pT[:].bitcast(BF16)
    pu = [P.ps("pu%d" % i, [128, NB], F32) for i in range(NU)]
    pd = [P.ps("pd%d" % i, [128, 512], F32) for i in range(2)]

    t_c = []
    t_c.append(P.op("sync", lambda e: e.dma_start(out=gnorm[:], in_=T["ffn_norm"][layer:layer + 1, :].partition_broadcast(128)), sem="d_c", dma=True))
    for j in range(3):
        t_c.append(P.op("sync", lambda e, j=j: e.dma_start(out=cw[:, j, :], in_=T["ffn_conv_w"][layer, j, :].rearrange("(c p) -> p c", p=128),
                                                             allow_slow_non_contiguous=True),
                        sem="d_c", dma=True))
    t_c.append(P.op("sync", lambda e: e.dma_start(out=cb[:], in_=T["ffn_conv_b"][layer, :].rearrange("(c p) -> p c", p=128),
                                                   allow_slow_non_contiguous=True), sem="d_c", dma=True))
    t_hw0 = P.op("gpsimd", lambda e: e.memset(hw[:], 0.0))
    t_wu = []
    for c in range(2 * NFF // 4):
        for kc in range(8):
            t_wu.append(P.op("gpsimd", lambda e, c=c, kc=kc: e.dma_start(out=wup[:, kc, c * 512:(c + 1) * 512],
                                                                          in_=T["ffn_w_up"][layer, kc * 128:(kc + 1) * 128, c * 512:(c + 1) * 512]),
                             sem="d_wu%d" % c, dma=True))
    t_wd = []
    for j in range(NFF):
        t_wd.append(P.op("gpsimd", lambda e, j=j: e.dma_start(out=wdn[:, j, :], in_=T["ffn_w_down"][layer, j * 128:(j + 1) * 128, :]),
                         sem="d_wd", dma=True))
    last = {}

    def L(k):
        return last.get(k)

    cnt = {"u": 0, "t": 0, "d": 0, "x": 0, "r": 0, "g": 0}
    t_fin = []
    bstate = {}

    def proA(bt, j):
        b = 0
        r0 = bt * NB + j * 128
        t_x = P.op("sync", lambda e: e.dma_start(out=xt[b][:], in_=xsrc[r0:r0 + 128, :]),
                   waits=[L("xt_use%d" % b)], sem="d_x%d" % b, dma=True)
        t_sq = P.op("scalar", lambda e: e.activation(out=junk[:], in_=xt[b][:], func=AF.Square, accum_out=ssq[:]),
                    waits=[t_x, L("ssq_use"), L("hn_use")])
        t_sd = P.op("scalar", lambda e: e.activation(out=sd[:], in_=ssq[:], func=AF.Sqrt, scale=1.0 / D, bias=EPS), waits=[t_sq, L("sd_use")])
        last["ssq_use"] = t_sd
        t_rs = P.op("vector", lambda e: e.reciprocal(out=rstd[:], in_=sd[:]), waits=[t_sd, L("rstd_use")])
        last["sd_use"] = t_rs
        t_hn = P.op("vector", lambda e: e.scalar_tensor_tensor(out=hn[:], in0=xt[b][:], scalar=rstd[:, 0:1], in1=gnorm[:],
                                                               op0=ALU.mult, op1=ALU.mult), waits=[t_rs, L("hn_use")] + t_c)
        last["rstd_use"] = t_hn
        last["xt_use%d" % b] = t_hn
        bstate[(bt, j)] = t_hn

    def proB(bt, j):
        hb = hnT[bt % 2]
        t_hn = bstate[(bt, j)]
        tt = None
        for kc in range(8):
            tt = P.op("tensor", lambda e, kc=kc: e.transpose(out=pTb[:, kc * 128:(kc + 1) * 128], in_=hn[:, kc * 128:(kc + 1) * 128],
                                                             identity=ident_b[:]), waits=[t_hn, t_idb, L("pT_use")])
        last["hn_use"] = tt
        te = P.op("vector", lambda e: e.tensor_copy(out=hb[:, :, j * 128:(j + 1) * 128], in_=pTb.rearrange("p (a q) -> p a q", a=8)),
                  waits=[tt, L("hnT_use%d" % (bt % 2))])
        last["pT_use"] = te
        bstate.setdefault(bt, []).append(te)

    def prologue(bt):
        for j in range(4):
            proA(bt, j)
            proB(bt, j)

    def chunk(bt, ch, hb, thT):
        ui = cnt["u"] % NU
        cnt["u"] += 1
        ti = cnt["t"] % NT1
        cnt["t"] += 1
        p_ = pu[ui]
        t1 = T1[ti]
        wtok = t_wu[(ch // 4) * 8:(ch // 4) * 8 + 8]
        tm = None
        for kc in range(8):
            tm = P.op("tensor", lambda e, kc=kc: e.matmul(p_[:], lhsT=wup[:, kc, ch * 128:(ch + 1) * 128], rhs=hb[:, kc, :],
                                                          start=(kc == 0), stop=(kc == 7)),
                      waits=thT + wtok + [L("pu_use%d" % ui)])
        tB = P.op("scalar", lambda e: e.activation(out=t1[:], in_=p_[:], func=AF.Identity, scale=cw[:, 2, ch:ch + 1], bias=cb[:, ch:ch + 1]),
                  waits=[tm, L("T1_use%d" % ti)] + t_c)
        tS = P.op("vector", lambda e: e.tensor_copy(out=hcur[:, ch, :], in_=p_[:, NB - 2:NB]), waits=[tB, L("hcur_use")])
        tC = P.op("vector", lambda e: e.scalar_tensor_tensor(out=t1[:, 1:NB], in0=p_[:, 0:NB - 1], scalar=cw[:, 1, ch:ch + 1], in1=t1[:, 1:NB],
                                                             op0=ALU.mult, op1=ALU.add), waits=[tB])
        tD = P.op("vector", lambda e: e.scalar_tensor_tensor(out=t1[:, 2:NB], in0=p_[:, 0:NB - 2], scalar=cw[:, 0, ch:ch + 1], in1=t1[:, 2:NB],
                                                             op0=ALU.mult, op1=ALU.add), waits=[tC])
        last["pu_use%d" % ui] = tD
        tH = P.op("gpsimd", lambda e: e.tensor_tensor(out=t1[:, 0:2], in0=t1[:, 0:2], in1=hw[:, ch, :], op=ALU.add), waits=[tC, t_hw0, L("hw_w")])
        return tm, tS, tD, tH, ti, t1

    def batch(bt):
        hb = hnT[bt % 2]
        thT = bstate[bt]
        t_hT = []
        tSs = []
        tHs = []
        tm = None
        pend = []

        def finish(j, tDg, tHg, tig, t1g, tDv, tHv, tiv, t1v):
            gi = cnt["g"] % NG
            cnt["g"] += 1
            gg = G[gi]
            tSil = P.op("scalar", lambda e: e.activation(out=gg[:], in_=t1g[:], func=AF.Silu), waits=[tDg, tHg, L("G_use%d" % gi)])
            last["T1_use%d" % tig] = tSil
            tF = P.op("gpsimd", lambda e: e.tensor_tensor(out=hT[:, j, :], in0=t1v[:], in1=gg[:], op=ALU.mult),
                      waits=[tDv, tHv, tSil, L("hT_use")])
            last["G_use%d" % gi] = tF
            last["T1_use%d" % tiv] = tF
            t_hT.append(tF)

        for j in range(NFF):
            tm, tS1, tDg, tHg, tig, t1g = chunk(bt, j, hb, thT)
            tm, tS2, tDv, tHv, tiv, t1v = chunk(bt, NFF + j, hb, thT)
            if pend:
                finish(*pend.pop())
            pend.append((j, tDg, tHg, tig, t1g, tDv, tHv, tiv, t1v))
            tSs += [tS1, tS2]
            tHs += [tHg, tHv]
            if bt + 1 < nbat:
                if j in (2, 7, 12, 17):
                    proA(bt + 1, (j - 2) // 5)
                if j in (4, 9, 14, 19):
                    proB(bt + 1, (j - 4) // 5)
        finish(*pend.pop())
        last["hnT_use%d" % (bt % 2)] = tm
        a1 = P.op("gpsimd", lambda e: e.tensor_tensor(out=htmp[0][:], in0=hcur[:, :, 1], in1=cw[:, 1, :], op=ALU.mult), waits=tSs + tHs + t_c)
        a2 = P.op("gpsimd", lambda e: e.tensor_tensor(out=htmp[1][:], in0=hcur[:, :, 0], in1=cw[:, 0, :], op=ALU.mult), waits=tSs + tHs)
        a3 = P.op("gpsimd", lambda e: e.tensor_tensor(out=hw[:, :, 0], in0=htmp[0][:], in1=htmp[1][:], op=ALU.add), waits=[a1, a2])
        a4 = P.op("gpsimd", lambda e: e.tensor_tensor(out=hw[:, :, 1], in0=hcur[:, :, 1], in1=cw[:, 0, :], op=ALU.mult), waits=[a1, a2, a3])
        last["hw_w"] = [a3, a4]
        last["hcur_use"] = [a1, a2, a4]
        tdn = None
        for j4 in range(4):
            rb = 0
            r0 = bt * NB + j4 * 128
            t_xr = P.op("sync", lambda e, r0=r0, rb=rb: e.dma_start(out=xr[rb][:], in_=xsrc[r0:r0 + 128, :]),
                        waits=[L("xr_use%d" % rb)], sem="d_r%d" % rb, dma=True)
            tadds = []
            for hf in range(2):
                di = cnt["d"] % 2
                cnt["d"] += 1
                for j in range(NFF):
                    tdn = P.op("tensor", lambda e, j=j, j4=j4, hf=hf, di=di: e.matmul(pd[di][:], lhsT=hT[:, j, j4 * 128:(j4 + 1) * 128],
                                                                                      rhs=wdn[:, j, hf * 512:(hf + 1) * 512], start=(j == 0), stop=(j == NFF - 1)),
                               waits=t_hT + t_wd + [L("pd_use%d" % di)])
                tadd = P.op("vector", lambda e, hf=hf, di=di, rb=rb: e.tensor_tensor(out=xr[rb][:, hf * 512:(hf + 1) * 512],
                                                                                      in0=xr[rb][:, hf * 512:(hf + 1) * 512], in1=pd[di][:], op=ALU.add),
                            waits=[tdn, t_xr])
                last["pd_use%d" % di] = tadd
                tadds.append(tadd)
            t_o = P.op("sync", lambda e, r0=r0, rb=rb: e.dma_start(out=xdst[r0:r0 + 128, :], in_=xr[rb][:]),
                       waits=tadds, sem="d_xw%d" % rb, dma=True)
            last["xr_use%d" % rb] = t_o
            t_fin.append(t_o)
        last["hT_use"] = tdn

    prologue(0)
    for bt in range(nbat):
        batch(bt)
    P.op("gpsimd", lambda e: e.memset(fin[:], 0.0), waits=t_fin)
    P.flush()


def host_consts():
    pos = np.arange(S, dtype=np.float32)
    inv = (np.float32(500000.0) ** (-np.arange(0, 16, 2, dtype=np.float32) / np.float32(16))).astype(np.float32)
    ang = (pos[:, None] * inv[None, :]).astype(np.float32)
    cos = np.cos(ang).astype(np.float32)
    sin = np.sin(ang).astype(np.float32)
    ropec = np.zeros((3, 128, NT, 8), np.float32)
    ropes = np.zeros((3, 128, NT, 8), np.float32)
    for v, dil in enumerate(DILS):
        Lg = S // dil
        pp = np.arange(S)
        r, a = np.divmod(pp, Lg)
        t = a * dil + r
        ropec[v] = cos[t].reshape(NT, 128, 8).transpose(1, 0, 2)
        ropes[v] = sin[t].reshape(NT, 128, 8).transpose(1, 0, 2)
    k = np.arange(128)[:, None]
    q = np.arange(128)[None, :]
    prev = (k >= q).astype(np.float32)
    cur = (k <= q).astype(np.float32)
    zero = np.zeros_like(prev)
    bm = np.zeros((128, 3, 8, 128), np.float32)
    for j in range(4):
        bm[:, 0, 2 * j] = prev
        bm[:, 0, 2 * j + 1] = cur
        bm[:, 1, 2 * j] = zero if j == 0 else prev
        bm[:, 1, 2 * j + 1] = cur
        bm[:, 2, 2 * j] = zero if j % 2 == 0 else prev
        bm[:, 2, 2 * j + 1] = cur
    bm = bm.reshape(128, 3, 1024).astype(ml_dtypes.bfloat16)
    oh = (np.arange(S)[None, :] // 256 == np.arange(16)[:, None]).astype(np.float32).astype(ml_dtypes.bfloat16)
    return {"ropec": ropec, "ropes": ropes, "bmask": bm, "onehot": oh}


WEIGHT_SHAPES = {
    "attn_norm": [2, D], "a_w_qkv": [1, D, 9216], "a_q_norm": [1, 3, HD], "a_k_norm": [1, 3, HD], "a_w_o": [1, D, D],
    "b_w_qkv": [1, D, 3072], "b_q_norm": [1, HD], "b_k_norm": [1, HD], "b_w_o": [1, D, D],
    "ffn_norm": [2, D], "ffn_w_up": [2, D, 2 * DFF], "ffn_conv_w": [2, 3, 2 * DFF], "ffn_conv_b": [2, 2 * DFF],
    "ffn_w_down": [2, DFF, D],
}


def build_nc(phases=("A0", "B0", "C0", "A1", "B1", "C1"), debug_out=None):
    nc = bass.Bass("TRN2", target_bir_lowering=False)
    T = {}
    x = nc.dram_tensor("x", [S, D], F32, kind="ExternalInput").ap()
    for k, shp in WEIGHT_SHAPES.items():
        T[k] = nc.dram_tensor(k, shp, F32, kind="ExternalInput").ap()
    T["ropec"] = nc.dram_tensor("ropec", [3, 128, NT, 8], F32, kind="ExternalInput").ap()
    T["ropes"] = nc.dram_tensor("ropes", [3, 128, NT, 8], F32, kind="ExternalInput").ap()
    T["bmask"] = nc.dram_tensor("bmask", [128, 3, 1024], BF16, kind="ExternalInput").ap()
    T["onehot"] = nc.dram_tensor("onehot", [16, S], BF16, kind="ExternalInput").ap()
    y = nc.dram_tensor("y", [S, D], F32, kind="ExternalOutput").ap()

    def scratch(name, shape, dt):
        kind = "ExternalOutput" if (debug_out and name in debug_out) else "Internal"
        return nc.dram_tensor(name, shape, dt, kind=kind).ap()
    R1 = scratch("R1", [S, D], F32)
    R2 = scratch("R2", [S, D], F32)
    R3 = scratch("R3", [S, D], F32)
    T["xin"] = [x, R2]
    T["xmid"] = [R1, R3]
    T["xout"] = [R2, y]
    T["QTs"] = [scratch("QT0", [3, H, HD, S], BF16), scratch("QT1", [1, H, HD + 16, S], BF16)]
    T["KTs"] = [scratch("KT0", [3, H, HD, S], BF16), scratch("KT1", [1, H, HD, S], BF16)]
    T["Vs"] = [scratch("V0", [3, S, H, 128], BF16), scratch("V1", [1, S, H, 128], BF16)]
    sems = Sems(nc)
    fns = {"A": phase_qkv, "B": phase_attn, "C": phase_ffn}
    for ph in phases:
        fns[ph[0]](nc, sems, ph + "_", T, int(ph[1]))
    return nc


_CACHE = {}


def kernel(**inputs):
    if "nc" not in _CACHE:
        _CACHE["nc"] = build_nc()
        _CACHE["consts"] = host_consts()
    nc = _CACHE["nc"]
    consts = _CACHE["consts"]
    x = np.ascontiguousarray(np.asarray(inputs["x"], dtype=np.float32))
    shared = {k: np.ascontiguousarray(np.asarray(inputs[k], dtype=np.float32)) for k in WEIGHT_SHAPES}
    shared.update(consts)
    in_maps = []
    for b in range(8):
        m = dict(shared)
        m["x"] = x[b]
        in_maps.append(m)
    res = run_bass_kernel_spmd(nc, in_maps, core_ids=list(range(8)))
    return np.stack([np.asarray(r["y"], dtype=np.float32) for r in res.results], axis=0)
```

```python
import contextlib
import os
import numpy as np
import ml_dtypes
import concourse.bass as bass
import concourse.mybir as mybir
from concourse.bass_utils import run_bass_kernel_spmd

F32 = mybir.dt.float32
BF16 = mybir.dt.bfloat16
AF = mybir.ActivationFunctionType
ALU = mybir.AluOpType
AX = mybir.AxisListType

S = 4096
D = 1024
NT = S // 128
H = 16
HD = 64
DFF = 2816
NFF = DFF // 128
EPS = 1e-6
BIG = 30000.0
DILS = (1, 4, 16)
ENGS = ("tensor", "vector", "scalar", "gpsimd", "sync")
GATE_MODE = int(os.environ.get('GATE_MODE', 0))
GAIN_ENG = os.environ.get('GAIN_ENG', 'gpsimd')


class Tok:
    __slots__ = ("sem", "val")

    def __init__(self, sem, val):
        self.sem = sem
        self.val = val


class Sems:
    def __init__(self, nc):
        self.nc = nc
        self.h = {}
        self.cnt = {}

    def get(self, name):
        if name not in self.h:
            self.h[name] = self.nc.alloc_semaphore(name=name)
            self.cnt[name] = 0
        return name

    def release(self):
        if self.h:
            self.nc.clear_and_free_semaphores(list(self.h.values()))
            self.nc.all_engine_barrier()
        self.h = {}
        self.cnt = {}


class Prog:
    def __init__(self, nc, sems, tag):
        self.nc = nc
        self.S = Sems(nc)
        self.tag = tag
        self.q = {e: [] for e in ENGS}
        self.stack = contextlib.ExitStack()
        self.seen = {e: {} for e in ENGS}
        self.used = {}

    def sb(self, name, shape, dt):
        return self.stack.enter_context(self.nc.sbuf_tensor(self.tag + name, shape, dt))

    def ps(self, name, shape, dt):
        return self.stack.enter_context(self.nc.psum_tensor(self.tag + name, shape, dt))

    def op(self, eng, fn, waits=(), sem=None, dma=False):
        if sem is None:
            sem = "s_" + eng
        sem = self.S.get(self.tag + sem)
        need = {}

        def add(t):
            if t is None:
                return
            if isinstance(t, (list, tuple)):
                for u in t:
                    add(u)
                return
            if need.get(t.sem, -1) < t.val:
                need[t.sem] = t.val
        add(list(waits))
        seen = self.seen[eng]
        wl = []
        for s, v in need.items():
            if seen.get(s, -1) >= v:
                continue
            seen[s] = v
            wl.append((s, v))
            self.used.setdefault(s, set()).add(v)
        self.S.cnt[sem] += 1
        idx = self.S.cnt[sem]
        if dma:
            self.used.setdefault(sem, set()).add(idx)
        self.q[eng].append((wl, fn, sem, idx, dma))
        return Tok(sem, idx)

    def flush(self):
        nc = self.nc
        sems = self.S.h
        qs = self.q
        val = {}
        for s, n in self.S.cnt.items():
            used = self.used.get(s, set())
            run = 0
            m = {}
            for i in range(1, n + 1):
                if i in used:
                    run += 1
                m[i] = run
            val[s] = m
        used_all = self.used
        with nc.Block() as block:
            def mk(name):
                def body(e):
                    for wl, fn, sem, idx, dma in qs[name]:
                        for s, v in wl:
                            e.wait_ge(sems[s], val[s][v] * (16 if s in dma_sems else 1))
                        ins = fn(e)
                        if idx in used_all.get(sem, ()):
                            ins.then_inc(sems[sem], 16 if dma else 1)
                return body
            dma_sems = set()
            for name in ENGS:
                for wl, fn, sem, idx, dma in qs[name]:
                    if dma:
                        dma_sems.add(sem)
            for name in ENGS:
                if qs[name]:
                    getattr(block, name)(mk(name))
        self.stack.close()
        self.S.release()


def mk_identity(P, ident, dt_one=1.0):
    t0 = P.op("gpsimd", lambda e: e.memset(ident[:], 0.0))
    return P.op("gpsimd", lambda e: e.affine_select(out=ident[:], in_=ident[:], pattern=[[-1, 128]],
                                                    compare_op=ALU.not_equal, fill=1.0, base=0,
                                                    channel_multiplier=1), waits=[t0])


def phase_qkv(nc, sems, tag, T, layer):
    P = Prog(nc, sems, tag)
    moba = layer == 1
    ngrp = 1 if moba else 3
    xsrc = T["xin"][layer]
    wq = T["b_w_qkv"] if moba else T["a_w_qkv"]

    ident_b = P.sb("identb", [128, 128], BF16)
    ident_f = P.sb("identf", [128, 128], F32)
    t_idb = mk_identity(P, ident_b)
    t_idf = mk_identity(P, ident_f)
    gnorm = P.sb("gnorm", [128, D], F32)
    t_gn = P.op("sync", lambda e: e.dma_start(out=gnorm[:], in_=T["attn_norm"][layer:layer + 1, :].partition_broadcast(128)),
                sem="d_c", dma=True)
    gqk = P.sb("gqk", [128, ngrp, 2, HD], F32)
    t_gq = []
    for g in range(ngrp):
        for s_, nm in ((0, "q"), (1, "k")):
            src = (T["b_%s_norm" % nm][0:1, :] if moba else T["a_%s_norm" % nm][0, g:g + 1, :])
            t_gq.append(P.op("sync", lambda e, g=g, s_=s_, src=src: e.dma_start(
                out=gqk[:, g, s_, :], in_=src.partition_broadcast(128)), sem="d_c", dma=True))
    t_gn = t_gq[-1]
    t_gq = [t_gq[-1]]
    ropecs = [P.sb("ropec%d" % i, [128, NT, 8], F32) for i in range(2)]
    ropess = [P.sb("ropes%d" % i, [128, NT, 8], F32) for i in range(2)]
    junkx = P.sb("junkx", [128, D], BF16)
    wsb = [P.sb("w%d" % i, [128, 8, 3, 1024], BF16) for i in range(2 if not moba else 1)]
    xt = [P.sb("xt%d" % i, [128, D], F32) for i in range(3)]
    junks = [P.sb("junk%d" % i, [128, 1024], F32) for i in range(2)]
    ssq = P.sb("ssq", [128, 1], F32)
    sd = P.sb("sd", [128, 1], F32)
    rstd = P.sb("rstd", [128, 1], F32)
    hn = P.sb("hn", [128, D], BF16)
    hnT = [P.sb("hnT%d" % i, [128, 8, 128], BF16) for i in range(2)]
    ss2 = P.sb("ss2", [128, 32], F32)
    sd2 = P.sb("sd2", [128, 32], F32)
    rs2 = P.sb("rs2", [128, 32], F32)
    qk = [P.sb("qk%d" % i, [128, 2, H, HD], F32) for i in range(2)]
    rt = [P.sb("rt%d" % i, [128, 2 * H, 8], F32) for i in range(4)]
    vaug = [P.sb("vaug%d" % i, [128, H, 128], BF16) for i in range(2)]
    qT4 = [P.sb("qT4_%d" % i, [128, 8, 512], BF16) for i in range(2)]
    kT4 = [P.sb("kT4_%d" % i, [128, 8, 512], BF16) for i in range(2)]
    pT = P.ps("pT", [128, 512], F32)
    pTb = pT[:].bitcast(BF16)
    pqs = [P.ps("pq%d" % i, [128, 1024], F32) for i in range(2)]
    pv = P.ps("pv", [128, 512], F32)
    ptr = P.ps("ptr", [128, 8, 128], F32)
    if moba:
        qT32 = P.sb("qT32", [128, 8, 128], F32)
        ksum = P.sb("ksum", [128, 8, 2, 16], F32)
        kpart = P.sb("kpart", [128, 8], F32)
        gs = P.sb("gs", [128, H, 16], F32)
        t8 = P.sb("t8", [128, H, 8], F32)
        mv = P.sb("mv", [128, H, 16], F32)
        mT4 = [P.sb("mT4_%d" % i, [128, 2, 512], BF16) for i in range(2)]

    t_ones = []
    for i in range(2):
        t_ones.append(P.op("gpsimd", lambda e, i=i: e.memset(vaug[i][:], 1.0)))
    t_ks0 = P.op("gpsimd", lambda e: e.memset(ksum[:], 0.0)) if moba else None

    last = {}

    def L(k):
        return last.get(k)

    tw_use = [None, None]
    t_outdma = []
    tiles = [(g, n) for g in range(ngrp) for n in range(NT)]
    tiles = tiles[:int(os.environ.get('KA_TILES', len(tiles)))]
    state = {}

    def load_w(g):
        wb = wsb[g % len(wsb)]
        toks = []
        for kc in range(8):
            for s_ in range(3):
                if moba:
                    c0 = s_ * 1024
                else:
                    c0 = (s_ * 3 + g) * 1024
                toks.append(P.op("gpsimd", lambda e, wb=wb, kc=kc, s_=s_, c0=c0: e.dma_start(
                    out=wb[:, kc, s_, :], in_=wq[0, kc * 128:(kc + 1) * 128, c0:c0 + 1024]),
                    waits=[tw_use[g % len(wsb)]], sem="d_w%d" % (g % len(wsb)), dma=True))
        return toks

    def load_rope(g):
        v = 0 if moba else g
        rb = g % 2
        a = P.op("sync", lambda e: e.dma_start(out=ropecs[rb][:], in_=T["ropec"][v]), waits=[L("rope_use%d" % rb)], sem="d_r%d" % rb, dma=True)
        b = P.op("sync", lambda e: e.dma_start(out=ropess[rb][:], in_=T["ropes"][v]), waits=[L("rope_use%d" % rb)], sem="d_r%d" % rb, dma=True)
        return [a, b]

    def x_rows(g, n):
        dil = 1 if moba else DILS[g]
        Lg = S // dil
        p0 = n * 128
        r, a0 = divmod(p0, Lg)
        v = xsrc.rearrange("(a r) d -> r a d", r=dil)
        return v[r, a0:a0 + 128, :]

    def xload(idx):
        g, n = tiles[idx]
        b3 = idx % 3
        state["xl%d" % idx] = P.op("sync", lambda e: e.dma_start(out=xt[b3][:], in_=x_rows(g, n)),
                                   waits=[L("xt_use%d" % b3)], sem="d_x%d" % b3, dma=True)

    def prepA(idx):
        g, n = tiles[idx]
        st = state.setdefault(idx, {})
        if n == 0:
            if g == 0:
                state["w0"] = load_w(0)
            state["rope%d" % g] = load_rope(g)
        if n == 1 and g + 1 < ngrp:
            state["w%d" % (g + 1)] = load_w(g + 1)
        b3 = idx % 3
        t_x = state["xl%d" % idx]
        t_sq = P.op("scalar", lambda e: e.activation(out=junkx[:], in_=xt[b3][:], func=AF.Square, accum_out=ssq[:]),
                    waits=[t_x, L("ssq_use")])
        t_sd = P.op("scalar", lambda e: e.activation(out=sd[:], in_=ssq[:], func=AF.Sqrt, scale=1.0 / D, bias=EPS),
                    waits=[t_sq, L("sd_use")])
        last["ssq_use"] = t_sd
        t_rs = P.op("vector", lambda e: e.reciprocal(out=rstd[:], in_=sd[:]), waits=[t_sd, L("rstd_use")])
        last["sd_use"] = t_rs
        t_hn = P.op("vector", lambda e: e.scalar_tensor_tensor(out=hn[:], in0=xt[b3][:], scalar=rstd[:, 0:1], in1=gnorm[:],
                                                               op0=ALU.mult, op1=ALU.mult),
                    waits=[t_rs, t_gn, L("hn_use")])
        last["rstd_use"] = t_hn
        last["xt_use%d" % b3] = t_hn
        st["t_hn"] = t_hn

    def prepB(idx):
        st = state[idx]
        b = idx % 2
        t_hn = st["t_hn"]
        tt = None
        for kc in range(8):
            tt = P.op("tensor", lambda e, kc=kc: e.transpose(out=pTb[:, kc * 128:(kc + 1) * 128], in_=hn[:, kc * 128:(kc + 1) * 128],
                                                             identity=ident_b[:]),
                      waits=[t_hn, t_idb, L("pT_use")])
        last["hn_use"] = tt
        t_hT = P.op("scalar", lambda e: e.activation(out=hnT[b][:].rearrange("p a b -> p (a b)"), in_=pTb, func=AF.Copy),
                    waits=[tt, L("hnT_use%d" % b)])
        last["pT_use"] = t_hT
        st["hT"] = t_hT

    def main_tile(idx):
        g, n = tiles[idx]
        st = state[idx]
        b = idx % 2
        t_hT = st["hT"]
        tw = state["w%d" % g]
        wb = wsb[g % len(wsb)]
        qkb = qk[b]
        va = vaug[b]
        gsel = 0 if moba else g

        def mm(dst, s_, c0, extra):
            t = None
            for kc in range(8):
                t = P.op("tensor", lambda e, kc=kc: e.matmul(dst, lhsT=hnT[b][:, kc, :], rhs=wb[:, kc, s_, c0:c0 + 512],
                                                            start=(kc == 0), stop=(kc == 7)),
                         waits=[t_hT, tw, extra])
            return t
        t_gs_ = []
        tv = None
        tvm = None
        for s_ in range(2):
            pp = pqs[s_]
            jk = junks[s_]
            tq = None
            for hf in range(2):
                tq = mm(pp[:, hf * 512:(hf + 1) * 512], s_, hf * 512, L("pq_use%d" % s_))
            hf = s_
            tvm = mm(pv[:], 2, hf * 512, L("pv_use"))
            t_sq2 = P.op("scalar", lambda e, pp=pp, jk=jk: e.activation(out=jk[:], in_=pp[:], func=AF.Square), waits=[tq, L("junk%d" % s_)])
            pvv = pv[:].rearrange("p (i e d) -> p i e d", e=2, d=HD)
            vv = va[:, hf * 8:(hf + 1) * 8, :].rearrange("p (i e) c -> p i e c", e=2)
            tv0 = P.op("scalar", lambda e, pvv=pvv, vv=vv: e.activation(out=vv[:, :, 0, 0:HD], in_=pvv[:, :, 0, :], func=AF.Copy),
                       waits=[tvm, L("vaug_use%d" % b)] + t_ones)
            tv = P.op("scalar", lambda e, pvv=pvv, vv=vv: e.activation(out=vv[:, :, 1, HD:128], in_=pvv[:, :, 1, :], func=AF.Copy),
                      waits=[tvm, L("vaug_use%d" % b)] + t_ones)
            last["pv_use"] = tv
            t_ss2 = P.op("vector", lambda e, jk=jk, s_=s_: e.tensor_reduce(out=ss2[:, s_ * 16:(s_ + 1) * 16], in_=jk[:].rearrange("p (h d) -> p h d", d=HD),
                                                                        op=ALU.add, axis=AX.X), waits=[t_sq2, L("ss2_use%d" % s_)])
            last["junk%d" % s_] = t_ss2
            t_sd2 = P.op("scalar", lambda e, s_=s_: e.activation(out=sd2[:, s_ * 16:(s_ + 1) * 16], in_=ss2[:, s_ * 16:(s_ + 1) * 16], func=AF.Sqrt,
                                                                  scale=1.0 / HD, bias=EPS), waits=[t_ss2, L("sd2_use%d" % s_)])
            last["ss2_use%d" % s_] = t_sd2
            t_rs2 = P.op("vector", lambda e, s_=s_: e.reciprocal(out=rs2[:, s_ * 16:(s_ + 1) * 16], in_=sd2[:, s_ * 16:(s_ + 1) * 16]),
                         waits=[t_sd2, L("rs2_use%d" % s_)])
            last["sd2_use%d" % s_] = t_rs2
            t_nm = P.op("vector", lambda e, pp=pp, s_=s_: e.tensor_tensor(out=qkb[:, s_, :, :],
                                                                           in0=pp[:].rearrange("p (h d) -> p h d", d=HD),
                                                                           in1=rs2[:, s_ * 16:(s_ + 1) * 16].unsqueeze(2).broadcast_to([128, H, HD]), op=ALU.mult),
                        waits=[t_rs2, L("qk_use%d" % b)])
            last["rs2_use%d" % s_] = t_nm
            last["pq_use%d" % s_] = t_nm
            t_gs_.append(P.op(GAIN_ENG, lambda e, s_=s_: e.tensor_tensor(out=qkb[:, s_, :, :], in0=qkb[:, s_, :, :],
                                                                          in1=gqk[:, gsel, s_, :].unsqueeze(1).broadcast_to([128, H, HD]),
                                                                          op=ALU.mult), waits=[t_nm] + t_gq))
        st["t_g"] = t_gs_
        tw_use[g % len(wsb)] = tv
        t_vo = P.op("sync", lambda e: e.dma_start(out=T["Vs"][layer][g, n * 128:(n + 1) * 128, :, :], in_=va[:]),
                    waits=[tv], sem="d_vo%d" % b, dma=True)
        last["vaug_use%d" % b] = t_vo
        t_outdma.append(t_vo)
        last["hnT_use%d" % b] = tvm

    def stage1r(idx):
        g, n = tiles[idx]
        st = state[idx]
        b = idx % 2
        qkb = qk[b]
        t_g = st["t_g"]
        qv = qkb[:].rearrange("p s h d -> p (s h) d")
        x1 = qv[:, :, 0:8]
        x2 = qv[:, :, 8:16]
        cb = ropecs[g % 2][:, n, :].unsqueeze(1).broadcast_to([128, 32, 8])
        sbb = ropess[g % 2][:, n, :].unsqueeze(1).broadcast_to([128, 32, 8])
        tr = state["rope%d" % g]
        w0 = [t_g, tr, L("rt_use")]
        a1 = P.op("vector", lambda e: e.tensor_tensor(out=rt[0][:], in0=x1, in1=cb, op=ALU.mult), waits=w0)
        a2 = P.op("vector", lambda e: e.tensor_tensor(out=rt[1][:], in0=x2, in1=sbb, op=ALU.mult), waits=w0)
        a3 = P.op("vector", lambda e: e.tensor_tensor(out=rt[2][:], in0=x2, in1=cb, op=ALU.mult), waits=w0)
        a4 = P.op("vector", lambda e: e.tensor_tensor(out=rt[3][:], in0=x1, in1=sbb, op=ALU.mult), waits=w0)
        a5 = P.op("vector", lambda e: e.tensor_tensor(out=x1, in0=rt[0][:], in1=rt[1][:], op=ALU.subtract), waits=[a1, a2, a3, a4])
        a6 = P.op("vector", lambda e: e.tensor_tensor(out=x2, in0=rt[2][:], in1=rt[3][:], op=ALU.add), waits=[a1, a2, a3, a4, a5])
        last["rt_use"] = a6
        last["rope_use%d" % (g % 2)] = a6
        st["qkg"] = [a5, a6]

    def stage2(idx):
        g, n = tiles[idx]
        st = state[idx]
        b = idx % 2
        qkb = qk[b]
        n4, j4 = divmod(n, 4)
        b4 = (idx // 4) % 2
        for s_, dst4, nm in ((0, qT4[b4], "Q"), (1, kT4[b4], "K")):
            tt = None
            for c in range(8):
                tt = P.op("tensor", lambda e, c=c, s_=s_: e.transpose(out=ptr[:, c, :], in_=qkb[:, s_, 2 * c:2 * c + 2, :].rearrange("p h d -> p (h d)"),
                                                                      identity=ident_f[:]),
                          waits=[st["qkg"], t_idf, L("ptr_use")])
            wv = [tt, L("%s4_use%d" % (nm, b4))]
            if moba and s_ == 0:
                t32 = P.op("scalar", lambda e: e.activation(out=qT32[:], in_=ptr[:], func=AF.Copy), waits=[tt, L("qT32_use")])
                te = P.op("gpsimd", lambda e, dst4=dst4: e.tensor_copy(out=dst4[:, :, j4 * 128:(j4 + 1) * 128], in_=qT32[:]),
                          waits=[t32, L("%s4_use%d" % (nm, b4))])
                last["qT32_use"] = te
                tl = [t32]
            else:
                te = P.op("scalar", lambda e, dst4=dst4: e.activation(out=dst4[:, :, j4 * 128:(j4 + 1) * 128], in_=ptr[:], func=AF.Copy),
                          waits=wv)
                tl = [te]
            if moba and s_ == 1:
                tkp = P.op("vector", lambda e: e.tensor_reduce(out=kpart[:], in_=ptr[:], op=ALU.add, axis=AX.X),
                           waits=[tt, te, L("kpart_use")])
                j = n // 2
                tks0 = P.op("vector", lambda e, j=j: e.tensor_tensor(out=ksum[0:HD, :, 0, j], in0=ksum[0:HD, :, 0, j], in1=kpart[0:HD, :], op=ALU.add),
                            waits=[tkp, t_ks0, L("ksum_w"), L("qT32_use")])
                tks = P.op("vector", lambda e, j=j: e.tensor_tensor(out=ksum[HD:128, :, 1, j], in0=ksum[HD:128, :, 1, j], in1=kpart[HD:128, :], op=ALU.add),
                           waits=[tkp, tks0, t_ks0, L("ksum_w"), L("qT32_use")])
                last["kpart_use"] = tks
                last["ksum_w"] = tks
                tl.append(tks)
            last["ptr_use"] = tl
            st["te%d" % s_] = te
        last["qk_use%d" % b] = last["ptr_use"]
        st["tq32"] = last["ptr_use"]
        if j4 == 3:
            for s_, src4, nm, dstT in ((0, qT4[b4], "Q", T["QTs"][layer]), (1, kT4[b4], "K", T["KTs"][layer])):
                tds = []
                for e2 in range(2):
                    dv = dstT[g].rearrange("(c e) r t -> e r c t", e=2)[e2, 0:HD, :, n4 * 512:(n4 + 1) * 512]
                    tds.append(P.op("sync", lambda e, dv=dv, src4=src4, e2=e2: e.dma_start(out=dv, in_=src4[e2 * HD:(e2 + 1) * HD, :, :]),
                                    waits=[state[idx - 3]["te%d" % s_], state[idx - 2]["te%d" % s_], state[idx - 1]["te%d" % s_], st["te%d" % s_]],
                                    sem="d_%so%d" % (nm, b4), dma=True))
                last["%s4_use%d" % (nm, b4)] = tds
                t_outdma.extend(tds)

    def gate_a(idx):
        g, n = tiles[idx]
        st = state[idx]
        ob = n // 2
        pg = pT[:, 0:256]
        tg = None
        tq32 = st["tq32"]
        for c in range(8):
            tg = P.op("tensor", lambda e, c=c: e.matmul(pg[:, c * 32:(c + 1) * 32], lhsT=qT32[:, c, :],
                                                        rhs=ksum[:, c, :, :].rearrange("p e j -> p (e j)"), start=True, stop=True),
                      waits=[tq32, L("ksum_w"), L("pT_use")])
        last["qT32_use"] = [tg, L("qT32_use")]
        t_gs = P.op("vector", lambda e: e.tensor_copy(out=gs[:].rearrange("p h j -> p (h j)"), in_=pg), waits=[tg, L("gs_use")])
        last["pT_use"] = t_gs
        t_ms = t_gs
        if ob < 16:
            t_ms = P.op("vector", lambda e: e.memset(gs[:, :, ob:16], -1e30), waits=[t_gs])
        tm8 = []
        for h in range(H):
            tm8.append(P.op("vector", lambda e, h=h: e.max(out=t8[:, h, :], in_=gs[:, h, :]), waits=[t_ms, L("t8_use")]))
        t_sel = P.op("vector", lambda e: e.tensor_tensor(out=mv[:], in0=gs[:], in1=t8[:, :, 2:3].broadcast_to([128, H, 16]), op=ALU.is_ge),
                     waits=tm8 + [L("mv_use")])
        last["t8_use"] = t_sel
        last["gs_use"] = t_sel
        t_mv = P.op("vector", lambda e: e.tensor_scalar(out=mv[:], in0=mv[:], scalar1=-1.0, scalar2=BIG, op0=ALU.add, op1=ALU.mult),
                    waits=[t_sel])
        t_own = P.op("vector", lambda e: e.memset(mv[:, :, ob:ob + 1], 0.0), waits=[t_mv])
        st["t_own"] = t_own

    def gate_b(idx):
        g, n = tiles[idx]
        st = state[idx]
        b4 = (idx // 4) % 2
        n4, j4 = divmod(n, 4)
        t_own = st["t_own"]
        pm = pT[:, 256:512].rearrange("p (a q) -> p a q", a=2)
        tt = None
        for a in range(2):
            tt = P.op("tensor", lambda e, a=a: e.transpose(out=pm[:, a, :], in_=mv[:, a * 8:(a + 1) * 8, :].rearrange("p h j -> p (h j)"),
                                                           identity=ident_f[:]), waits=[t_own, t_idf, L("pm_use"), L("pT_use")])
        last["mv_use"] = tt
        tm = P.op("vector", lambda e: e.tensor_copy(out=mT4[b4][:, :, j4 * 128:(j4 + 1) * 128], in_=pm), waits=[tt, L("m4_use%d" % b4)])
        last["pm_use"] = tm
        last["pT_use"] = [last["pT_use"], tm]
        st["tm"] = tm
        if j4 == 3:
            tds = []
            m4 = mT4[b4]
            for h in range(H):
                dv = T["QTs"][layer][0, h, HD:HD + 16, n4 * 512:(n4 + 1) * 512]
                tds.append(P.op("gpsimd", lambda e, dv=dv, h=h, m4=m4: e.dma_start(out=dv, in_=m4[(h % 8) * 16:(h % 8) * 16 + 16, h // 8, :]),
                                waits=[state[idx - 3]["tm"], state[idx - 2]["tm"], state[idx - 1]["tm"], st["tm"]],
                                sem="d_mo%d" % b4, dma=True))
            last["m4_use%d" % b4] = tds
            t_outdma.extend(tds)

    xload(0)
    xload(1)
    prepA(0)
    prepB(0)
    for idx in range(len(tiles)):
        if idx + 2 < len(tiles):
            xload(idx + 2)
        if idx + 1 < len(tiles):
            prepA(idx + 1)
        main_tile(idx)
        if moba and GATE_MODE == 0:
            if idx >= 3:
                gate_b(idx - 3)
            if idx >= 2:
                gate_a(idx - 2)
        if idx + 1 < len(tiles):
            prepB(idx + 1)
        stage1r(idx)
        if idx >= 1:
            stage2(idx - 1)
            if moba and GATE_MODE == 1:
                gate_a(idx - 1)
                gate_b(idx - 1)
    nt_ = len(tiles)
    if moba and GATE_MODE == 0:
        gate_b(nt_ - 3)
        gate_a(nt_ - 2)
    stage2(nt_ - 1)
    if moba and GATE_MODE == 1:
        gate_a(nt_ - 1)
        gate_b(nt_ - 1)
    if moba and GATE_MODE == 0:
        gate_b(nt_ - 2)
        gate_a(nt_ - 1)
        gate_b(nt_ - 1)
    P.op("gpsimd", lambda e: e.memset(ssq[:], 0.0), waits=t_outdma + [L("ssq_use")])
    P.flush()


def phase_attn(nc, sems, tag, T, layer):
    P = Prog(nc, sems, tag)
    moba = layer == 1
    ngrp = 1 if moba else 3
    xsrc = T["xin"][layer]
    xdst = T["xmid"][layer]
    wo = T["b_w_o"] if moba else T["a_w_o"]
    QTs, KTs, Vs = T["QTs"][layer], T["KTs"][layer], T["Vs"][layer]
    KR = HD + 16 if moba else HD

    attnT = P.sb("attnT", [128, 8, S], BF16)
    acc = [P.sb("acc%d" % i, [128, S], F32) for i in range(2)]
    den = P.sb("den", [128, S], F32)
    qt = [P.sb("qt%d" % i, [KR, S], BF16) for i in range(2)]
    kt = [P.sb("kt%d" % i, [KR, S], BF16) for i in range(2)]
    vt = [P.sb("vt%d" % i, [128, NT, 128], BF16) for i in range(2)]
    pbuf = [P.sb("pb%d" % i, [128, 1024], BF16) for i in range(3)]
    masks = P.sb("masks", [128, 3, 1024], BF16)
    wosb = P.sb("wosb", [128, 8, D], BF16)
    fin = P.sb("fin", [128, 1], F32)
    psS = [P.ps("psS%d" % i, [128, 1024], F32) for i in range(3)]
    psO = [P.ps("psO%d" % i, [128, 512], F32) for i in range(2)]

    t_mk = P.op("sync", lambda e: e.dma_start(out=masks[:], in_=T["bmask"]), sem="d_c", dma=True)
    t_oh = []
    if moba:
        for i in range(2):
            t_oh.append(P.op("sync", lambda e, i=i: e.dma_start(out=kt[i][HD:HD + 16, :], in_=T["onehot"]), sem="d_c", dma=True))
    if t_oh:
        t_mk = t_oh[-1]
        t_oh = [t_oh[-1]]
    t_wo = []
    for kc in range(8):
        t_wo.append(P.op("gpsimd", lambda e, kc=kc: e.dma_start(out=wosb[:, kc, :], in_=wo[0, kc * 128:(kc + 1) * 128, :]),
                         sem="d_wo", dma=True))
    last = {}

    def L(k):
        return last.get(k)

    cnt = {"s": 0, "o": 0, "p": 0, "ld": 0}
    t_attn = []

    def load_head(g, h):
        i = cnt["ld"] % 2
        cnt["ld"] += 1
        c, e2 = divmod(h, 2)
        toks = []
        if moba:
            qsrc = QTs[0, h, :, :]
            toks.append(P.op("sync", lambda e: e.dma_start(out=qt[i][:], in_=qsrc), waits=[L("ld_use%d" % i)], sem="d_ld%d" % i, dma=True))
        else:
            qsrc = QTs[g, h, 0:HD, :]
            toks.append(P.op("sync", lambda e: e.dma_start(out=qt[i][0:HD, :], in_=qsrc), waits=[L("ld_use%d" % i)], sem="d_ld%d" % i, dma=True))
        ksrc = KTs[g, h, 0:HD, :]
        toks.append(P.op("sync", lambda e: e.dma_start(out=kt[i][0:HD, :], in_=ksrc), waits=[L("ld_use%d" % i)], sem="d_ld%d" % i, dma=True))
        vsrc = Vs[g].rearrange("(n p) h c -> p n h c", p=128)[:, :, h, :]
        toks.append(P.op("gpsimd", lambda e: e.dma_start(out=vt[i][:], in_=vsrc), waits=[L("ld_use%d" % i)], sem="d_lv%d" % i, dma=True))
        return i, toks

    def acc_view(a, g, b):
        if moba or g == 0:
            return a[:, b * 512:(b + 1) * 512]
        dil = DILS[g]
        nbseg = 32 // dil
        if g == 1:
            n0 = 4 * b
            r, a0 = divmod(n0, nbseg)
            a0 *= 128
            return a[:].rearrange("p (a r) -> p r a", r=dil)[:, r, a0:a0 + 512]
        return a[:].rearrange("p (a r) -> p r a", r=dil)[:, 2 * b:2 * b + 2, :]

    items = []

    def band_item(g, h, b, ld, first, lastb):
        it = {}
        e2 = h % 2
        a = acc[e2]
        nbseg = 32 // DILS[g]

        def p1():
            i, tl = ld["get"]()
            si = cnt["s"] % 3
            cnt["s"] += 1
            ps = psS[si]
            pb = pbuf[si]
            ts = None
            for j in range(4):
                n = 4 * b + j
                npv = max(n - 1, 0)
                ts = P.op("tensor", lambda e, j=j, npv=npv, n=n: e.matmul(ps[:, (2 * j) * 128:(2 * j + 1) * 128], lhsT=kt[i][0:HD, npv * 128:(npv + 1) * 128],
                                                                           rhs=qt[i][0:HD, n * 128:(n + 1) * 128], start=True, stop=True),
                          waits=[tl, L("psS_use%d" % si)])
                ts = P.op("tensor", lambda e, j=j, n=n: e.matmul(ps[:, (2 * j + 1) * 128:(2 * j + 2) * 128], lhsT=kt[i][0:HD, n * 128:(n + 1) * 128],
                                                                  rhs=qt[i][0:HD, n * 128:(n + 1) * 128], start=True, stop=True),
                          waits=[tl, L("psS_use%d" % si)])
            te = P.op("scalar", lambda e: e.activation(out=pb[:], in_=ps[:], func=AF.Exp, scale=0.125),
                      waits=[ts, L("pb_use%d" % si)])
            last["psS_use%d" % si] = te
            if g == 2:
                mvv = 2
            elif (4 * b) % nbseg == 0:
                mvv = 1
            else:
                mvv = 0
            tm = P.op("vector", lambda e: e.tensor_tensor(out=pb[:], in0=pb[:], in1=masks[:, mvv, :], op=ALU.mult),
                      waits=[te, t_mk])
            it.update(i=i, tl=tl, si=si, pb=pb, tm=tm)

        def p2():
            i, tl, si, pb, tm = it["i"], it["tl"], it["si"], it["pb"], it["tm"]
            oi = cnt["o"] % 2
            cnt["o"] += 1
            po = psO[oi]
            to = None
            for j in range(4):
                n = 4 * b + j
                npv = max(n - 1, 0)
                to = P.op("tensor", lambda e, j=j, npv=npv: e.matmul(po[:, j * 128:(j + 1) * 128], lhsT=vt[i][:, npv, :], rhs=pb[:, (2 * j) * 128:(2 * j + 1) * 128],
                                                                      start=True, stop=False), waits=[tm, tl, L("psO_use%d" % oi)])
                to = P.op("tensor", lambda e, j=j, n=n: e.matmul(po[:, j * 128:(j + 1) * 128], lhsT=vt[i][:, n, :], rhs=pb[:, (2 * j + 1) * 128:(2 * j + 2) * 128],
                                                                  start=False, stop=True), waits=[tm, tl, L("psO_use%d" % oi)])
            last["pb_use%d" % si] = to
            av = acc_view(a, g, b)
            pov = po[:].rearrange("p (r a) -> p r a", r=2) if g == 2 else po[:]
            if first:
                ta = P.op("vector", lambda e: e.tensor_copy(out=av, in_=pov), waits=[to, L("acc_use%d" % e2)])
            else:
                ta = P.op("vector", lambda e: e.tensor_tensor(out=av, in0=av, in1=pov, op=ALU.add), waits=[to, L("acc_w%d" % e2)])
            last["psO_use%d" % oi] = ta
            last["acc_w%d" % e2] = ta
            if lastb:
                last["ld_use%d" % i] = to
        it["p1"] = p1
        it["p2"] = p2
        return it

    def moba_item(h, sc, kp, ld, po_box):
        it = {}
        e2 = h % 2
        a = acc[e2]
        nk = 4 * sc + 4
        kts = (2 * kp, 2 * kp + 1)

        def p1():
            i, tl = ld["get"]()
            si = cnt["s"] % 3
            cnt["s"] += 1
            ps = psS[si]
            pb = pbuf[si]
            ts = None
            for u, ktile in enumerate(kts):
                c0 = max(0, ktile - 4 * sc) * 128
                ts = P.op("tensor", lambda e, u=u, ktile=ktile, c0=c0: e.matmul(
                    ps[:, u * 512 + c0:(u + 1) * 512], lhsT=kt[i][:, ktile * 128:(ktile + 1) * 128],
                    rhs=qt[i][:, sc * 512 + c0:(sc + 1) * 512], start=True, stop=True),
                    waits=[tl, L("psS_use%d" % si)] + t_oh)
            c00 = max(0, kts[0] - 4 * sc) * 128
            te = P.op("scalar", lambda e: e.activation(out=pb[:, c00:1024], in_=ps[:, c00:1024], func=AF.Exp, scale=0.125),
                      waits=[ts, L("pb_use%d" % si)])
            last["psS_use%d" % si] = te
            tms = [te]
            for u, ktile in enumerate(kts):
                if ktile >= 4 * sc:
                    c0 = (ktile - 4 * sc) * 128
                    tms.append(P.op("vector", lambda e, u=u, c0=c0: e.tensor_tensor(out=pb[:, u * 512 + c0:u * 512 + c0 + 128],
                                                                                      in0=pb[:, u * 512 + c0:u * 512 + c0 + 128],
                                                                                      in1=masks[:, 0, 128:256], op=ALU.mult), waits=[te, t_mk]))
            it.update(i=i, tl=tl, si=si, pb=pb, tm=tms)

        def p2():
            i, tl, si, pb, tm = it["i"], it["tl"], it["si"], it["pb"], it["tm"]
            if kp == 0:
                po_box["oi"] = cnt["o"] % 2
                cnt["o"] += 1
            oi = po_box["oi"]
            po = psO[oi]
            to = None
            for u, ktile in enumerate(kts):
                c0 = max(0, ktile - 4 * sc) * 128
                to = P.op("tensor", lambda e, u=u, ktile=ktile, c0=c0: e.matmul(po[:, c0:512], lhsT=vt[i][:, ktile, :], rhs=pb[:, u * 512 + c0:(u + 1) * 512],
                                                                                start=(ktile == 0), stop=(ktile == nk - 1)),
                          waits=[tm, tl, L("psO_use%d" % oi)])
            last["pb_use%d" % si] = to
            if kts[1] == nk - 1:
                ta = P.op("vector", lambda e: e.tensor_copy(out=a[:, sc * 512:(sc + 1) * 512], in_=po[:]), waits=[to, L("acc_use%d" % e2)])
                last["psO_use%d" % oi] = ta
                last["acc_w%d" % e2] = ta
                if sc == 7:
                    last["ld_use%d" % i] = to
        it["p1"] = p1
        it["p2"] = p2
        return it

    def finalize_item(pr):
        def fin_():
            t0 = P.op("sync", lambda e: e.dma_start(out=den[0:HD, :], in_=acc[0][HD:128, :]), waits=[L("acc_w0"), L("den_use")], sem="d_den", dma=True)
            t1 = P.op("sync", lambda e: e.dma_start(out=den[HD:128, :], in_=acc[1][0:HD, :]), waits=[L("acc_w1"), L("den_use")], sem="d_den", dma=True)
            tln = P.op("scalar", lambda e: e.activation(out=den[:], in_=den[:], func=AF.Ln), waits=[t0, t1])
            tex = P.op("scalar", lambda e: e.activation(out=den[:], in_=den[:], func=AF.Exp, scale=-1.0), waits=[tln])
            ta0 = P.op("vector", lambda e: e.tensor_tensor(out=attnT[0:HD, pr, :], in0=acc[0][0:HD, :], in1=den[0:HD, :], op=ALU.mult),
                       waits=[tex, L("acc_w0")])
            ta1 = P.op("vector", lambda e: e.tensor_tensor(out=attnT[HD:128, pr, :], in0=acc[1][HD:128, :], in1=den[HD:128, :], op=ALU.mult),
                       waits=[tex, L("acc_w1")])
            last["den_use"] = [ta0, ta1]
            last["acc_use0"] = [t0, ta0]
            last["acc_use1"] = [t1, ta1]
            t_attn.extend([ta0, ta1])
        return fin_

    npairs = int(os.environ.get('KB_PAIRS', 8))
    for pr in range(npairs):
        for g in range(ngrp):
            for e2 in range(2):
                h = 2 * pr + e2
                ld = {}

                def get(ld=ld, g=g, h=h):
                    if "v" not in ld:
                        ld["v"] = load_head(g, h)
                    return ld["v"]
                ld["get"] = get
                if moba:
                    for sc in range(8):
                        box = {}
                        for kp in range(2 * sc + 2):
                            items.append(moba_item(h, sc, kp, ld, box))
                else:
                    for b in range(8):
                        items.append(band_item(g, h, b, ld, g == 0, b == 7))
        items[-1]["after"] = finalize_item(pr)
    LOOK = 2
    for k in range(min(LOOK, len(items))):
        items[k]["p1"]()
    for k in range(len(items)):
        if k + LOOK < len(items):
            items[k + LOOK]["p1"]()
        items[k]["p2"]()
        if "after" in items[k]:
            items[k]["after"]()

    xt = [P.sb("xo%d" % i, [128, D], F32) for i in range(2)]
    t_fin = []
    for n in range(int(os.environ.get('KB_WO', NT))):
        b = n % 2
        t_x = P.op("sync", lambda e, n=n, b=b: e.dma_start(out=xt[b][:], in_=xsrc[n * 128:(n + 1) * 128, :]),
                   waits=[L("xo_use%d" % b)], sem="d_xo%d" % b, dma=True)
        tadds = []
        for hf in range(2):
            si = cnt["s"] % 3
            cnt["s"] += 1
            ps = psS[si]
            tm = None
            for kc in range(8):
                tm = P.op("tensor", lambda e, kc=kc, n=n, hf=hf, ps=ps: e.matmul(ps[:, 0:512], lhsT=attnT[:, kc, n * 128:(n + 1) * 128],
                                                                           rhs=wosb[:, kc, hf * 512:(hf + 1) * 512], start=(kc == 0), stop=(kc == 7)),
                          waits=t_attn + t_wo + [L("psS_use%d" % si)])
            tadd = P.op("vector", lambda e, hf=hf, b=b, ps=ps: e.tensor_tensor(out=xt[b][:, hf * 512:(hf + 1) * 512], in0=xt[b][:, hf * 512:(hf + 1) * 512],
                                                                                  in1=ps[:, 0:512], op=ALU.add), waits=[tm, t_x])
            last["psS_use%d" % si] = tadd
            tadds.append(tadd)
        t_o = P.op("sync", lambda e, n=n, b=b: e.dma_start(out=xdst[n * 128:(n + 1) * 128, :], in_=xt[b][:]), waits=tadds, sem="d_xw%d" % b, dma=True)
        last["xo_use%d" % b] = t_o
        t_fin.append(t_o)
    P.op("gpsimd", lambda e: e.memset(fin[:], 0.0), waits=t_fin)
    P.flush()


def phase_ffn(nc, sems, tag, T, layer):
    P = Prog(nc, sems, tag)
    xsrc = T["xmid"][layer]
    xdst = T["xout"][layer]
    NB = 512
    nbat = S // NB
    NU = 5
    NT1 = 5
    NG = 3

    wup = P.sb("wup", [128, 8, 2 * DFF], BF16)
    wdn = P.sb("wdn", [128, NFF, D], BF16)
    gnorm = P.sb("gnorm", [128, D], F32)
    cw = P.sb("cw", [128, 3, 2 * NFF], F32)
    cb = P.sb("cb", [128, 2 * NFF], F32)
    ident_b = P.sb("identb", [128, 128], BF16)
    t_idb = mk_identity(P, ident_b)
    xt = [P.sb("xt%d" % i, [128, D], F32) for i in range(1)]
    xr = [P.sb("xr%d" % i, [128, D], F32) for i in range(1)]
    hn = P.sb("hn", [128, D], BF16)
    junk = hn
    hnT = [P.sb("hnT%d" % i, [128, 8, NB], BF16) for i in range(2)]
    hT = P.sb("hT", [128, NFF, NB], BF16)
    T1 = [P.sb("T1_%d" % i, [128, NB], F32) for i in range(NT1)]
    G = [P.sb("G%d" % i, [128, NB], F32) for i in range(NG)]
    hcur = P.sb("hcur", [128, 2 * NFF, 2], F32)
    hw = P.sb("hw", [128, 2 * NFF, 2], F32)
    htmp = [P.sb("htmp%d" % i, [128, 2 * NFF], F32) for i in range(2)]
    ssq = P.sb("ssq", [128, 1], F32)
    sd = P.sb("sd", [128, 1], F32)
    rstd = P.sb("rstd", [128, 1], F32)
    fin = P.sb("fin", [128, 1], F32)
    pT = P.ps("pT", [128, 512], F32)
    pTb = pT[:].bitcast(BF16)
    pu = [P.ps("pu%d" % i, [128, NB], F32) for i in range(NU)]
    pd = [P.ps("pd%d" % i, [128, 512], F32) for i in range(2)]

    t_c = []
    t_c.append(P.op("sync", lambda e: e.dma_start(out=gnorm[:], in_=T["ffn_norm"][layer:layer + 1, :].partition_broadcast(128)), sem="d_c", dma=True))
    for j in range(3):
        t_c.append(P.op("sync", lambda e, j=j: e.dma_start(out=cw[:, j, :], in_=T["ffn_conv_w"][layer, j, :].rearrange("(c p) -> p c", p=128),
                                                             allow_slow_non_contiguous=True),
                        sem="d_c", dma=True))
    t_c.append(P.op("sync", lambda e: e.dma_start(out=cb[:], in_=T["ffn_conv_b"][layer, :].rearrange("(c p) -> p c", p=128),
                                                   allow_slow_non_contiguous=True), sem="d_c", dma=True))
    t_hw0 = P.op("gpsimd", lambda e: e.memset(hw[:], 0.0))
    t_wu = []
    for c in range(2 * NFF // 4):
        for kc in range(8):
            t_wu.append(P.op("gpsimd", lambda e, c=c, kc=kc: e.dma_start(out=wup[:, kc, c * 512:(c + 1) * 512],
                                                                          in_=T["ffn_w_up"][layer, kc * 128:(kc + 1) * 128, c * 512:(c + 1) * 512]),
                             sem="d_wu%d" % c, dma=True))
    t_wd = []
    for j in range(NFF):
        t_wd.append(P.op("gpsimd", lambda e, j=j: e.dma_start(out=wdn[:, j, :], in_=T["ffn_w_down"][layer, j * 128:(j + 1) * 128, :]),
                         sem="d_wd", dma=True))
    last = {}

    def L(k):
        return last.get(k)

    cnt = {"u": 0, "t": 0, "d": 0, "x": 0, "r": 0, "g": 0}
    t_fin = []
    bstate = {}

    def proA(bt, j):
        b = 0
        r0 = bt * NB + j * 128
        t_x = P.op("sync", lambda e: e.dma_start(out=xt[b][:], in_=xsrc[r0:r0 + 128, :]),
                   waits=[L("xt_use%d" % b)], sem="d_x%d" % b, dma=True)
        t_sq = P.op("scalar", lambda e: e.activation(out=junk[:], in_=xt[b][:], func=AF.Square, accum_out=ssq[:]),
                    waits=[t_x, L("ssq_use"), L("hn_use")])
        t_sd = P.op("scalar", lambda e: e.activation(out=sd[:], in_=ssq[:], func=AF.Sqrt, scale=1.0 / D, bias=EPS), waits=[t_sq, L("sd_use")])
        last["ssq_use"] = t_sd
        t_rs = P.op("vector", lambda e: e.reciprocal(out=rstd[:], in_=sd[:]), waits=[t_sd, L("rstd_use")])
        last["sd_use"] = t_rs
        t_hn = P.op("vector", lambda e: e.scalar_tensor_tensor(out=hn[:], in0=xt[b][:], scalar=rstd[:, 0:1], in1=gnorm[:],
                                                               op0=ALU.mult, op1=ALU.mult), waits=[t_rs, L("hn_use")] + t_c)
        last["rstd_use"] = t_hn
        last["xt_use%d" % b] = t_hn
        bstate[(bt, j)] = t_hn

    def proB(bt, j):
        hb = hnT[bt % 2]
        t_hn = bstate[(bt, j)]
        tt = None
        for kc in range(8):
            tt = P.op("tensor", lambda e, kc=kc: e.transpose(out=pTb[:, kc * 128:(kc + 1) * 128], in_=hn[:, kc * 128:(kc + 1) * 128],
                                                             identity=ident_b[:]), waits=[t_hn, t_idb, L("pT_use")])
        last["hn_use"] = tt
        te = P.op("vector", lambda e: e.tensor_copy(out=hb[:, :, j * 128:(j + 1) * 128], in_=pTb.rearrange("p (a q) -> p a q", a=8)),
                  waits=[tt, L("hnT_use%d" % (bt % 2))])
        last["pT_use"] = te
        bstate.setdefault(bt, []).append(te)

    def prologue(bt):
        for j in range(4):
            proA(bt, j)
            proB(bt, j)

    def chunk(bt, ch, hb, thT):
        ui = cnt["u"] % NU
        cnt["u"] += 1
        ti = cnt["t"] % NT1
        cnt["t"] += 1
        p_ = pu[ui]
        t1 = T1[ti]
        wtok = t_wu[(ch // 4) * 8:(ch // 4) * 8 + 8]
        tm = None
        for kc in range(8):
            tm = P.op("tensor", lambda e, kc=kc: e.matmul(p_[:], lhsT=wup[:, kc, ch * 128:(ch + 1) * 128], rhs=hb[:, kc, :],
                                                          start=(kc == 0), stop=(kc == 7)),
                      waits=thT + wtok + [L("pu_use%d" % ui)])
        tB = P.op("scalar", lambda e: e.activation(out=t1[:], in_=p_[:], func=AF.Identity, scale=cw[:, 2, ch:ch + 1], bias=cb[:, ch:ch + 1]),
                  waits=[tm, L("T1_use%d" % ti)] + t_c)
        tS = P.op("vector", lambda e: e.tensor_copy(out=hcur[:, ch, :], in_=p_[:, NB - 2:NB]), waits=[tB, L("hcur_use")])
        tC = P.op("vector", lambda e: e.scalar_tensor_tensor(out=t1[:, 1:NB], in0=p_[:, 0:NB - 1], scalar=cw[:, 1, ch:ch + 1], in1=t1[:, 1:NB],
                                                             op0=ALU.mult, op1=ALU.add), waits=[tB])
        tD = P.op("vector", lambda e: e.scalar_tensor_tensor(out=t1[:, 2:NB], in0=p_[:, 0:NB - 2], scalar=cw[:, 0, ch:ch + 1], in1=t1[:, 2:NB],
                                                             op0=ALU.mult, op1=ALU.add), waits=[tC])
        last["pu_use%d" % ui] = tD
        tH = P.op("gpsimd", lambda e: e.tensor_tensor(out=t1[:, 0:2], in0=t1[:, 0:2], in1=hw[:, ch, :], op=ALU.add), waits=[tC, t_hw0, L("hw_w")])
        return tm, tS, tD, tH, ti, t1

    def batch(bt):
        hb = hnT[bt % 2]
        thT = bstate[bt]
        t_hT = []
        tSs = []
        tHs = []
        tm = None
        pend = []

        def finish(j, tDg, tHg, tig, t1g, tDv, tHv, tiv, t1v):
            gi = cnt["g"] % NG
            cnt["g"] += 1
            gg = G[gi]
            tSil = P.op("scalar", lambda e: e.activation(out=gg[:], in_=t1g[:], func=AF.Silu), waits=[tDg, tHg, L("G_use%d" % gi)])
            last["T1_use%d" % tig] = tSil
            tF = P.op("gpsimd", lambda e: e.tensor_tensor(out=hT[:, j, :], in0=t1v[:], in1=gg[:], op=ALU.mult),
                      waits=[tDv, tHv, tSil, L("hT_use")])
            last["G_use%d" % gi] = tF
            last["T1_use%d" % tiv] = tF
            t_hT.append(tF)

        for j in range(NFF):
            tm, tS1, tDg, tHg, tig, t1g = chunk(bt, j, hb, thT)
            tm, tS2, tDv, tHv, tiv, t1v = chunk(bt, NFF + j, hb, thT)
            if pend:
                finish(*pend.pop())
            pend.append((j, tDg, tHg, tig, t1g, tDv, tHv, tiv, t1v))
            tSs += [tS1, tS2]
            tHs += [tHg, tHv]
            if bt + 1 < nbat:
                if j in (2, 7, 12, 17):
                    proA(bt + 1, (j - 2) // 5)
                if j in (4, 9, 14, 19):
                    proB(bt + 1, (j - 4) // 5)
        finish(*pend.pop())
        last["hnT_use%d" % (bt % 2)] = tm
        a1 = P.op("gpsimd", lambda e: e.tensor_tensor(out=htmp[0][:], in0=hcur[:, :, 1], in1=cw[:, 1, :], op=ALU.mult), waits=tSs + tHs + t_c)
        a2 = P.op("gpsimd", lambda e: e.tensor_tensor(out=htmp[1][:], in0=hcur[:, :, 0], in1=cw[:, 0, :], op=ALU.mult), waits=tSs + tHs)
        a3 = P.op("gpsimd", lambda e: e.tensor_tensor(out=hw[:, :, 0], in0=htmp[0][:], in1=htmp[1][:], op=ALU.add), waits=[a1, a2])
        a4 = P.op("gpsimd", lambda e: e.tensor_tensor(out=hw[:, :, 1], in0=hcur[:, :, 1], in1=cw[:, 0, :], op=ALU.mult), waits=[a1, a2, a3])
        last["hw_w"] = [a3, a4]
        last["hcur_use"] = [a1, a2, a4]
        tdn = None
        for j4 in range(4):
            rb = 0
            r0 = bt * NB + j4 * 128
            t_xr = P.op("sync", lambda e, r0=r0, rb=rb: e.dma_start(out=xr[rb][:], in_=xsrc[r0:r0 + 128, :]),
                        waits=[L("xr_use%d" % rb)], sem="d_r%d" % rb, dma=True)
            tadds = []
            for hf in range(2):
                di = cnt["d"] % 2
                cnt["d"] += 1
                for j in range(NFF):
                    tdn = P.op("tensor", lambda e, j=j, j4=j4, hf=hf, di=di: e.matmul(pd[di][:], lhsT=hT[:, j, j4 * 128:(j4 + 1) * 128],
                                                                                      rhs=wdn[:, j, hf * 512:(hf + 1) * 512], start=(j == 0), stop=(j == NFF - 1)),
                               waits=t_hT + t_wd + [L("pd_use%d" % di)])
                tadd = P.op("vector", lambda e, hf=hf, di=di, rb=rb: e.tensor_tensor(out=xr[rb][:, hf * 512:(hf + 1) * 512],
                                                                                      in0=xr[rb][:, hf * 512:(hf + 1) * 512], in1=pd[di][:], op=ALU.add),
                            waits=[tdn, t_xr])
                last["pd_use%d" % di] = tadd
                tadds.append(tadd)
            t_o = P.op("sync", lambda e, r0=r0, rb=rb: e.dma_start(out=xdst[r0:r0 + 128, :], in_=xr[rb][:]),
                       waits=tadds, sem="d_xw%d" % rb, dma=True)
            last["xr_use%d" % rb] = t_o
            t_fin.append(t_o)
        last["hT_use"] = tdn

    prologue(0)
    for bt in range(nbat):
        batch(bt)
    P.op("gpsimd", lambda e: e.memset(fin[:], 0.0), waits=t_fin)
    P.flush()


def host_consts():
    pos = np.arange(S, dtype=np.float32)
    inv = (np.float32(500000.0) ** (-np.arange(0, 16, 2, dtype=np.float32) / np.float32(16))).astype(np.float32)
    ang = (pos[:, None] * inv[None, :]).astype(np.float32)
    cos = np.cos(ang).astype(np.float32)
    sin = np.sin(ang).astype(np.float32)
    ropec = np.zeros((3, 128, NT, 8), np.float32)
    ropes = np.zeros((3, 128, NT, 8), np.float32)
    for v, dil in enumerate(DILS):
        Lg = S // dil
        pp = np.arange(S)
        r, a = np.divmod(pp, Lg)
        t = a * dil + r
        ropec[v] = cos[t].reshape(NT, 128, 8).transpose(1, 0, 2)
        ropes[v] = sin[t].reshape(NT, 128, 8).transpose(1, 0, 2)
    k = np.arange(128)[:, None]
    q = np.arange(128)[None, :]
    prev = (k >= q).astype(np.float32)
    cur = (k <= q).astype(np.float32)
    zero = np.zeros_like(prev)
    bm = np.zeros((128, 3, 8, 128), np.float32)
    for j in range(4):
        bm[:, 0, 2 * j] = prev
        bm[:, 0, 2 * j + 1] = cur
        bm[:, 1, 2 * j] = zero if j == 0 else prev
        bm[:, 1, 2 * j + 1] = cur
        bm[:, 2, 2 * j] = zero if j % 2 == 0 else prev
        bm[:, 2, 2 * j + 1] = cur
    bm = bm.reshape(128, 3, 1024).astype(ml_dtypes.bfloat16)
    oh = (np.arange(S)[None, :] // 256 == np.arange(16)[:, None]).astype(np.float32).astype(ml_dtypes.bfloat16)
    return {"ropec": ropec, "ropes": ropes, "bmask": bm, "onehot": oh}


WEIGHT_SHAPES = {
    "attn_norm": [2, D], "a_w_qkv": [1, D, 9216], "a_q_norm": [1, 3, HD], "a_k_norm": [1, 3, HD], "a_w_o": [1, D, D],
    "b_w_qkv": [1, D, 3072], "b_q_norm": [1, HD], "b_k_norm": [1, HD], "b_w_o": [1, D, D],
    "ffn_norm": [2, D], "ffn_w_up": [2, D, 2 * DFF], "ffn_conv_w": [2, 3, 2 * DFF], "ffn_conv_b": [2, 2 * DFF],
    "ffn_w_down": [2, DFF, D],
}


def build_nc(phases=("A0", "B0", "C0", "A1", "B1", "C1"), debug_out=None):
    nc = bass.Bass("TRN2", target_bir_lowering=False)
    T = {}
    x = nc.dram_tensor("x", [S, D], F32, kind="ExternalInput").ap()
    for k, shp in WEIGHT_SHAPES.items():
        T[k] = nc.dram_tensor(k, shp, F32, kind="ExternalInput").ap()
    T["ropec"] = nc.dram_tensor("ropec", [3, 128, NT, 8], F32, kind="ExternalInput").ap()
    T["ropes"] = nc.dram_tensor("ropes", [3, 128, NT, 8], F32, kind="ExternalInput").ap()
    T["bmask"] = nc.dram_tensor("bmask", [128, 3, 1024], BF16, kind="ExternalInput").ap()
    T["onehot"] = nc.dram_tensor("onehot", [16, S], BF16, kind="ExternalInput").ap()
    y = nc.dram_tensor("y", [S, D], F32, kind="ExternalOutput").ap()

    def scratch(name, shape, dt):
        kind = "ExternalOutput" if (debug_out and name in debug_out) else "Internal"
        return nc.dram_tensor(name, shape, dt, kind=kind).ap()
    R1 = scratch("R1", [S, D], F32)
    R2 = scratch("R2", [S, D], F32)
    R3 = scratch("R3", [S, D], F32)
    T["xin"] = [x, R2]
    T["xmid"] = [R1, R3]
    T["xout"] = [R2, y]
    T["QTs"] = [scratch("QT0", [3, H, HD, S], BF16), scratch("QT1", [1, H, HD + 16, S], BF16)]
    T["KTs"] = [scratch("KT0", [3, H, HD, S], BF16), scratch("KT1", [1, H, HD, S], BF16)]
    T["Vs"] = [scratch("V0", [3, S, H, 128], BF16), scratch("V1", [1, S, H, 128], BF16)]
    sems = Sems(nc)
    fns = {"A": phase_qkv, "B": phase_attn, "C": phase_ffn}
    for ph in phases:
        fns[ph[0]](nc, sems, ph + "_", T, int(ph[1]))
    return nc


_CACHE = {}


def kernel(**inputs):
    if "nc" not in _CACHE:
        _CACHE["nc"] = build_nc()
        _CACHE["consts"] = host_consts()
    nc = _CACHE["nc"]
    consts = _CACHE["consts"]
    x = np.ascontiguousarray(np.asarray(inputs["x"], dtype=np.float32))
    shared = {k: np.ascontiguousarray(np.asarray(inputs[k], dtype=np.float32)) for k in WEIGHT_SHAPES}
    shared.update(consts)
    in_maps = []
    for b in range(8):
        m = dict(shared)
        m["x"] = x[b]
        in_maps.append(m)
    res = run_bass_kernel_spmd(nc, in_maps, core_ids=list(range(8)))
    return np.stack([np.asarray(r["y"], dtype=np.float32) for r in res.results], axis=0)
```

```python
import contextlib
import os
import numpy as np
import ml_dtypes
import concourse.bass as bass
import concourse.mybir as mybir
from concourse.bass_utils import run_bass_kernel_spmd

F32 = mybir.dt.float32
BF16 = mybir.dt.bfloat16
AF = mybir.ActivationFunctionType
ALU = mybir.AluOpType
AX = mybir.AxisListType

S = 4096
D = 1024
NT = S // 128
H = 16
HD = 64
DFF = 2816
NFF = DFF // 128
EPS = 1e-6
BIG = 30000.0
DILS = (1, 4, 16)
ENGS = ("tensor", "vector", "scalar", "gpsimd", "sync")
GATE_MODE = int(os.environ.get('GATE_MODE', 0))
GAIN_ENG = os.environ.get('GAIN_ENG', 'gpsimd')


class Tok:
    __slots__ = ("sem", "val")

    def __init__(self, sem, val):
        self.sem = sem
        self.val = val


class Sems:
    def __init__(self, nc):
        self.nc = nc
        self.h = {}
        self.cnt = {}

    def get(self, name):
        if name not in self.h:
            self.h[name] = self.nc.alloc_semaphore(name=name)
            self.cnt[name] = 0
        return name

    def release(self):
        if self.h:
            self.nc.clear_and_free_semaphores(list(self.h.values()))
            self.nc.all_engine_barrier()
        self.h = {}
        self.cnt = {}


class Prog:
    def __init__(self, nc, sems, tag):
        self.nc = nc
        self.S = Sems(nc)
        self.tag = tag
        self.q = {e: [] for e in ENGS}
        self.stack = contextlib.ExitStack()
        self.seen = {e: {} for e in ENGS}
        self.used = {}

    def sb(self, name, shape, dt):
        return self.stack.enter_context(self.nc.sbuf_tensor(self.tag + name, shape, dt))

    def ps(self, name, shape, dt):
        return self.stack.enter_context(self.nc.psum_tensor(self.tag + name, shape, dt))

    def op(self, eng, fn, waits=(), sem=None, dma=False):
        if sem is None:
            sem = "s_" + eng
        sem = self.S.get(self.tag + sem)
        need = {}

        def add(t):
            if t is None:
                return
            if isinstance(t, (list, tuple)):
                for u in t:
                    add(u)
                return
            if need.get(t.sem, -1) < t.val:
                need[t.sem] = t.val
        add(list(waits))
        seen = self.seen[eng]
        wl = []
        for s, v in need.items():
            if seen.get(s, -1) >= v:
                continue
            seen[s] = v
            wl.append((s, v))
            self.used.setdefault(s, set()).add(v)
        self.S.cnt[sem] += 1
        idx = self.S.cnt[sem]
        if dma:
            self.used.setdefault(sem, set()).add(idx)
        self.q[eng].append((wl, fn, sem, idx, dma))
        return Tok(sem, idx)

    def flush(self):
        nc = self.nc
        sems = self.S.h
        qs = self.q
        val = {}
        for s, n in self.S.cnt.items():
            used = self.used.get(s, set())
            run = 0
            m = {}
            for i in range(1, n + 1):
                if i in used:
                    run += 1
                m[i] = run
            val[s] = m
        used_all = self.used
        with nc.Block() as block:
            def mk(name):
                def body(e):
                    for wl, fn, sem, idx, dma in qs[name]:
                        for s, v in wl:
                            e.wait_ge(sems[s], val[s][v] * (16 if s in dma_sems else 1))
                        ins = fn(e)
                        if idx in used_all.get(sem, ()):
                            ins.then_inc(sems[sem], 16 if dma else 1)
                return body
            dma_sems = set()
            for name in ENGS:
                for wl, fn, sem, idx, dma in qs[name]:
                    if dma:
                        dma_sems.add(sem)
            for name in ENGS:
                if qs[name]:
                    getattr(block, name)(mk(name))
        self.stack.close()
        self.S.release()


def mk_identity(P, ident, dt_one=1.0):
    t0 = P.op("gpsimd", lambda e: e.memset(ident[:], 0.0))
    return P.op("gpsimd", lambda e: e.affine_select(out=ident[:], in_=ident[:], pattern=[[-1, 128]],
                                                    compare_op=ALU.not_equal, fill=1.0, base=0,
                                                    channel_multiplier=1), waits=[t0])


def phase_qkv(nc, sems, tag, T, layer):
    P = Prog(nc, sems, tag)
    moba = layer == 1
    ngrp = 1 if moba else 3
    xsrc = T["xin"][layer]
    wq = T["b_w_qkv"] if moba else T["a_w_qkv"]

    ident_b = P.sb("identb", [128, 128], BF16)
    ident_f = P.sb("identf", [128, 128], F32)
    t_idb = mk_identity(P, ident_b)
    t_idf = mk_identity(P, ident_f)
    gnorm = P.sb("gnorm", [128, D], F32)
    t_gn = P.op("sync", lambda e: e.dma_start(out=gnorm[:], in_=T["attn_norm"][layer:layer + 1, :].partition_broadcast(128)),
                sem="d_c", dma=True)
    gqk = P.sb("gqk", [128, ngrp, 2, HD], F32)
    t_gq = []
    for g in range(ngrp):
        for s_, nm in ((0, "q"), (1, "k")):
            src = (T["b_%s_norm" % nm][0:1, :] if moba else T["a_%s_norm" % nm][0, g:g + 1, :])
            t_gq.append(P.op("sync", lambda e, g=g, s_=s_, src=src: e.dma_start(
                out=gqk[:, g, s_, :], in_=src.partition_broadcast(128)), sem="d_c", dma=True))
    t_gn = t_gq[-1]
    t_gq = [t_gq[-1]]
    ropecs = [P.sb("ropec%d" % i, [128, NT, 8], F32) for i in range(2)]
    ropess = [P.sb("ropes%d" % i, [128, NT, 8], F32) for i in range(2)]
    junkx = P.sb("junkx", [128, D], BF16)
    wsb = [P.sb("w%d" % i, [128, 8, 3, 1024], BF16) for i in range(2 if not moba else 1)]
    xt = [P.sb("xt%d" % i, [128, D], F32) for i in range(3)]
    junks = [P.sb("junk%d" % i, [128, 1024], F32) for i in range(2)]
    ssq = P.sb("ssq", [128, 1], F32)
    sd = P.sb("sd", [128, 1], F32)
    rstd = P.sb("rstd", [128, 1], F32)
    hn = P.sb("hn", [128, D], BF16)
    hnT = [P.sb("hnT%d" % i, [128, 8, 128], BF16) for i in range(2)]
    ss2 = P.sb("ss2", [128, 32], F32)
    sd2 = P.sb("sd2", [128, 32], F32)
    rs2 = P.sb("rs2", [128, 32], F32)
    qk = [P.sb("qk%d" % i, [128, 2, H, HD], F32) for i in range(2)]
    rt = [P.sb("rt%d" % i, [128, 2 * H, 8], F32) for i in range(4)]
    vaug = [P.sb("vaug%d" % i, [128, H, 128], BF16) for i in range(2)]
    qT4 = [P.sb("qT4_%d" % i, [128, 8, 512], BF16) for i in range(2)]
    kT4 = [P.sb("kT4_%d" % i, [128, 8, 512], BF16) for i in range(2)]
    pT = P.ps("pT", [128, 512], F32)
    pTb = pT[:].bitcast(BF16)
    pqs = [P.ps("pq%d" % i, [128, 1024], F32) for i in range(2)]
    pv = P.ps("pv", [128, 512], F32)
    ptr = P.ps("ptr", [128, 8, 128], F32)
    if moba:
        qT32 = P.sb("qT32", [128, 8, 128], F32)
        ksum = P.sb("ksum", [128, 8, 2, 16], F32)
        kpart = P.sb("kpart", [128, 8], F32)
        gs = P.sb("gs", [128, H, 16], F32)
        t8 = P.sb("t8", [128, H, 8], F32)
        mv = P.sb("mv", [128, H, 16], F32)
        mT4 = [P.sb("mT4_%d" % i, [128, 2, 512], BF16) for i in range(2)]

    t_ones = []
    for i in range(2):
        t_ones.append(P.op("gpsimd", lambda e, i=i: e.memset(vaug[i][:], 1.0)))
    t_ks0 = P.op("gpsimd", lambda e: e.memset(ksum[:], 0.0)) if moba else None

    last = {}

    def L(k):
        return last.get(k)

    tw_use = [None, None]
    t_outdma = []
    tiles = [(g, n) for g in range(ngrp) for n in range(NT)]
    tiles = tiles[:int(os.environ.get('KA_TILES', len(tiles)))]
    state = {}

    def load_w(g):
        wb = wsb[g % len(wsb)]
        toks = []
        for kc in range(8):
            for s_ in range(3):
                if moba:
                    c0 = s_ * 1024
                else:
                    c0 = (s_ * 3 + g) * 1024
                toks.append(P.op("gpsimd", lambda e, wb=wb, kc=kc, s_=s_, c0=c0: e.dma_start(
                    out=wb[:, kc, s_, :], in_=wq[0, kc * 128:(kc + 1) * 128, c0:c0 + 1024]),
                    waits=[tw_use[g % len(wsb)]], sem="d_w%d" % (g % len(wsb)), dma=True))
        return toks

    def load_rope(g):
        v = 0 if moba else g
        rb = g % 2
        a = P.op("sync", lambda e: e.dma_start(out=ropecs[rb][:], in_=T["ropec"][v]), waits=[L("rope_use%d" % rb)], sem="d_r%d" % rb, dma=True)
        b = P.op("sync", lambda e: e.dma_start(out=ropess[rb][:], in_=T["ropes"][v]), waits=[L("rope_use%d" % rb)], sem="d_r%d" % rb, dma=True)
        return [a, b]

    def x_rows(g, n):
        dil = 1 if moba else DILS[g]
        Lg = S // dil
        p0 = n * 128
        r, a0 = divmod(p0, Lg)
        v = xsrc.rearrange("(a r) d -> r a d", r=dil)
        return v[r, a0:a0 + 128, :]

    def xload(idx):
        g, n = tiles[idx]
        b3 = idx % 3
        state["xl%d" % idx] = P.op("sync", lambda e: e.dma_start(out=xt[b3][:], in_=x_rows(g, n)),
                                   waits=[L("xt_use%d" % b3)], sem="d_x%d" % b3, dma=True)

    def prepA(idx):
        g, n = tiles[idx]
        st = state.setdefault(idx, {})
        if n == 0:
            if g == 0:
                state["w0"] = load_w(0)
            state["rope%d" % g] = load_rope(g)
        if n == 1 and g + 1 < ngrp:
            state["w%d" % (g + 1)] = load_w(g + 1)
        b3 = idx % 3
        t_x = state["xl%d" % idx]
        t_sq = P.op("scalar", lambda e: e.activation(out=junkx[:], in_=xt[b3][:], func=AF.Square, accum_out=ssq[:]),
                    waits=[t_x, L("ssq_use")])
        t_sd = P.op("scalar", lambda e: e.activation(out=sd[:], in_=ssq[:], func=AF.Sqrt, scale=1.0 / D, bias=EPS),
                    waits=[t_sq, L("sd_use")])
        last["ssq_use"] = t_sd
        t_rs = P.op("vector", lambda e: e.reciprocal(out=rstd[:], in_=sd[:]), waits=[t_sd, L("rstd_use")])
        last["sd_use"] = t_rs
        t_hn = P.op("vector", lambda e: e.scalar_tensor_tensor(out=hn[:], in0=xt[b3][:], scalar=rstd[:, 0:1], in1=gnorm[:],
                                                               op0=ALU.mult, op1=ALU.mult),
                    waits=[t_rs, t_gn, L("hn_use")])
        last["rstd_use"] = t_hn
        last["xt_use%d" % b3] = t_hn
        st["t_hn"] = t_hn

    def prepB(idx):
        st = state[idx]
        b = idx % 2
        t_hn = st["t_hn"]
        tt = None
        for kc in range(8):
            tt = P.op("tensor", lambda e, kc=kc: e.transpose(out=pTb[:, kc * 128:(kc + 1) * 128], in_=hn[:, kc * 128:(kc + 1) * 128],
                                                             identity=ident_b[:]),
                      waits=[t_hn, t_idb, L("pT_use")])
        last["hn_use"] = tt
        t_hT = P.op("scalar", lambda e: e.activation(out=hnT[b][:].rearrange("p a b -> p (a b)"), in_=pTb, func=AF.Copy),
                    waits=[tt, L("hnT_use%d" % b)])
        last["pT_use"] = t_hT
        st["hT"] = t_hT

    def main_tile(idx):
        g, n = tiles[idx]
        st = state[idx]
        b = idx % 2
        t_hT = st["hT"]
        tw = state["w%d" % g]
        wb = wsb[g % len(wsb)]
        qkb = qk[b]
        va = vaug[b]
        gsel = 0 if moba else g

        def mm(dst, s_, c0, extra):
            t = None
            for kc in range(8):
                t = P.op("tensor", lambda e, kc=kc: e.matmul(dst, lhsT=hnT[b][:, kc, :], rhs=wb[:, kc, s_, c0:c0 + 512],
                                                            start=(kc == 0), stop=(kc == 7)),
                         waits=[t_hT, tw, extra])
            return t
        t_gs_ = []
        tv = None
        tvm = None
        for s_ in range(2):
            pp = pqs[s_]
            jk = junks[s_]
            tq = None
            for hf in range(2):
                tq = mm(pp[:, hf * 512:(hf + 1) * 512], s_, hf * 512, L("pq_use%d" % s_))
            hf = s_
            tvm = mm(pv[:], 2, hf * 512, L("pv_use"))
            t_sq2 = P.op("scalar", lambda e, pp=pp, jk=jk: e.activation(out=jk[:], in_=pp[:], func=AF.Square), waits=[tq, L("junk%d" % s_)])
            pvv = pv[:].rearrange("p (i e d) -> p i e d", e=2, d=HD)
            vv = va[:, hf * 8:(hf + 1) * 8, :].rearrange("p (i e) c -> p i e c", e=2)
            tv0 = P.op("scalar", lambda e, pvv=pvv, vv=vv: e.activation(out=vv[:, :, 0, 0:HD], in_=pvv[:, :, 0, :], func=AF.Copy),
                       waits=[tvm, L("vaug_use%d" % b)] + t_ones)
            tv = P.op("scalar", lambda e, pvv=pvv, vv=vv: e.activation(out=vv[:, :, 1, HD:128], in_=pvv[:, :, 1, :], func=AF.Copy),
                      waits=[tvm, L("vaug_use%d" % b)] + t_ones)
            last["pv_use"] = tv
            t_ss2 = P.op("vector", lambda e, jk=jk, s_=s_: e.tensor_reduce(out=ss2[:, s_ * 16:(s_ + 1) * 16], in_=jk[:].rearrange("p (h d) -> p h d", d=HD),
                                                                        op=ALU.add, axis=AX.X), waits=[t_sq2, L("ss2_use%d" % s_)])
            last["junk%d" % s_] = t_ss2
            t_sd2 = P.op("scalar", lambda e, s_=s_: e.activation(out=sd2[:, s_ * 16:(s_ + 1) * 16], in_=ss2[:, s_ * 16:(s_ + 1) * 16], func=AF.Sqrt,
                                                                  scale=1.0 / HD, bias=EPS), waits=[t_ss2, L("sd2_use%d" % s_)])
            last["ss2_use%d" % s_] = t_sd2
            t_rs2 = P.op("vector", lambda e, s_=s_: e.reciprocal(out=rs2[:, s_ * 16:(s_ + 1) * 16], in_=sd2[:, s_ * 16:(s_ + 1) * 16]),
                         waits=[t_sd2, L("rs2_use%d" % s_)])
            last["sd2_use%d" % s_] = t_rs2
            t_nm = P.op("vector", lambda e, pp=pp, s_=s_: e.tensor_tensor(out=qkb[:, s_, :, :],
                                                                           in0=pp[:].rearrange("p (h d) -> p h d", d=HD),
                                                                           in1=rs2[:, s_ * 16:(s_ + 1) * 16].unsqueeze(2).broadcast_to([128, H, HD]), op=ALU.mult),
                        waits=[t_rs2, L("qk_use%d" % b)])
            last["rs2_use%d" % s_] = t_nm
            last["pq_use%d" % s_] = t_nm
            t_gs_.append(P.op(GAIN_ENG, lambda e, s_=s_: e.tensor_tensor(out=qkb[:, s_, :, :], in0=qkb[:, s_, :, :],
                                                                          in1=gqk[:, gsel, s_, :].unsqueeze(1).broadcast_to([128, H, HD]),
                                                                          op=ALU.mult), waits=[t_nm] + t_gq))
        st["t_g"] = t_gs_
        tw_use[g % len(wsb)] = tv
        t_vo = P.op("sync", lambda e: e.dma_start(out=T["Vs"][layer][g, n * 128:(n + 1) * 128, :, :], in_=va[:]),
                    waits=[tv], sem="d_vo%d" % b, dma=True)
        last["vaug_use%d" % b] = t_vo
        t_outdma.append(t_vo)
        last["hnT_use%d" % b] = tvm

    def stage1r(idx):
        g, n = tiles[idx]
        st = state[idx]
        b = idx % 2
        qkb = qk[b]
        t_g = st["t_g"]
        qv = qkb[:].rearrange("p s h d -> p (s h) d")
        x1 = qv[:, :, 0:8]
        x2 = qv[:, :, 8:16]
        cb = ropecs[g % 2][:, n, :].unsqueeze(1).broadcast_to([128, 32, 8])
        sbb = ropess[g % 2][:, n, :].unsqueeze(1).broadcast_to([128, 32, 8])
        tr = state["rope%d" % g]
        w0 = [t_g, tr, L("rt_use")]
        a1 = P.op("vector", lambda e: e.tensor_tensor(out=rt[0][:], in0=x1, in1=cb, op=ALU.mult), waits=w0)
        a2 = P.op("vector", lambda e: e.tensor_tensor(out=rt[1][:], in0=x2, in1=sbb, op=ALU.mult), waits=w0)
        a3 = P.op("vector", lambda e: e.tensor_tensor(out=rt[2][:], in0=x2, in1=cb, op=ALU.mult), waits=w0)
        a4 = P.op("vector", lambda e: e.tensor_tensor(out=rt[3][:], in0=x1, in1=sbb, op=ALU.mult), waits=w0)
        a5 = P.op("vector", lambda e: e.tensor_tensor(out=x1, in0=rt[0][:], in1=rt[1][:], op=ALU.subtract), waits=[a1, a2, a3, a4])
        a6 = P.op("vector", lambda e: e.tensor_tensor(out=x2, in0=rt[2][:], in1=rt[3][:], op=ALU.add), waits=[a1, a2, a3, a4, a5])
        last["rt_use"] = a6
        last["rope_use%d" % (g % 2)] = a6
        st["qkg"] = [a5, a6]

    def stage2(idx):
        g, n = tiles[idx]
        st = state[idx]
        b = idx % 2
        qkb = qk[b]
        n4, j4 = divmod(n, 4)
        b4 = (idx // 4) % 2
        for s_, dst4, nm in ((0, qT4[b4], "Q"), (1, kT4[b4], "K")):
            tt = None
            for c in range(8):
                tt = P.op("tensor", lambda e, c=c, s_=s_: e.transpose(out=ptr[:, c, :], in_=qkb[:, s_, 2 * c:2 * c + 2, :].rearrange("p h d -> p (h d)"),
                                                                      identity=ident_f[:]),
                          waits=[st["qkg"], t_idf, L("ptr_use")])
            wv = [tt, L("%s4_use%d" % (nm, b4))]
            if moba and s_ == 0:
                t32 = P.op("scalar", lambda e: e.activation(out=qT32[:], in_=ptr[:], func=AF.Copy), waits=[tt, L("qT32_use")])
                te = P.op("gpsimd", lambda e, dst4=dst4: e.tensor_copy(out=dst4[:, :, j4 * 128:(j4 + 1) * 128], in_=qT32[:]),
                          waits=[t32, L("%s4_use%d" % (nm, b4))])
                last["qT32_use"] = te
                tl = [t32]
            else:
                te = P.op("scalar", lambda e, dst4=dst4: e.activation(out=dst4[:, :, j4 * 128:(j4 + 1) * 128], in_=ptr[:], func=AF.Copy),
                          waits=wv)
                tl = [te]
            if moba and s_ == 1:
                tkp = P.op("vector", lambda e: e.tensor_reduce(out=kpart[:], in_=ptr[:], op=ALU.add, axis=AX.X),
                           waits=[tt, te, L("kpart_use")])
                j = n // 2
                tks0 = P.op("vector", lambda e, j=j: e.tensor_tensor(out=ksum[0:HD, :, 0, j], in0=ksum[0:HD, :, 0, j], in1=kpart[0:HD, :], op=ALU.add),
                            waits=[tkp, t_ks0, L("ksum_w"), L("qT32_use")])
                tks = P.op("vector", lambda e, j=j: e.tensor_tensor(out=ksum[HD:128, :, 1, j], in0=ksum[HD:128, :, 1, j], in1=kpart[HD:128, :], op=ALU.add),
                           waits=[tkp, tks0, t_ks0, L("ksum_w"), L("qT32_use")])
                last["kpart_use"] = tks
                last["ksum_w"] = tks
                tl.append(tks)
            last["ptr_use"] = tl
            st["te%d" % s_] = te
        last["qk_use%d" % b] = last["ptr_use"]
        st["tq32"] = last["ptr_use"]
        if j4 == 3:
            for s_, src4, nm, dstT in ((0, qT4[b4], "Q", T["QTs"][layer]), (1, kT4[b4], "K", T["KTs"][layer])):
                tds = []
                for e2 in range(2):
                    dv = dstT[g].rearrange("(c e) r t -> e r c t", e=2)[e2, 0:HD, :, n4 * 512:(n4 + 1) * 512]
                    tds.append(P.op("sync", lambda e, dv=dv, src4=src4, e2=e2: e.dma_start(out=dv, in_=src4[e2 * HD:(e2 + 1) * HD, :, :]),
                                    waits=[state[idx - 3]["te%d" % s_], state[idx - 2]["te%d" % s_], state[idx - 1]["te%d" % s_], st["te%d" % s_]],
                                    sem="d_%so%d" % (nm, b4), dma=True))
                last["%s4_use%d" % (nm, b4)] = tds
                t_outdma.extend(tds)

    def gate_a(idx):
        g, n = tiles[idx]
        st = state[idx]
        ob = n // 2
        pg = pT[:, 0:256]
        tg = None
        tq32 = st["tq32"]
        for c in range(8):
            tg = P.op("tensor", lambda e, c=c: e.matmul(pg[:, c * 32:(c + 1) * 32], lhsT=qT32[:, c, :],
                                                        rhs=ksum[:, c, :, :].rearrange("p e j -> p (e j)"), start=True, stop=True),
                      waits=[tq32, L("ksum_w"), L("pT_use")])
        last["qT32_use"] = [tg, L("qT32_use")]
        t_gs = P.op("vector", lambda e: e.tensor_copy(out=gs[:].rearrange("p h j -> p (h j)"), in_=pg), waits=[tg, L("gs_use")])
        last["pT_use"] = t_gs
        t_ms = t_gs
        if ob < 16:
            t_ms = P.op("vector", lambda e: e.memset(gs[:, :, ob:16], -1e30), waits=[t_gs])
        tm8 = []
        for h in range(H):
            tm8.append(P.op("vector", lambda e, h=h: e.max(out=t8[:, h, :], in_=gs[:, h, :]), waits=[t_ms, L("t8_use")]))
        t_sel = P.op("vector", lambda e: e.tensor_tensor(out=mv[:], in0=gs[:], in1=t8[:, :, 2:3].broadcast_to([128, H, 16]), op=ALU.is_ge),
                     waits=tm8 + [L("mv_use")])
        last["t8_use"] = t_sel
        last["gs_use"] = t_sel
        t_mv = P.op("vector", lambda e: e.tensor_scalar(out=mv[:], in0=mv[:], scalar1=-1.0, scalar2=BIG, op0=ALU.add, op1=ALU.mult),
                    waits=[t_sel])
        t_own = P.op("vector", lambda e: e.memset(mv[:, :, ob:ob + 1], 0.0), waits=[t_mv])
        st["t_own"] = t_own

    def gate_b(idx):
        g, n = tiles[idx]
        st = state[idx]
        b4 = (idx // 4) % 2
        n4, j4 = divmod(n, 4)
        t_own = st["t_own"]
        pm = pT[:, 256:512].rearrange("p (a q) -> p a q", a=2)
        tt = None
        for a in range(2):
            tt = P.op("tensor", lambda e, a=a: e.transpose(out=pm[:, a, :], in_=mv[:, a * 8:(a + 1) * 8, :].rearrange("p h j -> p (h j)"),
                                                           identity=ident_f[:]), waits=[t_own, t_idf, L("pm_use"), L("pT_use")])
        last["mv_use"] = tt
        tm = P.op("vector", lambda e: e.tensor_copy(out=mT4[b4][:, :, j4 * 128:(j4 + 1) * 128], in_=pm), waits=[tt, L("m4_use%d" % b4)])
        last["pm_use"] = tm
        last["pT_use"] = [last["pT_use"], tm]
        st["tm"] = tm
        if j4 == 3:
            tds = []
            m4 = mT4[b4]
            for h in range(H):
                dv = T["QTs"][layer][0, h, HD:HD + 16, n4 * 512:(n4 + 1) * 512]
                tds.append(P.op("gpsimd", lambda e, dv=dv, h=h, m4=m4: e.dma_start(out=dv, in_=m4[(h % 8) * 16:(h % 8) * 16 + 16, h // 8, :]),
                                waits=[state[idx - 3]["tm"], state[idx - 2]["tm"], state[idx - 1]["tm"], st["tm"]],
                                sem="d_mo%d" % b4, dma=True))
            last["m4_use%d" % b4] = tds
            t_outdma.extend(tds)

    xload(0)
    xload(1)
    prepA(0)
    prepB(0)
    for idx in range(len(tiles)):
        if idx + 2 < len(tiles):
            xload(idx + 2)
        if idx + 1 < len(tiles):
            prepA(idx + 1)
        main_tile(idx)
        if moba and GATE_MODE == 0:
            if idx >= 3:
                gate_b(idx - 3)
            if idx >= 2:
                gate_a(idx - 2)
        if idx + 1 < len(tiles):
            prepB(idx + 1)
        stage1r(idx)
        if idx >= 1:
            stage2(idx - 1)
            if moba and GATE_MODE == 1:
                gate_a(idx - 1)
                gate_b(idx - 1)
    nt_ = len(tiles)
    if moba and GATE_MODE == 0:
        gate_b(nt_ - 3)
        gate_a(nt_ - 2)
    stage2(nt_ - 1)
    if moba and GATE_MODE == 1:
        gate_a(nt_ - 1)
        gate_b(nt_ - 1)
    if moba and GATE_MODE == 0:
        gate_b(nt_ - 2)
        gate_a(nt_ - 1)
        gate_b(nt_ - 1)
    P.op("gpsimd", lambda e: e.memset(ssq[:], 0.0), waits=t_outdma + [L("ssq_use")])
    P.flush()


def phase_attn(nc, sems, tag, T, layer):
    P = Prog(nc, sems, tag)
    moba = layer == 1
    ngrp = 1 if moba else 3
    xsrc = T["xin"][layer]
    xdst = T["xmid"][layer]
    wo = T["b_w_o"] if moba else T["a_w_o"]
    QTs, KTs, Vs = T["QTs"][layer], T["KTs"][layer], T["Vs"][layer]
    KR = HD + 16 if moba else HD

    attnT = P.sb("attnT", [128, 8, S], BF16)
    acc = [P.sb("acc%d" % i, [128, S], F32) for i in range(2)]
    den = P.sb("den", [128, S], F32)
    qt = [P.sb("qt%d" % i, [KR, S], BF16) for i in range(2)]
    kt = [P.sb("kt%d" % i, [KR, S], BF16) for i in range(2)]
    vt = [P.sb("vt%d" % i, [128, NT, 128], BF16) for i in range(2)]
    pbuf = [P.sb("pb%d" % i, [128, 1024], BF16) for i in range(3)]
    masks = P.sb("masks", [128, 3, 1024], BF16)
    wosb = P.sb("wosb", [128, 8, D], BF16)
    fin = P.sb("fin", [128, 1], F32)
    psS = [P.ps("psS%d" % i, [128, 1024], F32) for i in range(3)]
    psO = [P.ps("psO%d" % i, [128, 512], F32) for i in range(2)]

    t_mk = P.op("sync", lambda e: e.dma_start(out=masks[:], in_=T["bmask"]), sem="d_c", dma=True)
    t_oh = []
    if moba:
        for i in range(2):
            t_oh.append(P.op("sync", lambda e, i=i: e.dma_start(out=kt[i][HD:HD + 16, :], in_=T["onehot"]), sem="d_c", dma=True))
    if t_oh:
        t_mk = t_oh[-1]
        t_oh = [t_oh[-1]]
    t_wo = []
    for kc in range(8):
        t_wo.append(P.op("gpsimd", lambda e, kc=kc: e.dma_start(out=wosb[:, kc, :], in_=wo[0, kc * 128:(kc + 1) * 128, :]),
                         sem="d_wo", dma=True))
    last = {}

    def L(k):
        return last.get(k)

    cnt = {"s": 0, "o": 0, "p": 0, "ld": 0}
    t_attn = []

    def load_head(g, h):
        i = cnt["ld"] % 2
        cnt["ld"] += 1
        c, e2 = divmod(h, 2)
        toks = []
        if moba:
            qsrc = QTs[0, h, :, :]
            toks.append(P.op("sync", lambda e: e.dma_start(out=qt[i][:], in_=qsrc), waits=[L("ld_use%d" % i)], sem="d_ld%d" % i, dma=True))
        else:
            qsrc = QTs[g, h, 0:HD, :]
            toks.append(P.op("sync", lambda e: e.dma_start(out=qt[i][0:HD, :], in_=qsrc), waits=[L("ld_use%d" % i)], sem="d_ld%d" % i, dma=True))
        ksrc = KTs[g, h, 0:HD, :]
        toks.append(P.op("sync", lambda e: e.dma_start(out=kt[i][0:HD, :], in_=ksrc), waits=[L("ld_use%d" % i)], sem="d_ld%d" % i, dma=True))
        vsrc = Vs[g].rearrange("(n p) h c -> p n h c", p=128)[:, :, h, :]
        toks.append(P.op("gpsimd", lambda e: e.dma_start(out=vt[i][:], in_=vsrc), waits=[L("ld_use%d" % i)], sem="d_lv%d" % i, dma=True))
        return i, toks

    def acc_view(a, g, b):
        if moba or g == 0:
            return a[:, b * 512:(b + 1) * 512]
        dil = DILS[g]
        nbseg = 32 // dil
        if g == 1:
            n0 = 4 * b
            r, a0 = divmod(n0, nbseg)
            a0 *= 128
            return a[:].rearrange("p (a r) -> p r a", r=dil)[:, r, a0:a0 + 512]
        return a[:].rearrange("p (a r) -> p r a", r=dil)[:, 2 * b:2 * b + 2, :]

    items = []

    def band_item(g, h, b, ld, first, lastb):
        it = {}
        e2 = h % 2
        a = acc[e2]
        nbseg = 32 // DILS[g]

        def p1():
            i, tl = ld["get"]()
            si = cnt["s"] % 3
            cnt["s"] += 1
            ps = psS[si]
            pb = pbuf[si]
            ts = None
            for j in range(4):
                n = 4 * b + j
                npv = max(n - 1, 0)
                ts = P.op("tensor", lambda e, j=j, npv=npv, n=n: e.matmul(ps[:, (2 * j) * 128:(2 * j + 1) * 128], lhsT=kt[i][0:HD, npv * 128:(npv + 1) * 128],
                                                                           rhs=qt[i][0:HD, n * 128:(n + 1) * 128], start=True, stop=True),
                          waits=[tl, L("psS_use%d" % si)])
                ts = P.op("tensor", lambda e, j=j, n=n: e.matmul(ps[:, (2 * j + 1) * 128:(2 * j + 2) * 128], lhsT=kt[i][0:HD, n * 128:(n + 1) * 128],
                                                                  rhs=qt[i][0:HD, n * 128:(n + 1) * 128], start=True, stop=True),
                          waits=[tl, L("psS_use%d" % si)])
            te = P.op("scalar", lambda e: e.activation(out=pb[:], in_=ps[:], func=AF.Exp, scale=0.125),
                      waits=[ts, L("pb_use%d" % si)])
            last["psS_use%d" % si] = te
            if g == 2:
                mvv = 2
            elif (4 * b) % nbseg == 0:
                mvv = 1
            else:
                mvv = 0
            tm = P.op("vector", lambda e: e.tensor_tensor(out=pb[:], in0=pb[:], in1=masks[:, mvv, :], op=ALU.mult),
                      waits=[te, t_mk])
            it.update(i=i, tl=tl, si=si, pb=pb, tm=tm)

        def p2():
            i, tl, si, pb, tm = it["i"], it["tl"], it["si"], it["pb"], it["tm"]
            oi = cnt["o"] % 2
            cnt["o"] += 1
            po = psO[oi]
            to = None
            for j in range(4):
                n = 4 * b + j
                npv = max(n - 1, 0)
                to = P.op("tensor", lambda e, j=j, npv=npv: e.matmul(po[:, j * 128:(j + 1) * 128], lhsT=vt[i][:, npv, :], rhs=pb[:, (2 * j) * 128:(2 * j + 1) * 128],
                                                                      start=True, stop=False), waits=[tm, tl, L("psO_use%d" % oi)])
                to = P.op("tensor", lambda e, j=j, n=n: e.matmul(po[:, j * 128:(j + 1) * 128], lhsT=vt[i][:, n, :], rhs=pb[:, (2 * j + 1) * 128:(2 * j + 2) * 128],
                                                                  start=False, stop=True), waits=[tm, tl, L("psO_use%d" % oi)])
            last["pb_use%d" % si] = to
            av = acc_view(a, g, b)
            pov = po[:].rearrange("p (r a) -> p r a", r=2) if g == 2 else po[:]
            if first:
                ta = P.op("vector", lambda e: e.tensor_copy(out=av, in_=pov), waits=[to, L("acc_use%d_%d" % (e2, b))])
            else:
                ta = P.op("vector", lambda e: e.tensor_tensor(out=av, in0=av, in1=pov, op=ALU.add), waits=[to, L("acc_w%d" % e2)])
            last["psO_use%d" % oi] = ta
            last["acc_w%d" % e2] = ta
            if lastb:
                last["ld_use%d" % i] = to
        it["p1"] = p1
        it["p2"] = p2
        return it

    def moba_item(h, sc, kp, ld, po_box):
        it = {}
        e2 = h % 2
        a = acc[e2]
        nk = 4 * sc + 4
        kts = (2 * kp, 2 * kp + 1)

        def p1():
            i, tl = ld["get"]()
            si = cnt["s"] % 3
            cnt["s"] += 1
            ps = psS[si]
            pb = pbuf[si]
            ts = None
            for u, ktile in enumerate(kts):
                c0 = max(0, ktile - 4 * sc) * 128
                ts = P.op("tensor", lambda e, u=u, ktile=ktile, c0=c0: e.matmul(
                    ps[:, u * 512 + c0:(u + 1) * 512], lhsT=kt[i][:, ktile * 128:(ktile + 1) * 128],
                    rhs=qt[i][:, sc * 512 + c0:(sc + 1) * 512], start=True, stop=True),
                    waits=[tl, L("psS_use%d" % si)] + t_oh)
            c00 = max(0, kts[0] - 4 * sc) * 128
            te = P.op("scalar", lambda e: e.activation(out=pb[:, c00:1024], in_=ps[:, c00:1024], func=AF.Exp, scale=0.125),
                      waits=[ts, L("pb_use%d" % si)])
            last["psS_use%d" % si] = te
            tms = [te]
            for u, ktile in enumerate(kts):
                if ktile >= 4 * sc:
                    c0 = (ktile - 4 * sc) * 128
                    tms.append(P.op("vector", lambda e, u=u, c0=c0: e.tensor_tensor(out=pb[:, u * 512 + c0:u * 512 + c0 + 128],
                                                                                      in0=pb[:, u * 512 + c0:u * 512 + c0 + 128],
                                                                                      in1=masks[:, 0, 128:256], op=ALU.mult), waits=[te, t_mk]))
            it.update(i=i, tl=tl, si=si, pb=pb, tm=tms)

        def p2():
            i, tl, si, pb, tm = it["i"], it["tl"], it["si"], it["pb"], it["tm"]
            if kp == 0:
                po_box["oi"] = cnt["o"] % 2
                cnt["o"] += 1
            oi = po_box["oi"]
            po = psO[oi]
            to = None
            for u, ktile in enumerate(kts):
                c0 = max(0, ktile - 4 * sc) * 128
                to = P.op("tensor", lambda e, u=u, ktile=ktile, c0=c0: e.matmul(po[:, c0:512], lhsT=vt[i][:, ktile, :], rhs=pb[:, u * 512 + c0:(u + 1) * 512],
                                                                                start=(ktile == 0), stop=(ktile == nk - 1)),
                          waits=[tm, tl, L("psO_use%d" % oi)])
            last["pb_use%d" % si] = to
            if kts[1] == nk - 1:
                ta = P.op("vector", lambda e: e.tensor_copy(out=a[:, sc * 512:(sc + 1) * 512], in_=po[:]), waits=[to, L("acc_use%d_%d" % (e2, sc))])
                last["psO_use%d" % oi] = ta
                last["acc_w%d" % e2] = ta
                if sc == 7:
                    last["ld_use%d" % i] = to
        it["p1"] = p1
        it["p2"] = p2
        return it

    def finalize_item(pr):
        def fin_():
            t0 = P.op("sync", lambda e: e.dma_start(out=den[0:HD, :], in_=acc[0][HD:128, :]), waits=[L("acc_w0"), L("den_use")], sem="d_den", dma=True)
            t1 = P.op("sync", lambda e: e.dma_start(out=den[HD:128, :], in_=acc[1][0:HD, :]), waits=[L("acc_w1"), L("den_use")], sem="d_den", dma=True)
            dus = []
            for c in range(8):
                cs = slice(c * 512, (c + 1) * 512)
                tln = P.op("scalar", lambda e, cs=cs: e.activation(out=den[:, cs], in_=den[:, cs], func=AF.Ln), waits=[t0, t1])
                tex = P.op("scalar", lambda e, cs=cs: e.activation(out=den[:, cs], in_=den[:, cs], func=AF.Exp, scale=-1.0), waits=[tln])
                ta0 = P.op("vector", lambda e, cs=cs: e.tensor_tensor(out=attnT[0:HD, pr, cs], in0=acc[0][0:HD, cs], in1=den[0:HD, cs], op=ALU.mult),
                           waits=[tex, L("acc_w0")])
                ta1 = P.op("vector", lambda e, cs=cs: e.tensor_tensor(out=attnT[HD:128, pr, cs], in0=acc[1][HD:128, cs], in1=den[HD:128, cs], op=ALU.mult),
                           waits=[tex, L("acc_w1")])
                last["acc_use0_%d" % c] = [t0, t1, ta0]
                last["acc_use1_%d" % c] = [t0, t1, ta1]
                dus += [ta0, ta1]
                t_attn.extend([ta0, ta1])
            last["den_use"] = dus
        return fin_

    npairs = int(os.environ.get('KB_PAIRS', 8))
    for pr in range(npairs):
        for g in range(ngrp):
            for e2 in range(2):
                h = 2 * pr + e2
                ld = {}

                def get(ld=ld, g=g, h=h):
                    if "v" not in ld:
                        ld["v"] = load_head(g, h)
                    return ld["v"]
                ld["get"] = get
                if moba:
                    for sc in range(8):
                        box = {}
                        for kp in range(2 * sc + 2):
                            items.append(moba_item(h, sc, kp, ld, box))
                else:
                    for b in range(8):
                        items.append(band_item(g, h, b, ld, g == 0, b == 7))
        items[-1]["after"] = finalize_item(pr)
    LOOK = 2
    for k in range(min(LOOK, len(items))):
        items[k]["p1"]()
    for k in range(len(items)):
        if k + LOOK < len(items):
            items[k + LOOK]["p1"]()
        items[k]["p2"]()
        if "after" in items[k]:
            items[k]["after"]()

    xt = [P.sb("xo%d" % i, [128, D], F32) for i in range(2)]
    t_fin = []
    for n in range(int(os.environ.get('KB_WO', NT))):
        b = n % 2
        t_x = P.op("sync", lambda e, n=n, b=b: e.dma_start(out=xt[b][:], in_=xsrc[n * 128:(n + 1) * 128, :]),
                   waits=[L("xo_use%d" % b)], sem="d_xo%d" % b, dma=True)
        tadds = []
        for hf in range(2):
            si = cnt["s"] % 3
            cnt["s"] += 1
            ps = psS[si]
            tm = None
            for kc in range(8):
                tm = P.op("tensor", lambda e, kc=kc, n=n, hf=hf, ps=ps: e.matmul(ps[:, 0:512], lhsT=attnT[:, kc, n * 128:(n + 1) * 128],
                                                                           rhs=wosb[:, kc, hf * 512:(hf + 1) * 512], start=(kc == 0), stop=(kc == 7)),
                          waits=t_attn + t_wo + [L("psS_use%d" % si)])
            tadd = P.op("vector", lambda e, hf=hf, b=b, ps=ps: e.tensor_tensor(out=xt[b][:, hf * 512:(hf + 1) * 512], in0=xt[b][:, hf * 512:(hf + 1) * 512],
                                                                                  in1=ps[:, 0:512], op=ALU.add), waits=[tm, t_x])
            last["psS_use%d" % si] = tadd
            tadds.append(tadd)
        t_o = P.op("sync", lambda e, n=n, b=b: e.dma_start(out=xdst[n * 128:(n + 1) * 128, :], in_=xt[b][:]), waits=tadds, sem="d_xw%d" % b, dma=True)
        last["xo_use%d" % b] = t_o
        t_fin.append(t_o)
    P.op("gpsimd", lambda e: e.memset(fin[:], 0.0), waits=t_fin)
    P.flush()


def phase_ffn(nc, sems, tag, T, layer):
    P = Prog(nc, sems, tag)
    xsrc = T["xmid"][layer]
    xdst = T["xout"][layer]
    NB = 512
    nbat = S // NB
    NU = 5
    NT1 = 5
    NG = 3

    wup = P.sb("wup", [128, 8, 2 * DFF], BF16)
    wdn = P.sb("wdn", [128, NFF, D], BF16)
    gnorm = P.sb("gnorm", [128, D], F32)
    cw = P.sb("cw", [128, 3, 2 * NFF], F32)
    cb = P.sb("cb", [128, 2 * NFF], F32)
    ident_b = P.sb("identb", [128, 128], BF16)
    t_idb = mk_identity(P, ident_b)
    xt = [P.sb("xt%d" % i, [128, D], F32) for i in range(1)]
    xr = [P.sb("xr%d" % i, [128, D], F32) for i in range(1)]
    hn = P.sb("hn", [128, D], BF16)
    junk = hn
    hnT = [P.sb("hnT%d" % i, [128, 8, NB], BF16) for i in range(2)]
    hT = P.sb("hT", [128, NFF, NB], BF16)
    T1 = [P.sb("T1_%d" % i, [128, NB], F32) for i in range(NT1)]
    G = [P.sb("G%d" % i, [128, NB], F32) for i in range(NG)]
    hcur = P.sb("hcur", [128, 2 * NFF, 2], F32)
    hw = P.sb("hw", [128, 2 * NFF, 2], F32)
    htmp = [P.sb("htmp%d" % i, [128, 2 * NFF], F32) for i in range(2)]
    ssq = P.sb("ssq", [128, 1], F32)
    sd = P.sb("sd", [128, 1], F32)
    rstd = P.sb("rstd", [128, 1], F32)
    fin = P.sb("fin", [128, 1], F32)
    pT = P.ps("pT", [128, 512], F32)
    pTb = pT[:].bitcast(BF16)
    pu = [P.ps("pu%d" % i, [128, NB], F32) for i in range(NU)]
    pd = [P.ps("pd%d" % i, [128, 512], F32) for i in range(2)]

    t_c = []
    t_c.append(P.op("sync", lambda e: e.dma_start(out=gnorm[:], in_=T["ffn_norm"][layer:layer + 1, :].partition_broadcast(128)), sem="d_c", dma=True))
    for j in range(3):
        t_c.append(P.op("sync", lambda e, j=j: e.dma_start(out=cw[:, j, :], in_=T["ffn_conv_w"][layer, j, :].rearrange("(c p) -> p c", p=128),
                                                             allow_slow_non_contiguous=True),
                        sem="d_c", dma=True))
    t_c.append(P.op("sync", lambda e: e.dma_start(out=cb[:], in_=T["ffn_conv_b"][layer, :].rearrange("(c p) -> p c", p=128),
                                                   allow_slow_non_contiguous=True), sem="d_c", dma=True))
    t_hw0 = P.op("gpsimd", lambda e: e.memset(hw[:], 0.0))
    t_wu = []
    for c in range(2 * NFF // 4):
        for kc in range(8):
            t_wu.append(P.op("gpsimd", lambda e, c=c, kc=kc: e.dma_start(out=wup[:, kc, c * 512:(c + 1) * 512],
                                                                          in_=T["ffn_w_up"][layer, kc * 128:(kc + 1) * 128, c * 512:(c + 1) * 512]),
                             sem="d_wu%d" % c, dma=True))
    t_wd = []
    for j in range(NFF):
        t_wd.append(P.op("gpsimd", lambda e, j=j: e.dma_start(out=wdn[:, j, :], in_=T["ffn_w_down"][layer, j * 128:(j + 1) * 128, :]),
                         sem="d_wd", dma=True))
    last = {}

    def L(k):
        return last.get(k)

    cnt = {"u": 0, "t": 0, "d": 0, "x": 0, "r": 0, "g": 0}
    t_fin = []
    bstate = {}

    def proA(bt, j):
        b = 0
        r0 = bt * NB + j * 128
        t_x = P.op("sync", lambda e: e.dma_start(out=xt[b][:], in_=xsrc[r0:r0 + 128, :]),
                   waits=[L("xt_use%d" % b)], sem="d_x%d" % b, dma=True)
        t_sq = P.op("scalar", lambda e: e.activation(out=junk[:], in_=xt[b][:], func=AF.Square, accum_out=ssq[:]),
                    waits=[t_x, L("ssq_use"), L("hn_use")])
        t_sd = P.op("scalar", lambda e: e.activation(out=sd[:], in_=ssq[:], func=AF.Sqrt, scale=1.0 / D, bias=EPS), waits=[t_sq, L("sd_use")])
        last["ssq_use"] = t_sd
        t_rs = P.op("vector", lambda e: e.reciprocal(out=rstd[:], in_=sd[:]), waits=[t_sd, L("rstd_use")])
        last["sd_use"] = t_rs
        t_hn = P.op("vector", lambda e: e.scalar_tensor_tensor(out=hn[:], in0=xt[b][:], scalar=rstd[:, 0:1], in1=gnorm[:],
                                                               op0=ALU.mult, op1=ALU.mult), waits=[t_rs, L("hn_use")] + t_c)
        last["rstd_use"] = t_hn
        last["xt_use%d" % b] = t_hn
        bstate[(bt, j)] = t_hn

    def proB(bt, j):
        hb = hnT[bt % 2]
        t_hn = bstate[(bt, j)]
        tt = None
        for kc in range(8):
            tt = P.op("tensor", lambda e, kc=kc: e.transpose(out=pTb[:, kc * 128:(kc + 1) * 128], in_=hn[:, kc * 128:(kc + 1) * 128],
                                                             identity=ident_b[:]), waits=[t_hn, t_idb, L("pT_use")])
        last["hn_use"] = tt
        te = P.op("vector", lambda e: e.tensor_copy(out=hb[:, :, j * 128:(j + 1) * 128], in_=pTb.rearrange("p (a q) -> p a q", a=8)),
                  waits=[tt, L("hnT_use%d" % (bt % 2))])
        last["pT_use"] = te
        bstate.setdefault(bt, []).append(te)

    def prologue(bt):
        for j in range(4):
            proA(bt, j)
            proB(bt, j)

    def chunk(bt, ch, hb, thT):
        ui = cnt["u"] % NU
        cnt["u"] += 1
        ti = cnt["t"] % NT1
        cnt["t"] += 1
        p_ = pu[ui]
        t1 = T1[ti]
        wtok = t_wu[(ch // 4) * 8:(ch // 4) * 8 + 8]
        tm = None
        for kc in range(8):
            tm = P.op("tensor", lambda e, kc=kc: e.matmul(p_[:], lhsT=wup[:, kc, ch * 128:(ch + 1) * 128], rhs=hb[:, kc, :],
                                                          start=(kc == 0), stop=(kc == 7)),
                      waits=thT + wtok + [L("pu_use%d" % ui)])
        tB = P.op("scalar", lambda e: e.activation(out=t1[:], in_=p_[:], func=AF.Identity, scale=cw[:, 2, ch:ch + 1], bias=cb[:, ch:ch + 1]),
                  waits=[tm, L("T1_use%d" % ti)] + t_c)
        tS = P.op("vector", lambda e: e.tensor_copy(out=hcur[:, ch, :], in_=p_[:, NB - 2:NB]), waits=[tB, L("hcur_use")])
        tC = P.op("vector", lambda e: e.scalar_tensor_tensor(out=t1[:, 1:NB], in0=p_[:, 0:NB - 1], scalar=cw[:, 1, ch:ch + 1], in1=t1[:, 1:NB],
                                                             op0=ALU.mult, op1=ALU.add), waits=[tB])
        tD = P.op("vector", lambda e: e.scalar_tensor_tensor(out=t1[:, 2:NB], in0=p_[:, 0:NB - 2], scalar=cw[:, 0, ch:ch + 1], in1=t1[:, 2:NB],
                                                             op0=ALU.mult, op1=ALU.add), waits=[tC])
        last["pu_use%d" % ui] = tD
        tH = P.op("gpsimd", lambda e: e.tensor_tensor(out=t1[:, 0:2], in0=t1[:, 0:2], in1=hw[:, ch, :], op=ALU.add), waits=[tC, t_hw0, L("hw_w")])
        return tm, tS, tD, tH, ti, t1

    def batch(bt):
        hb = hnT[bt % 2]
        thT = bstate[bt]
        t_hT = []
        tSs = []
        tHs = []
        tm = None
        pend = []

        def finish(j, tDg, tHg, tig, t1g, tDv, tHv, tiv, t1v):
            gi = cnt["g"] % NG
            cnt["g"] += 1
            gg = G[gi]
            tSil = P.op("scalar", lambda e: e.activation(out=gg[:], in_=t1g[:], func=AF.Silu), waits=[tDg, tHg, L("G_use%d" % gi)])
            last["T1_use%d" % tig] = tSil
            tF = P.op("gpsimd", lambda e: e.tensor_tensor(out=hT[:, j, :], in0=t1v[:], in1=gg[:], op=ALU.mult),
                      waits=[tDv, tHv, tSil, L("hT_use")])
            last["G_use%d" % gi] = tF
            last["T1_use%d" % tiv] = tF
            t_hT.append(tF)

        for j in range(NFF):
            tm, tS1, tDg, tHg, tig, t1g = chunk(bt, j, hb, thT)
            tm, tS2, tDv, tHv, tiv, t1v = chunk(bt, NFF + j, hb, thT)
            if pend:
                finish(*pend.pop())
            pend.append((j, tDg, tHg, tig, t1g, tDv, tHv, tiv, t1v))
            tSs += [tS1, tS2]
            tHs += [tHg, tHv]
            if bt + 1 < nbat:
                if j in (2, 7, 12, 17):
                    proA(bt + 1, (j - 2) // 5)
                if j in (4, 9, 14, 19):
                    proB(bt + 1, (j - 4) // 5)
        finish(*pend.pop())
        last["hnT_use%d" % (bt % 2)] = tm
        a1 = P.op("gpsimd", lambda e: e.tensor_tensor(out=htmp[0][:], in0=hcur[:, :, 1], in1=cw[:, 1, :], op=ALU.mult), waits=tSs + tHs + t_c)
        a2 = P.op("gpsimd", lambda e: e.tensor_tensor(out=htmp[1][:], in0=hcur[:, :, 0], in1=cw[:, 0, :], op=ALU.mult), waits=tSs + tHs)
        a3 = P.op("gpsimd", lambda e: e.tensor_tensor(out=hw[:, :, 0], in0=htmp[0][:], in1=htmp[1][:], op=ALU.add), waits=[a1, a2])
        a4 = P.op("gpsimd", lambda e: e.tensor_tensor(out=hw[:, :, 1], in0=hcur[:, :, 1], in1=cw[:, 0, :], op=ALU.mult), waits=[a1, a2, a3])
        last["hw_w"] = [a3, a4]
        last["hcur_use"] = [a1, a2, a4]
        tdn = None
        for j4 in range(4):
            rb = 0
            r0 = bt * NB + j4 * 128
            t_xr = P.op("sync", lambda e, r0=r0, rb=rb: e.dma_start(out=xr[rb][:], in_=xsrc[r0:r0 + 128, :]),
                        waits=[L("xr_use%d" % rb)], sem="d_r%d" % rb, dma=True)
            tadds = []
            for hf in range(2):
                di = cnt["d"] % 2
                cnt["d"] += 1
                for j in range(NFF):
                    tdn = P.op("tensor", lambda e, j=j, j4=j4, hf=hf, di=di: e.matmul(pd[di][:], lhsT=hT[:, j, j4 * 128:(j4 + 1) * 128],
                                                                                      rhs=wdn[:, j, hf * 512:(hf + 1) * 512], start=(j == 0), stop=(j == NFF - 1)),
                               waits=t_hT + t_wd + [L("pd_use%d" % di)])
                tadd = P.op("vector", lambda e, hf=hf, di=di, rb=rb: e.tensor_tensor(out=xr[rb][:, hf * 512:(hf + 1) * 512],
                                                                                      in0=xr[rb][:, hf * 512:(hf + 1) * 512], in1=pd[di][:], op=ALU.add),
                            waits=[tdn, t_xr])
                last["pd_use%d" % di] = tadd
                tadds.append(tadd)
            t_o = P.op("sync", lambda e, r0=r0, rb=rb: e.dma_start(out=xdst[r0:r0 + 128, :], in_=xr[rb][:]),
                       waits=tadds, sem="d_xw%d" % rb, dma=True)
            last["xr_use%d" % rb] = t_o
            t_fin.append(t_o)
        last["hT_use"] = tdn

    prologue(0)
    for bt in range(nbat):
        batch(bt)
    P.op("gpsimd", lambda e: e.memset(fin[:], 0.0), waits=t_fin)
    P.flush()


def host_consts():
    pos = np.arange(S, dtype=np.float32)
    inv = (np.float32(500000.0) ** (-np.arange(0, 16, 2, dtype=np.float32) / np.float32(16))).astype(np.float32)
    ang = (pos[:, None] * inv[None, :]).astype(np.float32)
    cos = np.cos(ang).astype(np.float32)
    sin = np.sin(ang).astype(np.float32)
    ropec = np.zeros((3, 128, NT, 8), np.float32)
    ropes = np.zeros((3, 128, NT, 8), np.float32)
    for v, dil in enumerate(DILS):
        Lg = S // dil
        pp = np.arange(S)
        r, a = np.divmod(pp, Lg)
        t = a * dil + r
        ropec[v] = cos[t].reshape(NT, 128, 8).transpose(1, 0, 2)
        ropes[v] = sin[t].reshape(NT, 128, 8).transpose(1, 0, 2)
    k = np.arange(128)[:, None]
    q = np.arange(128)[None, :]
    prev = (k >= q).astype(np.float32)
    cur = (k <= q).astype(np.float32)
    zero = np.zeros_like(prev)
    bm = np.zeros((128, 3, 8, 128), np.float32)
    for j in range(4):
        bm[:, 0, 2 * j] = prev
        bm[:, 0, 2 * j + 1] = cur
        bm[:, 1, 2 * j] = zero if j == 0 else prev
        bm[:, 1, 2 * j + 1] = cur
        bm[:, 2, 2 * j] = zero if j % 2 == 0 else prev
        bm[:, 2, 2 * j + 1] = cur
    bm = bm.reshape(128, 3, 1024).astype(ml_dtypes.bfloat16)
    oh = (np.arange(S)[None, :] // 256 == np.arange(16)[:, None]).astype(np.float32).astype(ml_dtypes.bfloat16)
    return {"ropec": ropec, "ropes": ropes, "bmask": bm, "onehot": oh}


WEIGHT_SHAPES = {
    "attn_norm": [2, D], "a_w_qkv": [1, D, 9216], "a_q_norm": [1, 3, HD], "a_k_norm": [1, 3, HD], "a_w_o": [1, D, D],
    "b_w_qkv": [1, D, 3072], "b_q_norm": [1, HD], "b_k_norm": [1, HD], "b_w_o": [1, D, D],
    "ffn_norm": [2, D], "ffn_w_up": [2, D, 2 * DFF], "ffn_conv_w": [2, 3, 2 * DFF], "ffn_conv_b": [2, 2 * DFF],
    "ffn_w_down": [2, DFF, D],
}


def build_nc(phases=("A0", "B0", "C0", "A1", "B1", "C1"), debug_out=None):
    nc = bass.Bass("TRN2", target_bir_lowering=False)
    T = {}
    x = nc.dram_tensor("x", [S, D], F32, kind="ExternalInput").ap()
    for k, shp in WEIGHT_SHAPES.items():
        T[k] = nc.dram_tensor(k, shp, F32, kind="ExternalInput").ap()
    T["ropec"] = nc.dram_tensor("ropec", [3, 128, NT, 8], F32, kind="ExternalInput").ap()
    T["ropes"] = nc.dram_tensor("ropes", [3, 128, NT, 8], F32, kind="ExternalInput").ap()
    T["bmask"] = nc.dram_tensor("bmask", [128, 3, 1024], BF16, kind="ExternalInput").ap()
    T["onehot"] = nc.dram_tensor("onehot", [16, S], BF16, kind="ExternalInput").ap()
    y = nc.dram_tensor("y", [S, D], F32, kind="ExternalOutput").ap()

    def scratch(name, shape, dt):
        kind = "ExternalOutput" if (debug_out and name in debug_out) else "Internal"
        return nc.dram_tensor(name, shape, dt, kind=kind).ap()
    R1 = scratch("R1", [S, D], F32)
    R2 = scratch("R2", [S, D], F32)
    R3 = scratch("R3", [S, D], F32)
    T["xin"] = [x, R2]
    T["xmid"] = [R1, R3]
    T["xout"] = [R2, y]
    T["QTs"] = [scratch("QT0", [3, H, HD, S], BF16), scratch("QT1", [1, H, HD + 16, S], BF16)]
    T["KTs"] = [scratch("KT0", [3, H, HD, S], BF16), scratch("KT1", [1, H, HD, S], BF16)]
    T["Vs"] = [scratch("V0", [3, S, H, 128], BF16), scratch("V1", [1, S, H, 128], BF16)]
    sems = Sems(nc)
    fns = {"A": phase_qkv, "B": phase_attn, "C": phase_ffn}
    for ph in phases:
        fns[ph[0]](nc, sems, ph + "_", T, int(ph[1]))
    return nc


_CACHE = {}


def kernel(**inputs):
    if "nc" not in _CACHE:
        _CACHE["nc"] = build_nc()
        _CACHE["consts"] = host_consts()
    nc = _CACHE["nc"]
    consts = _CACHE["consts"]
    x = np.ascontiguousarray(np.asarray(inputs["x"], dtype=np.float32))
    shared = {k: np.ascontiguousarray(np.asarray(inputs[k], dtype=np.float32)) for k in WEIGHT_SHAPES}
    shared.update(consts)
    in_maps = []
    for b in range(8):
        m = dict(shared)
        m["x"] = x[b]
        in_maps.append(m)
    res = run_bass_kernel_spmd(nc, in_maps, core_ids=list(range(8)))
    return np.stack([np.asarray(r["y"], dtype=np.float32) for r in res.results], axis=0)
```
